# Optimizing a Trainium2 kernel written in Bass

```python
import math
import jax
import jax.numpy as jnp
from jax import lax
import numpy as np

D_MODEL = 1024
BATCH = 16
SEQ = 2048
DEPTH = 2

N_BRANCH = 4
BRANCH = 512
ROPE_THETA = 500000.0
EPS = 1e-6
Q_BLOCK = 128

DA_HEADS = 4
DA_DHALF = 64
DA_DV = 2 * DA_DHALF
DA_ROT = DA_DHALF // 4

MB_HEADDIM = 64
MB_HEADS = BRANCH // MB_HEADDIM
MB_GROUPS = 2
MB_PER_GROUP = MB_HEADS // MB_GROUPS
MB_STATE = 128
MB_CONV = 4
MB_CHUNK = 128
MB_CONV_DIM = BRANCH + 2 * MB_GROUPS * MB_STATE

S5_GROUP = 16
S5_GROUPS = BRANCH // S5_GROUP
S5_STATE = 64

MLA_HEADS = 4
MLA_Q_RANK = 256
MLA_KV_RANK = 128
MLA_NOPE = 128
MLA_ROPE = 64
MLA_V = BRANCH // MLA_HEADS
MLA_QK = MLA_NOPE + MLA_ROPE

IN_LAYOUT = (
    ('da_q', DA_HEADS * 2 * DA_DHALF),
    ('da_k', DA_HEADS * 2 * DA_DHALF),
    ('da_v', DA_HEADS * DA_DV),
    ('da_gate', BRANCH),
    ('mb_z', BRANCH),
    ('mb_xbc', MB_CONV_DIM),
    ('mb_dt', MB_HEADS),
    ('s5_u', BRANCH),
    ('s5_gate', BRANCH),
    ('mla_cq', MLA_Q_RANK),
    ('mla_ckv', MLA_KV_RANK),
    ('mla_krope', MLA_ROPE),
    ('mla_gate', BRANCH),
    ('gate_a', D_MODEL),
    ('gate_b', D_MODEL),
    ('gate_c', D_MODEL),
    ('gate_d', D_MODEL),
)
D_IN = sum(n for _, n in IN_LAYOUT)

kernel_name = 'hybrid_gated_diffattn_ssd_s5_mla'


def in_proj(h, w, name):
    start = 0
    for nm, n in IN_LAYOUT:
        if nm == name:
            return h @ w[:, start:start + n]
        start += n
    raise KeyError(name)


def rmsnorm(x, g):
    xf = x.astype(jnp.float32)
    y = xf * lax.rsqrt(jnp.mean(xf * xf, axis=-1, keepdims=True) + EPS)
    return (y * g.astype(jnp.float32)).astype(x.dtype)


def rope_tables(positions, rot_dim, dtype):
    inv_freq = 1.0 / (ROPE_THETA ** (jnp.arange(0, rot_dim, 2, dtype=jnp.float32) / rot_dim))
    ang = positions.astype(jnp.float32)[..., None] * inv_freq
    return jnp.cos(ang).astype(dtype), jnp.sin(ang).astype(dtype)


def rotate(x, cos, sin):
    half = x.shape[-1] // 2
    x1, x2 = x[..., :half], x[..., half:]
    return jnp.concatenate([x1 * cos - x2 * sin, x2 * cos + x1 * sin], axis=-1)


def causal_block_attention(q, k, v, scale):
    bsz, s = q.shape[0], q.shape[1]
    nb = s // Q_BLOCK
    qb = q.reshape((bsz, nb, Q_BLOCK) + q.shape[2:]).swapaxes(0, 1)
    kpos = jnp.arange(s)

    def one_block(args):
        qi, i = args
        sc = jnp.einsum('bqhmd,bkhmd->bhmqk', qi, k).astype(jnp.float32) * scale
        qpos = i * Q_BLOCK + jnp.arange(Q_BLOCK)
        sc = jnp.where(kpos[None, :] <= qpos[:, None], sc, -jnp.inf)
        p = jax.nn.softmax(sc, axis=-1).astype(v.dtype)
        return jnp.einsum('bhmqk,bkhe->bqhme', p, v)

    out = lax.map(one_block, (qb, jnp.arange(nb)))
    return out.swapaxes(0, 1).reshape((bsz, s) + out.shape[3:])


def diff_attention(q, k, v, gate, positions, q_g, k_g, lq1, lk1, lq2, lk2, subln_g, lambda_init):
    bsz, s = q.shape[0], q.shape[1]
    f32 = jnp.float32
    q = rmsnorm(q.reshape(bsz, s, DA_HEADS, 2, DA_DHALF), q_g)
    k = rmsnorm(k.reshape(bsz, s, DA_HEADS, 2, DA_DHALF), k_g)
    cos, sin = rope_tables(positions, DA_ROT, q.dtype)
    cos, sin = cos[:, :, None, None, :], sin[:, :, None, None, :]
    q = jnp.concatenate([rotate(q[..., :DA_ROT], cos, sin), q[..., DA_ROT:]], axis=-1)
    k = jnp.concatenate([rotate(k[..., :DA_ROT], cos, sin), k[..., DA_ROT:]], axis=-1)
    v = v.reshape(bsz, s, DA_HEADS, DA_DV)
    o = causal_block_attention(q, k, v, DA_DHALF ** -0.5)
    lam = (jnp.exp(jnp.sum(lq1.astype(f32) * lk1.astype(f32)))
           - jnp.exp(jnp.sum(lq2.astype(f32) * lk2.astype(f32))) + lambda_init)
    o = o[..., 0, :] - lam.astype(o.dtype) * o[..., 1, :]
    o = rmsnorm(o, subln_g) * (1.0 - lambda_init)
    return o.reshape(bsz, s, BRANCH) * jax.nn.silu(gate)


def causal_depthwise_conv(x, w, b):
    c = x.shape[-1]
    y = lax.conv_general_dilated(x, w[:, None, :].astype(x.dtype), window_strides=(1,),
                                 padding=[(MB_CONV - 1, 0)],
                                 dimension_numbers=('NWC', 'WIO', 'NWC'),
                                 feature_group_count=c)
    return y + b


def segsum(a):
    t = a.shape[-1]
    cs = jnp.cumsum(a, axis=-1)
    diff = cs[..., :, None] - cs[..., None, :]
    return jnp.where(jnp.tril(jnp.ones((t, t), dtype=bool)), diff, -jnp.inf)


def ssd_chunked(X, A, Bm, Cm):
    bsz, s, g, r, p = X.shape
    n = Bm.shape[-1]
    c, l = s // MB_CHUNK, MB_CHUNK
    Xc = X.reshape(bsz, c, l, g, r, p)
    Bc = Bm.reshape(bsz, c, l, g, n)
    Cc = Cm.reshape(bsz, c, l, g, n)
    Ac = A.reshape(bsz, c, l, g, r).transpose(0, 1, 3, 4, 2)
    Acum = jnp.cumsum(Ac, axis=-1)
    Lmat = jnp.exp(segsum(Ac))
    CB = jnp.einsum('bclgn,bcsgn->bcgls', Cc, Bc)
    y_diag = jnp.einsum('bcgrls,bcsgrp->bclgrp', CB[:, :, :, None] * Lmat, Xc)
    decay_states = jnp.exp(Acum[..., -1:] - Acum)
    states = jnp.einsum('bclgn,bcgrl,bclgrp->bcgrpn', Bc, decay_states, Xc)
    states = jnp.concatenate([jnp.zeros_like(states[:, :1]), states], axis=1)
    a_last = jnp.pad(Acum[..., -1], ((0, 0), (1, 0), (0, 0), (0, 0)))
    decay_chunk = jnp.exp(segsum(a_last.transpose(0, 2, 3, 1)))
    new_states = jnp.einsum('bgrzc,bcgrpn->bzgrpn', decay_chunk, states)
    states_in = new_states[:, :-1]
    y_off = jnp.einsum('bclgn,bcgrpn,bcgrl->bclgrp', Cc, states_in, jnp.exp(Acum))
    return (y_diag + y_off).reshape(bsz, s, g, r, p)


def mamba2_branch(z, xbc, dt, conv_w, conv_b, dt_bias, a_log, d_skip, norm_g):
    bsz, s = z.shape[0], z.shape[1]
    f32 = jnp.float32
    xbc = jax.nn.silu(causal_depthwise_conv(xbc, conv_w, conv_b))
    xs = xbc[..., :BRANCH]
    Bm = xbc[..., BRANCH:BRANCH + MB_GROUPS * MB_STATE]
    Cm = xbc[..., BRANCH + MB_GROUPS * MB_STATE:]
    dt = jax.nn.softplus(dt.astype(f32) + dt_bias.astype(f32))
    A = -jnp.exp(a_log.astype(f32)).reshape(MB_GROUPS, MB_PER_GROUP)
    dtg = dt.reshape(bsz, s, MB_GROUPS, MB_PER_GROUP)
    X = xs.astype(f32).reshape(bsz, s, MB_GROUPS, MB_PER_GROUP, MB_HEADDIM)
    y = ssd_chunked(X * dtg[..., None], dtg * A,
                    Bm.astype(f32).reshape(bsz, s, MB_GROUPS, MB_STATE),
                    Cm.astype(f32).reshape(bsz, s, MB_GROUPS, MB_STATE))
    y = y + d_skip.astype(f32).reshape(MB_GROUPS, MB_PER_GROUP)[:, :, None] * X
    y = y.reshape(bsz, s, BRANCH).astype(z.dtype) * jax.nn.silu(z)
    y = rmsnorm(y.reshape(bsz, s, MB_GROUPS, BRANCH // MB_GROUPS),
                norm_g.reshape(MB_GROUPS, BRANCH // MB_GROUPS))
    return y.reshape(bsz, s, BRANCH)


def complex_linear_combine(e1, e2):
    a1r, a1i, b1r, b1i = e1
    a2r, a2i, b2r, b2i = e2
    return (a2r * a1r - a2i * a1i, a2r * a1i + a2i * a1r,
            a2r * b1r - a2i * b1i + b2r, a2r * b1i + a2i * b1r + b2i)


def s5_branch(u, gate, lam_re, lam_im, log_step, b_re, b_im, c_re, c_im, d_skip, w_glu, b_glu):
    bsz, s = u.shape[0], u.shape[1]
    f32 = jnp.float32
    uf = u.astype(f32)
    ug = uf.reshape(bsz, s, S5_GROUPS, S5_GROUP)
    lr, li = lam_re.astype(f32), lam_im.astype(f32)
    step = jnp.exp(log_step.astype(f32))[:, None]
    mag = jnp.exp(lr * step)
    ab_re, ab_im = mag * jnp.cos(li * step), mag * jnp.sin(li * step)
    den = lr * lr + li * li
    f_re = ((ab_re - 1.0) * lr + ab_im * li) / den
    f_im = (ab_im * lr - (ab_re - 1.0) * li) / den
    br, bi = b_re.astype(f32), b_im.astype(f32)
    bb_re = f_re[..., None] * br - f_im[..., None] * bi
    bb_im = f_re[..., None] * bi + f_im[..., None] * br
    bu_re = jnp.einsum('bsgc,gpc->sbgp', ug, bb_re)
    bu_im = jnp.einsum('bsgc,gpc->sbgp', ug, bb_im)
    a_re = jnp.broadcast_to(ab_re, (s, 1) + ab_re.shape)
    a_im = jnp.broadcast_to(ab_im, (s, 1) + ab_im.shape)
    _, _, h_re, h_im = lax.associative_scan(complex_linear_combine, (a_re, a_im, bu_re, bu_im), axis=0)
    y = (jnp.einsum('sbgp,gcp->bsgc', h_re, c_re.astype(f32))
         - jnp.einsum('sbgp,gcp->bsgc', h_im, c_im.astype(f32)))
    y = y.reshape(bsz, s, BRANCH) + d_skip.astype(f32) * uf
    y = jax.nn.gelu(y)
    y = y * jax.nn.sigmoid(y @ w_glu.astype(f32) + b_glu.astype(f32))
    return y.astype(u.dtype) * jax.nn.silu(gate)


def mla_branch(c_q, c_kv, k_rope, gate, positions, q_a_norm, w_uq, kv_a_norm, w_ukv, q_norm, k_norm):
    bsz, s = c_q.shape[0], c_q.shape[1]
    q = (rmsnorm(c_q, q_a_norm) @ w_uq).reshape(bsz, s, MLA_HEADS, MLA_QK)
    kv = (rmsnorm(c_kv, kv_a_norm) @ w_ukv).reshape(bsz, s, MLA_HEADS, MLA_NOPE + MLA_V)
    k_nope, v = kv[..., :MLA_NOPE], kv[..., MLA_NOPE:]
    k = jnp.concatenate([k_nope, jnp.broadcast_to(k_rope[:, :, None, :], (bsz, s, MLA_HEADS, MLA_ROPE))], axis=-1)
    q = rmsnorm(q, q_norm)
    k = rmsnorm(k, k_norm)
    cos, sin = rope_tables(positions, MLA_ROPE, q.dtype)
    cos, sin = cos[:, :, None, :], sin[:, :, None, :]
    q = jnp.concatenate([q[..., :MLA_NOPE], rotate(q[..., MLA_NOPE:], cos, sin)], axis=-1)
    k = jnp.concatenate([k[..., :MLA_NOPE], rotate(k[..., MLA_NOPE:], cos, sin)], axis=-1)
    o = causal_block_attention(q[:, :, :, None], k[:, :, :, None], v, MLA_QK ** -0.5)
    return o.reshape(bsz, s, BRANCH) * jax.nn.silu(gate)


def setup_inputs(seed: int = 0) -> dict:
    key = jax.random.key(seed)
    ks = list(jax.random.split(key, 64))
    f32 = jnp.float32
    L = DEPTH

    def normal(shape, scale):
        return scale * jax.random.normal(ks.pop(), shape, f32)

    def gain(shape):
        return 1.0 + 0.02 * jax.random.normal(ks.pop(), shape, f32)

    def log_uniform(shape, lo, hi):
        return jax.random.uniform(ks.pop(), shape, f32, math.log(lo), math.log(hi))

    x = normal((BATCH, SEQ, D_MODEL), 1.0)
    positions = (jnp.arange(SEQ, dtype=jnp.int32)[None, :]
                 + jax.random.randint(ks.pop(), (BATCH, 1), 0, SEQ, dtype=jnp.int32))
    dt0 = jnp.exp(log_uniform((L, MB_HEADS), 1e-3, 1e-1))
    mb_dt_bias = dt0 + jnp.log(-jnp.expm1(-dt0))
    mb_a_log = jnp.log(jax.random.uniform(ks.pop(), (L, MB_HEADS), f32, 1.0, 16.0))
    s5_lam_re = -0.5 + normal((L, S5_GROUPS, S5_STATE), 0.01)
    s5_lam_im = (jnp.pi * jnp.arange(S5_STATE, dtype=f32))[None, None, :] + normal((L, S5_GROUPS, S5_STATE), 0.01)
    return {
        'x': x,
        'positions': positions,
        'norm_g': gain((L, D_MODEL)),
        'w_in': normal((L, D_MODEL, D_IN), D_MODEL ** -0.5),
        'da_q_norm': gain((L, DA_DHALF)),
        'da_k_norm': gain((L, DA_DHALF)),
        'da_lambda_q1': normal((L, DA_DHALF), 0.1),
        'da_lambda_k1': normal((L, DA_DHALF), 0.1),
        'da_lambda_q2': normal((L, DA_DHALF), 0.1),
        'da_lambda_k2': normal((L, DA_DHALF), 0.1),
        'da_subln': gain((L, DA_DV)),
        'mb_conv_w': normal((L, MB_CONV, MB_CONV_DIM), MB_CONV ** -0.5),
        'mb_conv_b': normal((L, MB_CONV_DIM), 0.02),
        'mb_dt_bias': mb_dt_bias,
        'mb_a_log': mb_a_log,
        'mb_d': gain((L, MB_HEADS)),
        'mb_norm': gain((L, BRANCH)),
        's5_lam_re': s5_lam_re,
        's5_lam_im': s5_lam_im,
        's5_log_step': log_uniform((L, S5_GROUPS), 1e-3, 1e-1),
        's5_b_re': normal((L, S5_GROUPS, S5_STATE, S5_GROUP), (2 * S5_GROUP) ** -0.5),
        's5_b_im': normal((L, S5_GROUPS, S5_STATE, S5_GROUP), (2 * S5_GROUP) ** -0.5),
        's5_c_re': normal((L, S5_GROUPS, S5_GROUP, S5_STATE), S5_STATE ** -0.5),
        's5_c_im': normal((L, S5_GROUPS, S5_GROUP, S5_STATE), S5_STATE ** -0.5),
        's5_d': gain((L, BRANCH)),
        's5_w_glu': normal((L, BRANCH, BRANCH), BRANCH ** -0.5),
        's5_b_glu': normal((L, BRANCH), 0.02),
        'mla_q_a_norm': gain((L, MLA_Q_RANK)),
        'mla_w_uq': normal((L, MLA_Q_RANK, MLA_HEADS * MLA_QK), MLA_Q_RANK ** -0.5),
        'mla_kv_a_norm': gain((L, MLA_KV_RANK)),
        'mla_w_ukv': normal((L, MLA_KV_RANK, MLA_HEADS * (MLA_NOPE + MLA_V)), MLA_KV_RANK ** -0.5),
        'mla_q_norm': gain((L, MLA_QK)),
        'mla_k_norm': gain((L, MLA_QK)),
        'w_br': normal((L, N_BRANCH, BRANCH, D_MODEL), BRANCH ** -0.5),
        'w_out': normal((L, D_MODEL, D_MODEL), D_MODEL ** -0.5),
    }


def reference(x, positions, norm_g, w_in, da_q_norm, da_k_norm, da_lambda_q1, da_lambda_k1,
              da_lambda_q2, da_lambda_k2, da_subln, mb_conv_w, mb_conv_b, mb_dt_bias, mb_a_log,
              mb_d, mb_norm, s5_lam_re, s5_lam_im, s5_log_step, s5_b_re, s5_b_im, s5_c_re,
              s5_c_im, s5_d, s5_w_glu, s5_b_glu, mla_q_a_norm, mla_w_uq, mla_kv_a_norm,
              mla_w_ukv, mla_q_norm, mla_k_norm, w_br, w_out):
    gate_names = ('gate_a', 'gate_b', 'gate_c', 'gate_d')
    for l in range(DEPTH):
        h = rmsnorm(x, norm_g[l])
        w = w_in[l]
        lambda_init = 0.8 - 0.6 * math.exp(-0.3 * l)
        y_a = diff_attention(in_proj(h, w, 'da_q'), in_proj(h, w, 'da_k'), in_proj(h, w, 'da_v'),
                             in_proj(h, w, 'da_gate'), positions, da_q_norm[l], da_k_norm[l],
                             da_lambda_q1[l], da_lambda_k1[l], da_lambda_q2[l], da_lambda_k2[l],
                             da_subln[l], lambda_init)
        y_b = mamba2_branch(in_proj(h, w, 'mb_z'), in_proj(h, w, 'mb_xbc'), in_proj(h, w, 'mb_dt'),
                            mb_conv_w[l], mb_conv_b[l], mb_dt_bias[l], mb_a_log[l], mb_d[l], mb_norm[l])
        y_c = s5_branch(in_proj(h, w, 's5_u'), in_proj(h, w, 's5_gate'), s5_lam_re[l], s5_lam_im[l],
                        s5_log_step[l], s5_b_re[l], s5_b_im[l], s5_c_re[l], s5_c_im[l], s5_d[l],
                        s5_w_glu[l], s5_b_glu[l])
        y_d = mla_branch(in_proj(h, w, 'mla_cq'), in_proj(h, w, 'mla_ckv'), in_proj(h, w, 'mla_krope'),
                         in_proj(h, w, 'mla_gate'), positions, mla_q_a_norm[l], mla_w_uq[l],
                         mla_kv_a_norm[l], mla_w_ukv[l], mla_q_norm[l], mla_k_norm[l])
        branches = (y_a, y_b, y_c, y_d)
        merged = jax.nn.sigmoid(in_proj(h, w, gate_names[0])) * (branches[0] @ w_br[l, 0])
        for n in range(1, N_BRANCH):
            merged = merged + jax.nn.sigmoid(in_proj(h, w, gate_names[n])) * (branches[n] @ w_br[l, n])
        x = x + merged @ w_out[l]
    return x
```

```python
import numpy as np
import concourse.bass as bass
import concourse.mybir as mybir
from concourse.alu_op_type import AluOpType as ALU
from concourse.bass_utils import run_bass_kernel_spmd

F32 = mybir.dt.float32
BF16 = mybir.dt.bfloat16
I32 = mybir.dt.int32
AF = mybir.ActivationFunctionType
AX = mybir.AxisListType


class Op:
    __slots__ = ("eng", "fn", "reads", "writes", "dma", "waits", "inc", "ticket", "idx", "sem")

    def __init__(self, eng, fn, reads, writes, dma):
        self.eng, self.fn, self.reads, self.writes, self.dma = eng, fn, reads, writes, dma
        self.waits = []
        self.inc = False
        self.ticket = None
        self.sem = None


class Sched:
    ENGS = ("pe", "dve", "act", "pool", "sp")

    def __init__(self, nc, n_dma_sems=12):
        self.nc = nc
        self.ops = []
        self.n_dma_sems = n_dma_sems
        self._bar_ops = set()
        import os
        self.same_sync = True

    def add(self, eng, fn, reads=(), writes=(), dma=False):
        op = Op(eng, fn, tuple(reads), tuple(writes), dma)
        op.idx = len(self.ops)
        self.ops.append(op)
        return op

    def pe(self, fn, r=(), w=()):
        return self.add("pe", fn, r, w)

    def dve(self, fn, r=(), w=()):
        return self.add("dve", fn, r, w)

    def act(self, fn, r=(), w=()):
        return self.add("act", fn, r, w)

    def pool(self, fn, r=(), w=()):
        return self.add("pool", fn, r, w)

    def dma(self, q, fn, r=(), w=()):
        return self.add(q, fn, r, w, dma=True)

    def barrier(self):
        for e in self.ENGS:
            self.add(e, lambda eng: eng.drain(), (), (("__barX__", e),))
        for e in self.ENGS:
            op = self.add(e, None, tuple(("__barX__", f) for f in self.ENGS), ())
            op_bar = op
            self._bar_ops.add(op.idx)

    def finalize(self):
        last_writer = {}
        readers = {}
        cnt = {e: 0 for e in self.ENGS}
        dma_sem_total = {}
        dma_rr = {e: 0 for e in self.ENGS}
        waited = {}
        deps_of = []
        all_dma = []
        for op in self.ops:
            deps = set()
            for r in op.reads:
                j = last_writer.get(r)
                if j is not None:
                    deps.add(j)
            for w in op.writes:
                j = last_writer.get(w)
                if j is not None:
                    deps.add(j)
                for j in readers.get(w, ()):
                    deps.add(j)
            if op.idx in self._bar_ops:
                deps.update(all_dma)
            if op.dma:
                all_dma.append(op.idx)
            deps.discard(op.idx)
            deps = set(j for j in deps if self.ops[j].fn is not None)
            deps_of.append(deps)
            for r in op.reads:
                readers.setdefault(r, []).append(op.idx)
            for w in op.writes:
                last_writer[w] = op.idx
                readers[w] = []
        need_inc = [False] * len(self.ops)
        for op in self.ops:
            for j in deps_of[op.idx]:
                pj = self.ops[j]
                if pj.dma or pj.eng != op.eng or op.dma or (self.same_sync and pj.eng != 'pe'):
                    need_inc[j] = True
        for op in self.ops:
            if op.dma:
                need_inc[op.idx] = True
        for op in self.ops:
            e = op.eng
            if op.dma:
                k = dma_rr[e] % self.n_dma_sems
                dma_rr[e] += 1
                key = ("d", e, k)
                prev = dma_sem_total.get(key, 0)
                if prev > 0 and waited.get((e, key), 0) < prev:
                    op.waits.append((key, prev))
                    waited[(e, key)] = prev
                dma_sem_total[key] = prev + 16
                op.sem = key
                op.ticket = prev + 16
                op.inc = True
            elif need_inc[op.idx]:
                cnt[e] += 1
                op.sem = ("c", e)
                op.ticket = cnt[e]
                op.inc = True
            for j in sorted(deps_of[op.idx]):
                pj = self.ops[j]
                if (not pj.dma) and pj.eng == e and not op.dma and (e == 'pe' or not self.same_sync):
                    continue
                if (not pj.dma) and pj.eng == e and op.dma:
                    pass
                key, val = pj.sem, pj.ticket
                if waited.get((e, key), 0) >= val:
                    continue
                waited[(e, key)] = val
                op.waits.append((key, val))

    def emit(self, out_waits=()):
        nc = self.nc
        self.finalize()
        keys = set()
        for op in self.ops:
            if op.sem is not None:
                keys.add(op.sem)
        keys = sorted(keys)
        sems = {}
        import contextlib
        with contextlib.ExitStack() as st:
            for i, k in enumerate(keys):
                sems[k] = st.enter_context(nc.semaphore("s_" + "_".join(str(x) for x in k)))
            block = st.enter_context(nc.Block())
            ops = self.ops

            def run(eng_name, eng):
                last = None
                for op in ops:
                    if op.eng != eng_name:
                        continue
                    for (k, v) in op.waits:
                        eng.wait_ge(sems[k], v)
                    if op.fn is None:
                        assert not op.inc
                        continue
                    ins = op.fn(eng)
                    if op.inc:
                        ins.then_inc(sems[op.sem], 16 if op.dma else 1)
                    if op.dma:
                        last = op
                done = {}
                for op in ops:
                    if op.eng == eng_name and op.dma:
                        done[op.sem] = max(done.get(op.sem, 0), op.ticket)
                for k, v in done.items():
                    eng.wait_ge(sems[k], v)

            @block.tensor
            def _(e):
                run("pe", e)

            @block.vector
            def _(e):
                run("dve", e)

            @block.scalar
            def _(e):
                run("act", e)

            @block.gpsimd
            def _(e):
                run("pool", e)

            @block.sync
            def _(e):
                run("sp", e)


D_MODEL = 1024
D_IN = 9672
EPS = 1e-6
ROPE_THETA = 500000.0
OFF = dict(da_q=0, da_k=512, da_v=1024, da_gate=1536, mb_z=2048, mb_xbc=2560, mb_dt=3584, s5_u=3592,
           s5_gate=4104, mla_cq=4616, mla_ckv=4872, mla_krope=5000, mla_gate=5064, gate=5576)
TWO_PI = float(2.0 * np.pi)
MAGIC = 12582912.0

PB = {}
_o = 0
for _n, _w in [("ng", 8), ("daq", 1), ("dak", 1), ("convw", 32), ("convb", 8), ("s5d", 4), ("s5bglu", 4),
               ("mqa", 2), ("mkva", 1), ("mqn_n", 1), ("mqn_r", 1), ("mkn_n", 1), ("mkn_r", 1),
               ("lamre", 16), ("lamim", 16), ("lstep", 16),
               ("subln", 128), ("lq1", 64), ("lk1", 64), ("lq2", 64), ("lk2", 64),
               ("dtb", 8), ("alog", 8), ("mbd", 512), ("mbnorm", 512)]:
    PB[_n] = (_o, _w)
    _o += _w
NP_COLS = _o
CB = {}
_o = 0
for _n, _w in [("ident", 128), ("ones", 128), ("blk64", 128), ("maskU", 128), ("ptda", 128), ("ptmla", 128),
               ("jrow", 128), ("invf_da", 1), ("invf_mla", 1)]:
    CB[_n] = (_o, _w)
    _o += _w
NC_COLS = _o


def host_consts():
    c = np.zeros((128, NC_COLS), np.float32)
    def put(n, a):
        o, w = CB[n]
        c[:a.shape[0], o:o + w] = a
    i = np.arange(128)
    put("ident", np.eye(128, dtype=np.float32))
    put("ones", np.ones((128, 128), np.float32))
    put("blk64", (i[:, None] // 64 == i[None, :] // 64).astype(np.float32))
    put("maskU", (i[:, None] <= i[None, :]).astype(np.float32))
    P = np.zeros((128, 128), np.float32)
    for b in range(2):
        for d in range(8):
            P[b * 64 + d, b * 64 + d + 8] = -1.0
            P[b * 64 + d + 8, b * 64 + d] = 1.0
    put("ptda", P.T.copy())
    P = np.zeros((128, 128), np.float32)
    for d in range(32):
        P[d, d + 32] = -1.0
        P[d + 32, d] = 1.0
    put("ptmla", P.T.copy())
    put("jrow", np.broadcast_to(np.arange(128, dtype=np.float32)[None, :], (128, 128)))
    f_da = (1.0 / (ROPE_THETA ** (np.arange(0, 16, 2, dtype=np.float32) / np.float32(16)))).astype(np.float32)
    f_mla = (1.0 / (ROPE_THETA ** (np.arange(0, 64, 2, dtype=np.float32) / np.float32(64)))).astype(np.float32)
    v = np.zeros((128, 1), np.float32)
    for p in range(128):
        d = p % 64
        if d < 16:
            v[p, 0] = f_da[d % 8]
    put("invf_da", v)
    v = np.zeros((128, 1), np.float32)
    for p in range(64):
        v[p, 0] = f_mla[p % 32]
    put("invf_mla", v)
    return c


def host_params(inp, L):
    pb = np.zeros((L, 128, NP_COLS), np.float32)
    s5rep = np.zeros((L, 3, 128, 2048), np.float32)
    s5B = np.zeros((L, 2, 128, 16, 128), np.float32)
    s5C = np.zeros((L, 2, 128, 16, 128), np.float32)
    p = np.arange(128)
    for l in range(L):
        def put(n, a):
            o, w = PB[n]
            a = np.asarray(a, np.float32)
            if a.ndim == 1:
                a = a[:, None]
            pb[l, :a.shape[0], o:o + w] = a
        def rep(n, vec):
            o, w = PB[n]
            pb[l, :, o:o + w] = np.asarray(vec, np.float32).reshape(1, w)
        put("ng", inp["norm_g"][l].reshape(8, 128).T)
        put("daq", inp["da_q_norm"][l][p % 64])
        put("dak", inp["da_k_norm"][l][p % 64])
        cw = inp["mb_conv_w"][l]
        put("convw", cw.reshape(4, 8, 128).transpose(2, 1, 0).reshape(128, 32))
        put("convb", inp["mb_conv_b"][l].reshape(8, 128).T)
        put("s5d", inp["s5_d"][l].reshape(4, 128).T)
        put("s5bglu", inp["s5_b_glu"][l].reshape(4, 128).T)
        put("mqa", inp["mla_q_a_norm"][l].reshape(2, 128).T)
        put("mkva", inp["mla_kv_a_norm"][l])
        put("mqn_n", inp["mla_q_norm"][l][:128])
        put("mqn_r", inp["mla_q_norm"][l][128:192])
        put("mkn_n", inp["mla_k_norm"][l][:128])
        put("mkn_r", inp["mla_k_norm"][l][128:192])
        lre = inp["s5_lam_re"][l].reshape(16, 128).T
        lim = inp["s5_lam_im"][l].reshape(16, 128).T
        lst = np.repeat(inp["s5_log_step"][l], 64).reshape(16, 128).T
        put("lamre", lre); put("lamim", lim); put("lstep", lst)
        rep("subln", inp["da_subln"][l])
        rep("lq1", inp["da_lambda_q1"][l]); rep("lk1", inp["da_lambda_k1"][l])
        rep("lq2", inp["da_lambda_q2"][l]); rep("lk2", inp["da_lambda_k2"][l])
        rep("dtb", inp["mb_dt_bias"][l]); rep("alog", inp["mb_a_log"][l])
        rep("mbd", np.repeat(inp["mb_d"][l], 64)); rep("mbnorm", inp["mb_norm"][l])
        s5rep[l, 0] = inp["s5_lam_re"][l].reshape(1, 2048)
        s5rep[l, 1] = inp["s5_lam_im"][l].reshape(1, 2048)
        s5rep[l, 2] = np.repeat(inp["s5_log_step"][l], 64).reshape(1, 2048)
        for g in range(32):
            mt, j = g // 2, g % 2
            gi = g % 8
            s5B[l, 0, gi * 16:(gi + 1) * 16, mt, j * 64:(j + 1) * 64] = inp["s5_b_re"][l, g].T
            s5B[l, 1, gi * 16:(gi + 1) * 16, mt, j * 64:(j + 1) * 64] = inp["s5_b_im"][l, g].T
            s5C[l, 0, j * 64:(j + 1) * 64, mt, gi * 16:(gi + 1) * 16] = inp["s5_c_re"][l, g].T
            s5C[l, 1, j * 64:(j + 1) * 64, mt, gi * 16:(gi + 1) * 16] = inp["s5_c_im"][l, g].T
    return pb, s5rep, s5B, s5C


class _Shift:
    def __init__(self, ap):
        self.ap = ap

    def __getitem__(self, idx):
        p, f = idx
        return self.ap[:, f]


class Arena:
    def __init__(self, ap32, ncols):
        self.ap = ap32
        self.n = ncols
        self.off = 0

    def reset(self):
        self.off = 0

    def f32(self, *shape):
        n = int(np.prod(shape))
        assert self.off + n <= self.n, ("arena overflow", self.off, n, self.n)
        v = self.ap[:, self.off:self.off + n]
        self.off += n
        if len(shape) == 2:
            v = v.rearrange("p (a b) -> p a b", a=shape[0])
        return v

    def b16(self, *shape):
        n = int(np.prod(shape))
        n32 = (n + 1) // 2
        assert self.off + n32 <= self.n, ("arena overflow", self.off, n32, self.n)
        v = self.ap[:, self.off:self.off + n32].bitcast(BF16)
        if n32 * 2 != n:
            v = v[:, 0:n]
        self.off += n32
        if len(shape) == 2:
            v = v.rearrange("p (a b) -> p a b", a=shape[0])
        return v


def build_program(S_LEN=2048, NSEQ=2, DEPTH=2, branches=(0, 1, 2, 3), ARENA_COLS=19456, debug=False):
    import contextlib
    NT = S_LEN // 128
    BLK = min(512, S_LEN)
    NB = S_LEN // BLK
    TPB = BLK // 128
    L = DEPTH
    nc = bass.Bass("TRN2", target_bir_lowering=False)
    dx = nc.dram_tensor("x", [NSEQ, S_LEN, D_MODEL], F32, kind="ExternalInput").ap()
    dpos = nc.dram_tensor("pos", [NSEQ, S_LEN], I32, kind="ExternalInput").ap()
    dwin = nc.dram_tensor("w_in", [L, D_MODEL, D_IN], F32, kind="ExternalInput").ap()
    dwbr = nc.dram_tensor("w_br", [L, 4, 512, D_MODEL], F32, kind="ExternalInput").ap()
    dwout = nc.dram_tensor("w_out", [L, D_MODEL, D_MODEL], F32, kind="ExternalInput").ap()
    dwuq = nc.dram_tensor("w_uq", [L, 256, 768], F32, kind="ExternalInput").ap()
    dwukv = nc.dram_tensor("w_ukv", [L, 128, 1024], F32, kind="ExternalInput").ap()
    dwglu = nc.dram_tensor("w_glu", [L, 512, 512], F32, kind="ExternalInput").ap()
    dpb = nc.dram_tensor("pblob", [L, 128, NP_COLS], F32, kind="ExternalInput").ap()
    dcb = nc.dram_tensor("cblob", [128, NC_COLS], F32, kind="ExternalInput").ap()
    ds5rep = nc.dram_tensor("s5rep", [L, 3, 128, 2048], F32, kind="ExternalInput").ap()
    ds5B = nc.dram_tensor("s5B", [L, 2, 128, 16, 128], F32, kind="ExternalInput").ap()
    ds5C = nc.dram_tensor("s5C", [L, 2, 128, 16, 128], F32, kind="ExternalInput").ap()
    dout = nc.dram_tensor("out", [NSEQ, S_LEN, D_MODEL], F32, kind="ExternalOutput").ap()
    if debug:
        dbg_ht = nc.dram_tensor("dbg_ht", [128, 8 * S_LEN], F32, kind="ExternalOutput").ap()
        dbg_yb = nc.dram_tensor("dbg_yb", [128, 4 * S_LEN], F32, kind="ExternalOutput").ap()
        dbg_mg = nc.dram_tensor("dbg_mg", [128, 8 * S_LEN], F32, kind="ExternalOutput").ap()

    S = Sched(nc)
    st = contextlib.ExitStack()

    def sb(name, shape, dt=F32):
        return st.enter_context(nc.sbuf_tensor(name, shape, dt))[:]

    def psum(name, shape, dt=F32):
        return st.enter_context(nc.psum_tensor(name, shape, dt))[:]

    uid = [0]

    def U(prefix):
        uid[0] += 1
        return (prefix, uid[0])

    with st:
        CF = sb("CF", [128, NC_COLS])
        CBF = sb("CBF", [128, 6 * 128], BF16)
        PBL = sb("PBL", [128, NP_COLS])
        DRV = sb("DRV", [128, 256])
        HT = sb("HT", [128, 8, S_LEN], BF16)
        MG = sb("MG", [128, 8, S_LEN], BF16)
        YB = sb("YB", [128, 4, S_LEN], BF16)
        WB = [sb("WB%d" % i, [128, 4096], BF16) for i in range(3)]
        ARN = sb("ARN", [128, ARENA_COLS])
        AR = Arena(ARN, ARENA_COLS)
        PS = [psum("PS%d" % i, [128, 512]) for i in range(7)]
        PSB = psum("PSB", [128, 1024], BF16)

        def cf(n):
            o, w = CB[n]
            return CF[:, o:o + w]

        def pbl(n, a=None, b=None):
            o, w = PB[n]
            if a is None:
                return PBL[:, o:o + w]
            return PBL[:, o + a:o + (a + 1 if b is None else b)]

        IDB = CBF[:, 0:128]; ONESB = CBF[:, 128:256]; BLK64B = CBF[:, 256:384]
        MASKUB = CBF[:, 384:512]; PTDAB = CBF[:, 512:640]; PTMLAB = CBF[:, 640:768]
        ONESF = cf("ones"); MASKUF = cf("maskU")
        D_EPS = DRV[:, 0:1]; D_ONE = DRV[:, 1:2]; D_GQ = DRV[:, 2:3]; D_NEGLAM = DRV[:, 3:4]
        D_MQN_N = DRV[:, 4:5]; D_MQN_R = DRV[:, 5:6]; D_T1 = DRV[:, 6:7]; D_T2 = DRV[:, 7:8]
        D_NEGA = DRV[:, 8:16]; D_NEGONES = DRV[:, 16:144]

        wb_rr = [0]

        def wslot():
            i = wb_rr[0] % 3
            wb_rr[0] += 1
            return i

        def load_w(slot, src_ap, view):
            S.dma("pool", lambda e, o=view, i=src_ap: e.dma_start(out=o, in_=i), r=(), w=(("wb", slot),))

        def win_view(l, c0, ncols):
            return dwin[l, :, c0:c0 + ncols].rearrange("(kc p) c -> p kc c", p=128)

        def wb_view(slot, kc, ncols, off=0):
            return WB[slot][:, off:off + kc * ncols].rearrange("p (k c) -> p k c", k=kc)

        def mm(out_ap, pairs, r, w):
            def fn(e, out_ap=out_ap, pairs=pairs):
                ins = None
                n = len(pairs)
                for i, (a, b) in enumerate(pairs):
                    ins = e.matmul(out_ap, lhsT=a, rhs=b, start=(i == 0), stop=(i == n - 1))
                return ins
            S.pe(fn, r, w)

        def proj_pairs(wv, c0, mw, tok0, ntok):
            return [(wv[:, kc, c0:c0 + mw], HT[:, kc, tok0:tok0 + ntok]) for kc in range(8)]

        def tok_pairs(wv, c0, ncols, tile):
            return [(HT[:, kc, tile * 128:(tile + 1) * 128], wv[:, kc, c0:c0 + ncols]) for kc in range(8)]

        def rsqrt_act(out_ap, in_ap, scale, r, w):
            S.act(lambda e: e.activation(out=out_ap, in_=in_ap, func=AF.Ln, scale=scale, bias=D_EPS[0:out_ap.shape[0], :]), r, w)
            S.act(lambda e: e.activation(out=out_ap, in_=out_ap, func=AF.Exp, scale=-0.5), w, w)

        def sincos(v_ap, tmp_ap, out_sin, out_cos, r, kv, kt, ks, kc_):
            S.dve(lambda e: e.tensor_scalar(out=tmp_ap, in0=v_ap, scalar1=MAGIC, scalar2=None, op0=ALU.add), r + (kv,), (kt,))
            S.dve(lambda e: e.tensor_scalar(out=tmp_ap, in0=tmp_ap, scalar1=-MAGIC, scalar2=None, op0=ALU.add), (kt,), (kt,))
            S.dve(lambda e: e.tensor_tensor(out=tmp_ap, in0=v_ap, in1=tmp_ap, op=ALU.subtract), (kv, kt), (kt,))
            S.act(lambda e: e.activation(out=out_sin, in_=tmp_ap, func=AF.Sin, scale=TWO_PI), (kt,), (ks,))
            S.dve(lambda e: e.tensor_scalar(out=tmp_ap, in0=v_ap, scalar1=0.25, scalar2=MAGIC, op0=ALU.add, op1=ALU.add), (kv, ks), (kt,))
            S.dve(lambda e: e.tensor_scalar(out=tmp_ap, in0=tmp_ap, scalar1=-MAGIC, scalar2=None, op0=ALU.add), (kt,), (kt,))
            S.dve(lambda e: e.scalar_tensor_tensor(out=tmp_ap, in0=v_ap, scalar=0.25, in1=tmp_ap, op0=ALU.add, op1=ALU.subtract), (kv, kt), (kt,))
            S.act(lambda e: e.activation(out=out_cos, in_=tmp_ap, func=AF.Sin, scale=TWO_PI), (kt,), (kc_,))

        S.dma("sp", lambda e: e.dma_start(out=CF, in_=dcb), w=("CF",))
        for i, n in enumerate(["ident", "ones", "blk64", "maskU", "ptda", "ptmla"]):
            S.dve(lambda e, i=i, n=n: e.tensor_copy(out=CBF[:, i * 128:(i + 1) * 128], in_=cf(n)), ("CF",), ("CBF",))
        S.dve(lambda e: e.memset(D_EPS, EPS), (), ("DRVc",))
        S.dve(lambda e: e.memset(D_ONE, 1.0), (), ("DRVc",))
        S.dve(lambda e: e.memset(D_NEGONES, -1.0), (), ("DRVc",))
        CONST_R = ("CF", "CBF", "DRVc")

        def layer_setup(l):
            lam_init = 0.8 - 0.6 * float(np.exp(-0.3 * l))
            S.dma("sp", lambda e: e.dma_start(out=PBL, in_=dpb[l]), w=("PBL",))
            kd = "DRV"
            S.dve(lambda e: e.tensor_scalar(out=D_GQ, in0=pbl("daq"), scalar1=0.125, scalar2=None, op0=ALU.mult), ("PBL",), (kd,))
            S.dve(lambda e: e.tensor_scalar(out=D_MQN_N, in0=pbl("mqn_n"), scalar1=float(192 ** -0.5), scalar2=None, op0=ALU.mult), ("PBL",), (kd,))
            S.dve(lambda e: e.tensor_scalar(out=D_MQN_R, in0=pbl("mqn_r"), scalar1=float(192 ** -0.5), scalar2=None, op0=ALU.mult), ("PBL",), (kd,))
            tmp = DRV[:, 144:208]
            S.dve(lambda e: e.tensor_tensor(out=tmp, in0=pbl("lq1"), in1=pbl("lk1"), op=ALU.mult), ("PBL",), (kd,))
            S.dve(lambda e: e.tensor_reduce(out=D_T1, in_=tmp, axis=AX.X, op=ALU.add), (kd,), (kd,))
            S.dve(lambda e: e.tensor_tensor(out=tmp, in0=pbl("lq2"), in1=pbl("lk2"), op=ALU.mult), ("PBL",), (kd,))
            S.dve(lambda e: e.tensor_reduce(out=D_T2, in_=tmp, axis=AX.X, op=ALU.add), (kd,), (kd,))
            S.act(lambda e: e.activation(out=D_T1, in_=D_T1, func=AF.Exp), (kd,), (kd,))
            S.act(lambda e: e.activation(out=D_T2, in_=D_T2, func=AF.Exp), (kd,), (kd,))
            S.dve(lambda e: e.tensor_tensor(out=D_NEGLAM, in0=D_T2, in1=D_T1, op=ALU.subtract), (kd,), (kd,))
            S.dve(lambda e: e.tensor_scalar(out=D_NEGLAM, in0=D_NEGLAM, scalar1=-lam_init, scalar2=None, op0=ALU.add), (kd,), (kd,))
            S.dve(lambda e: e.tensor_scalar(out=pbl("subln"), in0=pbl("subln"), scalar1=float(1.0 - lam_init), scalar2=None, op0=ALU.mult), ("PBL", kd), ("PBL",))
            S.act(lambda e: e.activation(out=D_NEGA, in_=pbl("alog"), func=AF.Exp), ("PBL",), (kd,))
            S.dve(lambda e: e.tensor_scalar(out=D_NEGA, in0=D_NEGA, scalar1=-1.0, scalar2=None, op0=ALU.mult), (kd,), (kd,))
        LR = ("PBL", "DRV") + CONST_R

        def stage0(s, l):
            AR.reset()
            xt = [AR.f32(D_MODEL) for _ in range(2)]
            junk = AR.f32(D_MODEL)
            xn = AR.b16(TPB, D_MODEL)
            ssq = AR.f32(8)
            src = dx if l == 0 else dout
            for n in range(NB):
                for tt in range(TPB):
                    t = n * TPB + tt
                    b = t % 2
                    kx = ("xt", b)
                    S.dma("sp", lambda e, b=b, t=t: e.dma_start(out=xt[b], in_=src[s, t * 128:(t + 1) * 128, :]),
                          r=(("outd", s, t),), w=(kx,))
                    S.act(lambda e, b=b: e.activation(out=junk, in_=xt[b], func=AF.Square), (kx,), ("junk",))
                    S.dve(lambda e: e.tensor_reduce(out=ssq[:, 0:1], in_=junk, axis=AX.X, op=ALU.add), ("junk",), ("ssq",))
                    rsqrt_act(ssq[:, 0:1], ssq[:, 0:1], 1.0 / D_MODEL, ("ssq",) + LR, ("ssq",))
                    S.dve(lambda e, b=b, tt=tt: e.tensor_scalar(out=xn[:, tt, :], in0=xt[b], scalar1=ssq[:, 0:1], scalar2=None, op0=ALU.mult),
                          (kx, "ssq"), (("xn", tt),))
                for kc in range(8):
                    half = kc % 2
                    pt = PSB[:, half * 512:half * 512 + BLK]
                    def fn(e, kc=kc, pt=pt):
                        ins = None
                        for tt in range(TPB):
                            ins = e.transpose(out=pt[:, tt * 128:(tt + 1) * 128], in_=xn[:, tt, kc * 128:(kc + 1) * 128], identity=IDB)
                        return ins
                    S.pe(fn, tuple(("xn", tt) for tt in range(TPB)) + LR, (("psb", 0),))
                    S.dve(lambda e, kc=kc, pt=pt, n=n: e.tensor_scalar(out=HT[:, kc, n * BLK:(n + 1) * BLK], in0=pt, scalar1=pbl("ng", kc), scalar2=None, op0=ALU.mult),
                          (("psb", 0),) + LR, (("HT", n),))

        def merge(s, l, b, first):
            AR.reset()
            sg = [AR.f32(BLK) for _ in range(2)]
            tmp = [AR.f32(BLK) for _ in range(2)]
            sl_br = wslot()
            load_w(sl_br, dwbr[l, b].rearrange("(kc p) c -> p kc c", p=128), wb_view(sl_br, 4, 1024))
            wbr = wb_view(sl_br, 4, 1024)
            it = 0
            for fh in range(2):
                slg = wslot()
                load_w(slg, win_view(l, OFF["gate"] + b * 1024 + fh * 512, 512), wb_view(slg, 8, 512))
                wg = wb_view(slg, 8, 512)
                for f4 in range(4):
                    f = fh * 4 + f4
                    for n in range(NB):
                        pg = PS[(2 * it) % 6]; pb_ = PS[(2 * it + 1) % 6]
                        kpg = ("ps", (2 * it) % 6); kpb = ("ps", (2 * it + 1) % 6)
                        bi = it % 2
                        it += 1
                        mm(pg[:, 0:BLK], proj_pairs(wg, f4 * 128, 128, n * BLK, BLK), (("wb", slg), ("HT", n)), (kpg,))
                        mm(pb_[:, 0:BLK], [(wbr[:, k4, f * 128:(f + 1) * 128], YB[:, k4, n * BLK:(n + 1) * BLK]) for k4 in range(4)],
                           (("wb", sl_br), ("YB", n)), (kpb,))
                        S.act(lambda e, bi=bi, pg=pg: e.activation(out=sg[bi], in_=pg[:, 0:BLK], func=AF.Sigmoid), (kpg,), (("sg", bi),))
                        mgv = MG[:, f, n * BLK:(n + 1) * BLK]
                        if first:
                            S.dve(lambda e, bi=bi, pb_=pb_, mgv=mgv: e.tensor_tensor(out=mgv, in0=sg[bi], in1=pb_[:, 0:BLK], op=ALU.mult),
                                  (("sg", bi), kpb), (("MG", f, n),))
                        else:
                            S.dve(lambda e, bi=bi, pb_=pb_: e.tensor_tensor(out=tmp[bi], in0=sg[bi], in1=pb_[:, 0:BLK], op=ALU.mult),
                                  (("sg", bi), kpb), (("mtmp", bi),))
                            S.pool(lambda e, bi=bi, mgv=mgv: e.tensor_tensor(out=mgv, in0=mgv, in1=tmp[bi], op=ALU.add),
                                   (("mtmp", bi), ("MG", f, n)), (("MG", f, n),))

        def outproj(s, l):
            AR.reset()
            xt = [AR.f32(D_MODEL) for _ in range(2)]
            ot = [AR.f32(D_MODEL) for _ in range(2)]
            src = dx if l == 0 else dout
            sls = []
            for hh in range(2):
                sl = wslot()
                load_w(sl, dwout[l, :, hh * 512:(hh + 1) * 512].rearrange("(kc p) c -> p kc c", p=128), wb_view(sl, 8, 512))
                sls.append(sl)
            for t in range(NT):
                b = t % 2
                S.dma("sp", lambda e, b=b, t=t: e.dma_start(out=xt[b], in_=src[s, t * 128:(t + 1) * 128, :]),
                      r=(("outd", s, t),), w=(("xt", b),))
                for hh in range(2):
                    p = PS[(2 * t + hh) % 4]
                    kp = ("ps", (2 * t + hh) % 4)
                    wv = wb_view(sls[hh], 8, 512)
                    mm(p[:, 0:512], [(MG[:, kc, t * 128:(t + 1) * 128], wv[:, kc, :]) for kc in range(8)],
                       (("wb", sls[hh]),) + tuple(("MG", kc, t // TPB) for kc in range(8)), (kp,))
                    S.dve(lambda e, b=b, hh=hh, p=p: e.tensor_tensor(out=ot[b][:, hh * 512:(hh + 1) * 512], in0=p[:, 0:512], in1=xt[b][:, hh * 512:(hh + 1) * 512], op=ALU.add),
                          (kp, ("xt", b)), (("ot", b, hh),))
                S.dma("sp", lambda e, b=b, t=t: e.dma_start(out=dout[s, t * 128:(t + 1) * 128, :], in_=ot[b]),
                      r=(("ot", b, 0), ("ot", b, 1)), w=(("outd", s, t),))

        def rope_tables(s, invf_col, npart, Ct, St):
            pi_ = AR.f32(BLK).bitcast(I32)
            v = AR.f32(BLK); tmp = AR.f32(BLK)
            for n in range(NB):
                S.dma("sp", lambda e, n=n: e.dma_start(out=pi_[0:npart, :], in_=dpos[s:s + 1, n * BLK:(n + 1) * BLK].partition_broadcast(npart)),
                      w=("rp_pi",))
                S.dve(lambda e: e.tensor_copy(out=v[0:npart, :], in_=pi_[0:npart, :]), ("rp_pi",), ("rp_v",))
                S.dve(lambda e: e.tensor_scalar(out=v[0:npart, :], in0=v[0:npart, :], scalar1=invf_col[0:npart, :], scalar2=float(1.0 / TWO_PI), op0=ALU.mult, op1=ALU.mult),
                      ("rp_v",) + CONST_R, ("rp_v",))
                sincos(v[0:npart, :], tmp[0:npart, :], St[0:npart, n * BLK:(n + 1) * BLK], Ct[0:npart, n * BLK:(n + 1) * BLK],
                       (), "rp_v", "rp_t", ("rp_S", n), ("rp_C", n))

        def rope_apply(dst, xn_ap, pp, kpp, PT, npart, Ct, St, n, t1, t2, rkeys, wkey):
            mm(pp[0:npart, 0:BLK], [(PT[0:npart, 0:npart], xn_ap)], rkeys + CONST_R, (kpp,))
            S.dve(lambda e: e.tensor_tensor(out=t1[0:npart, :], in0=xn_ap, in1=Ct[0:npart, n * BLK:(n + 1) * BLK], op=ALU.mult),
                  rkeys + (("rp_C", n),), ("rp_t1",))
            S.dve(lambda e: e.tensor_tensor(out=t2[0:npart, :], in0=pp[0:npart, 0:BLK], in1=St[0:npart, n * BLK:(n + 1) * BLK], op=ALU.mult),
                  (kpp, ("rp_S", n)), ("rp_t2",))
            S.pool(lambda e: e.tensor_tensor(out=dst, in0=t1[0:npart, :], in1=t2[0:npart, :], op=ALU.add), ("rp_t1", "rp_t2"), (wkey,))

        def attn_block(qb, parts, vaug, PTt, dest, dkey):
            t0 = qb * TPB
            for j in range(t0 + TPB):
                lo = max(0, j - t0)
                c0 = lo * 128
                bi = j % 2
                st_ = PS[4 + bi]
                kst = ("ps", 4 + bi)
                rk = tuple(p[3](j) for p in parts) + tuple(p[4] for p in parts)
                mm(st_[:, c0:BLK], [(p[0][0:p[2], j * 128:(j + 1) * 128], p[1][0:p[2], c0:BLK]) for p in parts], rk, (kst,))
                S.act(lambda e, bi=bi, st_=st_, c0=c0: e.activation(out=PTt[bi][:, c0:BLK], in_=st_[:, c0:BLK], func=AF.Exp), (kst,), (("ptt", bi),))
                if j >= t0:
                    S.pool(lambda e, bi=bi, c0=c0: e.tensor_tensor(out=PTt[bi][:, c0:c0 + 128], in0=PTt[bi][:, c0:c0 + 128], in1=MASKUB, op=ALU.mult),
                           (("ptt", bi),) + CONST_R, (("ptt", bi),))
                for i in range(lo, TPB):
                    def fn(e, i=i, j=j, bi=bi):
                        return e.matmul(PS[i][:, 0:130], lhsT=PTt[bi][:, i * 128:(i + 1) * 128], rhs=vaug[:, j, :],
                                        start=(j == 0), stop=(j == t0 + i))
                    S.pe(fn, (("ptt", bi), ("vaug", j)), (("ps", i),))
            rec = AR_small["rec"]
            for i in range(TPB):
                S.dve(lambda e, i=i: e.reciprocal(out=rec[:, i:i + 1], in_=PS[i][:, 128:129]), (("ps", i),), (("rec", i),))
                S.dve(lambda e, i=i: e.tensor_scalar(out=dest[:, i, :], in0=PS[i][:, 0:128], scalar1=rec[:, i:i + 1], scalar2=None, op0=ALU.mult),
                      (("ps", i), ("rec", i)), (dkey + (i,),))

        AR_small = {}

        def finish_head(l, h, qb, obf, okeys, gate_c0, wg_view, wg_key, gs):
            pt = PSB[:, 0:BLK]
            def fn(e):
                ins = None
                for i in range(TPB):
                    ins = e.transpose(out=pt[:, i * 128:(i + 1) * 128], in_=obf[:, i, :], identity=IDB)
                return ins
            S.pe(fn, okeys + CONST_R, (("psb", 0),))
            pg = PS[6]
            mm(pg[:, 0:BLK], proj_pairs(wg_view, gate_c0, 128, qb * BLK, BLK), (wg_key, ("HT", qb)), (("ps", 6),))
            S.act(lambda e: e.activation(out=gs, in_=pg[:, 0:BLK], func=AF.Silu), (("ps", 6),), ("gsilu",))
            S.dve(lambda e: e.tensor_tensor(out=YB[:, h, qb * BLK:(qb + 1) * BLK], in0=pt, in1=gs, op=ALU.mult),
                  (("psb", 0), "gsilu"), (("YB", qb),))

        def branch_da(s, l):
            AR.reset()
            Ct = AR.b16(S_LEN); St = AR.b16(S_LEN)
            rope_tables(s, cf("invf_da"), 128, Ct, St)
            qT = AR.b16(S_LEN); kT = AR.b16(S_LEN)
            vaug = AR.b16(NT, 130)
            PTt = [AR.b16(BLK) for _ in range(2)]
            sq = AR.b16(BLK); qn = AR.b16(BLK); obf = AR.b16(TPB, 128); gs = AR.f32(BLK)
            rs = AR.f32(BLK); t1 = AR.f32(BLK); t2 = AR.f32(BLK)
            d0 = AR.f32(TPB, 128); d1 = AR.f32(TPB, 128); o = AR.f32(TPB, 128); junk = AR.f32(TPB, 128)
            ssq = AR.f32(TPB); AR_small["rec"] = AR.f32(TPB)
            S.dve(lambda e: e.memset(vaug[:, :, 128:130], 1.0), (), tuple(("vaug", j) for j in range(NT)))
            for h in range(4):
                sl = wslot()
                wv = WB[sl].rearrange("p (q k c) -> p q k c", q=4, k=8)
                for qi, nm in enumerate(["da_q", "da_k", "da_v", "da_gate"]):
                    load_w(sl, win_view(l, OFF[nm] + h * 128, 128), wv[:, qi])
                wkey = ("wb", sl)
                for which, dstT, gcol in ((0, qT, D_GQ), (1, kT, pbl("dak"))):
                    for n in range(NB):
                        pq = PS[n % 2]; kpq = ("ps", n % 2)
                        mm(pq[:, 0:BLK], proj_pairs(wv[:, which], 0, 128, n * BLK, BLK), (wkey, ("HT", n)), (kpq,))
                        S.act(lambda e, pq=pq: e.activation(out=sq, in_=pq[:, 0:BLK], func=AF.Square), (kpq,), ("sq",))
                        mm(PS[2][:, 0:BLK], [(BLK64B, sq)], ("sq",) + CONST_R, (("ps", 2),))
                        rsqrt_act(rs, PS[2][:, 0:BLK], 1.0 / 64, (("ps", 2),) + LR, ("rs",))
                        S.dve(lambda e, pq=pq, gcol=gcol: e.scalar_tensor_tensor(out=qn, in0=pq[:, 0:BLK], scalar=gcol, in1=rs, op0=ALU.mult, op1=ALU.mult),
                              (kpq, "rs") + LR, ("qn",))
                        rope_apply(dstT[:, n * BLK:(n + 1) * BLK], qn, PS[3], ("ps", 3), PTDAB, 128, Ct, St, n, t1, t2, ("qn",),
                                   (("qT" if which == 0 else "kT"), n))
                for n in range(NB):
                    pv = PS[n % 2]; kpv = ("ps", n % 2)
                    for tt in range(TPB):
                        t = n * TPB + tt
                        mm(pv[:, tt * 128:(tt + 1) * 128], tok_pairs(wv[:, 2], 0, 128, t), (wkey, ("HT", n)), (kpv,))
                    S.act(lambda e, pv=pv, n=n: e.activation(out=vaug[:, n * TPB:(n + 1) * TPB, 0:128], in_=pv[:, 0:BLK].rearrange("p (a b) -> p a b", a=TPB), func=AF.Copy),
                          (kpv,), tuple(("vaug", n * TPB + tt) for tt in range(TPB)))
                for qb in range(NB):
                    for m, dst in ((0, d0), (1, d1)):
                        kv = kT[m * 64:(m + 1) * 64, :]
                        qv = qT[m * 64:(m + 1) * 64, qb * BLK:(qb + 1) * BLK]
                        parts = [(_Shift(kv), _Shift(qv), 64, (lambda j: ("kT", j // TPB)), ("qT", qb))]
                        attn_block(qb, parts, vaug, PTt, dst, ("dd", m))
                    for i in range(TPB):
                        S.dve(lambda e, i=i: e.scalar_tensor_tensor(out=o[:, i, :], in0=d1[:, i, :], scalar=D_NEGLAM, in1=d0[:, i, :], op0=ALU.mult, op1=ALU.add),
                              (("dd", 0, i), ("dd", 1, i)) + LR, (("o", i),))
                    S.act(lambda e: e.activation(out=junk, in_=o, func=AF.Square), tuple(("o", i) for i in range(TPB)), ("ojunk",))
                    S.dve(lambda e: e.tensor_reduce(out=ssq, in_=junk, axis=AX.X, op=ALU.add), ("ojunk",), ("ossq",))
                    rsqrt_act(ssq, ssq, 1.0 / 128, ("ossq",) + LR, ("ossq",))
                    for i in range(TPB):
                        S.dve(lambda e, i=i: e.scalar_tensor_tensor(out=obf[:, i, :], in0=o[:, i, :], scalar=ssq[:, i:i + 1], in1=pbl("subln"), op0=ALU.mult, op1=ALU.mult),
                              (("o", i), "ossq") + LR, (("obf", i),))
                    finish_head(l, h, qb, obf, tuple(("obf", i) for i in range(TPB)), 0, wv[:, 3], wkey, gs)

        def branch_mla(s, l):
            AR.reset()
            Ct = AR.b16(S_LEN); St = AR.b16(S_LEN)
            rope_tables(s, cf("invf_mla"), 64, Ct, St)
            cqn = AR.b16(2, S_LEN); ckvn = AR.b16(S_LEN); krr = AR.b16(S_LEN)
            knope = AR.b16(S_LEN); krope = AR.b16(S_LEN)
            vaug = AR.b16(NT, 130)
            qnope = AR.b16(BLK); qrope = AR.b16(BLK)
            PTt = [AR.b16(BLK) for _ in range(2)]
            sqA = AR.b16(BLK); sqB = AR.b16(BLK); tb = AR.b16(BLK); obf = AR.b16(TPB, 128)
            wuq = AR.b16(2, 768); wukv = AR.b16(1024)
            gs = AR.f32(BLK); rs = AR.f32(BLK); t1 = AR.f32(BLK); t2 = AR.f32(BLK)
            o = AR.f32(TPB, 128); AR_small["rec"] = AR.f32(TPB)
            S.dve(lambda e: e.memset(vaug[:, :, 128:130], 1.0), (), tuple(("vaug", j) for j in range(NT)))
            S.dma("pool", lambda e: e.dma_start(out=wuq, in_=dwuq[l].rearrange("(kc p) c -> p kc c", p=128)), w=("wuq",))
            S.dma("pool", lambda e: e.dma_start(out=wukv, in_=dwukv[l]), w=("wukv",))
            sl = wslot()
            wc = wb_view(sl, 8, 448)
            load_w(sl, win_view(l, OFF["mla_cq"], 448), wc)
            wkey = ("wb", sl)
            for n in range(NB):
                blk = slice(n * BLK, (n + 1) * BLK)
                for c in range(2):
                    mm(PS[c][:, 0:BLK], proj_pairs(wc, c * 128, 128, n * BLK, BLK), (wkey, ("HT", n)), (("ps", c),))
                S.act(lambda e: e.activation(out=sqA, in_=PS[0][:, 0:BLK], func=AF.Square), (("ps", 0),), ("sqA",))
                S.act(lambda e: e.activation(out=sqB, in_=PS[1][:, 0:BLK], func=AF.Square), (("ps", 1),), ("sqB",))
                mm(PS[2][:, 0:BLK], [(ONESB, sqA), (ONESB, sqB)], ("sqA", "sqB") + CONST_R, (("ps", 2),))
                rsqrt_act(rs, PS[2][:, 0:BLK], 1.0 / 256, (("ps", 2),) + LR, ("rs",))
                for c in range(2):
                    S.dve(lambda e, c=c, blk=blk: e.scalar_tensor_tensor(out=cqn[:, c, blk], in0=PS[c][:, 0:BLK], scalar=pbl("mqa", c), in1=rs, op0=ALU.mult, op1=ALU.mult),
                          (("ps", c), "rs") + LR, (("cqn", n),))
                mm(PS[3][:, 0:BLK], proj_pairs(wc, 256, 128, n * BLK, BLK), (wkey, ("HT", n)), (("ps", 3),))
                S.act(lambda e: e.activation(out=sqA, in_=PS[3][:, 0:BLK], func=AF.Square), (("ps", 3),), ("sqA",))
                mm(PS[2][:, 0:BLK], [(ONESB, sqA)], ("sqA",) + CONST_R, (("ps", 2),))
                rsqrt_act(rs, PS[2][:, 0:BLK], 1.0 / 128, (("ps", 2),) + LR, ("rs",))
                S.dve(lambda e, blk=blk: e.scalar_tensor_tensor(out=ckvn[:, blk], in0=PS[3][:, 0:BLK], scalar=pbl("mkva"), in1=rs, op0=ALU.mult, op1=ALU.mult),
                      (("ps", 3), "rs") + LR, (("ckvn", n),))
                mm(PS[6][0:64, 0:BLK], proj_pairs(wc, 384, 64, n * BLK, BLK), (wkey, ("HT", n)), (("ps", 6),))
                S.act(lambda e, blk=blk: e.activation(out=krr[0:64, blk], in_=PS[6][0:64, 0:BLK], func=AF.Copy), (("ps", 6),), (("krr", n),))
            slg = wslot()
            wgv = wb_view(slg, 8, 512)
            load_w(slg, win_view(l, OFF["mla_gate"], 512), wgv)
            for h in range(4):
                for n in range(NB):
                    blk = slice(n * BLK, (n + 1) * BLK)
                    mm(PS[0][:, 0:BLK], [(wukv[:, h * 256:h * 256 + 128], ckvn[:, blk])], ("wukv", ("ckvn", n)), (("ps", 0),))
                    S.act(lambda e: e.activation(out=sqA, in_=PS[0][:, 0:BLK], func=AF.Square), (("ps", 0),), ("sqA",))
                    S.act(lambda e, blk=blk: e.activation(out=sqB[0:64, :], in_=krr[0:64, blk], func=AF.Square), (("krr", n),), ("sqB",))
                    mm(PS[2][:, 0:BLK], [(ONESB, sqA), (ONESB[0:64, :], sqB[0:64, :])], ("sqA", "sqB") + CONST_R, (("ps", 2),))
                    rsqrt_act(rs, PS[2][:, 0:BLK], 1.0 / 192, (("ps", 2),) + LR, ("rs",))
                    S.dve(lambda e, blk=blk: e.scalar_tensor_tensor(out=knope[:, blk], in0=PS[0][:, 0:BLK], scalar=pbl("mkn_n"), in1=rs, op0=ALU.mult, op1=ALU.mult),
                          (("ps", 0), "rs") + LR, (("knope", n),))
                    S.dve(lambda e, blk=blk: e.scalar_tensor_tensor(out=tb[0:64, :], in0=krr[0:64, blk], scalar=pbl("mkn_r")[0:64, :], in1=rs[0:64, :], op0=ALU.mult, op1=ALU.mult),
                          (("krr", n), "rs") + LR, ("tb",))
                    rope_apply(krope[0:64, blk], tb[0:64, :], PS[3], ("ps", 3), PTMLAB, 64, Ct, St, n, t1, t2, ("tb",), ("krope", n))
                    pv = PS[1]
                    for tt in range(TPB):
                        t = n * TPB + tt
                        mm(pv[:, tt * 128:(tt + 1) * 128], [(ckvn[:, t * 128:(t + 1) * 128], wukv[:, h * 256 + 128:h * 256 + 256])],
                           ("wukv", ("ckvn", n)), (("ps", 1),))
                    S.act(lambda e, n=n: e.activation(out=vaug[:, n * TPB:(n + 1) * TPB, 0:128], in_=PS[1][:, 0:BLK].rearrange("p (a b) -> p a b", a=TPB), func=AF.Copy),
                          (("ps", 1),), tuple(("vaug", n * TPB + tt) for tt in range(TPB)))
                for qb in range(NB):
                    blk = slice(qb * BLK, (qb + 1) * BLK)
                    c0 = h * 192
                    mm(PS[0][:, 0:BLK], [(wuq[:, c, c0:c0 + 128], cqn[:, c, blk]) for c in range(2)], ("wuq", ("cqn", qb)), (("ps", 0),))
                    mm(PS[1][0:64, 0:BLK], [(wuq[:, c, c0 + 128:c0 + 192], cqn[:, c, blk]) for c in range(2)], ("wuq", ("cqn", qb)), (("ps", 1),))
                    S.act(lambda e: e.activation(out=sqA, in_=PS[0][:, 0:BLK], func=AF.Square), (("ps", 0),), ("sqA",))
                    S.act(lambda e: e.activation(out=sqB[0:64, :], in_=PS[1][0:64, 0:BLK], func=AF.Square), (("ps", 1),), ("sqB",))
                    mm(PS[2][:, 0:BLK], [(ONESB, sqA), (ONESB[0:64, :], sqB[0:64, :])], ("sqA", "sqB") + CONST_R, (("ps", 2),))
                    rsqrt_act(rs, PS[2][:, 0:BLK], 1.0 / 192, (("ps", 2),) + LR, ("rs",))
                    S.dve(lambda e: e.scalar_tensor_tensor(out=qnope, in0=PS[0][:, 0:BLK], scalar=D_MQN_N, in1=rs, op0=ALU.mult, op1=ALU.mult),
                          (("ps", 0), "rs") + LR, ("qnope",))
                    S.dve(lambda e: e.scalar_tensor_tensor(out=tb[0:64, :], in0=PS[1][0:64, 0:BLK], scalar=D_MQN_R[0:64, :], in1=rs[0:64, :], op0=ALU.mult, op1=ALU.mult),
                          (("ps", 1), "rs") + LR, ("tb",))
                    rope_apply(qrope[0:64, :], tb[0:64, :], PS[3], ("ps", 3), PTMLAB, 64, Ct, St, qb, t1, t2, ("tb",), "qrope")
                    parts = [(knope, qnope, 128, (lambda j: ("knope", j // TPB)), "qnope"),
                             (krope, qrope, 64, (lambda j: ("krope", j // TPB)), "qrope")]
                    attn_block(qb, parts, vaug, PTt, o, ("oo",))
                    for i in range(TPB):
                        S.pool(lambda e, i=i: e.tensor_copy(out=obf[:, i, :], in_=o[:, i, :]), (("oo", i),), (("obf", i),))
                    finish_head(l, h, qb, obf, tuple(("obf", i) for i in range(TPB)), h * 128, wgv, ("wb", slg), gs)

        def branch_ssd(s, l):
            AR.reset()
            Xtok = AR.b16(NT, 512)
            Bfm = AR.b16(2, S_LEN); Cfm = AR.b16(2, S_LEN)
            xfm = AR.b16(BLK); Xdt = AR.b16(512); XdD = AR.b16(512); Btok = AR.b16(256)
            MT = [AR.b16(128) for _ in range(2)]
            Sbf = AR.b16(2, 256); ybt = AR.b16(512); wdt = AR.b16(8, 8)
            raw = [AR.f32(BLK + 3) for _ in range(2)]
            acc = AR.f32(BLK); zs = AR.f32(512); CBm = AR.f32(128)
            aTri = [AR.f32(128) for _ in range(2)]
            Dm = AR.f32(128); E = AR.f32(128)
            ydiag = AR.f32(512); yc = AR.f32(512); junk = AR.f32(512); Sst = AR.f32(2, 256)
            sm = AR.f32(64)
            dt_ = sm[:, 0:8]; a_ = sm[:, 8:16]; cs_ = sm[:, 16:24]; ecs = sm[:, 24:32]; etot = sm[:, 32:40]
            dec = sm[:, 40:48]; ssq = sm[:, 48:50]; tmp8 = sm[:, 56:64]
            slz = wslot(); wz = wb_view(slz, 8, 512)
            load_w(slz, win_view(l, OFF["mb_z"], 512), wz)
            S.dma("pool", lambda e: e.dma_start(out=wdt, in_=win_view(l, OFF["mb_dt"], 8)), w=("wdt",))
            S.dve(lambda e: e.memset(Sst, 0.0), (), ("Sst",))
            S.dve(lambda e: e.memset(Sbf, 0.0), (), ("Sbf",))
            for half in range(2):
                slx = wslot(); wx = wb_view(slx, 8, 512)
                load_w(slx, win_view(l, OFF["mb_xbc"] + half * 512, 512), wx)
                for f4 in range(4):
                    fc = half * 4 + f4
                    for n in range(NB):
                        rb = raw[n % 2]; krb = ("raw", n % 2)
                        p = PS[n % 2]; kp = ("ps", n % 2)
                        mm(p[:, 0:BLK], proj_pairs(wx, f4 * 128, 128, n * BLK, BLK), (("wb", slx), ("HT", n)), (kp,))
                        if n == 0:
                            S.dve(lambda e, rb=rb: e.memset(rb[:, 0:3], 0.0), (), (krb,))
                        else:
                            pr = raw[(n - 1) % 2]
                            S.pool(lambda e, rb=rb, pr=pr: e.tensor_copy(out=rb[:, 0:3], in_=pr[:, BLK:BLK + 3]), (("raw", (n - 1) % 2),), (krb,))
                        S.act(lambda e, rb=rb, p=p: e.activation(out=rb[:, 3:3 + BLK], in_=p[:, 0:BLK], func=AF.Copy), (kp,), (krb,))
                        cw = lambda k, fc=fc: pbl("convw", fc * 4 + k)
                        S.dve(lambda e, rb=rb, fc=fc, cw=cw: e.tensor_scalar(out=acc, in0=rb[:, 3:3 + BLK], scalar1=cw(3), scalar2=pbl("convb", fc), op0=ALU.mult, op1=ALU.add),
                              (krb,) + LR, ("acc",))
                        for k in (2, 1, 0):
                            S.dve(lambda e, rb=rb, k=k, cw=cw: e.scalar_tensor_tensor(out=acc, in0=rb[:, k:k + BLK], scalar=cw(k), in1=acc, op0=ALU.mult, op1=ALU.add),
                                  (krb, "acc") + LR, ("acc",))
                        if fc < 4:
                            dst, kd = xfm, "xfm"
                        elif fc < 6:
                            dst, kd = Bfm[:, fc - 4, n * BLK:(n + 1) * BLK], ("Bfm", n)
                        else:
                            dst, kd = Cfm[:, fc - 6, n * BLK:(n + 1) * BLK], ("Cfm", n)
                        S.act(lambda e, dst=dst: e.activation(out=dst, in_=acc, func=AF.Silu), ("acc",), (kd,))
                        if fc < 4:
                            pt = PSB[:, 0:BLK]
                            def fn(e, pt=pt):
                                ins = None
                                for tt in range(TPB):
                                    ins = e.transpose(out=pt[:, tt * 128:(tt + 1) * 128], in_=xfm[:, tt * 128:(tt + 1) * 128], identity=IDB)
                                return ins
                            S.pe(fn, ("xfm",) + CONST_R, (("psb", 0),))
                            S.dve(lambda e, pt=pt, n=n, fc=fc: e.tensor_copy(out=Xtok[:, n * TPB:(n + 1) * TPB, fc * 128:(fc + 1) * 128],
                                                                            in_=pt.rearrange("p (a b) -> p a b", a=TPB)),
                                  (("psb", 0),), (("Xtok", n),))
            for c in range(NT):
                n = c // TPB
                tok = slice(c * 128, (c + 1) * 128)
                mm(PS[0][:, 0:512], tok_pairs(wz, 0, 512, c), (("wb", slz), ("HT", n)), (("ps", 0),))
                S.act(lambda e: e.activation(out=zs, in_=PS[0][:, 0:512], func=AF.Silu), (("ps", 0),), ("zs",))
                mm(PS[1][:, 0:8], tok_pairs(wdt, 0, 8, c), ("wdt", ("HT", n)), (("ps", 1),))
                S.dve(lambda e: e.tensor_tensor(out=tmp8, in0=PS[1][:, 0:8], in1=pbl("dtb"), op=ALU.add), (("ps", 1),) + LR, ("tmp8",))
                S.act(lambda e: e.activation(out=tmp8, in_=tmp8, func=AF.Exp), ("tmp8",), ("tmp8",))
                S.act(lambda e: e.activation(out=dt_, in_=tmp8, func=AF.Ln, bias=D_ONE, scale=1.0), ("tmp8",) + LR, ("dt",))
                S.dve(lambda e: e.tensor_tensor(out=a_, in0=dt_, in1=D_NEGA, op=ALU.mult), ("dt",) + LR, ("a",))
                mm(PS[1][:, 8:16], [(MASKUF, a_)], ("a",) + CONST_R, (("ps", 1),))
                mm(PS[1][:, 16:24], [(ONESF, a_)], ("a",) + CONST_R, (("ps", 1),))
                S.act(lambda e: e.activation(out=cs_, in_=PS[1][:, 8:16], func=AF.Copy), (("ps", 1),), ("cs",))
                S.act(lambda e: e.activation(out=ecs, in_=PS[1][:, 8:16], func=AF.Exp), (("ps", 1),), ("ecs",))
                S.act(lambda e: e.activation(out=etot, in_=PS[1][:, 16:24], func=AF.Exp), (("ps", 1),), ("etot",))
                S.dve(lambda e: e.tensor_tensor(out=dec, in0=PS[1][:, 16:24], in1=cs_, op=ALU.subtract), (("ps", 1), "cs"), ("dec",))
                S.act(lambda e: e.activation(out=dec, in_=dec, func=AF.Exp), ("dec",), ("dec",))
                for h in range(8):
                    hs = slice(h * 64, (h + 1) * 64)
                    S.dve(lambda e, h=h, hs=hs, c=c: e.tensor_scalar(out=Xdt[:, hs], in0=Xtok[:, c, hs], scalar1=dt_[:, h:h + 1], scalar2=None, op0=ALU.mult),
                          (("Xtok", n), "dt"), ("Xdt",))
                    S.pool(lambda e, h=h, hs=hs: e.tensor_scalar(out=XdD[:, hs], in0=Xdt[:, hs], scalar1=dec[:, h:h + 1], scalar2=None, op0=ALU.mult),
                           ("Xdt", "dec"), ("XdD",))
                def fnb(e, tok=tok):
                    ins = None
                    for g in range(2):
                        ins = e.transpose(out=PSB[:, 512 + g * 128:512 + (g + 1) * 128], in_=Bfm[:, g, tok], identity=IDB)
                    return ins
                S.pe(fnb, (("Bfm", n),) + CONST_R, (("psb", 0),))
                S.act(lambda e: e.activation(out=Btok, in_=PSB[:, 512:768], func=AF.Copy), (("psb", 0),), ("Btok",))
                for gr in range(2):
                    mm(PS[2][:, 0:128], [(Bfm[:, gr, tok], Cfm[:, gr, tok])], (("Bfm", n), ("Cfm", n)), (("ps", 2),))
                    S.dve(lambda e: e.tensor_tensor(out=CBm, in0=PS[2][:, 0:128], in1=MASKUF, op=ALU.mult), (("ps", 2),) + CONST_R, ("CBm",))
                    for r_ in range(4):
                        h = gr * 4 + r_
                        hs = slice(h * 64, (h + 1) * 64)
                        at = aTri[h % 2]; kat = ("aTri", h % 2)
                        S.dve(lambda e, at=at, h=h: e.tensor_scalar(out=at, in0=MASKUF, scalar1=a_[:, h:h + 1], scalar2=None, op0=ALU.mult),
                              ("a",) + CONST_R, (kat,))
                        mm(PS[3][:, 0:128], [(ONESF, at), (at, D_NEGONES)], (kat,) + CONST_R, (("ps", 3),))
                        S.dve(lambda e: e.tensor_scalar(out=Dm, in0=PS[3][:, 0:128], scalar1=0.0, scalar2=None, op0=ALU.min), (("ps", 3),), ("Dm",))
                        S.act(lambda e: e.activation(out=E, in_=Dm, func=AF.Exp), ("Dm",), ("E",))
                        mt = MT[h % 2]; kmt = ("MT", h % 2)
                        S.dve(lambda e, mt=mt: e.tensor_tensor(out=mt, in0=E, in1=CBm, op=ALU.mult), ("E", "CBm"), (kmt,))
                        mm(PS[4][:, hs], [(mt, Xdt[:, hs])], (kmt, "Xdt"), (("ps", 4),))
                    gs_ = slice(gr * 256, (gr + 1) * 256)
                    mm(PS[5][:, gs_], [(Cfm[:, gr, tok], Sbf[:, gr, :])], (("Cfm", n), "Sbf"), (("ps", 5),))
                    mm(PS[6][:, gs_], [(Btok[:, gr * 128:(gr + 1) * 128], XdD[:, gs_])], ("Btok", "XdD"), (("ps", 6),))
                for h in range(8):
                    gr, r_ = h // 4, h % 4
                    S.dve(lambda e, h=h, gr=gr, r_=r_: e.scalar_tensor_tensor(out=Sst[:, gr, r_ * 64:(r_ + 1) * 64], in0=Sst[:, gr, r_ * 64:(r_ + 1) * 64],
                                                                             scalar=etot[:, h:h + 1], in1=PS[6][:, h * 64:(h + 1) * 64], op0=ALU.mult, op1=ALU.add),
                          ("Sst", "etot", ("ps", 6)), ("Sst",))
                S.pool(lambda e: e.tensor_copy(out=Sbf, in_=Sst), ("Sst",), ("Sbf",))
                S.act(lambda e: e.activation(out=ydiag, in_=PS[4][:, 0:512], func=AF.Copy), (("ps", 4),), ("ydiag",))
                for h in range(8):
                    hs = slice(h * 64, (h + 1) * 64)
                    S.dve(lambda e, h=h, hs=hs: e.scalar_tensor_tensor(out=yc[:, hs], in0=PS[5][:, hs], scalar=ecs[:, h:h + 1], in1=ydiag[:, hs], op0=ALU.mult, op1=ALU.add),
                          (("ps", 5), "ecs", "ydiag"), ("yc",))
                S.pool(lambda e, c=c: e.tensor_tensor(out=ydiag, in0=Xtok[:, c, :], in1=pbl("mbd"), op=ALU.mult), (("Xtok", n), "yc") + LR, ("ydiag",))
                S.dve(lambda e: e.tensor_tensor(out=yc, in0=yc, in1=ydiag, op=ALU.add), ("yc", "ydiag"), ("yc",))
                S.dve(lambda e: e.tensor_tensor(out=yc, in0=yc, in1=zs, op=ALU.mult), ("yc", "zs"), ("yc",))
                S.act(lambda e: e.activation(out=junk, in_=yc, func=AF.Square), ("yc",), ("junk",))
                S.dve(lambda e: e.tensor_reduce(out=ssq, in_=junk.rearrange("p (a b) -> p a b", a=2), axis=AX.X, op=ALU.add), ("junk",), ("ssq",))
                rsqrt_act(ssq, ssq, 1.0 / 256, ("ssq",) + LR, ("ssq",))
                for g2 in range(2):
                    gs_ = slice(g2 * 256, (g2 + 1) * 256)
                    S.dve(lambda e, g2=g2, gs_=gs_: e.scalar_tensor_tensor(out=ybt[:, gs_], in0=yc[:, gs_], scalar=ssq[:, g2:g2 + 1], in1=pbl("mbnorm")[:, gs_], op0=ALU.mult, op1=ALU.mult),
                          ("yc", "ssq") + LR, ("ybt",))
                def fnt(e):
                    ins = None
                    for k in range(4):
                        ins = e.transpose(out=PSB[:, k * 128:(k + 1) * 128], in_=ybt[:, k * 128:(k + 1) * 128], identity=IDB)
                    return ins
                S.pe(fnt, ("ybt",) + CONST_R, (("psb", 0),))
                S.act(lambda e, tok=tok: e.activation(out=YB[:, :, tok], in_=PSB[:, 0:512].rearrange("p (a b) -> p a b", a=4), func=AF.Copy),
                      (("psb", 0),), (("YB", n),))

        def branch_s5(s, l):
            AR.reset()
            cj = AR.f32(16, 128); sj = AR.f32(16, 128)
            Bre = AR.b16(16, 128); Bim = AR.b16(16, 128); Cre = AR.b16(16, 128); Cim = AR.b16(16, 128)
            smp = AR.f32(160)
            r_ = smp[:, 0:16]; th = smp[:, 16:32]; rc = smp[:, 32:48]; rsn = smp[:, 48:64]; nrs = smp[:, 64:80]
            tA = smp[:, 80:96]; tB = smp[:, 96:112]; tC = smp[:, 112:128]; tD = smp[:, 128:144]
            stt = AR.f32(16, 2)
            mark = AR.off
            vtab = AR.f32(16, 128); ttab = AR.f32(16, 128)
            S.act(lambda e: e.activation(out=tA, in_=pbl("lstep"), func=AF.Exp), LR, ("s5a",))
            S.dve(lambda e: e.tensor_tensor(out=tB, in0=pbl("lamre"), in1=tA, op=ALU.mult), ("s5a",) + LR, ("s5b",))
            S.act(lambda e: e.activation(out=r_, in_=tB, func=AF.Exp), ("s5b",), ("s5r",))
            S.dve(lambda e: e.tensor_tensor(out=th, in0=pbl("lamim"), in1=tA, op=ALU.mult), ("s5a",) + LR, ("s5th",))
            for mt in range(16):
                S.dve(lambda e, mt=mt: e.tensor_scalar(out=vtab[:, mt, :], in0=cf("jrow"), scalar1=th[:, mt:mt + 1], scalar2=float(1.0 / TWO_PI), op0=ALU.mult, op1=ALU.mult),
                      ("s5th",) + CONST_R, ("s5v",))
            sincos(vtab, ttab, sj, cj, (), "s5v", "s5t", "s5sj", "s5cj")
            S.dve(lambda e: e.tensor_scalar(out=tC, in0=th, scalar1=float(128.0 / TWO_PI), scalar2=None, op0=ALU.mult), ("s5th",), ("s5c",))
            sincos(tC, tD, tB, tA, ("s5b", "s5a"), "s5c", "s5d", "s5s128", "s5c128")
            S.dve(lambda e: e.tensor_tensor(out=rc, in0=r_, in1=tA, op=ALU.mult), ("s5r", "s5c128"), ("s5rc",))
            S.dve(lambda e: e.tensor_tensor(out=rsn, in0=r_, in1=tB, op=ALU.mult), ("s5r", "s5s128"), ("s5rs",))
            S.dve(lambda e: e.tensor_scalar(out=nrs, in0=rsn, scalar1=-1.0, scalar2=None, op0=ALU.mult), ("s5rs",), ("s5nrs",))
            MC = ("s5r", "s5rc", "s5rs", "s5nrs", "s5sj", "s5cj")
            S.dma("pool", lambda e: e.dma_start(out=Cre, in_=ds5C[l, 0]), w=("s5Cre",))
            S.dma("pool", lambda e: e.dma_start(out=Cim, in_=ds5C[l, 1]), w=("s5Cim",))
            AR.off = mark + 2 * 16 * 128
            W = 256
            names = ["lr", "li", "ls", "v", "t", "sn", "cs", "abr", "abi", "den", "fr", "fi", "braw_r", "braw_i", "u1", "u2"]
            Q = {nm: AR.f32(W) for nm in names}
            for q in range(2048 // W):
                cs_ = slice(q * W, (q + 1) * W)
                k = "s5q"
                for i, nm in enumerate(["lr", "li", "ls"]):
                    S.dma("sp", lambda e, nm=nm, i=i, cs_=cs_: e.dma_start(out=Q[nm], in_=ds5rep[l, i, :, cs_]), w=(("s5in", nm),))
                mts = slice(q * (W // 128), (q + 1) * (W // 128))
                S.dma("sp", lambda e, mts=mts: e.dma_start(out=Q["braw_r"].rearrange("p (a b) -> p a b", b=128), in_=ds5B[l, 0, :, mts, :]), w=(("s5in", "br"),))
                S.dma("sp", lambda e, mts=mts: e.dma_start(out=Q["braw_i"].rearrange("p (a b) -> p a b", b=128), in_=ds5B[l, 1, :, mts, :]), w=(("s5in", "bi"),))
                S.act(lambda e: e.activation(out=Q["ls"], in_=Q["ls"], func=AF.Exp), (("s5in", "ls"),), (("s5in", "ls"),))
                S.dve(lambda e: e.tensor_tensor(out=Q["u1"], in0=Q["lr"], in1=Q["ls"], op=ALU.mult), (("s5in", "lr"), ("s5in", "ls")), ("q_u1",))
                S.act(lambda e: e.activation(out=Q["u1"], in_=Q["u1"], func=AF.Exp), ("q_u1",), ("q_u1",))
                S.dve(lambda e: e.tensor_tensor(out=Q["v"], in0=Q["li"], in1=Q["ls"], op=ALU.mult), (("s5in", "li"), ("s5in", "ls")), ("q_v",))
                S.dve(lambda e: e.tensor_scalar(out=Q["v"], in0=Q["v"], scalar1=float(1.0 / TWO_PI), scalar2=None, op0=ALU.mult), ("q_v",), ("q_v",))
                sincos(Q["v"], Q["t"], Q["sn"], Q["cs"], (), "q_v", "q_t", "q_sn", "q_cs")
                S.dve(lambda e: e.tensor_tensor(out=Q["abr"], in0=Q["u1"], in1=Q["cs"], op=ALU.mult), ("q_u1", "q_cs"), ("q_abr",))
                S.dve(lambda e: e.tensor_scalar(out=Q["abr"], in0=Q["abr"], scalar1=-1.0, scalar2=None, op0=ALU.add), ("q_abr",), ("q_abr",))
                S.dve(lambda e: e.tensor_tensor(out=Q["abi"], in0=Q["u1"], in1=Q["sn"], op=ALU.mult), ("q_u1", "q_sn"), ("q_abi",))
                S.dve(lambda e: e.tensor_tensor(out=Q["den"], in0=Q["lr"], in1=Q["lr"], op=ALU.mult), (("s5in", "lr"),), ("q_den",))
                S.dve(lambda e: e.tensor_tensor(out=Q["u2"], in0=Q["li"], in1=Q["li"], op=ALU.mult), (("s5in", "li"),), ("q_u2",))
                S.dve(lambda e: e.tensor_tensor(out=Q["den"], in0=Q["den"], in1=Q["u2"], op=ALU.add), ("q_den", "q_u2"), ("q_den",))
                S.dve(lambda e: e.reciprocal(out=Q["den"], in_=Q["den"]), ("q_den",), ("q_den",))
                S.dve(lambda e: e.tensor_tensor(out=Q["fr"], in0=Q["abr"], in1=Q["lr"], op=ALU.mult), ("q_abr", ("s5in", "lr")), ("q_fr",))
                S.dve(lambda e: e.tensor_tensor(out=Q["u2"], in0=Q["abi"], in1=Q["li"], op=ALU.mult), ("q_abi", ("s5in", "li"), "q_den"), ("q_u2",))
                S.dve(lambda e: e.tensor_tensor(out=Q["fr"], in0=Q["fr"], in1=Q["u2"], op=ALU.add), ("q_fr", "q_u2"), ("q_fr",))
                S.dve(lambda e: e.tensor_tensor(out=Q["fr"], in0=Q["fr"], in1=Q["den"], op=ALU.mult), ("q_fr", "q_den"), ("q_fr",))
                S.dve(lambda e: e.tensor_tensor(out=Q["fi"], in0=Q["abi"], in1=Q["lr"], op=ALU.mult), ("q_abi", ("s5in", "lr")), ("q_fi",))
                S.dve(lambda e: e.tensor_tensor(out=Q["u2"], in0=Q["abr"], in1=Q["li"], op=ALU.mult), ("q_abr", ("s5in", "li"), "q_fr"), ("q_u2",))
                S.dve(lambda e: e.tensor_tensor(out=Q["fi"], in0=Q["fi"], in1=Q["u2"], op=ALU.subtract), ("q_fi", "q_u2"), ("q_fi",))
                S.dve(lambda e: e.tensor_tensor(out=Q["fi"], in0=Q["fi"], in1=Q["den"], op=ALU.mult), ("q_fi", "q_den"), ("q_fi",))
                brv = Bre.rearrange("p a b -> p (a b)")[:, cs_]; biv = Bim.rearrange("p a b -> p (a b)")[:, cs_]
                S.dve(lambda e: e.tensor_tensor(out=Q["u1"], in0=Q["fr"], in1=Q["braw_r"], op=ALU.mult), ("q_fr", ("s5in", "br"), "q_abr", "q_abi"), ("q_u1",))
                S.dve(lambda e: e.tensor_tensor(out=Q["u2"], in0=Q["fi"], in1=Q["braw_i"], op=ALU.mult), ("q_fi", ("s5in", "bi")), ("q_u2",))
                S.dve(lambda e, brv=brv: e.tensor_tensor(out=brv, in0=Q["u1"], in1=Q["u2"], op=ALU.subtract), ("q_u1", "q_u2"), ("s5Bre",))
                S.dve(lambda e: e.tensor_tensor(out=Q["u1"], in0=Q["fr"], in1=Q["braw_i"], op=ALU.mult), ("q_fr", ("s5in", "bi"), "s5Bre"), ("q_u1",))
                S.dve(lambda e: e.tensor_tensor(out=Q["u2"], in0=Q["fi"], in1=Q["braw_r"], op=ALU.mult), ("q_fi", ("s5in", "br"), "s5Bre"), ("q_u2",))
                S.dve(lambda e, biv=biv: e.tensor_tensor(out=biv, in0=Q["u1"], in1=Q["u2"], op=ALU.add), ("q_u1", "q_u2"), ("s5Bim",))
            S.barrier()
            AR.off = mark
            ufm = AR.b16(4, BLK); hre = AR.b16(BLK); nhim = AR.b16(BLK); ygb = AR.b16(4, BLK)
            t1 = AR.f32(BLK); t2 = AR.f32(BLK); wre = AR.f32(BLK); wim = AR.f32(BLK); gre = AR.f32(BLK); gim = AR.f32(BLK)
            yp = AR.f32(BLK); x2 = AR.f32(BLK); sgl = AR.f32(BLK); gsv = AR.f32(BLK)
            slu = wslot(); wu = wb_view(slu, 8, 512); load_w(slu, win_view(l, OFF["s5_u"], 512), wu)
            slg = wslot(); wg = wb_view(slg, 8, 512); load_w(slg, win_view(l, OFF["s5_gate"], 512), wg)
            slw = wslot(); wglu = wb_view(slw, 4, 512); load_w(slw, dwglu[l].rearrange("(kc p) c -> p kc c", p=128), wglu)
            GK = 1.5957691216057308
            for n in range(NB):
                for c4 in range(4):
                    p = PS[c4 % 2]; kp = ("ps", c4 % 2)
                    mm(p[:, 0:BLK], proj_pairs(wu, c4 * 128, 128, n * BLK, BLK), (("wb", slu), ("HT", n)), (kp,))
                    S.act(lambda e, c4=c4, p=p: e.activation(out=ufm[:, c4, :], in_=p[:, 0:BLK], func=AF.Copy), (kp,), (("ufm", c4),))
                for mt in range(16):
                    uc = mt // 4
                    cjb = cj[:, mt:mt + 1, :].broadcast_to([128, TPB, 128])
                    sjb = sj[:, mt:mt + 1, :].broadcast_to([128, TPB, 128])
                    v3 = lambda ap: ap.rearrange("p (a b) -> p a b", a=TPB)
                    pr = PS[2 + (mt % 2) * 2]; pi = PS[3 + (mt % 2) * 2]
                    kpr = ("ps", 2 + (mt % 2) * 2); kpi = ("ps", 3 + (mt % 2) * 2)
                    mm(pr[:, 0:BLK], [(Bre[:, mt, :], ufm[:, uc, :])], ("s5Bre", ("ufm", uc)), (kpr,))
                    mm(pi[:, 0:BLK], [(Bim[:, mt, :], ufm[:, uc, :])], ("s5Bim", ("ufm", uc)), (kpi,))
                    S.dve(lambda e, pr=pr, cjb=cjb: e.tensor_tensor(out=v3(t1), in0=v3(pr[:, 0:BLK]), in1=cjb, op=ALU.mult), (kpr,) + MC, ("t1",))
                    S.dve(lambda e, pi=pi, sjb=sjb: e.tensor_tensor(out=v3(t2), in0=v3(pi[:, 0:BLK]), in1=sjb, op=ALU.mult), (kpi,) + MC, ("t2",))
                    S.pool(lambda e: e.tensor_tensor(out=wre, in0=t1, in1=t2, op=ALU.add), ("t1", "t2"), ("wre",))
                    S.dve(lambda e, pi=pi, cjb=cjb: e.tensor_tensor(out=v3(t1), in0=v3(pi[:, 0:BLK]), in1=cjb, op=ALU.mult), (kpi, "wre") + MC, ("t1",))
                    S.dve(lambda e, pr=pr, sjb=sjb: e.tensor_tensor(out=v3(t2), in0=v3(pr[:, 0:BLK]), in1=sjb, op=ALU.mult), (kpr, "wre") + MC, ("t2",))
                    S.pool(lambda e: e.tensor_tensor(out=wim, in0=t1, in1=t2, op=ALU.subtract), ("t1", "t2"), ("wim",))
                    for k in range(TPB):
                        ck = n * TPB + k
                        c0 = k * 128
                        if ck > 0:
                            if k == 0:
                                lre, lim = stt[:, mt, 0:1], stt[:, mt, 1:2]
                            else:
                                lre, lim = gre[:, c0 - 1:c0], gim[:, c0 - 1:c0]
                            S.dve(lambda e, lre=lre, c0=c0, mt=mt: e.scalar_tensor_tensor(out=wre[:, c0:c0 + 1], in0=lre, scalar=rc[:, mt:mt + 1], in1=wre[:, c0:c0 + 1], op0=ALU.mult, op1=ALU.add),
                                  ("wre", "gre", ("stt", mt)) + MC, ("wre",))
                            S.dve(lambda e, lim=lim, c0=c0, mt=mt: e.scalar_tensor_tensor(out=wre[:, c0:c0 + 1], in0=lim, scalar=nrs[:, mt:mt + 1], in1=wre[:, c0:c0 + 1], op0=ALU.mult, op1=ALU.add),
                                  ("wre", "gim", ("stt", mt)) + MC, ("wre",))
                            S.dve(lambda e, lre=lre, c0=c0, mt=mt: e.scalar_tensor_tensor(out=wim[:, c0:c0 + 1], in0=lre, scalar=rsn[:, mt:mt + 1], in1=wim[:, c0:c0 + 1], op0=ALU.mult, op1=ALU.add),
                                  ("wim", "gre", ("stt", mt)) + MC, ("wim",))
                            S.dve(lambda e, lim=lim, c0=c0, mt=mt: e.scalar_tensor_tensor(out=wim[:, c0:c0 + 1], in0=lim, scalar=rc[:, mt:mt + 1], in1=wim[:, c0:c0 + 1], op0=ALU.mult, op1=ALU.add),
                                  ("wim", "gim", ("stt", mt)) + MC, ("wim",))
                        rb = r_[:, mt:mt + 1].broadcast_to([128, 128])
                        S.dve(lambda e, c0=c0, rb=rb: e.tensor_tensor_scan(out=gre[:, c0:c0 + 128], data0=rb, data1=wre[:, c0:c0 + 128], initial=0.0, op0=ALU.mult, op1=ALU.add),
                              ("wre",) + MC, ("gre",))
                        S.dve(lambda e, c0=c0, rb=rb: e.tensor_tensor_scan(out=gim[:, c0:c0 + 128], data0=rb, data1=wim[:, c0:c0 + 128], initial=0.0, op0=ALU.mult, op1=ALU.add),
                              ("wim",) + MC, ("gim",))
                    S.act(lambda e, mt=mt: e.activation(out=stt[:, mt, 0:1], in_=gre[:, BLK - 1:BLK], func=AF.Copy), ("gre",), (("stt", mt),))
                    S.act(lambda e, mt=mt: e.activation(out=stt[:, mt, 1:2], in_=gim[:, BLK - 1:BLK], func=AF.Copy), ("gim",), (("stt", mt),))
                    S.dve(lambda e, cjb=cjb: e.tensor_tensor(out=v3(t1), in0=v3(gre), in1=cjb, op=ALU.mult), ("gre",) + MC, ("t1",))
                    S.pool(lambda e, sjb=sjb: e.tensor_tensor(out=v3(t2), in0=v3(gim), in1=sjb, op=ALU.mult), ("gim",) + MC, ("t2",))
                    S.dve(lambda e: e.tensor_tensor(out=hre, in0=t1, in1=t2, op=ALU.subtract), ("t1", "t2"), ("hre",))
                    S.dve(lambda e, sjb=sjb: e.tensor_tensor(out=v3(t1), in0=v3(gre), in1=sjb, op=ALU.mult), ("gre", "hre") + MC, ("t1",))
                    S.pool(lambda e, cjb=cjb: e.tensor_tensor(out=v3(t2), in0=v3(gim), in1=cjb, op=ALU.mult), ("gim", "hre") + MC, ("t2",))
                    S.dve(lambda e: e.scalar_tensor_tensor(out=nhim, in0=t1, scalar=-1.0, in1=t2, op0=ALU.mult, op1=ALU.subtract), ("t1", "t2"), ("nhim",))
                    py = PS[6]
                    def fny(e, mt=mt, py=py):
                        e.matmul(py[:, 0:BLK], lhsT=Cre[:, mt, :], rhs=hre, start=(mt % 4 == 0), stop=False)
                        return e.matmul(py[:, 0:BLK], lhsT=Cim[:, mt, :], rhs=nhim, start=False, stop=(mt % 4 == 3))
                    S.pe(fny, ("hre", "nhim", "s5Cre", "s5Cim"), (("ps", 6),))
                    if mt % 4 == 3:
                        c4 = mt // 4
                        S.dve(lambda e, c4=c4, py=py: e.scalar_tensor_tensor(out=yp, in0=ufm[:, c4, :], scalar=pbl("s5d", c4), in1=py[:, 0:BLK], op0=ALU.mult, op1=ALU.add),
                              (("ufm", c4), ("ps", 6)) + LR, ("yp",))
                        S.pool(lambda e: e.tensor_tensor(out=x2, in0=yp, in1=yp, op=ALU.mult), ("yp",), ("x2",))
                        S.dve(lambda e: e.tensor_scalar(out=x2, in0=x2, scalar1=float(GK * 0.044715), scalar2=float(GK), op0=ALU.mult, op1=ALU.add), ("x2",), ("x2",))
                        S.dve(lambda e: e.tensor_tensor(out=x2, in0=x2, in1=yp, op=ALU.mult), ("x2", "yp"), ("x2",))
                        S.act(lambda e: e.activation(out=x2, in_=x2, func=AF.Sigmoid), ("x2",), ("x2",))
                        S.dve(lambda e, c4=c4: e.tensor_tensor(out=ygb[:, c4, :], in0=yp, in1=x2, op=ALU.mult), ("yp", "x2"), (("ygb", c4),))
                for c4 in range(4):
                    pg = PS[c4 % 2]; kpg = ("ps", c4 % 2)
                    mm(pg[:, 0:BLK], [(wglu[:, k4, c4 * 128:(c4 + 1) * 128], ygb[:, k4, :]) for k4 in range(4)],
                       (("wb", slw),) + tuple(("ygb", k4) for k4 in range(4)), (kpg,))
                    S.act(lambda e, c4=c4, pg=pg: e.activation(out=sgl, in_=pg[:, 0:BLK], func=AF.Sigmoid, bias=pbl("s5bglu", c4), scale=1.0), (kpg,) + LR, ("sgl",))
                    pgt = PS[2 + c4 % 2]; kpgt = ("ps", 2 + c4 % 2)
                    mm(pgt[:, 0:BLK], proj_pairs(wg, c4 * 128, 128, n * BLK, BLK), (("wb", slg), ("HT", n)), (kpgt,))
                    S.act(lambda e, pgt=pgt: e.activation(out=gsv, in_=pgt[:, 0:BLK], func=AF.Silu), (kpgt,), ("gsv",))
                    S.dve(lambda e, c4=c4: e.tensor_tensor(out=sgl, in0=sgl, in1=ygb[:, c4, :], op=ALU.mult), ("sgl", ("ygb", c4)), ("sgl",))
                    S.dve(lambda e, c4=c4, n=n: e.tensor_tensor(out=YB[:, c4, n * BLK:(n + 1) * BLK], in0=sgl, in1=gsv, op=ALU.mult), ("sgl", "gsv"), (("YB", n),))

        BR = {0: branch_da, 1: branch_ssd, 2: branch_s5, 3: branch_mla}
        for s in range(NSEQ):
            for l in range(L):
                layer_setup(l)
                stage0(s, l)
                S.barrier()
                first = True
                for b in range(4):
                    if b not in branches:
                        continue
                    BR[b](s, l)
                    S.barrier()
                    merge(s, l, b, first)
                    S.barrier()
                    first = False
                if first:
                    S.dve(lambda e: e.memset(MG, 0.0), (), tuple(("MG", f, n) for f in range(8) for n in range(NB)))
                if debug and s == 0 and l == 0:
                    for nm, src_, dst_ in (("ht", HT, dbg_ht), ("yb", YB, dbg_yb), ("mg", MG, dbg_mg)):
                        for q in range(src_.shape[1]):
                            dtmp = ARN[:, 0:S_LEN]
                            S.dve(lambda e, src_=src_, q=q, dtmp=dtmp: e.tensor_copy(out=dtmp, in_=src_[:, q, :]), (), ("dtmp",))
                            S.dma("sp", lambda e, dst_=dst_, q=q, dtmp=dtmp: e.dma_start(out=dst_[:, q * S_LEN:(q + 1) * S_LEN], in_=dtmp), r=("dtmp",), w=())
                    S.barrier()
                outproj(s, l)
                S.barrier()
        S.emit()
    return nc


_PROG_CACHE = {}


def run_cores(inputs, S_LEN, NSEQ, DEPTH, n_cores, branches=(0, 1, 2, 3), debug=False):
    key = (S_LEN, NSEQ, DEPTH, tuple(branches), debug)
    if key not in _PROG_CACHE:
        _PROG_CACHE[key] = build_program(S_LEN, NSEQ, DEPTH, branches, debug=debug)
    nc = _PROG_CACHE[key]
    f32 = lambda a: np.ascontiguousarray(np.asarray(a, dtype=np.float32))
    pb, s5rep, s5B, s5C = host_params({k: np.asarray(v) for k, v in inputs.items()}, DEPTH)
    shared = {
        "w_in": f32(inputs["w_in"])[:DEPTH], "w_br": f32(inputs["w_br"])[:DEPTH], "w_out": f32(inputs["w_out"])[:DEPTH],
        "w_uq": f32(inputs["mla_w_uq"])[:DEPTH], "w_ukv": f32(inputs["mla_w_ukv"])[:DEPTH], "w_glu": f32(inputs["s5_w_glu"])[:DEPTH],
        "pblob": pb, "cblob": host_consts(), "s5rep": s5rep, "s5B": s5B, "s5C": s5C,
    }
    x = f32(inputs["x"])
    pos = np.ascontiguousarray(np.asarray(inputs["positions"], dtype=np.int32))
    in_maps = []
    for c in range(n_cores):
        m = dict(shared)
        m["x"] = np.ascontiguousarray(x[c * NSEQ:(c + 1) * NSEQ])
        m["pos"] = np.ascontiguousarray(pos[c * NSEQ:(c + 1) * NSEQ])
        in_maps.append(m)
    res = run_bass_kernel_spmd(nc, in_maps, core_ids=list(range(n_cores)))
    outs = [np.asarray(r["out"]).reshape(NSEQ, S_LEN, D_MODEL) for r in res.results]
    if debug:
        global DBG
        DBG = {k: np.asarray(res.results[0][k]) for k in ("dbg_ht", "dbg_yb", "dbg_mg")}
    return np.concatenate(outs, axis=0).astype(np.float32)


def kernel(**inputs):
    return run_cores(inputs, 2048, 2, 2, 8)
```

```python
import numpy as np
import concourse.bass as bass
import concourse.mybir as mybir
from concourse.alu_op_type import AluOpType as ALU
from concourse.bass_utils import run_bass_kernel_spmd

F32 = mybir.dt.float32
BF16 = mybir.dt.bfloat16
I32 = mybir.dt.int32
AF = mybir.ActivationFunctionType
AX = mybir.AxisListType


class Op:
    __slots__ = ("eng", "fn", "reads", "writes", "dma", "waits", "inc", "ticket", "idx", "sem", "cost", "seg")

    def __init__(self, eng, fn, reads, writes, dma, cost=None):
        self.eng, self.fn, self.reads, self.writes, self.dma = eng, fn, reads, writes, dma
        self.cost = cost
        self.seg = 0
        self.waits = []
        self.inc = False
        self.ticket = None
        self.sem = None


class Sched:
    ENGS = ("pe", "dve", "act", "pool", "sp")

    def __init__(self, nc, n_dma_sems=12):
        self.nc = nc
        self.ops = []
        self.n_dma_sems = n_dma_sems
        self._bar_ops = set()
        self._seg = 0
        self.reorder = True
        import os
        self.same_sync = True

    DEFCOST = {"pe": 0.4, "dve": 0.55, "act": 0.5, "pool": 0.9, "sp": 0.1}

    def add(self, eng, fn, reads=(), writes=(), dma=False, cost=None):
        if cost is None:
            cost = 0.15 if dma else self.DEFCOST[eng]
        op = Op(eng, fn, tuple(reads), tuple(writes), dma, cost)
        op.idx = len(self.ops)
        op.seg = self._seg
        self.ops.append(op)
        return op

    def pe(self, fn, r=(), w=(), cost=None):
        return self.add("pe", fn, r, w, cost=cost)

    def dve(self, fn, r=(), w=()):
        return self.add("dve", fn, r, w)

    def act(self, fn, r=(), w=()):
        return self.add("act", fn, r, w)

    def pool(self, fn, r=(), w=()):
        return self.add("pool", fn, r, w)

    def dma(self, q, fn, r=(), w=()):
        return self.add(q, fn, r, w, dma=True)

    def barrier(self):
        self._seg += 1
        for e in self.ENGS:
            self.add(e, lambda eng: eng.drain(), (), (("__barX__", e),))
        for e in self.ENGS:
            op = self.add(e, None, tuple(("__barX__", f) for f in self.ENGS), ())
            self._bar_ops.add(op.idx)
        self._seg += 1

    def finalize(self):
        last_writer = {}
        readers = {}
        cnt = {e: 0 for e in self.ENGS}
        dma_sem_total = {}
        dma_rr = {e: 0 for e in self.ENGS}
        waited = {}
        deps_of = []
        all_dma = []
        for op in self.ops:
            deps = set()
            for r in op.reads:
                j = last_writer.get(r)
                if j is not None:
                    deps.add(j)
            for w in op.writes:
                j = last_writer.get(w)
                if j is not None:
                    deps.add(j)
                for j in readers.get(w, ()):
                    deps.add(j)
            if op.idx in self._bar_ops:
                deps.update(all_dma)
            if op.dma:
                all_dma.append(op.idx)
            deps.discard(op.idx)
            deps = set(j for j in deps if self.ops[j].fn is not None)
            deps_of.append(deps)
            for r in op.reads:
                readers.setdefault(r, []).append(op.idx)
            for w in op.writes:
                last_writer[w] = op.idx
                readers[w] = []
        need_inc = [False] * len(self.ops)
        for op in self.ops:
            for j in deps_of[op.idx]:
                pj = self.ops[j]
                if pj.dma or pj.eng != op.eng or op.dma or (self.same_sync and pj.eng != 'pe'):
                    need_inc[j] = True
        for op in self.ops:
            if op.dma:
                need_inc[op.idx] = True
        self.order = self._list_schedule(deps_of) if self.reorder else list(range(len(self.ops)))
        for oi in self.order:
            op = self.ops[oi]
            e = op.eng
            if op.dma:
                k = dma_rr[e] % self.n_dma_sems
                dma_rr[e] += 1
                key = ("d", e, k)
                prev = dma_sem_total.get(key, 0)
                if prev > 0 and waited.get((e, key), 0) < prev:
                    op.waits.append((key, prev))
                    waited[(e, key)] = prev
                dma_sem_total[key] = prev + 16
                op.sem = key
                op.ticket = prev + 16
                op.inc = True
            elif need_inc[op.idx]:
                cnt[e] += 1
                op.sem = ("c", e)
                op.ticket = cnt[e]
                op.inc = True
            for j in sorted(deps_of[op.idx]):
                pj = self.ops[j]
                if (not pj.dma) and pj.eng == e and not op.dma and (e == 'pe' or not self.same_sync):
                    continue
                if (not pj.dma) and pj.eng == e and op.dma:
                    pass
                key, val = pj.sem, pj.ticket
                if waited.get((e, key), 0) >= val:
                    continue
                waited[(e, key)] = val
                op.waits.append((key, val))

    def _list_schedule(self, deps_of):
        import heapq
        ops = self.ops
        n = len(ops)
        order = []
        i = 0
        SYNC = 0.25
        DMA_LAT = 2.5
        while i < n:
            j = i
            seg = ops[i].seg
            while j < n and ops[j].seg == seg:
                j += 1
            idxs = range(i, j)
            if j - i <= 2 or any(ops[k].fn is None for k in idxs) :
                order.extend(idxs)
                i = j
                continue
            indeg = {}
            users = {}
            for k in idxs:
                d = [x for x in deps_of[k] if i <= x < j]
                indeg[k] = len(d)
                for x in d:
                    users.setdefault(x, []).append(k)
            finish = {}
            ready_t = {k: 0.0 for k in idxs}
            efree = {e: 0.0 for e in self.ENGS}
            pend = {e: [] for e in self.ENGS}
            avail = {e: [] for e in self.ENGS}
            for k in idxs:
                if indeg[k] == 0:
                    heapq.heappush(pend[ops[k].eng], (0.0, k))
            done = 0
            tot = j - i
            sched = []
            while done < tot:
                best = None
                for e in self.ENGS:
                    T = efree[e]
                    while pend[e] and pend[e][0][0] <= T:
                        heapq.heappush(avail[e], heapq.heappop(pend[e])[1])
                    if avail[e]:
                        cand = (T, avail[e][0], e, True)
                    elif pend[e]:
                        cand = (pend[e][0][0], pend[e][0][1], e, False)
                    else:
                        continue
                    if best is None or cand[:2] < best[:2]:
                        best = cand
                start, k, e, from_avail = best
                if from_avail:
                    heapq.heappop(avail[e])
                else:
                    heapq.heappop(pend[e])
                op = ops[k]
                efree[e] = start + op.cost
                fin = start + op.cost + (DMA_LAT if op.dma else 0.0)
                finish[k] = fin
                sched.append((start, k))
                done += 1
                for u in users.get(k, ()):
                    lat = 0.0 if (ops[u].eng == e and not op.dma and e == "pe") else SYNC
                    ready_t[u] = max(ready_t[u], fin + lat)
                    indeg[u] -= 1
                    if indeg[u] == 0:
                        heapq.heappush(pend[ops[u].eng], (ready_t[u], u))
            sched.sort()
            order.extend(k for _, k in sched)
            i = j
        assert len(order) == n and len(set(order)) == n
        return order

    def emit(self, out_waits=()):
        nc = self.nc
        self.finalize()
        keys = set()
        for op in self.ops:
            if op.sem is not None:
                keys.add(op.sem)
        keys = sorted(keys)
        sems = {}
        import contextlib
        with contextlib.ExitStack() as st:
            for i, k in enumerate(keys):
                sems[k] = st.enter_context(nc.semaphore("s_" + "_".join(str(x) for x in k)))
            block = st.enter_context(nc.Block())
            ops = self.ops

            order = self.order

            def run(eng_name, eng):
                last = None
                for oi in order:
                    op = ops[oi]
                    if op.eng != eng_name:
                        continue
                    for (k, v) in op.waits:
                        eng.wait_ge(sems[k], v)
                    if op.fn is None:
                        assert not op.inc
                        continue
                    ins = op.fn(eng)
                    if op.inc:
                        ins.then_inc(sems[op.sem], 16 if op.dma else 1)
                    if op.dma:
                        last = op
                done = {}
                for op in ops:
                    if op.eng == eng_name and op.dma:
                        done[op.sem] = max(done.get(op.sem, 0), op.ticket)
                for k, v in done.items():
                    eng.wait_ge(sems[k], v)

            @block.tensor
            def _(e):
                run("pe", e)

            @block.vector
            def _(e):
                run("dve", e)

            @block.scalar
            def _(e):
                run("act", e)

            @block.gpsimd
            def _(e):
                run("pool", e)

            @block.sync
            def _(e):
                run("sp", e)


D_MODEL = 1024
D_IN = 9672
EPS = 1e-6
ROPE_THETA = 500000.0
OFF = dict(da_q=0, da_k=512, da_v=1024, da_gate=1536, mb_z=2048, mb_xbc=2560, mb_dt=3584, s5_u=3592,
           s5_gate=4104, mla_cq=4616, mla_ckv=4872, mla_krope=5000, mla_gate=5064, gate=5576)
TWO_PI = float(2.0 * np.pi)
import os as _os
DB_S5 = _os.environ.get('DB_S5', '1') == '1'
DB_SSD = _os.environ.get('DB_SSD', '0') == '1'
DB_DA = _os.environ.get('DB_DA', '0') == '1'
MAGIC = 12582912.0

PB = {}
_o = 0
for _n, _w in [("ng", 8), ("daq", 1), ("dak", 1), ("convw", 32), ("convb", 8), ("s5d", 4), ("s5bglu", 4),
               ("mqa", 2), ("mkva", 1), ("mqn_n", 1), ("mqn_r", 1), ("mkn_n", 1), ("mkn_r", 1),
               ("lamre", 16), ("lamim", 16), ("lstep", 16),
               ("subln", 128), ("lq1", 64), ("lk1", 64), ("lq2", 64), ("lk2", 64),
               ("dtb", 8), ("alog", 8), ("mbd", 512), ("mbnorm", 512)]:
    PB[_n] = (_o, _w)
    _o += _w
NP_COLS = _o
CB = {}
_o = 0
for _n, _w in [("ident", 128), ("ones", 128), ("blk64", 128), ("maskU", 128), ("ptda", 128), ("ptmla", 128),
               ("jrow", 128), ("invf_da", 1), ("invf_mla", 1)]:
    CB[_n] = (_o, _w)
    _o += _w
NC_COLS = _o


def host_consts():
    c = np.zeros((128, NC_COLS), np.float32)
    def put(n, a):
        o, w = CB[n]
        c[:a.shape[0], o:o + w] = a
    i = np.arange(128)
    put("ident", np.eye(128, dtype=np.float32))
    put("ones", np.ones((128, 128), np.float32))
    put("blk64", (i[:, None] // 64 == i[None, :] // 64).astype(np.float32))
    put("maskU", (i[:, None] <= i[None, :]).astype(np.float32))
    P = np.zeros((128, 128), np.float32)
    for b in range(2):
        for d in range(8):
            P[b * 64 + d, b * 64 + d + 8] = -1.0
            P[b * 64 + d + 8, b * 64 + d] = 1.0
    put("ptda", P.T.copy())
    P = np.zeros((128, 128), np.float32)
    for d in range(32):
        P[d, d + 32] = -1.0
        P[d + 32, d] = 1.0
    put("ptmla", P.T.copy())
    put("jrow", np.broadcast_to(np.arange(128, dtype=np.float32)[None, :], (128, 128)))
    f_da = (1.0 / (ROPE_THETA ** (np.arange(0, 16, 2, dtype=np.float32) / np.float32(16)))).astype(np.float32)
    f_mla = (1.0 / (ROPE_THETA ** (np.arange(0, 64, 2, dtype=np.float32) / np.float32(64)))).astype(np.float32)
    v = np.zeros((128, 1), np.float32)
    for p in range(128):
        d = p % 64
        if d < 16:
            v[p, 0] = f_da[d % 8]
    put("invf_da", v)
    v = np.zeros((128, 1), np.float32)
    for p in range(64):
        v[p, 0] = f_mla[p % 32]
    put("invf_mla", v)
    return c


def host_params(inp, L):
    pb = np.zeros((L, 128, NP_COLS), np.float32)
    s5rep = np.zeros((L, 3, 128, 2048), np.float32)
    s5B = np.zeros((L, 2, 128, 16, 128), np.float32)
    s5C = np.zeros((L, 2, 128, 16, 128), np.float32)
    p = np.arange(128)
    for l in range(L):
        def put(n, a):
            o, w = PB[n]
            a = np.asarray(a, np.float32)
            if a.ndim == 1:
                a = a[:, None]
            pb[l, :a.shape[0], o:o + w] = a
        def rep(n, vec):
            o, w = PB[n]
            pb[l, :, o:o + w] = np.asarray(vec, np.float32).reshape(1, w)
        put("ng", inp["norm_g"][l].reshape(8, 128).T)
        put("daq", inp["da_q_norm"][l][p % 64])
        put("dak", inp["da_k_norm"][l][p % 64])
        cw = inp["mb_conv_w"][l]
        put("convw", cw.reshape(4, 8, 128).transpose(2, 1, 0).reshape(128, 32))
        put("convb", inp["mb_conv_b"][l].reshape(8, 128).T)
        put("s5d", inp["s5_d"][l].reshape(4, 128).T)
        put("s5bglu", inp["s5_b_glu"][l].reshape(4, 128).T)
        put("mqa", inp["mla_q_a_norm"][l].reshape(2, 128).T)
        put("mkva", inp["mla_kv_a_norm"][l])
        put("mqn_n", inp["mla_q_norm"][l][:128])
        put("mqn_r", inp["mla_q_norm"][l][128:192])
        put("mkn_n", inp["mla_k_norm"][l][:128])
        put("mkn_r", inp["mla_k_norm"][l][128:192])
        lre = inp["s5_lam_re"][l].reshape(16, 128).T
        lim = inp["s5_lam_im"][l].reshape(16, 128).T
        lst = np.repeat(inp["s5_log_step"][l], 64).reshape(16, 128).T
        put("lamre", lre); put("lamim", lim); put("lstep", lst)
        rep("subln", inp["da_subln"][l])
        rep("lq1", inp["da_lambda_q1"][l]); rep("lk1", inp["da_lambda_k1"][l])
        rep("lq2", inp["da_lambda_q2"][l]); rep("lk2", inp["da_lambda_k2"][l])
        rep("dtb", inp["mb_dt_bias"][l]); rep("alog", inp["mb_a_log"][l])
        rep("mbd", np.repeat(inp["mb_d"][l], 64)); rep("mbnorm", inp["mb_norm"][l])
        s5rep[l, 0] = inp["s5_lam_re"][l].reshape(1, 2048)
        s5rep[l, 1] = inp["s5_lam_im"][l].reshape(1, 2048)
        s5rep[l, 2] = np.repeat(inp["s5_log_step"][l], 64).reshape(1, 2048)
        for g in range(32):
            mt, j = g // 2, g % 2
            gi = g % 8
            s5B[l, 0, gi * 16:(gi + 1) * 16, mt, j * 64:(j + 1) * 64] = inp["s5_b_re"][l, g].T
            s5B[l, 1, gi * 16:(gi + 1) * 16, mt, j * 64:(j + 1) * 64] = inp["s5_b_im"][l, g].T
            s5C[l, 0, j * 64:(j + 1) * 64, mt, gi * 16:(gi + 1) * 16] = inp["s5_c_re"][l, g].T
            s5C[l, 1, j * 64:(j + 1) * 64, mt, gi * 16:(gi + 1) * 16] = inp["s5_c_im"][l, g].T
    return pb, s5rep, s5B, s5C


class _Shift:
    def __init__(self, ap):
        self.ap = ap

    def __getitem__(self, idx):
        p, f = idx
        return self.ap[:, f]


class Arena:
    def __init__(self, ap32, ncols):
        self.ap = ap32
        self.n = ncols
        self.off = 0

    def reset(self):
        self.off = 0

    def f32(self, *shape):
        n = int(np.prod(shape))
        assert self.off + n <= self.n, ("arena overflow", self.off, n, self.n)
        v = self.ap[:, self.off:self.off + n]
        self.off += n
        if len(shape) == 2:
            v = v.rearrange("p (a b) -> p a b", a=shape[0])
        return v

    def b16(self, *shape):
        n = int(np.prod(shape))
        n32 = (n + 1) // 2
        assert self.off + n32 <= self.n, ("arena overflow", self.off, n32, self.n)
        v = self.ap[:, self.off:self.off + n32].bitcast(BF16)
        if n32 * 2 != n:
            v = v[:, 0:n]
        self.off += n32
        if len(shape) == 2:
            v = v.rearrange("p (a b) -> p a b", a=shape[0])
        return v


def build_program(S_LEN=2048, NSEQ=2, DEPTH=2, branches=(0, 1, 2, 3), ARENA_COLS=19456, debug=False):
    import contextlib
    NT = S_LEN // 128
    BLK = min(512, S_LEN)
    NB = S_LEN // BLK
    TPB = BLK // 128
    L = DEPTH
    nc = bass.Bass("TRN2", target_bir_lowering=False)
    dx = nc.dram_tensor("x", [NSEQ, S_LEN, D_MODEL], F32, kind="ExternalInput").ap()
    dpos = nc.dram_tensor("pos", [NSEQ, S_LEN], I32, kind="ExternalInput").ap()
    dwin = nc.dram_tensor("w_in", [L, D_MODEL, D_IN], F32, kind="ExternalInput").ap()
    dwbr = nc.dram_tensor("w_br", [L, 4, 512, D_MODEL], F32, kind="ExternalInput").ap()
    dwout = nc.dram_tensor("w_out", [L, D_MODEL, D_MODEL], F32, kind="ExternalInput").ap()
    dwuq = nc.dram_tensor("w_uq", [L, 256, 768], F32, kind="ExternalInput").ap()
    dwukv = nc.dram_tensor("w_ukv", [L, 128, 1024], F32, kind="ExternalInput").ap()
    dwglu = nc.dram_tensor("w_glu", [L, 512, 512], F32, kind="ExternalInput").ap()
    dpb = nc.dram_tensor("pblob", [L, 128, NP_COLS], F32, kind="ExternalInput").ap()
    dcb = nc.dram_tensor("cblob", [128, NC_COLS], F32, kind="ExternalInput").ap()
    ds5rep = nc.dram_tensor("s5rep", [L, 3, 128, 2048], F32, kind="ExternalInput").ap()
    ds5B = nc.dram_tensor("s5B", [L, 2, 128, 16, 128], F32, kind="ExternalInput").ap()
    ds5C = nc.dram_tensor("s5C", [L, 2, 128, 16, 128], F32, kind="ExternalInput").ap()
    dout = nc.dram_tensor("out", [NSEQ, S_LEN, D_MODEL], F32, kind="ExternalOutput").ap()
    if debug:
        dbg_ht = nc.dram_tensor("dbg_ht", [128, 8 * S_LEN], F32, kind="ExternalOutput").ap()
        dbg_yb = nc.dram_tensor("dbg_yb", [128, 4 * S_LEN], F32, kind="ExternalOutput").ap()
        dbg_mg = nc.dram_tensor("dbg_mg", [128, 8 * S_LEN], F32, kind="ExternalOutput").ap()

    S = Sched(nc)
    st = contextlib.ExitStack()

    def sb(name, shape, dt=F32):
        return st.enter_context(nc.sbuf_tensor(name, shape, dt))[:]

    def psum(name, shape, dt=F32):
        return st.enter_context(nc.psum_tensor(name, shape, dt))[:]

    uid = [0]

    def U(prefix):
        uid[0] += 1
        return (prefix, uid[0])

    with st:
        CF = sb("CF", [128, NC_COLS])
        CBF = sb("CBF", [128, 6 * 128], BF16)
        PBL = sb("PBL", [128, NP_COLS])
        DRV = sb("DRV", [128, 256])
        HT = sb("HT", [128, 8, S_LEN], BF16)
        MG = sb("MG", [128, 8, S_LEN], BF16)
        YB = sb("YB", [128, 4, S_LEN], BF16)
        WB = [sb("WB%d" % i, [128, 4096], BF16) for i in range(3)]
        ARN = sb("ARN", [128, ARENA_COLS])
        AR = Arena(ARN, ARENA_COLS)
        PS = [psum("PS%d" % i, [128, 512]) for i in range(7)]
        PSB = psum("PSB", [128, 1024], BF16)

        def cf(n):
            o, w = CB[n]
            return CF[:, o:o + w]

        def pbl(n, a=None, b=None):
            o, w = PB[n]
            if a is None:
                return PBL[:, o:o + w]
            return PBL[:, o + a:o + (a + 1 if b is None else b)]

        IDB = CBF[:, 0:128]; ONESB = CBF[:, 128:256]; BLK64B = CBF[:, 256:384]
        MASKUB = CBF[:, 384:512]; PTDAB = CBF[:, 512:640]; PTMLAB = CBF[:, 640:768]
        ONESF = cf("ones"); MASKUF = cf("maskU")
        D_EPS = DRV[:, 0:1]; D_ONE = DRV[:, 1:2]; D_GQ = DRV[:, 2:3]; D_NEGLAM = DRV[:, 3:4]
        D_MQN_N = DRV[:, 4:5]; D_MQN_R = DRV[:, 5:6]; D_T1 = DRV[:, 6:7]; D_T2 = DRV[:, 7:8]
        D_NEGA = DRV[:, 8:16]; D_NEGONES = DRV[:, 16:144]

        wb_rr = [0]

        def wslot():
            i = wb_rr[0] % 3
            wb_rr[0] += 1
            return i

        def load_w(slot, src_ap, view):
            S.dma("pool", lambda e, o=view, i=src_ap: e.dma_start(out=o, in_=i), r=(), w=(("wb", slot),))

        def win_view(l, c0, ncols):
            return dwin[l, :, c0:c0 + ncols].rearrange("(kc p) c -> p kc c", p=128)

        def wb_view(slot, kc, ncols, off=0):
            return WB[slot][:, off:off + kc * ncols].rearrange("p (k c) -> p k c", k=kc)

        def mm(out_ap, pairs, r, w):
            def fn(e, out_ap=out_ap, pairs=pairs):
                ins = None
                n = len(pairs)
                for i, (a, b) in enumerate(pairs):
                    ins = e.matmul(out_ap, lhsT=a, rhs=b, start=(i == 0), stop=(i == n - 1))
                return ins
            cst = 0.06
            for (a_, b_) in pairs:
                ncol = int(np.prod(b_.shape[1:]))
                cst += max(ncol, 64) / 1400.0 * (4.0 if a_.dtype == F32 else 1.0)
            S.pe(fn, r, w, cost=cst)

        def proj_pairs(wv, c0, mw, tok0, ntok):
            return [(wv[:, kc, c0:c0 + mw], HT[:, kc, tok0:tok0 + ntok]) for kc in range(8)]

        def tok_pairs(wv, c0, ncols, tile):
            return [(HT[:, kc, tile * 128:(tile + 1) * 128], wv[:, kc, c0:c0 + ncols]) for kc in range(8)]

        def rsqrt_act(out_ap, in_ap, scale, r, w):
            S.act(lambda e: e.activation(out=out_ap, in_=in_ap, func=AF.Ln, scale=scale, bias=D_EPS[0:out_ap.shape[0], :]), r, w)
            S.act(lambda e: e.activation(out=out_ap, in_=out_ap, func=AF.Exp, scale=-0.5), w, w)

        def sincos(v_ap, tmp_ap, out_sin, out_cos, r, kv, kt, ks, kc_):
            S.dve(lambda e: e.tensor_scalar(out=tmp_ap, in0=v_ap, scalar1=MAGIC, scalar2=None, op0=ALU.add), r + (kv,), (kt,))
            S.dve(lambda e: e.tensor_scalar(out=tmp_ap, in0=tmp_ap, scalar1=-MAGIC, scalar2=None, op0=ALU.add), (kt,), (kt,))
            S.dve(lambda e: e.tensor_tensor(out=tmp_ap, in0=v_ap, in1=tmp_ap, op=ALU.subtract), (kv, kt), (kt,))
            S.act(lambda e: e.activation(out=out_sin, in_=tmp_ap, func=AF.Sin, scale=TWO_PI), (kt,), (ks,))
            S.dve(lambda e: e.tensor_scalar(out=tmp_ap, in0=v_ap, scalar1=0.25, scalar2=MAGIC, op0=ALU.add, op1=ALU.add), (kv, ks), (kt,))
            S.dve(lambda e: e.tensor_scalar(out=tmp_ap, in0=tmp_ap, scalar1=-MAGIC, scalar2=None, op0=ALU.add), (kt,), (kt,))
            S.dve(lambda e: e.scalar_tensor_tensor(out=tmp_ap, in0=v_ap, scalar=0.25, in1=tmp_ap, op0=ALU.add, op1=ALU.subtract), (kv, kt), (kt,))
            S.act(lambda e: e.activation(out=out_cos, in_=tmp_ap, func=AF.Sin, scale=TWO_PI), (kt,), (kc_,))

        S.dma("sp", lambda e: e.dma_start(out=CF, in_=dcb), w=("CF",))
        for i, n in enumerate(["ident", "ones", "blk64", "maskU", "ptda", "ptmla"]):
            S.dve(lambda e, i=i, n=n: e.tensor_copy(out=CBF[:, i * 128:(i + 1) * 128], in_=cf(n)), ("CF",), ("CBF",))
        S.dve(lambda e: e.memset(D_EPS, EPS), (), ("DRVc",))
        S.dve(lambda e: e.memset(D_ONE, 1.0), (), ("DRVc",))
        S.dve(lambda e: e.memset(D_ONE, 1.0), (), ("DRVc",))
        S.dve(lambda e: e.memset(D_NEGONES, -1.0), (), ("DRVc",))
        CONST_R = ("CF", "CBF", "DRVc")

        def layer_setup(l):
            lam_init = 0.8 - 0.6 * float(np.exp(-0.3 * l))
            S.dma("sp", lambda e: e.dma_start(out=PBL, in_=dpb[l]), w=("PBL",))
            kd = "DRV"
            S.dve(lambda e: e.tensor_scalar(out=D_GQ, in0=pbl("daq"), scalar1=0.125, scalar2=None, op0=ALU.mult), ("PBL",), (kd,))
            S.dve(lambda e: e.tensor_scalar(out=D_MQN_N, in0=pbl("mqn_n"), scalar1=float(192 ** -0.5), scalar2=None, op0=ALU.mult), ("PBL",), (kd,))
            S.dve(lambda e: e.tensor_scalar(out=D_MQN_R, in0=pbl("mqn_r"), scalar1=float(192 ** -0.5), scalar2=None, op0=ALU.mult), ("PBL",), (kd,))
            tmp = DRV[:, 144:208]
            S.dve(lambda e: e.tensor_tensor(out=tmp, in0=pbl("lq1"), in1=pbl("lk1"), op=ALU.mult), ("PBL",), (kd,))
            S.dve(lambda e: e.tensor_reduce(out=D_T1, in_=tmp, axis=AX.X, op=ALU.add), (kd,), (kd,))
            S.dve(lambda e: e.tensor_tensor(out=tmp, in0=pbl("lq2"), in1=pbl("lk2"), op=ALU.mult), ("PBL",), (kd,))
            S.dve(lambda e: e.tensor_reduce(out=D_T2, in_=tmp, axis=AX.X, op=ALU.add), (kd,), (kd,))
            S.act(lambda e: e.activation(out=D_T1, in_=D_T1, func=AF.Exp), (kd,), (kd,))
            S.act(lambda e: e.activation(out=D_T2, in_=D_T2, func=AF.Exp), (kd,), (kd,))
            S.dve(lambda e: e.tensor_tensor(out=D_NEGLAM, in0=D_T2, in1=D_T1, op=ALU.subtract), (kd,), (kd,))
            S.dve(lambda e: e.tensor_scalar(out=D_NEGLAM, in0=D_NEGLAM, scalar1=-lam_init, scalar2=None, op0=ALU.add), (kd,), (kd,))
            S.dve(lambda e: e.tensor_scalar(out=pbl("subln"), in0=pbl("subln"), scalar1=float(1.0 - lam_init), scalar2=None, op0=ALU.mult), ("PBL", kd), ("PBL",))
            S.act(lambda e: e.activation(out=D_NEGA, in_=pbl("alog"), func=AF.Exp), ("PBL",), (kd,))
            S.dve(lambda e: e.tensor_scalar(out=D_NEGA, in0=D_NEGA, scalar1=-1.0, scalar2=None, op0=ALU.mult), (kd,), (kd,))
        LR = ("PBL", "DRV") + CONST_R

        def stage0(s, l):
            AR.reset()
            xt = [AR.f32(D_MODEL) for _ in range(2)]
            junk = AR.f32(D_MODEL)
            xn = AR.b16(TPB, D_MODEL)
            ssq = AR.f32(8)
            src = dx if l == 0 else dout
            for n in range(NB):
                for tt in range(TPB):
                    t = n * TPB + tt
                    b = t % 2
                    kx = ("xt", b)
                    S.dma("sp", lambda e, b=b, t=t: e.dma_start(out=xt[b], in_=src[s, t * 128:(t + 1) * 128, :]),
                          r=(("outd", s, t),), w=(kx,))
                    S.act(lambda e, b=b: e.activation(out=junk, in_=xt[b], func=AF.Square), (kx,), ("junk",))
                    S.dve(lambda e: e.tensor_reduce(out=ssq[:, 0:1], in_=junk, axis=AX.X, op=ALU.add), ("junk",), ("ssq",))
                    rsqrt_act(ssq[:, 0:1], ssq[:, 0:1], 1.0 / D_MODEL, ("ssq",) + LR, ("ssq",))
                    S.dve(lambda e, b=b, tt=tt: e.tensor_scalar(out=xn[:, tt, :], in0=xt[b], scalar1=ssq[:, 0:1], scalar2=None, op0=ALU.mult),
                          (kx, "ssq"), (("xn", tt),))
                for kc in range(8):
                    half = kc % 2
                    pt = PSB[:, half * 512:half * 512 + BLK]
                    def fn(e, kc=kc, pt=pt):
                        ins = None
                        for tt in range(TPB):
                            ins = e.transpose(out=pt[:, tt * 128:(tt + 1) * 128], in_=xn[:, tt, kc * 128:(kc + 1) * 128], identity=IDB)
                        return ins
                    S.pe(fn, tuple(("xn", tt) for tt in range(TPB)) + LR, (("psb", 0),))
                    S.dve(lambda e, kc=kc, pt=pt, n=n: e.tensor_scalar(out=HT[:, kc, n * BLK:(n + 1) * BLK], in0=pt, scalar1=pbl("ng", kc), scalar2=None, op0=ALU.mult),
                          (("psb", 0),) + LR, (("HT", n),))

        def merge(s, l, b, first):
            AR.reset()
            sg = [AR.f32(BLK) for _ in range(2)]
            tmp = [AR.f32(BLK) for _ in range(2)]
            sl_br = wslot()
            load_w(sl_br, dwbr[l, b].rearrange("(kc p) c -> p kc c", p=128), wb_view(sl_br, 4, 1024))
            wbr = wb_view(sl_br, 4, 1024)
            it = 0
            for fh in range(2):
                slg = wslot()
                load_w(slg, win_view(l, OFF["gate"] + b * 1024 + fh * 512, 512), wb_view(slg, 8, 512))
                wg = wb_view(slg, 8, 512)
                for f4 in range(4):
                    f = fh * 4 + f4
                    for n in range(NB):
                        pg = PS[(2 * it) % 6]; pb_ = PS[(2 * it + 1) % 6]
                        kpg = ("ps", (2 * it) % 6); kpb = ("ps", (2 * it + 1) % 6)
                        bi = it % 2
                        it += 1
                        mm(pg[:, 0:BLK], proj_pairs(wg, f4 * 128, 128, n * BLK, BLK), (("wb", slg), ("HT", n)), (kpg,))
                        mm(pb_[:, 0:BLK], [(wbr[:, k4, f * 128:(f + 1) * 128], YB[:, k4, n * BLK:(n + 1) * BLK]) for k4 in range(4)],
                           (("wb", sl_br), ("YB", n)), (kpb,))
                        S.act(lambda e, bi=bi, pg=pg: e.activation(out=sg[bi], in_=pg[:, 0:BLK], func=AF.Sigmoid), (kpg,), (("sg", bi),))
                        mgv = MG[:, f, n * BLK:(n + 1) * BLK]
                        if first:
                            S.dve(lambda e, bi=bi, pb_=pb_, mgv=mgv: e.tensor_tensor(out=mgv, in0=sg[bi], in1=pb_[:, 0:BLK], op=ALU.mult),
                                  (("sg", bi), kpb), (("MG", f, n),))
                        else:
                            S.dve(lambda e, bi=bi, pb_=pb_: e.tensor_tensor(out=tmp[bi], in0=sg[bi], in1=pb_[:, 0:BLK], op=ALU.mult),
                                  (("sg", bi), kpb), (("mtmp", bi),))
                            S.pool(lambda e, bi=bi, mgv=mgv: e.tensor_tensor(out=mgv, in0=mgv, in1=tmp[bi], op=ALU.add),
                                   (("mtmp", bi), ("MG", f, n)), (("MG", f, n),))

        def outproj(s, l):
            AR.reset()
            xt = [AR.f32(D_MODEL) for _ in range(2)]
            ot = [AR.f32(D_MODEL) for _ in range(2)]
            src = dx if l == 0 else dout
            sls = []
            for hh in range(2):
                sl = wslot()
                load_w(sl, dwout[l, :, hh * 512:(hh + 1) * 512].rearrange("(kc p) c -> p kc c", p=128), wb_view(sl, 8, 512))
                sls.append(sl)
            for t in range(NT):
                b = t % 2
                S.dma("sp", lambda e, b=b, t=t: e.dma_start(out=xt[b], in_=src[s, t * 128:(t + 1) * 128, :]),
                      r=(("outd", s, t),), w=(("xt", b),))
                for hh in range(2):
                    p = PS[(2 * t + hh) % 4]
                    kp = ("ps", (2 * t + hh) % 4)
                    wv = wb_view(sls[hh], 8, 512)
                    mm(p[:, 0:512], [(MG[:, kc, t * 128:(t + 1) * 128], wv[:, kc, :]) for kc in range(8)],
                       (("wb", sls[hh]),) + tuple(("MG", kc, t // TPB) for kc in range(8)), (kp,))
                    S.dve(lambda e, b=b, hh=hh, p=p: e.tensor_tensor(out=ot[b][:, hh * 512:(hh + 1) * 512], in0=p[:, 0:512], in1=xt[b][:, hh * 512:(hh + 1) * 512], op=ALU.add),
                          (kp, ("xt", b)), (("ot", b, hh),))
                S.dma("sp", lambda e, b=b, t=t: e.dma_start(out=dout[s, t * 128:(t + 1) * 128, :], in_=ot[b]),
                      r=(("ot", b, 0), ("ot", b, 1)), w=(("outd", s, t),))

        def rope_tables(s, invf_col, npart, Ct, St):
            pi_ = AR.f32(BLK).bitcast(I32)
            v = AR.f32(BLK); tmp = AR.f32(BLK)
            for n in range(NB):
                S.dma("sp", lambda e, n=n: e.dma_start(out=pi_[0:npart, :], in_=dpos[s:s + 1, n * BLK:(n + 1) * BLK].partition_broadcast(npart)),
                      w=("rp_pi",))
                S.dve(lambda e: e.tensor_copy(out=v[0:npart, :], in_=pi_[0:npart, :]), ("rp_pi",), ("rp_v",))
                S.dve(lambda e: e.tensor_scalar(out=v[0:npart, :], in0=v[0:npart, :], scalar1=invf_col[0:npart, :], scalar2=float(1.0 / TWO_PI), op0=ALU.mult, op1=ALU.mult),
                      ("rp_v",) + CONST_R, ("rp_v",))
                sincos(v[0:npart, :], tmp[0:npart, :], St[0:npart, n * BLK:(n + 1) * BLK], Ct[0:npart, n * BLK:(n + 1) * BLK],
                       (), "rp_v", "rp_t", ("rp_S", n), ("rp_C", n))

        def rope_apply(dst, xn_ap, pp, kpp, PT, npart, Ct, St, n, t1, t2, rkeys, wkey, par=0):
            mm(pp[0:npart, 0:BLK], [(PT[0:npart, 0:npart], xn_ap)], rkeys + CONST_R, (kpp,))
            S.dve(lambda e: e.tensor_tensor(out=t1[0:npart, :], in0=xn_ap, in1=Ct[0:npart, n * BLK:(n + 1) * BLK], op=ALU.mult),
                  rkeys + (("rp_C", n),), (("rp_t1", par),))
            S.dve(lambda e: e.tensor_tensor(out=t2[0:npart, :], in0=pp[0:npart, 0:BLK], in1=St[0:npart, n * BLK:(n + 1) * BLK], op=ALU.mult),
                  (kpp, ("rp_S", n)), (("rp_t2", par),))
            S.pool(lambda e: e.tensor_tensor(out=dst, in0=t1[0:npart, :], in1=t2[0:npart, :], op=ALU.add), (("rp_t1", par), ("rp_t2", par)), (wkey,))

        def attn_block(qb, parts, vaug, PTt, dest, dkey):
            t0 = qb * TPB
            for j in range(t0 + TPB):
                lo = max(0, j - t0)
                c0 = lo * 128
                bi = j % 2
                st_ = PS[4 + bi]
                kst = ("ps", 4 + bi)
                rk = tuple(p[3](j) for p in parts) + tuple(p[4] for p in parts)
                mm(st_[:, c0:BLK], [(p[0][0:p[2], j * 128:(j + 1) * 128], p[1][0:p[2], c0:BLK]) for p in parts], rk, (kst,))
                S.act(lambda e, bi=bi, st_=st_, c0=c0: e.activation(out=PTt[bi][:, c0:BLK], in_=st_[:, c0:BLK], func=AF.Exp), (kst,), (("ptt", bi),))
                if j >= t0:
                    S.pool(lambda e, bi=bi, c0=c0: e.tensor_tensor(out=PTt[bi][:, c0:c0 + 128], in0=PTt[bi][:, c0:c0 + 128], in1=MASKUB, op=ALU.mult),
                           (("ptt", bi),) + CONST_R, (("ptt", bi),))
                for i in range(lo, TPB):
                    def fn(e, i=i, j=j, bi=bi):
                        return e.matmul(PS[i][:, 0:130], lhsT=PTt[bi][:, i * 128:(i + 1) * 128], rhs=vaug[:, j, :],
                                        start=(j == 0), stop=(j == t0 + i))
                    S.pe(fn, (("ptt", bi), ("vaug", j)), (("ps", i),))
            rec = AR_small["rec"]
            for i in range(TPB):
                S.dve(lambda e, i=i: e.reciprocal(out=rec[:, i:i + 1], in_=PS[i][:, 128:129]), (("ps", i),), (("rec", i),))
                S.dve(lambda e, i=i: e.tensor_scalar(out=dest[:, i, :], in0=PS[i][:, 0:128], scalar1=rec[:, i:i + 1], scalar2=None, op0=ALU.mult),
                      (("ps", i), ("rec", i)), (dkey + (i,),))

        AR_small = {}

        def finish_head(l, h, qb, obf, okeys, gate_c0, wg_view, wg_key, gs):
            pt = PSB[:, 0:BLK]
            def fn(e):
                ins = None
                for i in range(TPB):
                    ins = e.transpose(out=pt[:, i * 128:(i + 1) * 128], in_=obf[:, i, :], identity=IDB)
                return ins
            S.pe(fn, okeys + CONST_R, (("psb", 0),))
            pg = PS[6]
            mm(pg[:, 0:BLK], proj_pairs(wg_view, gate_c0, 128, qb * BLK, BLK), (wg_key, ("HT", qb)), (("ps", 6),))
            S.act(lambda e: e.activation(out=gs, in_=pg[:, 0:BLK], func=AF.Silu), (("ps", 6),), ("gsilu",))
            S.dve(lambda e: e.tensor_tensor(out=YB[:, h, qb * BLK:(qb + 1) * BLK], in0=pt, in1=gs, op=ALU.mult),
                  (("psb", 0), "gsilu"), (("YB", qb),))

        def branch_da(s, l):
            AR.reset()
            Ct = AR.b16(S_LEN); St = AR.b16(S_LEN)
            rope_tables(s, cf("invf_da"), 128, Ct, St)
            qT = AR.b16(S_LEN); kT = AR.b16(S_LEN)
            vaug = AR.b16(NT, 130)
            PTt = [AR.b16(BLK) for _ in range(2)]
            sqs = [AR.b16(BLK) for _ in range(2)]; qns = [AR.b16(BLK) for _ in range(2)]; obf = AR.b16(TPB, 128); gs = AR.f32(BLK)
            rss = [AR.f32(BLK) for _ in range(2)]; t1s = [AR.f32(BLK) for _ in range(2)]; t2s = [AR.f32(BLK) for _ in range(2)]
            d0 = AR.f32(TPB, 128); d1 = AR.f32(TPB, 128); o = AR.f32(TPB, 128); junk = AR.f32(TPB, 128)
            ssq = AR.f32(TPB); AR_small["rec"] = AR.f32(TPB)
            S.dve(lambda e: e.memset(vaug[:, :, 128:130], 1.0), (), tuple(("vaug", j) for j in range(NT)))
            for h in range(4):
                sl = wslot()
                wv = WB[sl].rearrange("p (q k c) -> p q k c", q=4, k=8)
                for qi, nm in enumerate(["da_q", "da_k", "da_v", "da_gate"]):
                    load_w(sl, win_view(l, OFF[nm] + h * 128, 128), wv[:, qi])
                wkey = ("wb", sl)
                def da_qk(which, dstT, gcol, n, par):
                    sq, rs, qn, t1, t2 = sqs[par], rss[par], qns[par], t1s[par], t2s[par]
                    pq = PS[par]; kpq = ("ps", par)
                    pss = PS[2 + par]; kpss = ("ps", 2 + par)
                    ppm = PS[4 + par]; kppm = ("ps", 4 + par)
                    mm(pq[:, 0:BLK], proj_pairs(wv[:, which], 0, 128, n * BLK, BLK), (wkey, ("HT", n)), (kpq,))
                    S.act(lambda e: e.activation(out=sq, in_=pq[:, 0:BLK], func=AF.Square), (kpq,), (("sq", par),))
                    mm(pss[:, 0:BLK], [(BLK64B, sq)], (("sq", par),) + CONST_R, (kpss,))
                    rsqrt_act(rs, pss[:, 0:BLK], 1.0 / 64, (kpss,) + LR, (("rs", par),))
                    S.dve(lambda e: e.scalar_tensor_tensor(out=qn, in0=pq[:, 0:BLK], scalar=gcol, in1=rs, op0=ALU.mult, op1=ALU.mult),
                          (kpq, ("rs", par)) + LR, (("qn", par),))
                    rope_apply(dstT[:, n * BLK:(n + 1) * BLK], qn, ppm, kppm, PTDAB, 128, Ct, St, n, t1, t2, (("qn", par),),
                               (("qT" if which == 0 else "kT"), n), par=par)
                it_ = 0
                for which, dstT, gcol in ((0, qT, D_GQ), (1, kT, pbl("dak"))):
                    for n in range(NB):
                        da_qk(which, dstT, gcol, n, (it_ % 2) if DB_DA else 0)
                        it_ += 1
                for n in range(NB):
                    pv = PS[n % 2]; kpv = ("ps", n % 2)
                    for tt in range(TPB):
                        t = n * TPB + tt
                        mm(pv[:, tt * 128:(tt + 1) * 128], tok_pairs(wv[:, 2], 0, 128, t), (wkey, ("HT", n)), (kpv,))
                    S.act(lambda e, pv=pv, n=n: e.activation(out=vaug[:, n * TPB:(n + 1) * TPB, 0:128], in_=pv[:, 0:BLK].rearrange("p (a b) -> p a b", a=TPB), func=AF.Copy),
                          (kpv,), tuple(("vaug", n * TPB + tt) for tt in range(TPB)))
                for qb in range(NB):
                    for m, dst in ((0, d0), (1, d1)):
                        kv = kT[m * 64:(m + 1) * 64, :]
                        qv = qT[m * 64:(m + 1) * 64, qb * BLK:(qb + 1) * BLK]
                        parts = [(_Shift(kv), _Shift(qv), 64, (lambda j: ("kT", j // TPB)), ("qT", qb))]
                        attn_block(qb, parts, vaug, PTt, dst, ("dd", m))
                    for i in range(TPB):
                        S.dve(lambda e, i=i: e.scalar_tensor_tensor(out=o[:, i, :], in0=d1[:, i, :], scalar=D_NEGLAM, in1=d0[:, i, :], op0=ALU.mult, op1=ALU.add),
                              (("dd", 0, i), ("dd", 1, i)) + LR, (("o", i),))
                    S.act(lambda e: e.activation(out=junk, in_=o, func=AF.Square), tuple(("o", i) for i in range(TPB)), ("ojunk",))
                    S.dve(lambda e: e.tensor_reduce(out=ssq, in_=junk, axis=AX.X, op=ALU.add), ("ojunk",), ("ossq",))
                    rsqrt_act(ssq, ssq, 1.0 / 128, ("ossq",) + LR, ("ossq",))
                    for i in range(TPB):
                        S.dve(lambda e, i=i: e.scalar_tensor_tensor(out=obf[:, i, :], in0=o[:, i, :], scalar=ssq[:, i:i + 1], in1=pbl("subln"), op0=ALU.mult, op1=ALU.mult),
                              (("o", i), "ossq") + LR, (("obf", i),))
                    finish_head(l, h, qb, obf, tuple(("obf", i) for i in range(TPB)), 0, wv[:, 3], wkey, gs)

        def branch_mla(s, l):
            AR.reset()
            Ct = AR.b16(S_LEN); St = AR.b16(S_LEN)
            rope_tables(s, cf("invf_mla"), 64, Ct, St)
            cqn = AR.b16(2, S_LEN); ckvn = AR.b16(S_LEN); krr = AR.b16(S_LEN)
            knope = AR.b16(S_LEN); krope = AR.b16(S_LEN)
            vaug = AR.b16(NT, 130)
            qnope = AR.b16(BLK); qrope = AR.b16(BLK)
            PTt = [AR.b16(BLK) for _ in range(2)]
            sqA = AR.b16(BLK); sqB = AR.b16(BLK); tb = AR.b16(BLK); obf = AR.b16(TPB, 128)
            wuq = AR.b16(2, 768); wukv = AR.b16(1024)
            gs = AR.f32(BLK); rs = AR.f32(BLK); t1 = AR.f32(BLK); t2 = AR.f32(BLK)
            o = AR.f32(TPB, 128); AR_small["rec"] = AR.f32(TPB)
            S.dve(lambda e: e.memset(vaug[:, :, 128:130], 1.0), (), tuple(("vaug", j) for j in range(NT)))
            S.dma("pool", lambda e: e.dma_start(out=wuq, in_=dwuq[l].rearrange("(kc p) c -> p kc c", p=128)), w=("wuq",))
            S.dma("pool", lambda e: e.dma_start(out=wukv, in_=dwukv[l]), w=("wukv",))
            sl = wslot()
            wc = wb_view(sl, 8, 448)
            load_w(sl, win_view(l, OFF["mla_cq"], 448), wc)
            wkey = ("wb", sl)
            for n in range(NB):
                blk = slice(n * BLK, (n + 1) * BLK)
                for c in range(2):
                    mm(PS[c][:, 0:BLK], proj_pairs(wc, c * 128, 128, n * BLK, BLK), (wkey, ("HT", n)), (("ps", c),))
                S.act(lambda e: e.activation(out=sqA, in_=PS[0][:, 0:BLK], func=AF.Square), (("ps", 0),), ("sqA",))
                S.act(lambda e: e.activation(out=sqB, in_=PS[1][:, 0:BLK], func=AF.Square), (("ps", 1),), ("sqB",))
                mm(PS[2][:, 0:BLK], [(ONESB, sqA), (ONESB, sqB)], ("sqA", "sqB") + CONST_R, (("ps", 2),))
                rsqrt_act(rs, PS[2][:, 0:BLK], 1.0 / 256, (("ps", 2),) + LR, ("rs",))
                for c in range(2):
                    S.dve(lambda e, c=c, blk=blk: e.scalar_tensor_tensor(out=cqn[:, c, blk], in0=PS[c][:, 0:BLK], scalar=pbl("mqa", c), in1=rs, op0=ALU.mult, op1=ALU.mult),
                          (("ps", c), "rs") + LR, (("cqn", n),))
                mm(PS[3][:, 0:BLK], proj_pairs(wc, 256, 128, n * BLK, BLK), (wkey, ("HT", n)), (("ps", 3),))
                S.act(lambda e: e.activation(out=sqA, in_=PS[3][:, 0:BLK], func=AF.Square), (("ps", 3),), ("sqA",))
                mm(PS[2][:, 0:BLK], [(ONESB, sqA)], ("sqA",) + CONST_R, (("ps", 2),))
                rsqrt_act(rs, PS[2][:, 0:BLK], 1.0 / 128, (("ps", 2),) + LR, ("rs",))
                S.dve(lambda e, blk=blk: e.scalar_tensor_tensor(out=ckvn[:, blk], in0=PS[3][:, 0:BLK], scalar=pbl("mkva"), in1=rs, op0=ALU.mult, op1=ALU.mult),
                      (("ps", 3), "rs") + LR, (("ckvn", n),))
                mm(PS[6][0:64, 0:BLK], proj_pairs(wc, 384, 64, n * BLK, BLK), (wkey, ("HT", n)), (("ps", 6),))
                S.act(lambda e, blk=blk: e.activation(out=krr[0:64, blk], in_=PS[6][0:64, 0:BLK], func=AF.Copy), (("ps", 6),), (("krr", n),))
            slg = wslot()
            wgv = wb_view(slg, 8, 512)
            load_w(slg, win_view(l, OFF["mla_gate"], 512), wgv)
            for h in range(4):
                for n in range(NB):
                    blk = slice(n * BLK, (n + 1) * BLK)
                    mm(PS[0][:, 0:BLK], [(wukv[:, h * 256:h * 256 + 128], ckvn[:, blk])], ("wukv", ("ckvn", n)), (("ps", 0),))
                    S.act(lambda e: e.activation(out=sqA, in_=PS[0][:, 0:BLK], func=AF.Square), (("ps", 0),), ("sqA",))
                    S.act(lambda e, blk=blk: e.activation(out=sqB[0:64, :], in_=krr[0:64, blk], func=AF.Square), (("krr", n),), ("sqB",))
                    mm(PS[2][:, 0:BLK], [(ONESB, sqA), (ONESB[0:64, :], sqB[0:64, :])], ("sqA", "sqB") + CONST_R, (("ps", 2),))
                    rsqrt_act(rs, PS[2][:, 0:BLK], 1.0 / 192, (("ps", 2),) + LR, ("rs",))
                    S.dve(lambda e, blk=blk: e.scalar_tensor_tensor(out=knope[:, blk], in0=PS[0][:, 0:BLK], scalar=pbl("mkn_n"), in1=rs, op0=ALU.mult, op1=ALU.mult),
                          (("ps", 0), "rs") + LR, (("knope", n),))
                    S.dve(lambda e, blk=blk: e.scalar_tensor_tensor(out=tb[0:64, :], in0=krr[0:64, blk], scalar=pbl("mkn_r")[0:64, :], in1=rs[0:64, :], op0=ALU.mult, op1=ALU.mult),
                          (("krr", n), "rs") + LR, ("tb",))
                    rope_apply(krope[0:64, blk], tb[0:64, :], PS[3], ("ps", 3), PTMLAB, 64, Ct, St, n, t1, t2, ("tb",), ("krope", n))
                    pv = PS[1]
                    for tt in range(TPB):
                        t = n * TPB + tt
                        mm(pv[:, tt * 128:(tt + 1) * 128], [(ckvn[:, t * 128:(t + 1) * 128], wukv[:, h * 256 + 128:h * 256 + 256])],
                           ("wukv", ("ckvn", n)), (("ps", 1),))
                    S.act(lambda e, n=n: e.activation(out=vaug[:, n * TPB:(n + 1) * TPB, 0:128], in_=PS[1][:, 0:BLK].rearrange("p (a b) -> p a b", a=TPB), func=AF.Copy),
                          (("ps", 1),), tuple(("vaug", n * TPB + tt) for tt in range(TPB)))
                for qb in range(NB):
                    blk = slice(qb * BLK, (qb + 1) * BLK)
                    c0 = h * 192
                    mm(PS[0][:, 0:BLK], [(wuq[:, c, c0:c0 + 128], cqn[:, c, blk]) for c in range(2)], ("wuq", ("cqn", qb)), (("ps", 0),))
                    mm(PS[1][0:64, 0:BLK], [(wuq[:, c, c0 + 128:c0 + 192], cqn[:, c, blk]) for c in range(2)], ("wuq", ("cqn", qb)), (("ps", 1),))
                    S.act(lambda e: e.activation(out=sqA, in_=PS[0][:, 0:BLK], func=AF.Square), (("ps", 0),), ("sqA",))
                    S.act(lambda e: e.activation(out=sqB[0:64, :], in_=PS[1][0:64, 0:BLK], func=AF.Square), (("ps", 1),), ("sqB",))
                    mm(PS[2][:, 0:BLK], [(ONESB, sqA), (ONESB[0:64, :], sqB[0:64, :])], ("sqA", "sqB") + CONST_R, (("ps", 2),))
                    rsqrt_act(rs, PS[2][:, 0:BLK], 1.0 / 192, (("ps", 2),) + LR, ("rs",))
                    S.dve(lambda e: e.scalar_tensor_tensor(out=qnope, in0=PS[0][:, 0:BLK], scalar=D_MQN_N, in1=rs, op0=ALU.mult, op1=ALU.mult),
                          (("ps", 0), "rs") + LR, ("qnope",))
                    S.dve(lambda e: e.scalar_tensor_tensor(out=tb[0:64, :], in0=PS[1][0:64, 0:BLK], scalar=D_MQN_R[0:64, :], in1=rs[0:64, :], op0=ALU.mult, op1=ALU.mult),
                          (("ps", 1), "rs") + LR, ("tb",))
                    rope_apply(qrope[0:64, :], tb[0:64, :], PS[3], ("ps", 3), PTMLAB, 64, Ct, St, qb, t1, t2, ("tb",), "qrope")
                    parts = [(knope, qnope, 128, (lambda j: ("knope", j // TPB)), "qnope"),
                             (krope, qrope, 64, (lambda j: ("krope", j // TPB)), "qrope")]
                    attn_block(qb, parts, vaug, PTt, o, ("oo",))
                    for i in range(TPB):
                        S.pool(lambda e, i=i: e.tensor_copy(out=obf[:, i, :], in_=o[:, i, :]), (("oo", i),), (("obf", i),))
                    finish_head(l, h, qb, obf, tuple(("obf", i) for i in range(TPB)), h * 128, wgv, ("wb", slg), gs)

        def branch_ssd(s, l):
            AR.reset()
            Xtok = AR.b16(NT, 512)
            Bfm = AR.b16(2, S_LEN); Cfm = AR.b16(2, S_LEN)
            xfm = AR.b16(BLK); Xdt = AR.b16(512); XdD = AR.b16(512); Btok = AR.b16(256)
            MT = [AR.b16(128) for _ in range(2)]
            Sbf = AR.b16(2, 256); ybt = AR.b16(512); wdt = AR.b16(8, 8)
            raw = [AR.f32(BLK + 3) for _ in range(2)]
            acc = AR.f32(BLK); zs = AR.f32(512); CBm = AR.f32(128)
            aTri = [AR.f32(128) for _ in range(2)]
            Dm = AR.f32(128); E = AR.f32(128)
            ydiag = AR.f32(512); yc = AR.f32(512); junk = AR.f32(512); Sst = AR.f32(2, 256)
            sm = AR.f32(64)
            dt_ = sm[:, 0:8]; a_ = sm[:, 8:16]; cs_ = sm[:, 16:24]; ecs = sm[:, 24:32]; etot = sm[:, 32:40]
            dec = sm[:, 40:48]; ssq = sm[:, 48:50]; tmp8 = sm[:, 56:64]
            slz = wslot(); wz = wb_view(slz, 8, 512)
            load_w(slz, win_view(l, OFF["mb_z"], 512), wz)
            S.dma("pool", lambda e: e.dma_start(out=wdt, in_=win_view(l, OFF["mb_dt"], 8)), w=("wdt",))
            S.dve(lambda e: e.memset(Sst, 0.0), (), ("Sst",))
            S.dve(lambda e: e.memset(Sbf, 0.0), (), ("Sbf",))
            for half in range(2):
                slx = wslot(); wx = wb_view(slx, 8, 512)
                load_w(slx, win_view(l, OFF["mb_xbc"] + half * 512, 512), wx)
                for f4 in range(4):
                    fc = half * 4 + f4
                    for n in range(NB):
                        rb = raw[n % 2]; krb = ("raw", n % 2)
                        p = PS[n % 2]; kp = ("ps", n % 2)
                        mm(p[:, 0:BLK], proj_pairs(wx, f4 * 128, 128, n * BLK, BLK), (("wb", slx), ("HT", n)), (kp,))
                        if n == 0:
                            S.dve(lambda e, rb=rb: e.memset(rb[:, 0:3], 0.0), (), (krb,))
                        else:
                            pr = raw[(n - 1) % 2]
                            S.pool(lambda e, rb=rb, pr=pr: e.tensor_copy(out=rb[:, 0:3], in_=pr[:, BLK:BLK + 3]), (("raw", (n - 1) % 2),), (krb,))
                        S.act(lambda e, rb=rb, p=p: e.activation(out=rb[:, 3:3 + BLK], in_=p[:, 0:BLK], func=AF.Copy), (kp,), (krb,))
                        cw = lambda k, fc=fc: pbl("convw", fc * 4 + k)
                        S.dve(lambda e, rb=rb, fc=fc, cw=cw: e.tensor_scalar(out=acc, in0=rb[:, 3:3 + BLK], scalar1=cw(3), scalar2=pbl("convb", fc), op0=ALU.mult, op1=ALU.add),
                              (krb,) + LR, ("acc",))
                        for k in (2, 1, 0):
                            S.dve(lambda e, rb=rb, k=k, cw=cw: e.scalar_tensor_tensor(out=acc, in0=rb[:, k:k + BLK], scalar=cw(k), in1=acc, op0=ALU.mult, op1=ALU.add),
                                  (krb, "acc") + LR, ("acc",))
                        if fc < 4:
                            dst, kd = xfm, "xfm"
                        elif fc < 6:
                            dst, kd = Bfm[:, fc - 4, n * BLK:(n + 1) * BLK], ("Bfm", n)
                        else:
                            dst, kd = Cfm[:, fc - 6, n * BLK:(n + 1) * BLK], ("Cfm", n)
                        S.act(lambda e, dst=dst: e.activation(out=dst, in_=acc, func=AF.Silu), ("acc",), (kd,))
                        if fc < 4:
                            pt = PSB[:, 0:BLK]
                            def fn(e, pt=pt):
                                ins = None
                                for tt in range(TPB):
                                    ins = e.transpose(out=pt[:, tt * 128:(tt + 1) * 128], in_=xfm[:, tt * 128:(tt + 1) * 128], identity=IDB)
                                return ins
                            S.pe(fn, ("xfm",) + CONST_R, (("psb", 0),))
                            S.dve(lambda e, pt=pt, n=n, fc=fc: e.tensor_copy(out=Xtok[:, n * TPB:(n + 1) * TPB, fc * 128:(fc + 1) * 128],
                                                                            in_=pt.rearrange("p (a b) -> p a b", a=TPB)),
                                  (("psb", 0),), (("Xtok", n),))
            sets = [dict(zs=zs, sm=sm, Xdt=Xdt, XdD=XdD, Btok=Btok, CBm=CBm, ydiag=ydiag, yc=yc, junk=junk, ybt=ybt, Dm=Dm, E=E, aTri=aTri, MT=MT)]
            sets.append(dict(zs=AR.f32(512), sm=AR.f32(64), Xdt=AR.b16(512), XdD=AR.b16(512), Btok=AR.b16(256), CBm=AR.f32(128),
                             ydiag=AR.f32(512), yc=AR.f32(512), junk=AR.f32(512), ybt=AR.b16(512), Dm=AR.f32(128), E=AR.f32(128),
                             aTri=[AR.f32(128) for _ in range(2)], MT=[AR.b16(128) for _ in range(2)]))

            def ssd_chunk(c):
                par = (c % 2) if DB_SSD else 0
                T = sets[par]
                zs, sm, Xdt, XdD, Btok, CBm, ydiag, yc, junk, ybt, Dm, E, aTri, MT = (T[k_] for k_ in
                    ("zs", "sm", "Xdt", "XdD", "Btok", "CBm", "ydiag", "yc", "junk", "ybt", "Dm", "E", "aTri", "MT"))
                dt_ = sm[:, 0:8]; a_ = sm[:, 8:16]; cs_ = sm[:, 16:24]; ecs = sm[:, 24:32]; etot = sm[:, 32:40]
                dec = sm[:, 40:48]; ssq = sm[:, 48:50]; tmp8 = sm[:, 56:64]
                n = c // TPB
                tok = slice(c * 128, (c + 1) * 128)
                mm(PS[0][:, 0:512], tok_pairs(wz, 0, 512, c), (("wb", slz), ("HT", n)), (("ps", 0),))
                S.act(lambda e: e.activation(out=zs, in_=PS[0][:, 0:512], func=AF.Silu), (("ps", 0),), (("zs", par),))
                mm(PS[1][:, 0:8], tok_pairs(wdt, 0, 8, c), ("wdt", ("HT", n)), (("ps", 1),))
                S.dve(lambda e: e.tensor_tensor(out=tmp8, in0=PS[1][:, 0:8], in1=pbl("dtb"), op=ALU.add), (("ps", 1),) + LR, (("tmp8", par),))
                S.act(lambda e: e.activation(out=tmp8, in_=tmp8, func=AF.Exp), (("tmp8", par),), (("tmp8", par),))
                S.act(lambda e: e.activation(out=dt_, in_=tmp8, func=AF.Ln, bias=D_ONE, scale=1.0), (("tmp8", par),) + LR, (("dt", par),))
                S.dve(lambda e: e.tensor_tensor(out=a_, in0=dt_, in1=D_NEGA, op=ALU.mult), (("dt", par),) + LR, (("a", par),))
                mm(PS[1][:, 8:16], [(MASKUF, a_)], (("a", par),) + CONST_R, (("ps", 1),))
                mm(PS[1][:, 16:24], [(ONESF, a_)], (("a", par),) + CONST_R, (("ps", 1),))
                S.act(lambda e: e.activation(out=cs_, in_=PS[1][:, 8:16], func=AF.Copy), (("ps", 1),), (("cs", par),))
                S.act(lambda e: e.activation(out=ecs, in_=PS[1][:, 8:16], func=AF.Exp), (("ps", 1),), (("ecs", par),))
                S.act(lambda e: e.activation(out=etot, in_=PS[1][:, 16:24], func=AF.Exp), (("ps", 1),), (("etot", par),))
                S.dve(lambda e: e.tensor_tensor(out=dec, in0=PS[1][:, 16:24], in1=cs_, op=ALU.subtract), (("ps", 1), ("cs", par)), (("dec", par),))
                S.act(lambda e: e.activation(out=dec, in_=dec, func=AF.Exp), (("dec", par),), (("dec", par),))
                for h in range(8):
                    hs = slice(h * 64, (h + 1) * 64)
                    S.dve(lambda e, h=h, hs=hs, c=c: e.tensor_scalar(out=Xdt[:, hs], in0=Xtok[:, c, hs], scalar1=dt_[:, h:h + 1], scalar2=None, op0=ALU.mult),
                          (("Xtok", n), ("dt", par)), (("Xdt", par),))
                    S.pool(lambda e, h=h, hs=hs: e.tensor_scalar(out=XdD[:, hs], in0=Xdt[:, hs], scalar1=dec[:, h:h + 1], scalar2=None, op0=ALU.mult),
                           (("Xdt", par), ("dec", par)), (("XdD", par),))
                def fnb(e, tok=tok):
                    ins = None
                    for g in range(2):
                        ins = e.transpose(out=PSB[:, 512 + g * 128:512 + (g + 1) * 128], in_=Bfm[:, g, tok], identity=IDB)
                    return ins
                S.pe(fnb, (("Bfm", n),) + CONST_R, (("psb", 0),))
                S.act(lambda e: e.activation(out=Btok, in_=PSB[:, 512:768], func=AF.Copy), (("psb", 0),), (("Btok", par),))
                for gr in range(2):
                    mm(PS[2][:, 0:128], [(Bfm[:, gr, tok], Cfm[:, gr, tok])], (("Bfm", n), ("Cfm", n)), (("ps", 2),))
                    S.dve(lambda e: e.tensor_tensor(out=CBm, in0=PS[2][:, 0:128], in1=MASKUF, op=ALU.mult), (("ps", 2),) + CONST_R, (("CBm", par),))
                    for r_ in range(4):
                        h = gr * 4 + r_
                        hs = slice(h * 64, (h + 1) * 64)
                        at = aTri[h % 2]; kat = ("aTri", h % 2, par)
                        S.dve(lambda e, at=at, h=h: e.tensor_scalar(out=at, in0=MASKUF, scalar1=a_[:, h:h + 1], scalar2=None, op0=ALU.mult),
                              (("a", par),) + CONST_R, (kat,))
                        mm(PS[3][:, 0:128], [(ONESF, at), (at, D_NEGONES)], (kat,) + CONST_R, (("ps", 3),))
                        S.dve(lambda e: e.tensor_scalar(out=Dm, in0=PS[3][:, 0:128], scalar1=0.0, scalar2=None, op0=ALU.min), (("ps", 3),), (("Dm", par),))
                        S.act(lambda e: e.activation(out=E, in_=Dm, func=AF.Exp), (("Dm", par),), (("E", par),))
                        mt = MT[h % 2]; kmt = ("MT", h % 2, par)
                        S.dve(lambda e, mt=mt: e.tensor_tensor(out=mt, in0=E, in1=CBm, op=ALU.mult), (("E", par), ("CBm", par)), (kmt,))
                        mm(PS[4][:, hs], [(mt, Xdt[:, hs])], (kmt, ("Xdt", par)), (("ps", 4),))
                    gs_ = slice(gr * 256, (gr + 1) * 256)
                    mm(PS[5][:, gs_], [(Cfm[:, gr, tok], Sbf[:, gr, :])], (("Cfm", n), "Sbf"), (("ps", 5),))
                    mm(PS[6][:, gs_], [(Btok[:, gr * 128:(gr + 1) * 128], XdD[:, gs_])], (("Btok", par), ("XdD", par)), (("ps", 6),))
                for h in range(8):
                    gr, r_ = h // 4, h % 4
                    S.dve(lambda e, h=h, gr=gr, r_=r_: e.scalar_tensor_tensor(out=Sst[:, gr, r_ * 64:(r_ + 1) * 64], in0=Sst[:, gr, r_ * 64:(r_ + 1) * 64],
                                                                             scalar=etot[:, h:h + 1], in1=PS[6][:, h * 64:(h + 1) * 64], op0=ALU.mult, op1=ALU.add),
                          ("Sst", ("etot", par), ("ps", 6)), ("Sst",))
                S.pool(lambda e: e.tensor_copy(out=Sbf, in_=Sst), ("Sst",), ("Sbf",))
                S.act(lambda e: e.activation(out=ydiag, in_=PS[4][:, 0:512], func=AF.Copy), (("ps", 4),), (("ydiag", par),))
                for h in range(8):
                    hs = slice(h * 64, (h + 1) * 64)
                    S.dve(lambda e, h=h, hs=hs: e.scalar_tensor_tensor(out=yc[:, hs], in0=PS[5][:, hs], scalar=ecs[:, h:h + 1], in1=ydiag[:, hs], op0=ALU.mult, op1=ALU.add),
                          (("ps", 5), ("ecs", par), ("ydiag", par)), (("yc", par),))
                S.pool(lambda e, c=c: e.tensor_tensor(out=ydiag, in0=Xtok[:, c, :], in1=pbl("mbd"), op=ALU.mult), (("Xtok", n), ("yc", par)) + LR, (("ydiag", par),))
                S.dve(lambda e: e.tensor_tensor(out=yc, in0=yc, in1=ydiag, op=ALU.add), (("yc", par), ("ydiag", par)), (("yc", par),))
                S.dve(lambda e: e.tensor_tensor(out=yc, in0=yc, in1=zs, op=ALU.mult), (("yc", par), ("zs", par)), (("yc", par),))
                S.act(lambda e: e.activation(out=junk, in_=yc, func=AF.Square), (("yc", par),), (("junk", par),))
                S.dve(lambda e: e.tensor_reduce(out=ssq, in_=junk.rearrange("p (a b) -> p a b", a=2), axis=AX.X, op=ALU.add), (("junk", par),), (("ssq", par),))
                rsqrt_act(ssq, ssq, 1.0 / 256, (("ssq", par),) + LR, (("ssq", par),))
                for g2 in range(2):
                    gs_ = slice(g2 * 256, (g2 + 1) * 256)
                    S.dve(lambda e, g2=g2, gs_=gs_: e.scalar_tensor_tensor(out=ybt[:, gs_], in0=yc[:, gs_], scalar=ssq[:, g2:g2 + 1], in1=pbl("mbnorm")[:, gs_], op0=ALU.mult, op1=ALU.mult),
                          (("yc", par), ("ssq", par)) + LR, (("ybt", par),))
                def fnt(e):
                    ins = None
                    for k in range(4):
                        ins = e.transpose(out=PSB[:, k * 128:(k + 1) * 128], in_=ybt[:, k * 128:(k + 1) * 128], identity=IDB)
                    return ins
                S.pe(fnt, (("ybt", par),) + CONST_R, (("psb", 0),))
                S.act(lambda e, tok=tok: e.activation(out=YB[:, :, tok], in_=PSB[:, 0:512].rearrange("p (a b) -> p a b", a=4), func=AF.Copy),
                      (("psb", 0),), (("YB", n),))


            for c in range(NT):
                ssd_chunk(c)

        def branch_s5(s, l):
            AR.reset()
            cj = AR.f32(16, 128); sj = AR.f32(16, 128)
            Bre = AR.b16(16, 128); Bim = AR.b16(16, 128); Cre = AR.b16(16, 128); Cim = AR.b16(16, 128)
            smp = AR.f32(192)
            r_ = smp[:, 0:16]; th = smp[:, 16:32]; rc = smp[:, 32:48]; rsn = smp[:, 48:64]; nrs = smp[:, 64:80]
            tA = smp[:, 80:96]; tB = smp[:, 96:112]; tC = smp[:, 112:128]; tD = smp[:, 128:144]; tE = smp[:, 144:160]; tF = smp[:, 160:176]
            stt = AR.f32(16, 2)
            mark = AR.off
            vtab = AR.f32(16, 128); ttab = AR.f32(16, 128)
            S.act(lambda e: e.activation(out=tA, in_=pbl("lstep"), func=AF.Exp), LR, ("s5a",))
            S.dve(lambda e: e.tensor_tensor(out=tB, in0=pbl("lamre"), in1=tA, op=ALU.mult), ("s5a",) + LR, ("s5b",))
            S.act(lambda e: e.activation(out=r_, in_=tB, func=AF.Exp), ("s5b",), ("s5r",))
            S.dve(lambda e: e.tensor_tensor(out=th, in0=pbl("lamim"), in1=tA, op=ALU.mult), ("s5a",) + LR, ("s5th",))
            for mt in range(16):
                S.dve(lambda e, mt=mt: e.tensor_scalar(out=vtab[:, mt, :], in0=cf("jrow"), scalar1=th[:, mt:mt + 1], scalar2=float(1.0 / TWO_PI), op0=ALU.mult, op1=ALU.mult),
                      ("s5th",) + CONST_R, ("s5v",))
            sincos(vtab, ttab, sj, cj, (), "s5v", "s5t", "s5sj", "s5cj")
            S.dve(lambda e: e.tensor_scalar(out=tC, in0=th, scalar1=float(128.0 / TWO_PI), scalar2=None, op0=ALU.mult), ("s5th",), ("s5c",))
            sincos(tC, tD, tE, tF, (), "s5c", "s5d", "s5s128", "s5c128")
            S.dve(lambda e: e.tensor_tensor(out=rc, in0=r_, in1=tF, op=ALU.mult), ("s5r", "s5c128"), ("s5rc",))
            S.dve(lambda e: e.tensor_tensor(out=rsn, in0=r_, in1=tE, op=ALU.mult), ("s5r", "s5s128"), ("s5rs",))
            S.dve(lambda e: e.tensor_scalar(out=nrs, in0=rsn, scalar1=-1.0, scalar2=None, op0=ALU.mult), ("s5rs",), ("s5nrs",))
            MC = ("s5r", "s5rc", "s5rs", "s5nrs", "s5sj", "s5cj")
            S.dma("pool", lambda e: e.dma_start(out=Cre, in_=ds5C[l, 0]), w=("s5Cre",))
            S.dma("pool", lambda e: e.dma_start(out=Cim, in_=ds5C[l, 1]), w=("s5Cim",))
            AR.off = mark + 2 * 16 * 128
            W = 256
            names = ["lr", "li", "ls", "v", "t", "sn", "cs", "abr", "abi", "den", "fr", "fi", "braw_r", "braw_i", "u1", "u2"]
            Q = {nm: AR.f32(W) for nm in names}
            for q in range(2048 // W):
                cs_ = slice(q * W, (q + 1) * W)
                k = "s5q"
                for i, nm in enumerate(["lr", "li", "ls"]):
                    S.dma("sp", lambda e, nm=nm, i=i, cs_=cs_: e.dma_start(out=Q[nm], in_=ds5rep[l, i, :, cs_]), w=(("s5in", nm),))
                mts = slice(q * (W // 128), (q + 1) * (W // 128))
                S.dma("sp", lambda e, mts=mts: e.dma_start(out=Q["braw_r"].rearrange("p (a b) -> p a b", b=128), in_=ds5B[l, 0, :, mts, :]), w=(("s5in", "br"),))
                S.dma("sp", lambda e, mts=mts: e.dma_start(out=Q["braw_i"].rearrange("p (a b) -> p a b", b=128), in_=ds5B[l, 1, :, mts, :]), w=(("s5in", "bi"),))
                S.act(lambda e: e.activation(out=Q["ls"], in_=Q["ls"], func=AF.Exp), (("s5in", "ls"),), (("s5in", "ls"),))
                S.dve(lambda e: e.tensor_tensor(out=Q["u1"], in0=Q["lr"], in1=Q["ls"], op=ALU.mult), (("s5in", "lr"), ("s5in", "ls")), ("q_u1",))
                S.act(lambda e: e.activation(out=Q["u1"], in_=Q["u1"], func=AF.Exp), ("q_u1",), ("q_u1",))
                S.dve(lambda e: e.tensor_tensor(out=Q["v"], in0=Q["li"], in1=Q["ls"], op=ALU.mult), (("s5in", "li"), ("s5in", "ls")), ("q_v",))
                S.dve(lambda e: e.tensor_scalar(out=Q["v"], in0=Q["v"], scalar1=float(1.0 / TWO_PI), scalar2=None, op0=ALU.mult), ("q_v",), ("q_v",))
                sincos(Q["v"], Q["t"], Q["sn"], Q["cs"], (), "q_v", "q_t", "q_sn", "q_cs")
                S.dve(lambda e: e.tensor_tensor(out=Q["abr"], in0=Q["u1"], in1=Q["cs"], op=ALU.mult), ("q_u1", "q_cs"), ("q_abr",))
                S.dve(lambda e: e.tensor_scalar(out=Q["abr"], in0=Q["abr"], scalar1=-1.0, scalar2=None, op0=ALU.add), ("q_abr",), ("q_abr",))
                S.dve(lambda e: e.tensor_tensor(out=Q["abi"], in0=Q["u1"], in1=Q["sn"], op=ALU.mult), ("q_u1", "q_sn"), ("q_abi",))
                S.dve(lambda e: e.tensor_tensor(out=Q["den"], in0=Q["lr"], in1=Q["lr"], op=ALU.mult), (("s5in", "lr"),), ("q_den",))
                S.dve(lambda e: e.tensor_tensor(out=Q["u2"], in0=Q["li"], in1=Q["li"], op=ALU.mult), (("s5in", "li"),), ("q_u2",))
                S.dve(lambda e: e.tensor_tensor(out=Q["den"], in0=Q["den"], in1=Q["u2"], op=ALU.add), ("q_den", "q_u2"), ("q_den",))
                S.dve(lambda e: e.reciprocal(out=Q["den"], in_=Q["den"]), ("q_den",), ("q_den",))
                S.dve(lambda e: e.tensor_tensor(out=Q["fr"], in0=Q["abr"], in1=Q["lr"], op=ALU.mult), ("q_abr", ("s5in", "lr")), ("q_fr",))
                S.dve(lambda e: e.tensor_tensor(out=Q["u2"], in0=Q["abi"], in1=Q["li"], op=ALU.mult), ("q_abi", ("s5in", "li"), "q_den"), ("q_u2",))
                S.dve(lambda e: e.tensor_tensor(out=Q["fr"], in0=Q["fr"], in1=Q["u2"], op=ALU.add), ("q_fr", "q_u2"), ("q_fr",))
                S.dve(lambda e: e.tensor_tensor(out=Q["fr"], in0=Q["fr"], in1=Q["den"], op=ALU.mult), ("q_fr", "q_den"), ("q_fr",))
                S.dve(lambda e: e.tensor_tensor(out=Q["fi"], in0=Q["abi"], in1=Q["lr"], op=ALU.mult), ("q_abi", ("s5in", "lr")), ("q_fi",))
                S.dve(lambda e: e.tensor_tensor(out=Q["u2"], in0=Q["abr"], in1=Q["li"], op=ALU.mult), ("q_abr", ("s5in", "li"), "q_fr"), ("q_u2",))
                S.dve(lambda e: e.tensor_tensor(out=Q["fi"], in0=Q["fi"], in1=Q["u2"], op=ALU.subtract), ("q_fi", "q_u2"), ("q_fi",))
                S.dve(lambda e: e.tensor_tensor(out=Q["fi"], in0=Q["fi"], in1=Q["den"], op=ALU.mult), ("q_fi", "q_den"), ("q_fi",))
                brv = Bre.rearrange("p a b -> p (a b)")[:, cs_]; biv = Bim.rearrange("p a b -> p (a b)")[:, cs_]
                S.dve(lambda e: e.tensor_tensor(out=Q["u1"], in0=Q["fr"], in1=Q["braw_r"], op=ALU.mult), ("q_fr", ("s5in", "br"), "q_abr", "q_abi"), ("q_u1",))
                S.dve(lambda e: e.tensor_tensor(out=Q["u2"], in0=Q["fi"], in1=Q["braw_i"], op=ALU.mult), ("q_fi", ("s5in", "bi")), ("q_u2",))
                S.dve(lambda e, brv=brv: e.tensor_tensor(out=brv, in0=Q["u1"], in1=Q["u2"], op=ALU.subtract), ("q_u1", "q_u2"), ("s5Bre",))
                S.dve(lambda e: e.tensor_tensor(out=Q["u1"], in0=Q["fr"], in1=Q["braw_i"], op=ALU.mult), ("q_fr", ("s5in", "bi"), "s5Bre"), ("q_u1",))
                S.dve(lambda e: e.tensor_tensor(out=Q["u2"], in0=Q["fi"], in1=Q["braw_r"], op=ALU.mult), ("q_fi", ("s5in", "br"), "s5Bre"), ("q_u2",))
                S.dve(lambda e, biv=biv: e.tensor_tensor(out=biv, in0=Q["u1"], in1=Q["u2"], op=ALU.add), ("q_u1", "q_u2"), ("s5Bim",))
            S.barrier()
            AR.off = mark
            ufm = AR.b16(4, BLK); ygb = AR.b16(4, BLK)
            hre_ = [AR.b16(BLK) for _ in range(2)]; nhim_ = [AR.b16(BLK) for _ in range(2)]
            t1_ = [AR.f32(BLK)] * 2; t2_ = [AR.f32(BLK)] * 2; wre_ = [AR.f32(BLK) for _ in range(2)]; wim_ = [AR.f32(BLK) for _ in range(2)]
            gre_ = [AR.f32(BLK) for _ in range(2)]; gim_ = [AR.f32(BLK) for _ in range(2)]
            yp = AR.f32(BLK); x2 = AR.f32(BLK); sgl = AR.f32(BLK); gsv = AR.f32(BLK)
            slu = wslot(); wu = wb_view(slu, 8, 512); load_w(slu, win_view(l, OFF["s5_u"], 512), wu)
            slg = wslot(); wg = wb_view(slg, 8, 512); load_w(slg, win_view(l, OFF["s5_gate"], 512), wg)
            slw = wslot(); wglu = wb_view(slw, 4, 512); load_w(slw, dwglu[l].rearrange("(kc p) c -> p kc c", p=128), wglu)
            GK = 1.5957691216057308
            for n in range(NB):
                for c4 in range(4):
                    p = PS[c4 % 2]; kp = ("ps", c4 % 2)
                    mm(p[:, 0:BLK], proj_pairs(wu, c4 * 128, 128, n * BLK, BLK), (("wb", slu), ("HT", n)), (kp,))
                    S.act(lambda e, c4=c4, p=p: e.activation(out=ufm[:, c4, :], in_=p[:, 0:BLK], func=AF.Copy), (kp,), (("ufm", c4),))
                for mt in range(16):
                    uc = mt // 4
                    pb2 = (mt % 2) if DB_S5 else 0
                    hre, nhim, t1, t2, wre, wim, gre, gim = hre_[pb2], nhim_[pb2], t1_[pb2], t2_[pb2], wre_[pb2], wim_[pb2], gre_[pb2], gim_[pb2]
                    K = lambda nm, pb2=pb2: (nm, 0 if nm in ('t1', 't2') else pb2)
                    cjb = cj[:, mt:mt + 1, :].broadcast_to([128, TPB, 128])
                    sjb = sj[:, mt:mt + 1, :].broadcast_to([128, TPB, 128])
                    v3 = lambda ap: ap.rearrange("p (a b) -> p a b", a=TPB)
                    pr = PS[2 + (mt % 2) * 2]; pi = PS[3 + (mt % 2) * 2]
                    kpr = ("ps", 2 + (mt % 2) * 2); kpi = ("ps", 3 + (mt % 2) * 2)
                    mm(pr[:, 0:BLK], [(Bre[:, mt, :], ufm[:, uc, :])], ("s5Bre", ("ufm", uc)), (kpr,))
                    mm(pi[:, 0:BLK], [(Bim[:, mt, :], ufm[:, uc, :])], ("s5Bim", ("ufm", uc)), (kpi,))
                    S.dve(lambda e, pr=pr, cjb=cjb, t1=t1: e.tensor_tensor(out=v3(t1), in0=v3(pr[:, 0:BLK]), in1=cjb, op=ALU.mult), (kpr,) + MC, (K("t1"),))
                    S.dve(lambda e, pi=pi, sjb=sjb, t2=t2: e.tensor_tensor(out=v3(t2), in0=v3(pi[:, 0:BLK]), in1=sjb, op=ALU.mult), (kpi,) + MC, (K("t2"),))
                    S.pool(lambda e, wre=wre, t1=t1, t2=t2: e.tensor_tensor(out=wre, in0=t1, in1=t2, op=ALU.add), (K("t1"), K("t2")), (K("wre"),))
                    S.dve(lambda e, pi=pi, cjb=cjb, t1=t1: e.tensor_tensor(out=v3(t1), in0=v3(pi[:, 0:BLK]), in1=cjb, op=ALU.mult), (kpi, K("wre")) + MC, (K("t1"),))
                    S.dve(lambda e, pr=pr, sjb=sjb, t2=t2: e.tensor_tensor(out=v3(t2), in0=v3(pr[:, 0:BLK]), in1=sjb, op=ALU.mult), (kpr, K("wre")) + MC, (K("t2"),))
                    S.pool(lambda e, wim=wim, t1=t1, t2=t2: e.tensor_tensor(out=wim, in0=t1, in1=t2, op=ALU.subtract), (K("t1"), K("t2")), (K("wim"),))
                    for k in range(TPB):
                        ck = n * TPB + k
                        c0 = k * 128
                        if ck > 0:
                            if k == 0:
                                lre, lim = stt[:, mt, 0:1], stt[:, mt, 1:2]
                            else:
                                lre, lim = gre[:, c0 - 1:c0], gim[:, c0 - 1:c0]
                            S.dve(lambda e, lre=lre, c0=c0, mt=mt, wre=wre: e.scalar_tensor_tensor(out=wre[:, c0:c0 + 1], in0=lre, scalar=rc[:, mt:mt + 1], in1=wre[:, c0:c0 + 1], op0=ALU.mult, op1=ALU.add),
                                  (K("wre"), K("gre"), ("stt", mt)) + MC, (K("wre"),))
                            S.dve(lambda e, lim=lim, c0=c0, mt=mt, wre=wre: e.scalar_tensor_tensor(out=wre[:, c0:c0 + 1], in0=lim, scalar=nrs[:, mt:mt + 1], in1=wre[:, c0:c0 + 1], op0=ALU.mult, op1=ALU.add),
                                  (K("wre"), K("gim"), ("stt", mt)) + MC, (K("wre"),))
                            S.dve(lambda e, lre=lre, c0=c0, mt=mt, wim=wim: e.scalar_tensor_tensor(out=wim[:, c0:c0 + 1], in0=lre, scalar=rsn[:, mt:mt + 1], in1=wim[:, c0:c0 + 1], op0=ALU.mult, op1=ALU.add),
                                  (K("wim"), K("gre"), ("stt", mt)) + MC, (K("wim"),))
                            S.dve(lambda e, lim=lim, c0=c0, mt=mt, wim=wim: e.scalar_tensor_tensor(out=wim[:, c0:c0 + 1], in0=lim, scalar=rc[:, mt:mt + 1], in1=wim[:, c0:c0 + 1], op0=ALU.mult, op1=ALU.add),
                                  (K("wim"), K("gim"), ("stt", mt)) + MC, (K("wim"),))
                        rb = r_[:, mt:mt + 1].broadcast_to([128, 128])
                        S.dve(lambda e, c0=c0, rb=rb, gre=gre, wre=wre: e.tensor_tensor_scan(out=gre[:, c0:c0 + 128], data0=rb, data1=wre[:, c0:c0 + 128], initial=0.0, op0=ALU.mult, op1=ALU.add),
                              (K("wre"),) + MC, (K("gre"),))
                        S.dve(lambda e, c0=c0, rb=rb, gim=gim, wim=wim: e.tensor_tensor_scan(out=gim[:, c0:c0 + 128], data0=rb, data1=wim[:, c0:c0 + 128], initial=0.0, op0=ALU.mult, op1=ALU.add),
                              (K("wim"),) + MC, (K("gim"),))
                    S.act(lambda e, mt=mt, gre=gre: e.activation(out=stt[:, mt, 0:1], in_=gre[:, BLK - 1:BLK], func=AF.Copy), (K("gre"),), (("stt", mt),))
                    S.act(lambda e, mt=mt, gim=gim: e.activation(out=stt[:, mt, 1:2], in_=gim[:, BLK - 1:BLK], func=AF.Copy), (K("gim"),), (("stt", mt),))
                    S.dve(lambda e, cjb=cjb, t1=t1, gre=gre: e.tensor_tensor(out=v3(t1), in0=v3(gre), in1=cjb, op=ALU.mult), (K("gre"),) + MC, (K("t1"),))
                    S.pool(lambda e, sjb=sjb, t2=t2, gim=gim: e.tensor_tensor(out=v3(t2), in0=v3(gim), in1=sjb, op=ALU.mult), (K("gim"),) + MC, (K("t2"),))
                    S.dve(lambda e, hre=hre, t1=t1, t2=t2: e.tensor_tensor(out=hre, in0=t1, in1=t2, op=ALU.subtract), (K("t1"), K("t2")), (K("hre"),))
                    S.dve(lambda e, sjb=sjb, t1=t1, gre=gre: e.tensor_tensor(out=v3(t1), in0=v3(gre), in1=sjb, op=ALU.mult), (K("gre"), K("hre")) + MC, (K("t1"),))
                    S.pool(lambda e, cjb=cjb, t2=t2, gim=gim: e.tensor_tensor(out=v3(t2), in0=v3(gim), in1=cjb, op=ALU.mult), (K("gim"), K("hre")) + MC, (K("t2"),))
                    S.dve(lambda e, nhim=nhim, t1=t1, t2=t2: e.scalar_tensor_tensor(out=nhim, in0=t1, scalar=-1.0, in1=t2, op0=ALU.mult, op1=ALU.subtract), (K("t1"), K("t2")), (K("nhim"),))
                    py = PS[6]
                    def fny(e, mt=mt, py=py, hre=hre, nhim=nhim):
                        e.matmul(py[:, 0:BLK], lhsT=Cre[:, mt, :], rhs=hre, start=(mt % 4 == 0), stop=False)
                        return e.matmul(py[:, 0:BLK], lhsT=Cim[:, mt, :], rhs=nhim, start=False, stop=(mt % 4 == 3))
                    S.pe(fny, (K("hre"), K("nhim"), "s5Cre", "s5Cim"), (("ps", 6),))
                    if mt % 4 == 3:
                        c4 = mt // 4
                        S.dve(lambda e, c4=c4, py=py: e.scalar_tensor_tensor(out=yp, in0=ufm[:, c4, :], scalar=pbl("s5d", c4), in1=py[:, 0:BLK], op0=ALU.mult, op1=ALU.add),
                              (("ufm", c4), ("ps", 6)) + LR, ("yp",))
                        S.pool(lambda e: e.tensor_tensor(out=x2, in0=yp, in1=yp, op=ALU.mult), ("yp",), ("x2",))
                        S.dve(lambda e: e.tensor_scalar(out=x2, in0=x2, scalar1=float(GK * 0.044715), scalar2=float(GK), op0=ALU.mult, op1=ALU.add), ("x2",), ("x2",))
                        S.dve(lambda e: e.tensor_tensor(out=x2, in0=x2, in1=yp, op=ALU.mult), ("x2", "yp"), ("x2",))
                        S.act(lambda e: e.activation(out=x2, in_=x2, func=AF.Sigmoid), ("x2",), ("x2",))
                        S.dve(lambda e, c4=c4: e.tensor_tensor(out=ygb[:, c4, :], in0=yp, in1=x2, op=ALU.mult), ("yp", "x2"), (("ygb", c4),))
                for c4 in range(4):
                    pg = PS[c4 % 2]; kpg = ("ps", c4 % 2)
                    mm(pg[:, 0:BLK], [(wglu[:, k4, c4 * 128:(c4 + 1) * 128], ygb[:, k4, :]) for k4 in range(4)],
                       (("wb", slw),) + tuple(("ygb", k4) for k4 in range(4)), (kpg,))
                    S.act(lambda e, c4=c4, pg=pg: e.activation(out=sgl, in_=pg[:, 0:BLK], func=AF.Sigmoid, bias=pbl("s5bglu", c4), scale=1.0), (kpg,) + LR, ("sgl",))
                    pgt = PS[2 + c4 % 2]; kpgt = ("ps", 2 + c4 % 2)
                    mm(pgt[:, 0:BLK], proj_pairs(wg, c4 * 128, 128, n * BLK, BLK), (("wb", slg), ("HT", n)), (kpgt,))
                    S.act(lambda e, pgt=pgt: e.activation(out=gsv, in_=pgt[:, 0:BLK], func=AF.Silu), (kpgt,), ("gsv",))
                    S.dve(lambda e, c4=c4: e.tensor_tensor(out=sgl, in0=sgl, in1=ygb[:, c4, :], op=ALU.mult), ("sgl", ("ygb", c4)), ("sgl",))
                    S.dve(lambda e, c4=c4, n=n: e.tensor_tensor(out=YB[:, c4, n * BLK:(n + 1) * BLK], in0=sgl, in1=gsv, op=ALU.mult), ("sgl", "gsv"), (("YB", n),))

        BR = {0: branch_da, 1: branch_ssd, 2: branch_s5, 3: branch_mla}
        for s in range(NSEQ):
            for l in range(L):
                layer_setup(l)
                stage0(s, l)
                S.barrier()
                first = True
                for b in range(4):
                    if b not in branches:
                        continue
                    BR[b](s, l)
                    S.barrier()
                    merge(s, l, b, first)
                    S.barrier()
                    first = False
                if first:
                    S.dve(lambda e: e.memset(MG, 0.0), (), tuple(("MG", f, n) for f in range(8) for n in range(NB)))
                if debug and s == 0 and l == 0:
                    for nm, src_, dst_ in (("ht", HT, dbg_ht), ("yb", YB, dbg_yb), ("mg", MG, dbg_mg)):
                        for q in range(src_.shape[1]):
                            dtmp = ARN[:, 0:S_LEN]
                            S.dve(lambda e, src_=src_, q=q, dtmp=dtmp: e.tensor_copy(out=dtmp, in_=src_[:, q, :]), (), ("dtmp",))
                            S.dma("sp", lambda e, dst_=dst_, q=q, dtmp=dtmp: e.dma_start(out=dst_[:, q * S_LEN:(q + 1) * S_LEN], in_=dtmp), r=("dtmp",), w=())
                    S.barrier()
                outproj(s, l)
                S.barrier()
        S.emit()
    return nc


_PROG_CACHE = {}


def run_cores(inputs, S_LEN, NSEQ, DEPTH, n_cores, branches=(0, 1, 2, 3), debug=False):
    key = (S_LEN, NSEQ, DEPTH, tuple(branches), debug)
    if key not in _PROG_CACHE:
        _PROG_CACHE[key] = build_program(S_LEN, NSEQ, DEPTH, branches, debug=debug)
    nc = _PROG_CACHE[key]
    f32 = lambda a: np.ascontiguousarray(np.asarray(a, dtype=np.float32))
    pb, s5rep, s5B, s5C = host_params({k: np.asarray(v) for k, v in inputs.items()}, DEPTH)
    shared = {
        "w_in": f32(inputs["w_in"])[:DEPTH], "w_br": f32(inputs["w_br"])[:DEPTH], "w_out": f32(inputs["w_out"])[:DEPTH],
        "w_uq": f32(inputs["mla_w_uq"])[:DEPTH], "w_ukv": f32(inputs["mla_w_ukv"])[:DEPTH], "w_glu": f32(inputs["s5_w_glu"])[:DEPTH],
        "pblob": pb, "cblob": host_consts(), "s5rep": s5rep, "s5B": s5B, "s5C": s5C,
    }
    x = f32(inputs["x"])
    pos = np.ascontiguousarray(np.asarray(inputs["positions"], dtype=np.int32))
    in_maps = []
    for c in range(n_cores):
        m = dict(shared)
        m["x"] = np.ascontiguousarray(x[c * NSEQ:(c + 1) * NSEQ])
        m["pos"] = np.ascontiguousarray(pos[c * NSEQ:(c + 1) * NSEQ])
        in_maps.append(m)
    res = run_bass_kernel_spmd(nc, in_maps, core_ids=list(range(n_cores)))
    outs = [np.asarray(r["out"]).reshape(NSEQ, S_LEN, D_MODEL) for r in res.results]
    if debug:
        global DBG
        DBG = {k: np.asarray(res.results[0][k]) for k in ("dbg_ht", "dbg_yb", "dbg_mg")}
    return np.concatenate(outs, axis=0).astype(np.float32)


def kernel(**inputs):
    return run_cores(inputs, 2048, 2, 2, 8)
```

```python
import numpy as np
import concourse.bass as bass
import concourse.mybir as mybir
from concourse.alu_op_type import AluOpType as ALU
from concourse.bass_utils import run_bass_kernel_spmd

F32 = mybir.dt.float32
BF16 = mybir.dt.bfloat16
I32 = mybir.dt.int32
AF = mybir.ActivationFunctionType
AX = mybir.AxisListType


class Op:
    __slots__ = ("eng", "fn", "reads", "writes", "dma", "waits", "inc", "ticket", "idx", "sem", "cost", "seg")

    def __init__(self, eng, fn, reads, writes, dma, cost=None):
        self.eng, self.fn, self.reads, self.writes, self.dma = eng, fn, reads, writes, dma
        self.cost = cost
        self.seg = 0
        self.waits = []
        self.inc = False
        self.ticket = None
        self.sem = None


class Sched:
    ENGS = ("pe", "dve", "act", "pool", "sp")

    def __init__(self, nc, n_dma_sems=12):
        self.nc = nc
        self.ops = []
        self.n_dma_sems = n_dma_sems
        self._bar_ops = set()
        self._seg = 0
        self.reorder = True
        import os
        self.same_sync = True

    DEFCOST = {"pe": 0.4, "dve": 0.55, "act": 0.5, "pool": 0.9, "sp": 0.1}

    def add(self, eng, fn, reads=(), writes=(), dma=False, cost=None):
        if cost is None:
            cost = 0.15 if dma else self.DEFCOST[eng]
        op = Op(eng, fn, tuple(reads), tuple(writes), dma, cost)
        op.idx = len(self.ops)
        op.seg = self._seg
        self.ops.append(op)
        return op

    def pe(self, fn, r=(), w=(), cost=None):
        return self.add("pe", fn, r, w, cost=cost)

    def dve(self, fn, r=(), w=()):
        return self.add("dve", fn, r, w)

    def act(self, fn, r=(), w=()):
        return self.add("act", fn, r, w)

    def pool(self, fn, r=(), w=()):
        return self.add("pool", fn, r, w)

    def dma(self, q, fn, r=(), w=()):
        return self.add(q, fn, r, w, dma=True)

    def barrier(self):
        self._seg += 1
        for e in self.ENGS:
            self.add(e, lambda eng: eng.drain(), (), (("__barX__", e),))
        for e in self.ENGS:
            op = self.add(e, None, tuple(("__barX__", f) for f in self.ENGS), ())
            self._bar_ops.add(op.idx)
        self._seg += 1

    def finalize(self):
        last_writer = {}
        readers = {}
        cnt = {e: 0 for e in self.ENGS}
        dma_sem_total = {}
        dma_rr = {e: 0 for e in self.ENGS}
        waited = {}
        deps_of = []
        all_dma = []
        for op in self.ops:
            deps = set()
            for r in op.reads:
                j = last_writer.get(r)
                if j is not None:
                    deps.add(j)
            for w in op.writes:
                j = last_writer.get(w)
                if j is not None:
                    deps.add(j)
                for j in readers.get(w, ()):
                    deps.add(j)
            if op.idx in self._bar_ops:
                deps.update(all_dma)
            if op.dma:
                all_dma.append(op.idx)
            deps.discard(op.idx)
            deps = set(j for j in deps if self.ops[j].fn is not None)
            deps_of.append(deps)
            for r in op.reads:
                readers.setdefault(r, []).append(op.idx)
            for w in op.writes:
                last_writer[w] = op.idx
                readers[w] = []
        need_inc = [False] * len(self.ops)
        for op in self.ops:
            for j in deps_of[op.idx]:
                pj = self.ops[j]
                if pj.dma or pj.eng != op.eng or op.dma or (self.same_sync and pj.eng != 'pe'):
                    need_inc[j] = True
        for op in self.ops:
            if op.dma:
                need_inc[op.idx] = True
        self.order = self._list_schedule(deps_of) if self.reorder else list(range(len(self.ops)))
        for oi in self.order:
            op = self.ops[oi]
            e = op.eng
            if op.dma:
                k = dma_rr[e] % self.n_dma_sems
                dma_rr[e] += 1
                key = ("d", e, k)
                prev = dma_sem_total.get(key, 0)
                if prev > 0 and waited.get((e, key), 0) < prev:
                    op.waits.append((key, prev))
                    waited[(e, key)] = prev
                dma_sem_total[key] = prev + 16
                op.sem = key
                op.ticket = prev + 16
                op.inc = True
            elif need_inc[op.idx]:
                cnt[e] += 1
                op.sem = ("c", e)
                op.ticket = cnt[e]
                op.inc = True
            for j in sorted(deps_of[op.idx]):
                pj = self.ops[j]
                if (not pj.dma) and pj.eng == e and not op.dma and (e == 'pe' or not self.same_sync):
                    continue
                if (not pj.dma) and pj.eng == e and op.dma:
                    pass
                key, val = pj.sem, pj.ticket
                if waited.get((e, key), 0) >= val:
                    continue
                waited[(e, key)] = val
                op.waits.append((key, val))

    def _list_schedule(self, deps_of):
        import heapq
        ops = self.ops
        n = len(ops)
        order = []
        i = 0
        SYNC = 0.25
        DMA_LAT = 2.5
        while i < n:
            j = i
            seg = ops[i].seg
            while j < n and ops[j].seg == seg:
                j += 1
            idxs = range(i, j)
            if j - i <= 2 or any(ops[k].fn is None for k in idxs) :
                order.extend(idxs)
                i = j
                continue
            indeg = {}
            users = {}
            for k in idxs:
                d = [x for x in deps_of[k] if i <= x < j]
                indeg[k] = len(d)
                for x in d:
                    users.setdefault(x, []).append(k)
            blev = {}
            for k in reversed(idxs):
                m_ = 0.0
                for u in users.get(k, ()):
                    if blev[u] > m_:
                        m_ = blev[u]
                blev[k] = m_ + ops[k].cost + (DMA_LAT if ops[k].dma else 0.0) + SYNC
            finish = {}
            ready_t = {k: 0.0 for k in idxs}
            efree = {e: 0.0 for e in self.ENGS}
            pend = {e: [] for e in self.ENGS}
            avail = {e: [] for e in self.ENGS}
            for k in idxs:
                if indeg[k] == 0:
                    heapq.heappush(pend[ops[k].eng], (0.0, k))
            done = 0
            tot = j - i
            sched = []
            while done < tot:
                best = None
                for e in self.ENGS:
                    T = efree[e]
                    while pend[e] and pend[e][0][0] <= T:
                        k_ = heapq.heappop(pend[e])[1]
                        heapq.heappush(avail[e], (-blev[k_], k_))
                    if avail[e]:
                        cand = (T, avail[e][0][1], e, True)
                    elif pend[e]:
                        cand = (pend[e][0][0], pend[e][0][1], e, False)
                    else:
                        continue
                    if best is None or cand[:2] < best[:2]:
                        best = cand
                start, k, e, from_avail = best
                if from_avail:
                    heapq.heappop(avail[e])
                else:
                    heapq.heappop(pend[e])
                op = ops[k]
                efree[e] = start + op.cost
                fin = start + op.cost + (DMA_LAT if op.dma else 0.0)
                finish[k] = fin
                sched.append((start, k))
                done += 1
                for u in users.get(k, ()):
                    lat = 0.0 if (ops[u].eng == e and not op.dma and e == "pe") else SYNC
                    ready_t[u] = max(ready_t[u], fin + lat)
                    indeg[u] -= 1
                    if indeg[u] == 0:
                        heapq.heappush(pend[ops[u].eng], (ready_t[u], u))
            sched.sort()
            order.extend(k for _, k in sched)
            i = j
        assert len(order) == n and len(set(order)) == n
        return order

    def emit(self, out_waits=()):
        nc = self.nc
        self.finalize()
        keys = set()
        for op in self.ops:
            if op.sem is not None:
                keys.add(op.sem)
        keys = sorted(keys)
        sems = {}
        import contextlib
        with contextlib.ExitStack() as st:
            for i, k in enumerate(keys):
                sems[k] = st.enter_context(nc.semaphore("s_" + "_".join(str(x) for x in k)))
            block = st.enter_context(nc.Block())
            ops = self.ops

            order = self.order

            def run(eng_name, eng):
                last = None
                for oi in order:
                    op = ops[oi]
                    if op.eng != eng_name:
                        continue
                    for (k, v) in op.waits:
                        eng.wait_ge(sems[k], v)
                    if op.fn is None:
                        assert not op.inc
                        continue
                    ins = op.fn(eng)
                    if op.inc:
                        ins.then_inc(sems[op.sem], 16 if op.dma else 1)
                    if op.dma:
                        last = op
                done = {}
                for op in ops:
                    if op.eng == eng_name and op.dma:
                        done[op.sem] = max(done.get(op.sem, 0), op.ticket)
                for k, v in done.items():
                    eng.wait_ge(sems[k], v)

            @block.tensor
            def _(e):
                run("pe", e)

            @block.vector
            def _(e):
                run("dve", e)

            @block.scalar
            def _(e):
                run("act", e)

            @block.gpsimd
            def _(e):
                run("pool", e)

            @block.sync
            def _(e):
                run("sp", e)


D_MODEL = 1024
D_IN = 9672
EPS = 1e-6
ROPE_THETA = 500000.0
OFF = dict(da_q=0, da_k=512, da_v=1024, da_gate=1536, mb_z=2048, mb_xbc=2560, mb_dt=3584, s5_u=3592,
           s5_gate=4104, mla_cq=4616, mla_ckv=4872, mla_krope=5000, mla_gate=5064, gate=5576)
TWO_PI = float(2.0 * np.pi)
import os as _os
DB_S5 = _os.environ.get('DB_S5', '1') == '1'
DB_SSD = _os.environ.get('DB_SSD', '0') == '1'
DB_DA = _os.environ.get('DB_DA', '0') == '1'
MAGIC = 12582912.0

PB = {}
_o = 0
for _n, _w in [("ng", 8), ("daq", 1), ("dak", 1), ("convw", 32), ("convb", 8), ("s5d", 4), ("s5bglu", 4),
               ("mqa", 2), ("mkva", 1), ("mqn_n", 1), ("mqn_r", 1), ("mkn_n", 1), ("mkn_r", 1),
               ("lamre", 16), ("lamim", 16), ("lstep", 16),
               ("subln", 128), ("lq1", 64), ("lk1", 64), ("lq2", 64), ("lk2", 64),
               ("dtb", 8), ("alog", 8), ("mbd", 512), ("mbnorm", 512)]:
    PB[_n] = (_o, _w)
    _o += _w
NP_COLS = _o
CB = {}
_o = 0
for _n, _w in [("ident", 128), ("ones", 128), ("blk64", 128), ("maskU", 128), ("ptda", 128), ("ptmla", 128),
               ("jrow", 128), ("invf_da", 1), ("invf_mla", 1)]:
    CB[_n] = (_o, _w)
    _o += _w
NC_COLS = _o


def host_consts():
    c = np.zeros((128, NC_COLS), np.float32)
    def put(n, a):
        o, w = CB[n]
        c[:a.shape[0], o:o + w] = a
    i = np.arange(128)
    put("ident", np.eye(128, dtype=np.float32))
    put("ones", np.ones((128, 128), np.float32))
    put("blk64", (i[:, None] // 64 == i[None, :] // 64).astype(np.float32))
    put("maskU", (i[:, None] <= i[None, :]).astype(np.float32))
    P = np.zeros((128, 128), np.float32)
    for b in range(2):
        for d in range(8):
            P[b * 64 + d, b * 64 + d + 8] = -1.0
            P[b * 64 + d + 8, b * 64 + d] = 1.0
    put("ptda", P.T.copy())
    P = np.zeros((128, 128), np.float32)
    for d in range(32):
        P[d, d + 32] = -1.0
        P[d + 32, d] = 1.0
    put("ptmla", P.T.copy())
    put("jrow", np.broadcast_to(np.arange(128, dtype=np.float32)[None, :], (128, 128)))
    f_da = (1.0 / (ROPE_THETA ** (np.arange(0, 16, 2, dtype=np.float32) / np.float32(16)))).astype(np.float32)
    f_mla = (1.0 / (ROPE_THETA ** (np.arange(0, 64, 2, dtype=np.float32) / np.float32(64)))).astype(np.float32)
    v = np.zeros((128, 1), np.float32)
    for p in range(128):
        d = p % 64
        if d < 16:
            v[p, 0] = f_da[d % 8]
    put("invf_da", v)
    v = np.zeros((128, 1), np.float32)
    for p in range(64):
        v[p, 0] = f_mla[p % 32]
    put("invf_mla", v)
    return c


def host_params(inp, L):
    pb = np.zeros((L, 128, NP_COLS), np.float32)
    s5rep = np.zeros((L, 3, 128, 2048), np.float32)
    s5B = np.zeros((L, 2, 128, 16, 128), np.float32)
    s5C = np.zeros((L, 2, 128, 16, 128), np.float32)
    p = np.arange(128)
    for l in range(L):
        def put(n, a):
            o, w = PB[n]
            a = np.asarray(a, np.float32)
            if a.ndim == 1:
                a = a[:, None]
            pb[l, :a.shape[0], o:o + w] = a
        def rep(n, vec):
            o, w = PB[n]
            pb[l, :, o:o + w] = np.asarray(vec, np.float32).reshape(1, w)
        put("ng", inp["norm_g"][l].reshape(8, 128).T)
        put("daq", inp["da_q_norm"][l][p % 64])
        put("dak", inp["da_k_norm"][l][p % 64])
        cw = inp["mb_conv_w"][l]
        put("convw", cw.reshape(4, 8, 128).transpose(2, 1, 0).reshape(128, 32))
        put("convb", inp["mb_conv_b"][l].reshape(8, 128).T)
        put("s5d", inp["s5_d"][l].reshape(4, 128).T)
        put("s5bglu", inp["s5_b_glu"][l].reshape(4, 128).T)
        put("mqa", inp["mla_q_a_norm"][l].reshape(2, 128).T)
        put("mkva", inp["mla_kv_a_norm"][l])
        put("mqn_n", inp["mla_q_norm"][l][:128])
        put("mqn_r", inp["mla_q_norm"][l][128:192])
        put("mkn_n", inp["mla_k_norm"][l][:128])
        put("mkn_r", inp["mla_k_norm"][l][128:192])
        lre = inp["s5_lam_re"][l].reshape(16, 128).T
        lim = inp["s5_lam_im"][l].reshape(16, 128).T
        lst = np.repeat(inp["s5_log_step"][l], 64).reshape(16, 128).T
        put("lamre", lre); put("lamim", lim); put("lstep", lst)
        rep("subln", inp["da_subln"][l])
        rep("lq1", inp["da_lambda_q1"][l]); rep("lk1", inp["da_lambda_k1"][l])
        rep("lq2", inp["da_lambda_q2"][l]); rep("lk2", inp["da_lambda_k2"][l])
        rep("dtb", inp["mb_dt_bias"][l]); rep("alog", inp["mb_a_log"][l])
        rep("mbd", np.repeat(inp["mb_d"][l], 64)); rep("mbnorm", inp["mb_norm"][l])
        s5rep[l, 0] = inp["s5_lam_re"][l].reshape(1, 2048)
        s5rep[l, 1] = inp["s5_lam_im"][l].reshape(1, 2048)
        s5rep[l, 2] = np.repeat(inp["s5_log_step"][l], 64).reshape(1, 2048)
        for g in range(32):
            mt, j = g // 2, g % 2
            gi = g % 8
            s5B[l, 0, gi * 16:(gi + 1) * 16, mt, j * 64:(j + 1) * 64] = inp["s5_b_re"][l, g].T
            s5B[l, 1, gi * 16:(gi + 1) * 16, mt, j * 64:(j + 1) * 64] = inp["s5_b_im"][l, g].T
            s5C[l, 0, j * 64:(j + 1) * 64, mt, gi * 16:(gi + 1) * 16] = inp["s5_c_re"][l, g].T
            s5C[l, 1, j * 64:(j + 1) * 64, mt, gi * 16:(gi + 1) * 16] = inp["s5_c_im"][l, g].T
    return pb, s5rep, s5B, s5C


class _Shift:
    def __init__(self, ap):
        self.ap = ap

    def __getitem__(self, idx):
        p, f = idx
        return self.ap[:, f]


class Arena:
    def __init__(self, ap32, ncols):
        self.ap = ap32
        self.n = ncols
        self.off = 0

    def reset(self):
        self.off = 0

    def f32(self, *shape):
        n = int(np.prod(shape))
        assert self.off + n <= self.n, ("arena overflow", self.off, n, self.n)
        v = self.ap[:, self.off:self.off + n]
        self.off += n
        if len(shape) == 2:
            v = v.rearrange("p (a b) -> p a b", a=shape[0])
        return v

    def b16(self, *shape):
        n = int(np.prod(shape))
        n32 = (n + 1) // 2
        assert self.off + n32 <= self.n, ("arena overflow", self.off, n32, self.n)
        v = self.ap[:, self.off:self.off + n32].bitcast(BF16)
        if n32 * 2 != n:
            v = v[:, 0:n]
        self.off += n32
        if len(shape) == 2:
            v = v.rearrange("p (a b) -> p a b", a=shape[0])
        return v


def build_program(S_LEN=2048, NSEQ=2, DEPTH=2, branches=(0, 1, 2, 3), ARENA_COLS=19456, debug=False):
    import contextlib
    NT = S_LEN // 128
    BLK = min(512, S_LEN)
    NB = S_LEN // BLK
    TPB = BLK // 128
    L = DEPTH
    nc = bass.Bass("TRN2", target_bir_lowering=False)
    dx = nc.dram_tensor("x", [NSEQ, S_LEN, D_MODEL], F32, kind="ExternalInput").ap()
    dpos = nc.dram_tensor("pos", [NSEQ, S_LEN], I32, kind="ExternalInput").ap()
    dwin = nc.dram_tensor("w_in", [L, D_MODEL, D_IN], F32, kind="ExternalInput").ap()
    dwbr = nc.dram_tensor("w_br", [L, 4, 512, D_MODEL], F32, kind="ExternalInput").ap()
    dwout = nc.dram_tensor("w_out", [L, D_MODEL, D_MODEL], F32, kind="ExternalInput").ap()
    dwuq = nc.dram_tensor("w_uq", [L, 256, 768], F32, kind="ExternalInput").ap()
    dwukv = nc.dram_tensor("w_ukv", [L, 128, 1024], F32, kind="ExternalInput").ap()
    dwglu = nc.dram_tensor("w_glu", [L, 512, 512], F32, kind="ExternalInput").ap()
    dpb = nc.dram_tensor("pblob", [L, 128, NP_COLS], F32, kind="ExternalInput").ap()
    dcb = nc.dram_tensor("cblob", [128, NC_COLS], F32, kind="ExternalInput").ap()
    ds5rep = nc.dram_tensor("s5rep", [L, 3, 128, 2048], F32, kind="ExternalInput").ap()
    ds5B = nc.dram_tensor("s5B", [L, 2, 128, 16, 128], F32, kind="ExternalInput").ap()
    ds5C = nc.dram_tensor("s5C", [L, 2, 128, 16, 128], F32, kind="ExternalInput").ap()
    dout = nc.dram_tensor("out", [NSEQ, S_LEN, D_MODEL], F32, kind="ExternalOutput").ap()
    if debug:
        dbg_ht = nc.dram_tensor("dbg_ht", [128, 8 * S_LEN], F32, kind="ExternalOutput").ap()
        dbg_yb = nc.dram_tensor("dbg_yb", [128, 4 * S_LEN], F32, kind="ExternalOutput").ap()
        dbg_mg = nc.dram_tensor("dbg_mg", [128, 8 * S_LEN], F32, kind="ExternalOutput").ap()

    S = Sched(nc)
    st = contextlib.ExitStack()

    def sb(name, shape, dt=F32):
        return st.enter_context(nc.sbuf_tensor(name, shape, dt))[:]

    def psum(name, shape, dt=F32):
        return st.enter_context(nc.psum_tensor(name, shape, dt))[:]

    uid = [0]

    def U(prefix):
        uid[0] += 1
        return (prefix, uid[0])

    with st:
        CF = sb("CF", [128, NC_COLS])
        CBF = sb("CBF", [128, 6 * 128], BF16)
        PBL = sb("PBL", [128, NP_COLS])
        DRV = sb("DRV", [128, 256])
        HT = sb("HT", [128, 8, S_LEN], BF16)
        MG = sb("MG", [128, 8, S_LEN], BF16)
        YB = sb("YB", [128, 4, S_LEN], BF16)
        WB = [sb("WB%d" % i, [128, 4096], BF16) for i in range(3)]
        ARN = sb("ARN", [128, ARENA_COLS])
        AR = Arena(ARN, ARENA_COLS)
        PS = [psum("PS%d" % i, [128, 512]) for i in range(7)]
        PSB = psum("PSB", [128, 1024], BF16)

        def cf(n):
            o, w = CB[n]
            return CF[:, o:o + w]

        def pbl(n, a=None, b=None):
            o, w = PB[n]
            if a is None:
                return PBL[:, o:o + w]
            return PBL[:, o + a:o + (a + 1 if b is None else b)]

        IDB = CBF[:, 0:128]; ONESB = CBF[:, 128:256]; BLK64B = CBF[:, 256:384]
        MASKUB = CBF[:, 384:512]; PTDAB = CBF[:, 512:640]; PTMLAB = CBF[:, 640:768]
        ONESF = cf("ones"); MASKUF = cf("maskU")
        D_EPS = DRV[:, 0:1]; D_ONE = DRV[:, 1:2]; D_GQ = DRV[:, 2:3]; D_NEGLAM = DRV[:, 3:4]
        D_MQN_N = DRV[:, 4:5]; D_MQN_R = DRV[:, 5:6]; D_T1 = DRV[:, 6:7]; D_T2 = DRV[:, 7:8]
        D_NEGA = DRV[:, 8:16]; D_NEGONES = DRV[:, 16:144]

        wb_rr = [0]

        def wslot():
            i = wb_rr[0] % 3
            wb_rr[0] += 1
            return i

        def load_w(slot, src_ap, view):
            S.dma("pool", lambda e, o=view, i=src_ap: e.dma_start(out=o, in_=i), r=(), w=(("wb", slot),))

        def win_view(l, c0, ncols):
            return dwin[l, :, c0:c0 + ncols].rearrange("(kc p) c -> p kc c", p=128)

        def wb_view(slot, kc, ncols, off=0):
            return WB[slot][:, off:off + kc * ncols].rearrange("p (k c) -> p k c", k=kc)

        def mm(out_ap, pairs, r, w):
            def fn(e, out_ap=out_ap, pairs=pairs):
                ins = None
                n = len(pairs)
                for i, (a, b) in enumerate(pairs):
                    ins = e.matmul(out_ap, lhsT=a, rhs=b, start=(i == 0), stop=(i == n - 1))
                return ins
            cst = 0.06
            for (a_, b_) in pairs:
                ncol = int(np.prod(b_.shape[1:]))
                cst += max(ncol, 64) / 1400.0 * (4.0 if a_.dtype == F32 else 1.0)
            S.pe(fn, r, w, cost=cst)

        def proj_pairs(wv, c0, mw, tok0, ntok):
            return [(wv[:, kc, c0:c0 + mw], HT[:, kc, tok0:tok0 + ntok]) for kc in range(8)]

        def tok_pairs(wv, c0, ncols, tile):
            return [(HT[:, kc, tile * 128:(tile + 1) * 128], wv[:, kc, c0:c0 + ncols]) for kc in range(8)]

        def rsqrt_act(out_ap, in_ap, scale, r, w):
            S.act(lambda e: e.activation(out=out_ap, in_=in_ap, func=AF.Ln, scale=scale, bias=D_EPS[0:out_ap.shape[0], :]), r, w)
            S.act(lambda e: e.activation(out=out_ap, in_=out_ap, func=AF.Exp, scale=-0.5), w, w)

        def sincos(v_ap, tmp_ap, out_sin, out_cos, r, kv, kt, ks, kc_):
            S.dve(lambda e: e.tensor_scalar(out=tmp_ap, in0=v_ap, scalar1=MAGIC, scalar2=None, op0=ALU.add), r + (kv,), (kt,))
            S.dve(lambda e: e.tensor_scalar(out=tmp_ap, in0=tmp_ap, scalar1=-MAGIC, scalar2=None, op0=ALU.add), (kt,), (kt,))
            S.dve(lambda e: e.tensor_tensor(out=tmp_ap, in0=v_ap, in1=tmp_ap, op=ALU.subtract), (kv, kt), (kt,))
            S.act(lambda e: e.activation(out=out_sin, in_=tmp_ap, func=AF.Sin, scale=TWO_PI), (kt,), (ks,))
            S.dve(lambda e: e.tensor_scalar(out=tmp_ap, in0=v_ap, scalar1=0.25, scalar2=MAGIC, op0=ALU.add, op1=ALU.add), (kv, ks), (kt,))
            S.dve(lambda e: e.tensor_scalar(out=tmp_ap, in0=tmp_ap, scalar1=-MAGIC, scalar2=None, op0=ALU.add), (kt,), (kt,))
            S.dve(lambda e: e.scalar_tensor_tensor(out=tmp_ap, in0=v_ap, scalar=0.25, in1=tmp_ap, op0=ALU.add, op1=ALU.subtract), (kv, kt), (kt,))
            S.act(lambda e: e.activation(out=out_cos, in_=tmp_ap, func=AF.Sin, scale=TWO_PI), (kt,), (kc_,))

        S.dma("sp", lambda e: e.dma_start(out=CF, in_=dcb), w=("CF",))
        for i, n in enumerate(["ident", "ones", "blk64", "maskU", "ptda", "ptmla"]):
            S.dve(lambda e, i=i, n=n: e.tensor_copy(out=CBF[:, i * 128:(i + 1) * 128], in_=cf(n)), ("CF",), ("CBF",))
        S.dve(lambda e: e.memset(D_EPS, EPS), (), ("DRVc",))
        S.dve(lambda e: e.memset(D_ONE, 1.0), (), ("DRVc",))
        S.dve(lambda e: e.memset(D_ONE, 1.0), (), ("DRVc",))
        S.dve(lambda e: e.memset(D_NEGONES, -1.0), (), ("DRVc",))
        CONST_R = ("CF", "CBF", "DRVc")

        def layer_setup(l):
            lam_init = 0.8 - 0.6 * float(np.exp(-0.3 * l))
            S.dma("sp", lambda e: e.dma_start(out=PBL, in_=dpb[l]), w=("PBL",))
            kd = "DRV"
            S.dve(lambda e: e.tensor_scalar(out=D_GQ, in0=pbl("daq"), scalar1=0.125, scalar2=None, op0=ALU.mult), ("PBL",), (kd,))
            S.dve(lambda e: e.tensor_scalar(out=D_MQN_N, in0=pbl("mqn_n"), scalar1=float(192 ** -0.5), scalar2=None, op0=ALU.mult), ("PBL",), (kd,))
            S.dve(lambda e: e.tensor_scalar(out=D_MQN_R, in0=pbl("mqn_r"), scalar1=float(192 ** -0.5), scalar2=None, op0=ALU.mult), ("PBL",), (kd,))
            tmp = DRV[:, 144:208]
            S.dve(lambda e: e.tensor_tensor(out=tmp, in0=pbl("lq1"), in1=pbl("lk1"), op=ALU.mult), ("PBL",), (kd,))
            S.dve(lambda e: e.tensor_reduce(out=D_T1, in_=tmp, axis=AX.X, op=ALU.add), (kd,), (kd,))
            S.dve(lambda e: e.tensor_tensor(out=tmp, in0=pbl("lq2"), in1=pbl("lk2"), op=ALU.mult), ("PBL",), (kd,))
            S.dve(lambda e: e.tensor_reduce(out=D_T2, in_=tmp, axis=AX.X, op=ALU.add), (kd,), (kd,))
            S.act(lambda e: e.activation(out=D_T1, in_=D_T1, func=AF.Exp), (kd,), (kd,))
            S.act(lambda e: e.activation(out=D_T2, in_=D_T2, func=AF.Exp), (kd,), (kd,))
            S.dve(lambda e: e.tensor_tensor(out=D_NEGLAM, in0=D_T2, in1=D_T1, op=ALU.subtract), (kd,), (kd,))
            S.dve(lambda e: e.tensor_scalar(out=D_NEGLAM, in0=D_NEGLAM, scalar1=-lam_init, scalar2=None, op0=ALU.add), (kd,), (kd,))
            S.dve(lambda e: e.tensor_scalar(out=pbl("subln"), in0=pbl("subln"), scalar1=float(1.0 - lam_init), scalar2=None, op0=ALU.mult), ("PBL", kd), ("PBL",))
            S.act(lambda e: e.activation(out=D_NEGA, in_=pbl("alog"), func=AF.Exp), ("PBL",), (kd,))
            S.dve(lambda e: e.tensor_scalar(out=D_NEGA, in0=D_NEGA, scalar1=-1.0, scalar2=None, op0=ALU.mult), (kd,), (kd,))
        LR = ("PBL", "DRV") + CONST_R

        def stage0(s, l):
            AR.reset()
            xt = [AR.f32(D_MODEL) for _ in range(2)]
            junk = AR.f32(D_MODEL)
            xn = AR.b16(TPB, D_MODEL)
            ssq = AR.f32(8)
            src = dx if l == 0 else dout
            for n in range(NB):
                for tt in range(TPB):
                    t = n * TPB + tt
                    b = t % 2
                    kx = ("xt", b)
                    S.dma("sp", lambda e, b=b, t=t: e.dma_start(out=xt[b], in_=src[s, t * 128:(t + 1) * 128, :]),
                          r=(("outd", s, t),), w=(kx,))
                    S.act(lambda e, b=b: e.activation(out=junk, in_=xt[b], func=AF.Square), (kx,), ("junk",))
                    S.dve(lambda e: e.tensor_reduce(out=ssq[:, 0:1], in_=junk, axis=AX.X, op=ALU.add), ("junk",), ("ssq",))
                    rsqrt_act(ssq[:, 0:1], ssq[:, 0:1], 1.0 / D_MODEL, ("ssq",) + LR, ("ssq",))
                    S.dve(lambda e, b=b, tt=tt: e.tensor_scalar(out=xn[:, tt, :], in0=xt[b], scalar1=ssq[:, 0:1], scalar2=None, op0=ALU.mult),
                          (kx, "ssq"), (("xn", tt),))
                for kc in range(8):
                    half = kc % 2
                    pt = PSB[:, half * 512:half * 512 + BLK]
                    def fn(e, kc=kc, pt=pt):
                        ins = None
                        for tt in range(TPB):
                            ins = e.transpose(out=pt[:, tt * 128:(tt + 1) * 128], in_=xn[:, tt, kc * 128:(kc + 1) * 128], identity=IDB)
                        return ins
                    S.pe(fn, tuple(("xn", tt) for tt in range(TPB)) + LR, (("psb", 0),))
                    S.dve(lambda e, kc=kc, pt=pt, n=n: e.tensor_scalar(out=HT[:, kc, n * BLK:(n + 1) * BLK], in0=pt, scalar1=pbl("ng", kc), scalar2=None, op0=ALU.mult),
                          (("psb", 0),) + LR, (("HT", n),))

        def merge(s, l, b, first):
            AR.reset()
            sg = [AR.f32(BLK) for _ in range(2)]
            tmp = [AR.f32(BLK) for _ in range(2)]
            sl_br = wslot()
            load_w(sl_br, dwbr[l, b].rearrange("(kc p) c -> p kc c", p=128), wb_view(sl_br, 4, 1024))
            wbr = wb_view(sl_br, 4, 1024)
            it = 0
            for fh in range(2):
                slg = wslot()
                load_w(slg, win_view(l, OFF["gate"] + b * 1024 + fh * 512, 512), wb_view(slg, 8, 512))
                wg = wb_view(slg, 8, 512)
                for f4 in range(4):
                    f = fh * 4 + f4
                    for n in range(NB):
                        pg = PS[(2 * it) % 6]; pb_ = PS[(2 * it + 1) % 6]
                        kpg = ("ps", (2 * it) % 6); kpb = ("ps", (2 * it + 1) % 6)
                        bi = it % 2
                        it += 1
                        mm(pg[:, 0:BLK], proj_pairs(wg, f4 * 128, 128, n * BLK, BLK), (("wb", slg), ("HT", n)), (kpg,))
                        mm(pb_[:, 0:BLK], [(wbr[:, k4, f * 128:(f + 1) * 128], YB[:, k4, n * BLK:(n + 1) * BLK]) for k4 in range(4)],
                           (("wb", sl_br), ("YB", n)), (kpb,))
                        S.act(lambda e, bi=bi, pg=pg: e.activation(out=sg[bi], in_=pg[:, 0:BLK], func=AF.Sigmoid), (kpg,), (("sg", bi),))
                        mgv = MG[:, f, n * BLK:(n + 1) * BLK]
                        if first:
                            S.dve(lambda e, bi=bi, pb_=pb_, mgv=mgv: e.tensor_tensor(out=mgv, in0=sg[bi], in1=pb_[:, 0:BLK], op=ALU.mult),
                                  (("sg", bi), kpb), (("MG", f, n),))
                        else:
                            S.dve(lambda e, bi=bi, pb_=pb_: e.tensor_tensor(out=tmp[bi], in0=sg[bi], in1=pb_[:, 0:BLK], op=ALU.mult),
                                  (("sg", bi), kpb), (("mtmp", bi),))
                            S.pool(lambda e, bi=bi, mgv=mgv: e.tensor_tensor(out=mgv, in0=mgv, in1=tmp[bi], op=ALU.add),
                                   (("mtmp", bi), ("MG", f, n)), (("MG", f, n),))

        def outproj(s, l):
            AR.reset()
            xt = [AR.f32(D_MODEL) for _ in range(2)]
            ot = [AR.f32(D_MODEL) for _ in range(2)]
            src = dx if l == 0 else dout
            sls = []
            for hh in range(2):
                sl = wslot()
                load_w(sl, dwout[l, :, hh * 512:(hh + 1) * 512].rearrange("(kc p) c -> p kc c", p=128), wb_view(sl, 8, 512))
                sls.append(sl)
            for t in range(NT):
                b = t % 2
                S.dma("sp", lambda e, b=b, t=t: e.dma_start(out=xt[b], in_=src[s, t * 128:(t + 1) * 128, :]),
                      r=(("outd", s, t),), w=(("xt", b),))
                for hh in range(2):
                    p = PS[(2 * t + hh) % 4]
                    kp = ("ps", (2 * t + hh) % 4)
                    wv = wb_view(sls[hh], 8, 512)
                    mm(p[:, 0:512], [(MG[:, kc, t * 128:(t + 1) * 128], wv[:, kc, :]) for kc in range(8)],
                       (("wb", sls[hh]),) + tuple(("MG", kc, t // TPB) for kc in range(8)), (kp,))
                    S.dve(lambda e, b=b, hh=hh, p=p: e.tensor_tensor(out=ot[b][:, hh * 512:(hh + 1) * 512], in0=p[:, 0:512], in1=xt[b][:, hh * 512:(hh + 1) * 512], op=ALU.add),
                          (kp, ("xt", b)), (("ot", b, hh),))
                S.dma("sp", lambda e, b=b, t=t: e.dma_start(out=dout[s, t * 128:(t + 1) * 128, :], in_=ot[b]),
                      r=(("ot", b, 0), ("ot", b, 1)), w=(("outd", s, t),))

        def rope_tables(s, invf_col, npart, Ct, St):
            pi_ = AR.f32(BLK).bitcast(I32)
            v = AR.f32(BLK); tmp = AR.f32(BLK)
            for n in range(NB):
                S.dma("sp", lambda e, n=n: e.dma_start(out=pi_[0:npart, :], in_=dpos[s:s + 1, n * BLK:(n + 1) * BLK].partition_broadcast(npart)),
                      w=("rp_pi",))
                S.dve(lambda e: e.tensor_copy(out=v[0:npart, :], in_=pi_[0:npart, :]), ("rp_pi",), ("rp_v",))
                S.dve(lambda e: e.tensor_scalar(out=v[0:npart, :], in0=v[0:npart, :], scalar1=invf_col[0:npart, :], scalar2=float(1.0 / TWO_PI), op0=ALU.mult, op1=ALU.mult),
                      ("rp_v",) + CONST_R, ("rp_v",))
                sincos(v[0:npart, :], tmp[0:npart, :], St[0:npart, n * BLK:(n + 1) * BLK], Ct[0:npart, n * BLK:(n + 1) * BLK],
                       (), "rp_v", "rp_t", ("rp_S", n), ("rp_C", n))

        def rope_apply(dst, xn_ap, pp, kpp, PT, npart, Ct, St, n, t1, t2, rkeys, wkey, par=0):
            mm(pp[0:npart, 0:BLK], [(PT[0:npart, 0:npart], xn_ap)], rkeys + CONST_R, (kpp,))
            S.dve(lambda e: e.tensor_tensor(out=t1[0:npart, :], in0=xn_ap, in1=Ct[0:npart, n * BLK:(n + 1) * BLK], op=ALU.mult),
                  rkeys + (("rp_C", n),), (("rp_t1", par),))
            S.dve(lambda e: e.tensor_tensor(out=t2[0:npart, :], in0=pp[0:npart, 0:BLK], in1=St[0:npart, n * BLK:(n + 1) * BLK], op=ALU.mult),
                  (kpp, ("rp_S", n)), (("rp_t2", par),))
            S.pool(lambda e: e.tensor_tensor(out=dst, in0=t1[0:npart, :], in1=t2[0:npart, :], op=ALU.add), (("rp_t1", par), ("rp_t2", par)), (wkey,))

        def attn_block(qb, parts, vaug, PTt, dest, dkey):
            t0 = qb * TPB
            for j in range(t0 + TPB):
                lo = max(0, j - t0)
                c0 = lo * 128
                bi = j % 2
                st_ = PS[4 + bi]
                kst = ("ps", 4 + bi)
                rk = tuple(p[3](j) for p in parts) + tuple(p[4] for p in parts)
                mm(st_[:, c0:BLK], [(p[0][0:p[2], j * 128:(j + 1) * 128], p[1][0:p[2], c0:BLK]) for p in parts], rk, (kst,))
                S.act(lambda e, bi=bi, st_=st_, c0=c0: e.activation(out=PTt[bi][:, c0:BLK], in_=st_[:, c0:BLK], func=AF.Exp), (kst,), (("ptt", bi),))
                if j >= t0:
                    S.pool(lambda e, bi=bi, c0=c0: e.tensor_tensor(out=PTt[bi][:, c0:c0 + 128], in0=PTt[bi][:, c0:c0 + 128], in1=MASKUB, op=ALU.mult),
                           (("ptt", bi),) + CONST_R, (("ptt", bi),))
                for i in range(lo, TPB):
                    def fn(e, i=i, j=j, bi=bi):
                        return e.matmul(PS[i][:, 0:130], lhsT=PTt[bi][:, i * 128:(i + 1) * 128], rhs=vaug[:, j, :],
                                        start=(j == 0), stop=(j == t0 + i))
                    S.pe(fn, (("ptt", bi), ("vaug", j)), (("ps", i),))
            rec = AR_small["rec"]
            for i in range(TPB):
                S.dve(lambda e, i=i: e.reciprocal(out=rec[:, i:i + 1], in_=PS[i][:, 128:129]), (("ps", i),), (("rec", i),))
                S.dve(lambda e, i=i: e.tensor_scalar(out=dest[:, i, :], in0=PS[i][:, 0:128], scalar1=rec[:, i:i + 1], scalar2=None, op0=ALU.mult),
                      (("ps", i), ("rec", i)), (dkey + (i,),))

        AR_small = {}

        def finish_head(l, h, qb, obf, okeys, gate_c0, wg_view, wg_key, gs):
            pt = PSB[:, 0:BLK]
            def fn(e):
                ins = None
                for i in range(TPB):
                    ins = e.transpose(out=pt[:, i * 128:(i + 1) * 128], in_=obf[:, i, :], identity=IDB)
                return ins
            S.pe(fn, okeys + CONST_R, (("psb", 0),))
            pg = PS[6]
            mm(pg[:, 0:BLK], proj_pairs(wg_view, gate_c0, 128, qb * BLK, BLK), (wg_key, ("HT", qb)), (("ps", 6),))
            S.act(lambda e: e.activation(out=gs, in_=pg[:, 0:BLK], func=AF.Silu), (("ps", 6),), ("gsilu",))
            S.dve(lambda e: e.tensor_tensor(out=YB[:, h, qb * BLK:(qb + 1) * BLK], in0=pt, in1=gs, op=ALU.mult),
                  (("psb", 0), "gsilu"), (("YB", qb),))

        def branch_da(s, l):
            AR.reset()
            Ct = AR.b16(S_LEN); St = AR.b16(S_LEN)
            rope_tables(s, cf("invf_da"), 128, Ct, St)
            qT = AR.b16(S_LEN); kT = AR.b16(S_LEN)
            vaug = AR.b16(NT, 130)
            PTt = [AR.b16(BLK) for _ in range(2)]
            sqs = [AR.b16(BLK) for _ in range(2)]; qns = [AR.b16(BLK) for _ in range(2)]; obf = AR.b16(TPB, 128); gs = AR.f32(BLK)
            rss = [AR.f32(BLK) for _ in range(2)]; t1s = [AR.f32(BLK) for _ in range(2)]; t2s = [AR.f32(BLK) for _ in range(2)]
            d0 = AR.f32(TPB, 128); d1 = AR.f32(TPB, 128); o = AR.f32(TPB, 128); junk = AR.f32(TPB, 128)
            ssq = AR.f32(TPB); AR_small["rec"] = AR.f32(TPB)
            S.dve(lambda e: e.memset(vaug[:, :, 128:130], 1.0), (), tuple(("vaug", j) for j in range(NT)))
            for h in range(4):
                sl = wslot()
                wv = WB[sl].rearrange("p (q k c) -> p q k c", q=4, k=8)
                for qi, nm in enumerate(["da_q", "da_k", "da_v", "da_gate"]):
                    load_w(sl, win_view(l, OFF[nm] + h * 128, 128), wv[:, qi])
                wkey = ("wb", sl)
                def da_qk(which, dstT, gcol, n, par):
                    sq, rs, qn, t1, t2 = sqs[par], rss[par], qns[par], t1s[par], t2s[par]
                    pq = PS[par]; kpq = ("ps", par)
                    pss = PS[2 + par]; kpss = ("ps", 2 + par)
                    ppm = PS[4 + par]; kppm = ("ps", 4 + par)
                    mm(pq[:, 0:BLK], proj_pairs(wv[:, which], 0, 128, n * BLK, BLK), (wkey, ("HT", n)), (kpq,))
                    S.act(lambda e: e.activation(out=sq, in_=pq[:, 0:BLK], func=AF.Square), (kpq,), (("sq", par),))
                    mm(pss[:, 0:BLK], [(BLK64B, sq)], (("sq", par),) + CONST_R, (kpss,))
                    rsqrt_act(rs, pss[:, 0:BLK], 1.0 / 64, (kpss,) + LR, (("rs", par),))
                    S.dve(lambda e: e.scalar_tensor_tensor(out=qn, in0=pq[:, 0:BLK], scalar=gcol, in1=rs, op0=ALU.mult, op1=ALU.mult),
                          (kpq, ("rs", par)) + LR, (("qn", par),))
                    rope_apply(dstT[:, n * BLK:(n + 1) * BLK], qn, ppm, kppm, PTDAB, 128, Ct, St, n, t1, t2, (("qn", par),),
                               (("qT" if which == 0 else "kT"), n), par=par)
                it_ = 0
                for which, dstT, gcol in ((0, qT, D_GQ), (1, kT, pbl("dak"))):
                    for n in range(NB):
                        da_qk(which, dstT, gcol, n, (it_ % 2) if DB_DA else 0)
                        it_ += 1
                for n in range(NB):
                    pv = PS[n % 2]; kpv = ("ps", n % 2)
                    for tt in range(TPB):
                        t = n * TPB + tt
                        mm(pv[:, tt * 128:(tt + 1) * 128], tok_pairs(wv[:, 2], 0, 128, t), (wkey, ("HT", n)), (kpv,))
                    S.act(lambda e, pv=pv, n=n: e.activation(out=vaug[:, n * TPB:(n + 1) * TPB, 0:128], in_=pv[:, 0:BLK].rearrange("p (a b) -> p a b", a=TPB), func=AF.Copy),
                          (kpv,), tuple(("vaug", n * TPB + tt) for tt in range(TPB)))
                for qb in range(NB):
                    for m, dst in ((0, d0), (1, d1)):
                        kv = kT[m * 64:(m + 1) * 64, :]
                        qv = qT[m * 64:(m + 1) * 64, qb * BLK:(qb + 1) * BLK]
                        parts = [(_Shift(kv), _Shift(qv), 64, (lambda j: ("kT", j // TPB)), ("qT", qb))]
                        attn_block(qb, parts, vaug, PTt, dst, ("dd", m))
                    for i in range(TPB):
                        S.dve(lambda e, i=i: e.scalar_tensor_tensor(out=o[:, i, :], in0=d1[:, i, :], scalar=D_NEGLAM, in1=d0[:, i, :], op0=ALU.mult, op1=ALU.add),
                              (("dd", 0, i), ("dd", 1, i)) + LR, (("o", i),))
                    S.act(lambda e: e.activation(out=junk, in_=o, func=AF.Square), tuple(("o", i) for i in range(TPB)), ("ojunk",))
                    S.dve(lambda e: e.tensor_reduce(out=ssq, in_=junk, axis=AX.X, op=ALU.add), ("ojunk",), ("ossq",))
                    rsqrt_act(ssq, ssq, 1.0 / 128, ("ossq",) + LR, ("ossq",))
                    for i in range(TPB):
                        S.dve(lambda e, i=i: e.scalar_tensor_tensor(out=obf[:, i, :], in0=o[:, i, :], scalar=ssq[:, i:i + 1], in1=pbl("subln"), op0=ALU.mult, op1=ALU.mult),
                              (("o", i), "ossq") + LR, (("obf", i),))
                    finish_head(l, h, qb, obf, tuple(("obf", i) for i in range(TPB)), 0, wv[:, 3], wkey, gs)

        def branch_mla(s, l):
            AR.reset()
            Ct = AR.b16(S_LEN); St = AR.b16(S_LEN)
            rope_tables(s, cf("invf_mla"), 64, Ct, St)
            cqn = AR.b16(2, S_LEN); ckvn = AR.b16(S_LEN); krr = AR.b16(S_LEN)
            knope = AR.b16(S_LEN); krope = AR.b16(S_LEN)
            vaug = AR.b16(NT, 130)
            qnope = AR.b16(BLK); qrope = AR.b16(BLK)
            PTt = [AR.b16(BLK) for _ in range(2)]
            sqA = AR.b16(BLK); sqB = AR.b16(BLK); tb = AR.b16(BLK); obf = AR.b16(TPB, 128)
            wuq = AR.b16(2, 768); wukv = AR.b16(1024)
            gs = AR.f32(BLK); rs = AR.f32(BLK); t1 = AR.f32(BLK); t2 = AR.f32(BLK)
            o = AR.f32(TPB, 128); AR_small["rec"] = AR.f32(TPB)
            S.dve(lambda e: e.memset(vaug[:, :, 128:130], 1.0), (), tuple(("vaug", j) for j in range(NT)))
            S.dma("pool", lambda e: e.dma_start(out=wuq, in_=dwuq[l].rearrange("(kc p) c -> p kc c", p=128)), w=("wuq",))
            S.dma("pool", lambda e: e.dma_start(out=wukv, in_=dwukv[l]), w=("wukv",))
            sl = wslot()
            wc = wb_view(sl, 8, 448)
            load_w(sl, win_view(l, OFF["mla_cq"], 448), wc)
            wkey = ("wb", sl)
            for n in range(NB):
                blk = slice(n * BLK, (n + 1) * BLK)
                for c in range(2):
                    mm(PS[c][:, 0:BLK], proj_pairs(wc, c * 128, 128, n * BLK, BLK), (wkey, ("HT", n)), (("ps", c),))
                S.act(lambda e: e.activation(out=sqA, in_=PS[0][:, 0:BLK], func=AF.Square), (("ps", 0),), ("sqA",))
                S.act(lambda e: e.activation(out=sqB, in_=PS[1][:, 0:BLK], func=AF.Square), (("ps", 1),), ("sqB",))
                mm(PS[2][:, 0:BLK], [(ONESB, sqA), (ONESB, sqB)], ("sqA", "sqB") + CONST_R, (("ps", 2),))
                rsqrt_act(rs, PS[2][:, 0:BLK], 1.0 / 256, (("ps", 2),) + LR, ("rs",))
                for c in range(2):
                    S.dve(lambda e, c=c, blk=blk: e.scalar_tensor_tensor(out=cqn[:, c, blk], in0=PS[c][:, 0:BLK], scalar=pbl("mqa", c), in1=rs, op0=ALU.mult, op1=ALU.mult),
                          (("ps", c), "rs") + LR, (("cqn", n),))
                mm(PS[3][:, 0:BLK], proj_pairs(wc, 256, 128, n * BLK, BLK), (wkey, ("HT", n)), (("ps", 3),))
                S.act(lambda e: e.activation(out=sqA, in_=PS[3][:, 0:BLK], func=AF.Square), (("ps", 3),), ("sqA",))
                mm(PS[2][:, 0:BLK], [(ONESB, sqA)], ("sqA",) + CONST_R, (("ps", 2),))
                rsqrt_act(rs, PS[2][:, 0:BLK], 1.0 / 128, (("ps", 2),) + LR, ("rs",))
                S.dve(lambda e, blk=blk: e.scalar_tensor_tensor(out=ckvn[:, blk], in0=PS[3][:, 0:BLK], scalar=pbl("mkva"), in1=rs, op0=ALU.mult, op1=ALU.mult),
                      (("ps", 3), "rs") + LR, (("ckvn", n),))
                mm(PS[6][0:64, 0:BLK], proj_pairs(wc, 384, 64, n * BLK, BLK), (wkey, ("HT", n)), (("ps", 6),))
                S.act(lambda e, blk=blk: e.activation(out=krr[0:64, blk], in_=PS[6][0:64, 0:BLK], func=AF.Copy), (("ps", 6),), (("krr", n),))
            slg = wslot()
            wgv = wb_view(slg, 8, 512)
            load_w(slg, win_view(l, OFF["mla_gate"], 512), wgv)
            for h in range(4):
                for n in range(NB):
                    blk = slice(n * BLK, (n + 1) * BLK)
                    mm(PS[0][:, 0:BLK], [(wukv[:, h * 256:h * 256 + 128], ckvn[:, blk])], ("wukv", ("ckvn", n)), (("ps", 0),))
                    S.act(lambda e: e.activation(out=sqA, in_=PS[0][:, 0:BLK], func=AF.Square), (("ps", 0),), ("sqA",))
                    S.act(lambda e, blk=blk: e.activation(out=sqB[0:64, :], in_=krr[0:64, blk], func=AF.Square), (("krr", n),), ("sqB",))
                    mm(PS[2][:, 0:BLK], [(ONESB, sqA), (ONESB[0:64, :], sqB[0:64, :])], ("sqA", "sqB") + CONST_R, (("ps", 2),))
                    rsqrt_act(rs, PS[2][:, 0:BLK], 1.0 / 192, (("ps", 2),) + LR, ("rs",))
                    S.dve(lambda e, blk=blk: e.scalar_tensor_tensor(out=knope[:, blk], in0=PS[0][:, 0:BLK], scalar=pbl("mkn_n"), in1=rs, op0=ALU.mult, op1=ALU.mult),
                          (("ps", 0), "rs") + LR, (("knope", n),))
                    S.dve(lambda e, blk=blk: e.scalar_tensor_tensor(out=tb[0:64, :], in0=krr[0:64, blk], scalar=pbl("mkn_r")[0:64, :], in1=rs[0:64, :], op0=ALU.mult, op1=ALU.mult),
                          (("krr", n), "rs") + LR, ("tb",))
                    rope_apply(krope[0:64, blk], tb[0:64, :], PS[3], ("ps", 3), PTMLAB, 64, Ct, St, n, t1, t2, ("tb",), ("krope", n))
                    pv = PS[1]
                    for tt in range(TPB):
                        t = n * TPB + tt
                        mm(pv[:, tt * 128:(tt + 1) * 128], [(ckvn[:, t * 128:(t + 1) * 128], wukv[:, h * 256 + 128:h * 256 + 256])],
                           ("wukv", ("ckvn", n)), (("ps", 1),))
                    S.act(lambda e, n=n: e.activation(out=vaug[:, n * TPB:(n + 1) * TPB, 0:128], in_=PS[1][:, 0:BLK].rearrange("p (a b) -> p a b", a=TPB), func=AF.Copy),
                          (("ps", 1),), tuple(("vaug", n * TPB + tt) for tt in range(TPB)))
                for qb in range(NB):
                    blk = slice(qb * BLK, (qb + 1) * BLK)
                    c0 = h * 192
                    mm(PS[0][:, 0:BLK], [(wuq[:, c, c0:c0 + 128], cqn[:, c, blk]) for c in range(2)], ("wuq", ("cqn", qb)), (("ps", 0),))
                    mm(PS[1][0:64, 0:BLK], [(wuq[:, c, c0 + 128:c0 + 192], cqn[:, c, blk]) for c in range(2)], ("wuq", ("cqn", qb)), (("ps", 1),))
                    S.act(lambda e: e.activation(out=sqA, in_=PS[0][:, 0:BLK], func=AF.Square), (("ps", 0),), ("sqA",))
                    S.act(lambda e: e.activation(out=sqB[0:64, :], in_=PS[1][0:64, 0:BLK], func=AF.Square), (("ps", 1),), ("sqB",))
                    mm(PS[2][:, 0:BLK], [(ONESB, sqA), (ONESB[0:64, :], sqB[0:64, :])], ("sqA", "sqB") + CONST_R, (("ps", 2),))
                    rsqrt_act(rs, PS[2][:, 0:BLK], 1.0 / 192, (("ps", 2),) + LR, ("rs",))
                    S.dve(lambda e: e.scalar_tensor_tensor(out=qnope, in0=PS[0][:, 0:BLK], scalar=D_MQN_N, in1=rs, op0=ALU.mult, op1=ALU.mult),
                          (("ps", 0), "rs") + LR, ("qnope",))
                    S.dve(lambda e: e.scalar_tensor_tensor(out=tb[0:64, :], in0=PS[1][0:64, 0:BLK], scalar=D_MQN_R[0:64, :], in1=rs[0:64, :], op0=ALU.mult, op1=ALU.mult),
                          (("ps", 1), "rs") + LR, ("tb",))
                    rope_apply(qrope[0:64, :], tb[0:64, :], PS[3], ("ps", 3), PTMLAB, 64, Ct, St, qb, t1, t2, ("tb",), "qrope")
                    parts = [(knope, qnope, 128, (lambda j: ("knope", j // TPB)), "qnope"),
                             (krope, qrope, 64, (lambda j: ("krope", j // TPB)), "qrope")]
                    attn_block(qb, parts, vaug, PTt, o, ("oo",))
                    for i in range(TPB):
                        S.pool(lambda e, i=i: e.tensor_copy(out=obf[:, i, :], in_=o[:, i, :]), (("oo", i),), (("obf", i),))
                    finish_head(l, h, qb, obf, tuple(("obf", i) for i in range(TPB)), h * 128, wgv, ("wb", slg), gs)

        def branch_ssd(s, l):
            AR.reset()
            Xtok = AR.b16(NT, 512)
            Bfm = AR.b16(2, S_LEN); Cfm = AR.b16(2, S_LEN)
            xfm = AR.b16(BLK); Xdt = AR.b16(512); XdD = AR.b16(512); Btok = AR.b16(256)
            MT = [AR.b16(128) for _ in range(2)]
            Sbf = AR.b16(2, 256); ybt = AR.b16(512); wdt = AR.b16(8, 8)
            raw = [AR.f32(BLK + 3) for _ in range(2)]
            acc = AR.f32(BLK); zs = AR.f32(512); CBm = AR.f32(128)
            aTri = [AR.f32(128) for _ in range(2)]
            Dm = AR.f32(128); E = AR.f32(128)
            ydiag = AR.f32(512); yc = AR.f32(512); junk = AR.f32(512); Sst = AR.f32(2, 256)
            sm = AR.f32(64)
            dt_ = sm[:, 0:8]; a_ = sm[:, 8:16]; cs_ = sm[:, 16:24]; ecs = sm[:, 24:32]; etot = sm[:, 32:40]
            dec = sm[:, 40:48]; ssq = sm[:, 48:50]; tmp8 = sm[:, 56:64]
            slz = wslot(); wz = wb_view(slz, 8, 512)
            load_w(slz, win_view(l, OFF["mb_z"], 512), wz)
            S.dma("pool", lambda e: e.dma_start(out=wdt, in_=win_view(l, OFF["mb_dt"], 8)), w=("wdt",))
            S.dve(lambda e: e.memset(Sst, 0.0), (), ("Sst",))
            S.dve(lambda e: e.memset(Sbf, 0.0), (), ("Sbf",))
            for half in range(2):
                slx = wslot(); wx = wb_view(slx, 8, 512)
                load_w(slx, win_view(l, OFF["mb_xbc"] + half * 512, 512), wx)
                for f4 in range(4):
                    fc = half * 4 + f4
                    for n in range(NB):
                        rb = raw[n % 2]; krb = ("raw", n % 2)
                        p = PS[n % 2]; kp = ("ps", n % 2)
                        mm(p[:, 0:BLK], proj_pairs(wx, f4 * 128, 128, n * BLK, BLK), (("wb", slx), ("HT", n)), (kp,))
                        if n == 0:
                            S.dve(lambda e, rb=rb: e.memset(rb[:, 0:3], 0.0), (), (krb,))
                        else:
                            pr = raw[(n - 1) % 2]
                            S.pool(lambda e, rb=rb, pr=pr: e.tensor_copy(out=rb[:, 0:3], in_=pr[:, BLK:BLK + 3]), (("raw", (n - 1) % 2),), (krb,))
                        S.act(lambda e, rb=rb, p=p: e.activation(out=rb[:, 3:3 + BLK], in_=p[:, 0:BLK], func=AF.Copy), (kp,), (krb,))
                        cw = lambda k, fc=fc: pbl("convw", fc * 4 + k)
                        S.dve(lambda e, rb=rb, fc=fc, cw=cw: e.tensor_scalar(out=acc, in0=rb[:, 3:3 + BLK], scalar1=cw(3), scalar2=pbl("convb", fc), op0=ALU.mult, op1=ALU.add),
                              (krb,) + LR, ("acc",))
                        for k in (2, 1, 0):
                            S.dve(lambda e, rb=rb, k=k, cw=cw: e.scalar_tensor_tensor(out=acc, in0=rb[:, k:k + BLK], scalar=cw(k), in1=acc, op0=ALU.mult, op1=ALU.add),
                                  (krb, "acc") + LR, ("acc",))
                        if fc < 4:
                            dst, kd = xfm, "xfm"
                        elif fc < 6:
                            dst, kd = Bfm[:, fc - 4, n * BLK:(n + 1) * BLK], ("Bfm", n)
                        else:
                            dst, kd = Cfm[:, fc - 6, n * BLK:(n + 1) * BLK], ("Cfm", n)
                        S.act(lambda e, dst=dst: e.activation(out=dst, in_=acc, func=AF.Silu), ("acc",), (kd,))
                        if fc < 4:
                            pt = PSB[:, 0:BLK]
                            def fn(e, pt=pt):
                                ins = None
                                for tt in range(TPB):
                                    ins = e.transpose(out=pt[:, tt * 128:(tt + 1) * 128], in_=xfm[:, tt * 128:(tt + 1) * 128], identity=IDB)
                                return ins
                            S.pe(fn, ("xfm",) + CONST_R, (("psb", 0),))
                            S.dve(lambda e, pt=pt, n=n, fc=fc: e.tensor_copy(out=Xtok[:, n * TPB:(n + 1) * TPB, fc * 128:(fc + 1) * 128],
                                                                            in_=pt.rearrange("p (a b) -> p a b", a=TPB)),
                                  (("psb", 0),), (("Xtok", n),))
            sets = [dict(zs=zs, sm=sm, Xdt=Xdt, XdD=XdD, Btok=Btok, CBm=CBm, ydiag=ydiag, yc=yc, junk=junk, ybt=ybt, Dm=Dm, E=E, aTri=aTri, MT=MT)]
            sets.append(dict(zs=AR.f32(512), sm=AR.f32(64), Xdt=AR.b16(512), XdD=AR.b16(512), Btok=AR.b16(256), CBm=AR.f32(128),
                             ydiag=AR.f32(512), yc=AR.f32(512), junk=AR.f32(512), ybt=AR.b16(512), Dm=AR.f32(128), E=AR.f32(128),
                             aTri=[AR.f32(128) for _ in range(2)], MT=[AR.b16(128) for _ in range(2)]))

            def ssd_chunk(c):
                par = (c % 2) if DB_SSD else 0
                T = sets[par]
                zs, sm, Xdt, XdD, Btok, CBm, ydiag, yc, junk, ybt, Dm, E, aTri, MT = (T[k_] for k_ in
                    ("zs", "sm", "Xdt", "XdD", "Btok", "CBm", "ydiag", "yc", "junk", "ybt", "Dm", "E", "aTri", "MT"))
                dt_ = sm[:, 0:8]; a_ = sm[:, 8:16]; cs_ = sm[:, 16:24]; ecs = sm[:, 24:32]; etot = sm[:, 32:40]
                dec = sm[:, 40:48]; ssq = sm[:, 48:50]; tmp8 = sm[:, 56:64]
                n = c // TPB
                tok = slice(c * 128, (c + 1) * 128)
                mm(PS[0][:, 0:512], tok_pairs(wz, 0, 512, c), (("wb", slz), ("HT", n)), (("ps", 0),))
                S.act(lambda e: e.activation(out=zs, in_=PS[0][:, 0:512], func=AF.Silu), (("ps", 0),), (("zs", par),))
                mm(PS[1][:, 0:8], tok_pairs(wdt, 0, 8, c), ("wdt", ("HT", n)), (("ps", 1),))
                S.dve(lambda e: e.tensor_tensor(out=tmp8, in0=PS[1][:, 0:8], in1=pbl("dtb"), op=ALU.add), (("ps", 1),) + LR, (("tmp8", par),))
                S.act(lambda e: e.activation(out=tmp8, in_=tmp8, func=AF.Exp), (("tmp8", par),), (("tmp8", par),))
                S.act(lambda e: e.activation(out=dt_, in_=tmp8, func=AF.Ln, bias=D_ONE, scale=1.0), (("tmp8", par),) + LR, (("dt", par),))
                S.dve(lambda e: e.tensor_tensor(out=a_, in0=dt_, in1=D_NEGA, op=ALU.mult), (("dt", par),) + LR, (("a", par),))
                mm(PS[1][:, 8:16], [(MASKUF, a_)], (("a", par),) + CONST_R, (("ps", 1),))
                mm(PS[1][:, 16:24], [(ONESF, a_)], (("a", par),) + CONST_R, (("ps", 1),))
                S.act(lambda e: e.activation(out=cs_, in_=PS[1][:, 8:16], func=AF.Copy), (("ps", 1),), (("cs", par),))
                S.act(lambda e: e.activation(out=ecs, in_=PS[1][:, 8:16], func=AF.Exp), (("ps", 1),), (("ecs", par),))
                S.act(lambda e: e.activation(out=etot, in_=PS[1][:, 16:24], func=AF.Exp), (("ps", 1),), (("etot", par),))
                S.dve(lambda e: e.tensor_tensor(out=dec, in0=PS[1][:, 16:24], in1=cs_, op=ALU.subtract), (("ps", 1), ("cs", par)), (("dec", par),))
                S.act(lambda e: e.activation(out=dec, in_=dec, func=AF.Exp), (("dec", par),), (("dec", par),))
                for h in range(8):
                    hs = slice(h * 64, (h + 1) * 64)
                    S.dve(lambda e, h=h, hs=hs, c=c: e.tensor_scalar(out=Xdt[:, hs], in0=Xtok[:, c, hs], scalar1=dt_[:, h:h + 1], scalar2=None, op0=ALU.mult),
                          (("Xtok", n), ("dt", par)), (("Xdt", par),))
                    S.pool(lambda e, h=h, hs=hs: e.tensor_scalar(out=XdD[:, hs], in0=Xdt[:, hs], scalar1=dec[:, h:h + 1], scalar2=None, op0=ALU.mult),
                           (("Xdt", par), ("dec", par)), (("XdD", par),))
                def fnb(e, tok=tok):
                    ins = None
                    for g in range(2):
                        ins = e.transpose(out=PSB[:, 512 + g * 128:512 + (g + 1) * 128], in_=Bfm[:, g, tok], identity=IDB)
                    return ins
                S.pe(fnb, (("Bfm", n),) + CONST_R, (("psb", 0),))
                S.act(lambda e: e.activation(out=Btok, in_=PSB[:, 512:768], func=AF.Copy), (("psb", 0),), (("Btok", par),))
                for gr in range(2):
                    mm(PS[2][:, 0:128], [(Bfm[:, gr, tok], Cfm[:, gr, tok])], (("Bfm", n), ("Cfm", n)), (("ps", 2),))
                    S.dve(lambda e: e.tensor_tensor(out=CBm, in0=PS[2][:, 0:128], in1=MASKUF, op=ALU.mult), (("ps", 2),) + CONST_R, (("CBm", par),))
                    for r_ in range(4):
                        h = gr * 4 + r_
                        hs = slice(h * 64, (h + 1) * 64)
                        at = aTri[h % 2]; kat = ("aTri", h % 2, par)
                        S.dve(lambda e, at=at, h=h: e.tensor_scalar(out=at, in0=MASKUF, scalar1=a_[:, h:h + 1], scalar2=None, op0=ALU.mult),
                              (("a", par),) + CONST_R, (kat,))
                        mm(PS[3][:, 0:128], [(ONESF, at), (at, D_NEGONES)], (kat,) + CONST_R, (("ps", 3),))
                        S.dve(lambda e: e.tensor_scalar(out=Dm, in0=PS[3][:, 0:128], scalar1=0.0, scalar2=None, op0=ALU.min), (("ps", 3),), (("Dm", par),))
                        S.act(lambda e: e.activation(out=E, in_=Dm, func=AF.Exp), (("Dm", par),), (("E", par),))
                        mt = MT[h % 2]; kmt = ("MT", h % 2, par)
                        S.dve(lambda e, mt=mt: e.tensor_tensor(out=mt, in0=E, in1=CBm, op=ALU.mult), (("E", par), ("CBm", par)), (kmt,))
                        mm(PS[4][:, hs], [(mt, Xdt[:, hs])], (kmt, ("Xdt", par)), (("ps", 4),))
                    gs_ = slice(gr * 256, (gr + 1) * 256)
                    mm(PS[5][:, gs_], [(Cfm[:, gr, tok], Sbf[:, gr, :])], (("Cfm", n), "Sbf"), (("ps", 5),))
                    mm(PS[6][:, gs_], [(Btok[:, gr * 128:(gr + 1) * 128], XdD[:, gs_])], (("Btok", par), ("XdD", par)), (("ps", 6),))
                for h in range(8):
                    gr, r_ = h // 4, h % 4
                    S.dve(lambda e, h=h, gr=gr, r_=r_: e.scalar_tensor_tensor(out=Sst[:, gr, r_ * 64:(r_ + 1) * 64], in0=Sst[:, gr, r_ * 64:(r_ + 1) * 64],
                                                                             scalar=etot[:, h:h + 1], in1=PS[6][:, h * 64:(h + 1) * 64], op0=ALU.mult, op1=ALU.add),
                          ("Sst", ("etot", par), ("ps", 6)), ("Sst",))
                S.pool(lambda e: e.tensor_copy(out=Sbf, in_=Sst), ("Sst",), ("Sbf",))
                S.act(lambda e: e.activation(out=ydiag, in_=PS[4][:, 0:512], func=AF.Copy), (("ps", 4),), (("ydiag", par),))
                for h in range(8):
                    hs = slice(h * 64, (h + 1) * 64)
                    S.dve(lambda e, h=h, hs=hs: e.scalar_tensor_tensor(out=yc[:, hs], in0=PS[5][:, hs], scalar=ecs[:, h:h + 1], in1=ydiag[:, hs], op0=ALU.mult, op1=ALU.add),
                          (("ps", 5), ("ecs", par), ("ydiag", par)), (("yc", par),))
                S.pool(lambda e, c=c: e.tensor_tensor(out=ydiag, in0=Xtok[:, c, :], in1=pbl("mbd"), op=ALU.mult), (("Xtok", n), ("yc", par)) + LR, (("ydiag", par),))
                S.dve(lambda e: e.tensor_tensor(out=yc, in0=yc, in1=ydiag, op=ALU.add), (("yc", par), ("ydiag", par)), (("yc", par),))
                S.dve(lambda e: e.tensor_tensor(out=yc, in0=yc, in1=zs, op=ALU.mult), (("yc", par), ("zs", par)), (("yc", par),))
                S.act(lambda e: e.activation(out=junk, in_=yc, func=AF.Square), (("yc", par),), (("junk", par),))
                S.dve(lambda e: e.tensor_reduce(out=ssq, in_=junk.rearrange("p (a b) -> p a b", a=2), axis=AX.X, op=ALU.add), (("junk", par),), (("ssq", par),))
                rsqrt_act(ssq, ssq, 1.0 / 256, (("ssq", par),) + LR, (("ssq", par),))
                for g2 in range(2):
                    gs_ = slice(g2 * 256, (g2 + 1) * 256)
                    S.dve(lambda e, g2=g2, gs_=gs_: e.scalar_tensor_tensor(out=ybt[:, gs_], in0=yc[:, gs_], scalar=ssq[:, g2:g2 + 1], in1=pbl("mbnorm")[:, gs_], op0=ALU.mult, op1=ALU.mult),
                          (("yc", par), ("ssq", par)) + LR, (("ybt", par),))
                def fnt(e):
                    ins = None
                    for k in range(4):
                        ins = e.transpose(out=PSB[:, k * 128:(k + 1) * 128], in_=ybt[:, k * 128:(k + 1) * 128], identity=IDB)
                    return ins
                S.pe(fnt, (("ybt", par),) + CONST_R, (("psb", 0),))
                S.act(lambda e, tok=tok: e.activation(out=YB[:, :, tok], in_=PSB[:, 0:512].rearrange("p (a b) -> p a b", a=4), func=AF.Copy),
                      (("psb", 0),), (("YB", n),))


            for c in range(NT):
                ssd_chunk(c)

        def branch_s5(s, l):
            AR.reset()
            cj = AR.f32(16, 128); sj = AR.f32(16, 128)
            Bre = AR.b16(16, 128); Bim = AR.b16(16, 128); Cre = AR.b16(16, 128); Cim = AR.b16(16, 128)
            smp = AR.f32(192)
            r_ = smp[:, 0:16]; th = smp[:, 16:32]; rc = smp[:, 32:48]; rsn = smp[:, 48:64]; nrs = smp[:, 64:80]
            tA = smp[:, 80:96]; tB = smp[:, 96:112]; tC = smp[:, 112:128]; tD = smp[:, 128:144]; tE = smp[:, 144:160]; tF = smp[:, 160:176]
            stt = AR.f32(16, 2)
            mark = AR.off
            vtab = AR.f32(16, 128); ttab = AR.f32(16, 128)
            S.act(lambda e: e.activation(out=tA, in_=pbl("lstep"), func=AF.Exp), LR, ("s5a",))
            S.dve(lambda e: e.tensor_tensor(out=tB, in0=pbl("lamre"), in1=tA, op=ALU.mult), ("s5a",) + LR, ("s5b",))
            S.act(lambda e: e.activation(out=r_, in_=tB, func=AF.Exp), ("s5b",), ("s5r",))
            S.dve(lambda e: e.tensor_tensor(out=th, in0=pbl("lamim"), in1=tA, op=ALU.mult), ("s5a",) + LR, ("s5th",))
            for mt in range(16):
                S.dve(lambda e, mt=mt: e.tensor_scalar(out=vtab[:, mt, :], in0=cf("jrow"), scalar1=th[:, mt:mt + 1], scalar2=float(1.0 / TWO_PI), op0=ALU.mult, op1=ALU.mult),
                      ("s5th",) + CONST_R, ("s5v",))
            sincos(vtab, ttab, sj, cj, (), "s5v", "s5t", "s5sj", "s5cj")
            S.dve(lambda e: e.tensor_scalar(out=tC, in0=th, scalar1=float(128.0 / TWO_PI), scalar2=None, op0=ALU.mult), ("s5th",), ("s5c",))
            sincos(tC, tD, tE, tF, (), "s5c", "s5d", "s5s128", "s5c128")
            S.dve(lambda e: e.tensor_tensor(out=rc, in0=r_, in1=tF, op=ALU.mult), ("s5r", "s5c128"), ("s5rc",))
            S.dve(lambda e: e.tensor_tensor(out=rsn, in0=r_, in1=tE, op=ALU.mult), ("s5r", "s5s128"), ("s5rs",))
            S.dve(lambda e: e.tensor_scalar(out=nrs, in0=rsn, scalar1=-1.0, scalar2=None, op0=ALU.mult), ("s5rs",), ("s5nrs",))
            MC = ("s5r", "s5rc", "s5rs", "s5nrs", "s5sj", "s5cj")
            S.dma("pool", lambda e: e.dma_start(out=Cre, in_=ds5C[l, 0]), w=("s5Cre",))
            S.dma("pool", lambda e: e.dma_start(out=Cim, in_=ds5C[l, 1]), w=("s5Cim",))
            AR.off = mark + 2 * 16 * 128
            W = 256
            names = ["lr", "li", "ls", "v", "t", "sn", "cs", "abr", "abi", "den", "fr", "fi", "braw_r", "braw_i", "u1", "u2"]
            Q = {nm: AR.f32(W) for nm in names}
            for q in range(2048 // W):
                cs_ = slice(q * W, (q + 1) * W)
                k = "s5q"
                for i, nm in enumerate(["lr", "li", "ls"]):
                    S.dma("sp", lambda e, nm=nm, i=i, cs_=cs_: e.dma_start(out=Q[nm], in_=ds5rep[l, i, :, cs_]), w=(("s5in", nm),))
                mts = slice(q * (W // 128), (q + 1) * (W // 128))
                S.dma("sp", lambda e, mts=mts: e.dma_start(out=Q["braw_r"].rearrange("p (a b) -> p a b", b=128), in_=ds5B[l, 0, :, mts, :]), w=(("s5in", "br"),))
                S.dma("sp", lambda e, mts=mts: e.dma_start(out=Q["braw_i"].rearrange("p (a b) -> p a b", b=128), in_=ds5B[l, 1, :, mts, :]), w=(("s5in", "bi"),))
                S.act(lambda e: e.activation(out=Q["ls"], in_=Q["ls"], func=AF.Exp), (("s5in", "ls"),), (("s5in", "ls"),))
                S.dve(lambda e: e.tensor_tensor(out=Q["u1"], in0=Q["lr"], in1=Q["ls"], op=ALU.mult), (("s5in", "lr"), ("s5in", "ls")), ("q_u1",))
                S.act(lambda e: e.activation(out=Q["u1"], in_=Q["u1"], func=AF.Exp), ("q_u1",), ("q_u1",))
                S.dve(lambda e: e.tensor_tensor(out=Q["v"], in0=Q["li"], in1=Q["ls"], op=ALU.mult), (("s5in", "li"), ("s5in", "ls")), ("q_v",))
                S.dve(lambda e: e.tensor_scalar(out=Q["v"], in0=Q["v"], scalar1=float(1.0 / TWO_PI), scalar2=None, op0=ALU.mult), ("q_v",), ("q_v",))
                sincos(Q["v"], Q["t"], Q["sn"], Q["cs"], (), "q_v", "q_t", "q_sn", "q_cs")
                S.dve(lambda e: e.tensor_tensor(out=Q["abr"], in0=Q["u1"], in1=Q["cs"], op=ALU.mult), ("q_u1", "q_cs"), ("q_abr",))
                S.dve(lambda e: e.tensor_scalar(out=Q["abr"], in0=Q["abr"], scalar1=-1.0, scalar2=None, op0=ALU.add), ("q_abr",), ("q_abr",))
                S.dve(lambda e: e.tensor_tensor(out=Q["abi"], in0=Q["u1"], in1=Q["sn"], op=ALU.mult), ("q_u1", "q_sn"), ("q_abi",))
                S.dve(lambda e: e.tensor_tensor(out=Q["den"], in0=Q["lr"], in1=Q["lr"], op=ALU.mult), (("s5in", "lr"),), ("q_den",))
                S.dve(lambda e: e.tensor_tensor(out=Q["u2"], in0=Q["li"], in1=Q["li"], op=ALU.mult), (("s5in", "li"),), ("q_u2",))
                S.dve(lambda e: e.tensor_tensor(out=Q["den"], in0=Q["den"], in1=Q["u2"], op=ALU.add), ("q_den", "q_u2"), ("q_den",))
                S.dve(lambda e: e.reciprocal(out=Q["den"], in_=Q["den"]), ("q_den",), ("q_den",))
                S.dve(lambda e: e.tensor_tensor(out=Q["fr"], in0=Q["abr"], in1=Q["lr"], op=ALU.mult), ("q_abr", ("s5in", "lr")), ("q_fr",))
                S.dve(lambda e: e.tensor_tensor(out=Q["u2"], in0=Q["abi"], in1=Q["li"], op=ALU.mult), ("q_abi", ("s5in", "li"), "q_den"), ("q_u2",))
                S.dve(lambda e: e.tensor_tensor(out=Q["fr"], in0=Q["fr"], in1=Q["u2"], op=ALU.add), ("q_fr", "q_u2"), ("q_fr",))
                S.dve(lambda e: e.tensor_tensor(out=Q["fr"], in0=Q["fr"], in1=Q["den"], op=ALU.mult), ("q_fr", "q_den"), ("q_fr",))
                S.dve(lambda e: e.tensor_tensor(out=Q["fi"], in0=Q["abi"], in1=Q["lr"], op=ALU.mult), ("q_abi", ("s5in", "lr")), ("q_fi",))
                S.dve(lambda e: e.tensor_tensor(out=Q["u2"], in0=Q["abr"], in1=Q["li"], op=ALU.mult), ("q_abr", ("s5in", "li"), "q_fr"), ("q_u2",))
                S.dve(lambda e: e.tensor_tensor(out=Q["fi"], in0=Q["fi"], in1=Q["u2"], op=ALU.subtract), ("q_fi", "q_u2"), ("q_fi",))
                S.dve(lambda e: e.tensor_tensor(out=Q["fi"], in0=Q["fi"], in1=Q["den"], op=ALU.mult), ("q_fi", "q_den"), ("q_fi",))
                brv = Bre.rearrange("p a b -> p (a b)")[:, cs_]; biv = Bim.rearrange("p a b -> p (a b)")[:, cs_]
                S.dve(lambda e: e.tensor_tensor(out=Q["u1"], in0=Q["fr"], in1=Q["braw_r"], op=ALU.mult), ("q_fr", ("s5in", "br"), "q_abr", "q_abi"), ("q_u1",))
                S.dve(lambda e: e.tensor_tensor(out=Q["u2"], in0=Q["fi"], in1=Q["braw_i"], op=ALU.mult), ("q_fi", ("s5in", "bi")), ("q_u2",))
                S.dve(lambda e, brv=brv: e.tensor_tensor(out=brv, in0=Q["u1"], in1=Q["u2"], op=ALU.subtract), ("q_u1", "q_u2"), ("s5Bre",))
                S.dve(lambda e: e.tensor_tensor(out=Q["u1"], in0=Q["fr"], in1=Q["braw_i"], op=ALU.mult), ("q_fr", ("s5in", "bi"), "s5Bre"), ("q_u1",))
                S.dve(lambda e: e.tensor_tensor(out=Q["u2"], in0=Q["fi"], in1=Q["braw_r"], op=ALU.mult), ("q_fi", ("s5in", "br"), "s5Bre"), ("q_u2",))
                S.dve(lambda e, biv=biv: e.tensor_tensor(out=biv, in0=Q["u1"], in1=Q["u2"], op=ALU.add), ("q_u1", "q_u2"), ("s5Bim",))
            S.barrier()
            AR.off = mark
            ufm = AR.b16(4, BLK); ygb = AR.b16(4, BLK)
            hre_ = [AR.b16(BLK) for _ in range(2)]; nhim_ = [AR.b16(BLK) for _ in range(2)]
            t1_ = [AR.f32(BLK)] * 2; t2_ = [AR.f32(BLK)] * 2; wre_ = [AR.f32(BLK) for _ in range(2)]; wim_ = [AR.f32(BLK) for _ in range(2)]
            gre_ = [AR.f32(BLK) for _ in range(2)]; gim_ = [AR.f32(BLK) for _ in range(2)]
            yp = AR.f32(BLK); x2 = AR.f32(BLK); sgl = AR.f32(BLK); gsv = AR.f32(BLK)
            slu = wslot(); wu = wb_view(slu, 8, 512); load_w(slu, win_view(l, OFF["s5_u"], 512), wu)
            slg = wslot(); wg = wb_view(slg, 8, 512); load_w(slg, win_view(l, OFF["s5_gate"], 512), wg)
            slw = wslot(); wglu = wb_view(slw, 4, 512); load_w(slw, dwglu[l].rearrange("(kc p) c -> p kc c", p=128), wglu)
            GK = 1.5957691216057308
            for n in range(NB):
                for c4 in range(4):
                    p = PS[c4 % 2]; kp = ("ps", c4 % 2)
                    mm(p[:, 0:BLK], proj_pairs(wu, c4 * 128, 128, n * BLK, BLK), (("wb", slu), ("HT", n)), (kp,))
                    S.act(lambda e, c4=c4, p=p: e.activation(out=ufm[:, c4, :], in_=p[:, 0:BLK], func=AF.Copy), (kp,), (("ufm", c4),))
                for mt in range(16):
                    uc = mt // 4
                    pb2 = (mt % 2) if DB_S5 else 0
                    hre, nhim, t1, t2, wre, wim, gre, gim = hre_[pb2], nhim_[pb2], t1_[pb2], t2_[pb2], wre_[pb2], wim_[pb2], gre_[pb2], gim_[pb2]
                    K = lambda nm, pb2=pb2: (nm, 0 if nm in ('t1', 't2') else pb2)
                    cjb = cj[:, mt:mt + 1, :].broadcast_to([128, TPB, 128])
                    sjb = sj[:, mt:mt + 1, :].broadcast_to([128, TPB, 128])
                    v3 = lambda ap: ap.rearrange("p (a b) -> p a b", a=TPB)
                    pr = PS[2 + (mt % 2) * 2]; pi = PS[3 + (mt % 2) * 2]
                    kpr = ("ps", 2 + (mt % 2) * 2); kpi = ("ps", 3 + (mt % 2) * 2)
                    mm(pr[:, 0:BLK], [(Bre[:, mt, :], ufm[:, uc, :])], ("s5Bre", ("ufm", uc)), (kpr,))
                    mm(pi[:, 0:BLK], [(Bim[:, mt, :], ufm[:, uc, :])], ("s5Bim", ("ufm", uc)), (kpi,))
                    S.dve(lambda e, pr=pr, cjb=cjb, t1=t1: e.tensor_tensor(out=v3(t1), in0=v3(pr[:, 0:BLK]), in1=cjb, op=ALU.mult), (kpr,) + MC, (K("t1"),))
                    S.dve(lambda e, pi=pi, sjb=sjb, t2=t2: e.tensor_tensor(out=v3(t2), in0=v3(pi[:, 0:BLK]), in1=sjb, op=ALU.mult), (kpi,) + MC, (K("t2"),))
                    S.pool(lambda e, wre=wre, t1=t1, t2=t2: e.tensor_tensor(out=wre, in0=t1, in1=t2, op=ALU.add), (K("t1"), K("t2")), (K("wre"),))
                    S.dve(lambda e, pi=pi, cjb=cjb, t1=t1: e.tensor_tensor(out=v3(t1), in0=v3(pi[:, 0:BLK]), in1=cjb, op=ALU.mult), (kpi, K("wre")) + MC, (K("t1"),))
                    S.dve(lambda e, pr=pr, sjb=sjb, t2=t2: e.tensor_tensor(out=v3(t2), in0=v3(pr[:, 0:BLK]), in1=sjb, op=ALU.mult), (kpr, K("wre")) + MC, (K("t2"),))
                    S.pool(lambda e, wim=wim, t1=t1, t2=t2: e.tensor_tensor(out=wim, in0=t1, in1=t2, op=ALU.subtract), (K("t1"), K("t2")), (K("wim"),))
                    for k in range(TPB):
                        ck = n * TPB + k
                        c0 = k * 128
                        if ck > 0:
                            if k == 0:
                                lre, lim = stt[:, mt, 0:1], stt[:, mt, 1:2]
                            else:
                                lre, lim = gre[:, c0 - 1:c0], gim[:, c0 - 1:c0]
                            S.dve(lambda e, lre=lre, c0=c0, mt=mt, wre=wre: e.scalar_tensor_tensor(out=wre[:, c0:c0 + 1], in0=lre, scalar=rc[:, mt:mt + 1], in1=wre[:, c0:c0 + 1], op0=ALU.mult, op1=ALU.add),
                                  (K("wre"), K("gre"), ("stt", mt)) + MC, (K("wre"),))
                            S.dve(lambda e, lim=lim, c0=c0, mt=mt, wre=wre: e.scalar_tensor_tensor(out=wre[:, c0:c0 + 1], in0=lim, scalar=nrs[:, mt:mt + 1], in1=wre[:, c0:c0 + 1], op0=ALU.mult, op1=ALU.add),
                                  (K("wre"), K("gim"), ("stt", mt)) + MC, (K("wre"),))
                            S.dve(lambda e, lre=lre, c0=c0, mt=mt, wim=wim: e.scalar_tensor_tensor(out=wim[:, c0:c0 + 1], in0=lre, scalar=rsn[:, mt:mt + 1], in1=wim[:, c0:c0 + 1], op0=ALU.mult, op1=ALU.add),
                                  (K("wim"), K("gre"), ("stt", mt)) + MC, (K("wim"),))
                            S.dve(lambda e, lim=lim, c0=c0, mt=mt, wim=wim: e.scalar_tensor_tensor(out=wim[:, c0:c0 + 1], in0=lim, scalar=rc[:, mt:mt + 1], in1=wim[:, c0:c0 + 1], op0=ALU.mult, op1=ALU.add),
                                  (K("wim"), K("gim"), ("stt", mt)) + MC, (K("wim"),))
                        rb = r_[:, mt:mt + 1].broadcast_to([128, 128])
                        S.dve(lambda e, c0=c0, rb=rb, gre=gre, wre=wre: e.tensor_tensor_scan(out=gre[:, c0:c0 + 128], data0=rb, data1=wre[:, c0:c0 + 128], initial=0.0, op0=ALU.mult, op1=ALU.add),
                              (K("wre"),) + MC, (K("gre"),))
                        S.dve(lambda e, c0=c0, rb=rb, gim=gim, wim=wim: e.tensor_tensor_scan(out=gim[:, c0:c0 + 128], data0=rb, data1=wim[:, c0:c0 + 128], initial=0.0, op0=ALU.mult, op1=ALU.add),
                              (K("wim"),) + MC, (K("gim"),))
                    S.act(lambda e, mt=mt, gre=gre: e.activation(out=stt[:, mt, 0:1], in_=gre[:, BLK - 1:BLK], func=AF.Copy), (K("gre"),), (("stt", mt),))
                    S.act(lambda e, mt=mt, gim=gim: e.activation(out=stt[:, mt, 1:2], in_=gim[:, BLK - 1:BLK], func=AF.Copy), (K("gim"),), (("stt", mt),))
                    S.dve(lambda e, cjb=cjb, t1=t1, gre=gre: e.tensor_tensor(out=v3(t1), in0=v3(gre), in1=cjb, op=ALU.mult), (K("gre"),) + MC, (K("t1"),))
                    S.pool(lambda e, sjb=sjb, t2=t2, gim=gim: e.tensor_tensor(out=v3(t2), in0=v3(gim), in1=sjb, op=ALU.mult), (K("gim"),) + MC, (K("t2"),))
                    S.dve(lambda e, hre=hre, t1=t1, t2=t2: e.tensor_tensor(out=hre, in0=t1, in1=t2, op=ALU.subtract), (K("t1"), K("t2")), (K("hre"),))
                    S.dve(lambda e, sjb=sjb, t1=t1, gre=gre: e.tensor_tensor(out=v3(t1), in0=v3(gre), in1=sjb, op=ALU.mult), (K("gre"), K("hre")) + MC, (K("t1"),))
                    S.pool(lambda e, cjb=cjb, t2=t2, gim=gim: e.tensor_tensor(out=v3(t2), in0=v3(gim), in1=cjb, op=ALU.mult), (K("gim"), K("hre")) + MC, (K("t2"),))
                    S.dve(lambda e, nhim=nhim, t1=t1, t2=t2: e.scalar_tensor_tensor(out=nhim, in0=t1, scalar=-1.0, in1=t2, op0=ALU.mult, op1=ALU.subtract), (K("t1"), K("t2")), (K("nhim"),))
                    py = PS[6]
                    def fny(e, mt=mt, py=py, hre=hre, nhim=nhim):
                        e.matmul(py[:, 0:BLK], lhsT=Cre[:, mt, :], rhs=hre, start=(mt % 4 == 0), stop=False)
                        return e.matmul(py[:, 0:BLK], lhsT=Cim[:, mt, :], rhs=nhim, start=False, stop=(mt % 4 == 3))
                    S.pe(fny, (K("hre"), K("nhim"), "s5Cre", "s5Cim"), (("ps", 6),))
                    if mt % 4 == 3:
                        c4 = mt // 4
                        S.dve(lambda e, c4=c4, py=py: e.scalar_tensor_tensor(out=yp, in0=ufm[:, c4, :], scalar=pbl("s5d", c4), in1=py[:, 0:BLK], op0=ALU.mult, op1=ALU.add),
                              (("ufm", c4), ("ps", 6)) + LR, ("yp",))
                        S.pool(lambda e: e.tensor_tensor(out=x2, in0=yp, in1=yp, op=ALU.mult), ("yp",), ("x2",))
                        S.dve(lambda e: e.tensor_scalar(out=x2, in0=x2, scalar1=float(GK * 0.044715), scalar2=float(GK), op0=ALU.mult, op1=ALU.add), ("x2",), ("x2",))
                        S.dve(lambda e: e.tensor_tensor(out=x2, in0=x2, in1=yp, op=ALU.mult), ("x2", "yp"), ("x2",))
                        S.act(lambda e: e.activation(out=x2, in_=x2, func=AF.Sigmoid), ("x2",), ("x2",))
                        S.dve(lambda e, c4=c4: e.tensor_tensor(out=ygb[:, c4, :], in0=yp, in1=x2, op=ALU.mult), ("yp", "x2"), (("ygb", c4),))
                for c4 in range(4):
                    pg = PS[c4 % 2]; kpg = ("ps", c4 % 2)
                    mm(pg[:, 0:BLK], [(wglu[:, k4, c4 * 128:(c4 + 1) * 128], ygb[:, k4, :]) for k4 in range(4)],
                       (("wb", slw),) + tuple(("ygb", k4) for k4 in range(4)), (kpg,))
                    S.act(lambda e, c4=c4, pg=pg: e.activation(out=sgl, in_=pg[:, 0:BLK], func=AF.Sigmoid, bias=pbl("s5bglu", c4), scale=1.0), (kpg,) + LR, ("sgl",))
                    pgt = PS[2 + c4 % 2]; kpgt = ("ps", 2 + c4 % 2)
                    mm(pgt[:, 0:BLK], proj_pairs(wg, c4 * 128, 128, n * BLK, BLK), (("wb", slg), ("HT", n)), (kpgt,))
                    S.act(lambda e, pgt=pgt: e.activation(out=gsv, in_=pgt[:, 0:BLK], func=AF.Silu), (kpgt,), ("gsv",))
                    S.dve(lambda e, c4=c4: e.tensor_tensor(out=sgl, in0=sgl, in1=ygb[:, c4, :], op=ALU.mult), ("sgl", ("ygb", c4)), ("sgl",))
                    S.dve(lambda e, c4=c4, n=n: e.tensor_tensor(out=YB[:, c4, n * BLK:(n + 1) * BLK], in0=sgl, in1=gsv, op=ALU.mult), ("sgl", "gsv"), (("YB", n),))

        BR = {0: branch_da, 1: branch_ssd, 2: branch_s5, 3: branch_mla}
        for s in range(NSEQ):
            for l in range(L):
                layer_setup(l)
                stage0(s, l)
                S.barrier()
                first = True
                for b in range(4):
                    if b not in branches:
                        continue
                    BR[b](s, l)
                    S.barrier()
                    merge(s, l, b, first)
                    S.barrier()
                    first = False
                if first:
                    S.dve(lambda e: e.memset(MG, 0.0), (), tuple(("MG", f, n) for f in range(8) for n in range(NB)))
                if debug and s == 0 and l == 0:
                    for nm, src_, dst_ in (("ht", HT, dbg_ht), ("yb", YB, dbg_yb), ("mg", MG, dbg_mg)):
                        for q in range(src_.shape[1]):
                            dtmp = ARN[:, 0:S_LEN]
                            S.dve(lambda e, src_=src_, q=q, dtmp=dtmp: e.tensor_copy(out=dtmp, in_=src_[:, q, :]), (), ("dtmp",))
                            S.dma("sp", lambda e, dst_=dst_, q=q, dtmp=dtmp: e.dma_start(out=dst_[:, q * S_LEN:(q + 1) * S_LEN], in_=dtmp), r=("dtmp",), w=())
                    S.barrier()
                outproj(s, l)
                S.barrier()
        S.emit()
    return nc


_PROG_CACHE = {}


def run_cores(inputs, S_LEN, NSEQ, DEPTH, n_cores, branches=(0, 1, 2, 3), debug=False):
    key = (S_LEN, NSEQ, DEPTH, tuple(branches), debug)
    if key not in _PROG_CACHE:
        _PROG_CACHE[key] = build_program(S_LEN, NSEQ, DEPTH, branches, debug=debug)
    nc = _PROG_CACHE[key]
    f32 = lambda a: np.ascontiguousarray(np.asarray(a, dtype=np.float32))
    pb, s5rep, s5B, s5C = host_params({k: np.asarray(v) for k, v in inputs.items()}, DEPTH)
    shared = {
        "w_in": f32(inputs["w_in"])[:DEPTH], "w_br": f32(inputs["w_br"])[:DEPTH], "w_out": f32(inputs["w_out"])[:DEPTH],
        "w_uq": f32(inputs["mla_w_uq"])[:DEPTH], "w_ukv": f32(inputs["mla_w_ukv"])[:DEPTH], "w_glu": f32(inputs["s5_w_glu"])[:DEPTH],
        "pblob": pb, "cblob": host_consts(), "s5rep": s5rep, "s5B": s5B, "s5C": s5C,
    }
    x = f32(inputs["x"])
    pos = np.ascontiguousarray(np.asarray(inputs["positions"], dtype=np.int32))
    in_maps = []
    for c in range(n_cores):
        m = dict(shared)
        m["x"] = np.ascontiguousarray(x[c * NSEQ:(c + 1) * NSEQ])
        m["pos"] = np.ascontiguousarray(pos[c * NSEQ:(c + 1) * NSEQ])
        in_maps.append(m)
    res = run_bass_kernel_spmd(nc, in_maps, core_ids=list(range(n_cores)))
    outs = [np.asarray(r["out"]).reshape(NSEQ, S_LEN, D_MODEL) for r in res.results]
    if debug:
        global DBG
        DBG = {k: np.asarray(res.results[0][k]) for k in ("dbg_ht", "dbg_yb", "dbg_mg")}
    return np.concatenate(outs, axis=0).astype(np.float32)


def kernel(**inputs):
    return run_cores(inputs, 2048, 2, 2, 8)
```

```python
import numpy as np
import concourse.bass as bass
import concourse.mybir as mybir
from concourse.alu_op_type import AluOpType as ALU
from concourse.bass_utils import run_bass_kernel_spmd

F32 = mybir.dt.float32
BF16 = mybir.dt.bfloat16
I32 = mybir.dt.int32
AF = mybir.ActivationFunctionType
AX = mybir.AxisListType


class Op:
    __slots__ = ("eng", "fn", "reads", "writes", "dma", "waits", "inc", "ticket", "idx", "sem", "cost", "seg")

    def __init__(self, eng, fn, reads, writes, dma, cost=None):
        self.eng, self.fn, self.reads, self.writes, self.dma = eng, fn, reads, writes, dma
        self.cost = cost
        self.seg = 0
        self.waits = []
        self.inc = False
        self.ticket = None
        self.sem = None


class Sched:
    ENGS = ("pe", "dve", "act", "pool", "sp")

    def __init__(self, nc, n_dma_sems=12):
        self.nc = nc
        self.ops = []
        self.n_dma_sems = n_dma_sems
        self._bar_ops = set()
        self._seg = 0
        self.reorder = True
        import os
        self.same_sync = True

    DEFCOST = {"pe": 0.4, "dve": 0.55, "act": 0.5, "pool": 0.9, "sp": 0.1}

    def add(self, eng, fn, reads=(), writes=(), dma=False, cost=None):
        if cost is None:
            cost = 0.15 if dma else self.DEFCOST[eng]
        op = Op(eng, fn, tuple(reads), tuple(writes), dma, cost)
        op.idx = len(self.ops)
        op.seg = self._seg
        self.ops.append(op)
        return op

    def pe(self, fn, r=(), w=(), cost=None):
        return self.add("pe", fn, r, w, cost=cost)

    def dve(self, fn, r=(), w=()):
        return self.add("dve", fn, r, w)

    def act(self, fn, r=(), w=()):
        return self.add("act", fn, r, w)

    def pool(self, fn, r=(), w=()):
        return self.add("pool", fn, r, w)

    def dma(self, q, fn, r=(), w=()):
        return self.add(q, fn, r, w, dma=True)

    def barrier(self):
        self._seg += 1
        for e in self.ENGS:
            self.add(e, lambda eng: eng.drain(), (), (("__barX__", e),))
        for e in self.ENGS:
            op = self.add(e, None, tuple(("__barX__", f) for f in self.ENGS), ())
            self._bar_ops.add(op.idx)
        self._seg += 1

    def finalize(self):
        last_writer = {}
        readers = {}
        cnt = {e: 0 for e in self.ENGS}
        dma_sem_total = {}
        dma_rr = {e: 0 for e in self.ENGS}
        waited = {}
        deps_of = []
        all_dma = []
        for op in self.ops:
            deps = set()
            for r in op.reads:
                j = last_writer.get(r)
                if j is not None:
                    deps.add(j)
            for w in op.writes:
                j = last_writer.get(w)
                if j is not None:
                    deps.add(j)
                for j in readers.get(w, ()):
                    deps.add(j)
            if op.idx in self._bar_ops:
                deps.update(all_dma)
            if op.dma:
                all_dma.append(op.idx)
            deps.discard(op.idx)
            deps = set(j for j in deps if self.ops[j].fn is not None)
            deps_of.append(deps)
            for r in op.reads:
                readers.setdefault(r, []).append(op.idx)
            for w in op.writes:
                last_writer[w] = op.idx
                readers[w] = []
        need_inc = [False] * len(self.ops)
        for op in self.ops:
            for j in deps_of[op.idx]:
                pj = self.ops[j]
                if pj.dma or pj.eng != op.eng or op.dma or (self.same_sync and pj.eng != 'pe'):
                    need_inc[j] = True
        for op in self.ops:
            if op.dma:
                need_inc[op.idx] = True
        self.order = self._list_schedule(deps_of) if self.reorder else list(range(len(self.ops)))
        for oi in self.order:
            op = self.ops[oi]
            e = op.eng
            if op.dma:
                k = dma_rr[e] % self.n_dma_sems
                dma_rr[e] += 1
                key = ("d", e, k)
                prev = dma_sem_total.get(key, 0)
                if prev > 0 and waited.get((e, key), 0) < prev:
                    op.waits.append((key, prev))
                    waited[(e, key)] = prev
                dma_sem_total[key] = prev + 16
                op.sem = key
                op.ticket = prev + 16
                op.inc = True
            elif need_inc[op.idx]:
                cnt[e] += 1
                op.sem = ("c", e)
                op.ticket = cnt[e]
                op.inc = True
            for j in sorted(deps_of[op.idx]):
                pj = self.ops[j]
                if (not pj.dma) and pj.eng == e and not op.dma and (e == 'pe' or not self.same_sync):
                    continue
                if (not pj.dma) and pj.eng == e and op.dma:
                    pass
                key, val = pj.sem, pj.ticket
                if waited.get((e, key), 0) >= val:
                    continue
                waited[(e, key)] = val
                op.waits.append((key, val))

    def _list_schedule(self, deps_of):
        import heapq
        ops = self.ops
        n = len(ops)
        order = []
        i = 0
        SYNC = 0.25
        DMA_LAT = 2.5
        while i < n:
            j = i
            seg = ops[i].seg
            while j < n and ops[j].seg == seg:
                j += 1
            idxs = range(i, j)
            if j - i <= 2 or any(ops[k].fn is None for k in idxs) :
                order.extend(idxs)
                i = j
                continue
            indeg = {}
            users = {}
            for k in idxs:
                d = [x for x in deps_of[k] if i <= x < j]
                indeg[k] = len(d)
                for x in d:
                    users.setdefault(x, []).append(k)
            blev = {}
            for k in reversed(idxs):
                m_ = 0.0
                for u in users.get(k, ()):
                    if blev[u] > m_:
                        m_ = blev[u]
                blev[k] = m_ + ops[k].cost + (DMA_LAT if ops[k].dma else 0.0) + SYNC
            finish = {}
            ready_t = {k: 0.0 for k in idxs}
            efree = {e: 0.0 for e in self.ENGS}
            pend = {e: [] for e in self.ENGS}
            avail = {e: [] for e in self.ENGS}
            for k in idxs:
                if indeg[k] == 0:
                    heapq.heappush(pend[ops[k].eng], (0.0, k))
            done = 0
            tot = j - i
            sched = []
            while done < tot:
                best = None
                for e in self.ENGS:
                    T = efree[e]
                    while pend[e] and pend[e][0][0] <= T:
                        k_ = heapq.heappop(pend[e])[1]
                        heapq.heappush(avail[e], (-blev[k_], k_))
                    if avail[e]:
                        cand = (T, avail[e][0][1], e, True)
                    elif pend[e]:
                        cand = (pend[e][0][0], pend[e][0][1], e, False)
                    else:
                        continue
                    if best is None or cand[:2] < best[:2]:
                        best = cand
                start, k, e, from_avail = best
                if from_avail:
                    heapq.heappop(avail[e])
                else:
                    heapq.heappop(pend[e])
                op = ops[k]
                efree[e] = start + op.cost
                fin = start + op.cost + (DMA_LAT if op.dma else 0.0)
                finish[k] = fin
                sched.append((start, k))
                done += 1
                for u in users.get(k, ()):
                    lat = 0.0 if (ops[u].eng == e and not op.dma and e == "pe") else SYNC
                    ready_t[u] = max(ready_t[u], fin + lat)
                    indeg[u] -= 1
                    if indeg[u] == 0:
                        heapq.heappush(pend[ops[u].eng], (ready_t[u], u))
            sched.sort()
            order.extend(k for _, k in sched)
            i = j
        assert len(order) == n and len(set(order)) == n
        return order

    def emit(self, out_waits=()):
        nc = self.nc
        self.finalize()
        keys = set()
        for op in self.ops:
            if op.sem is not None:
                keys.add(op.sem)
        keys = sorted(keys)
        sems = {}
        import contextlib
        with contextlib.ExitStack() as st:
            for i, k in enumerate(keys):
                sems[k] = st.enter_context(nc.semaphore("s_" + "_".join(str(x) for x in k)))
            block = st.enter_context(nc.Block())
            ops = self.ops

            order = self.order

            def run(eng_name, eng):
                last = None
                for oi in order:
                    op = ops[oi]
                    if op.eng != eng_name:
                        continue
                    for (k, v) in op.waits:
                        eng.wait_ge(sems[k], v)
                    if op.fn is None:
                        assert not op.inc
                        continue
                    ins = op.fn(eng)
                    if op.inc:
                        ins.then_inc(sems[op.sem], 16 if op.dma else 1)
                    if op.dma:
                        last = op
                done = {}
                for op in ops:
                    if op.eng == eng_name and op.dma:
                        done[op.sem] = max(done.get(op.sem, 0), op.ticket)
                for k, v in done.items():
                    eng.wait_ge(sems[k], v)

            @block.tensor
            def _(e):
                run("pe", e)

            @block.vector
            def _(e):
                run("dve", e)

            @block.scalar
            def _(e):
                run("act", e)

            @block.gpsimd
            def _(e):
                run("pool", e)

            @block.sync
            def _(e):
                run("sp", e)


D_MODEL = 1024
D_IN = 9672
EPS = 1e-6
ROPE_THETA = 500000.0
OFF = dict(da_q=0, da_k=512, da_v=1024, da_gate=1536, mb_z=2048, mb_xbc=2560, mb_dt=3584, s5_u=3592,
           s5_gate=4104, mla_cq=4616, mla_ckv=4872, mla_krope=5000, mla_gate=5064, gate=5576)
TWO_PI = float(2.0 * np.pi)
import os as _os
DB_S5 = _os.environ.get('DB_S5', '1') == '1'
DB_SSD = _os.environ.get('DB_SSD', '0') == '1'
DB_DA = _os.environ.get('DB_DA', '0') == '1'
MAGIC = 12582912.0

PB = {}
_o = 0
for _n, _w in [("ng", 8), ("daq", 1), ("dak", 1), ("convw", 32), ("convb", 8), ("s5d", 4), ("s5bglu", 4),
               ("mqa", 2), ("mkva", 1), ("mqn_n", 1), ("mqn_r", 1), ("mkn_n", 1), ("mkn_r", 1),
               ("lamre", 16), ("lamim", 16), ("lstep", 16),
               ("subln", 128), ("lq1", 64), ("lk1", 64), ("lq2", 64), ("lk2", 64),
               ("dtb", 8), ("alog", 8), ("mbd", 512), ("mbnorm", 512)]:
    PB[_n] = (_o, _w)
    _o += _w
NP_COLS = _o
CB = {}
_o = 0
for _n, _w in [("ident", 128), ("ones", 128), ("blk64", 128), ("maskU", 128), ("ptda", 128), ("ptmla", 128),
               ("jrow", 128), ("invf_da", 1), ("invf_mla", 1)]:
    CB[_n] = (_o, _w)
    _o += _w
NC_COLS = _o


def host_consts():
    c = np.zeros((128, NC_COLS), np.float32)
    def put(n, a):
        o, w = CB[n]
        c[:a.shape[0], o:o + w] = a
    i = np.arange(128)
    put("ident", np.eye(128, dtype=np.float32))
    put("ones", np.ones((128, 128), np.float32))
    put("blk64", (i[:, None] // 64 == i[None, :] // 64).astype(np.float32))
    put("maskU", (i[:, None] <= i[None, :]).astype(np.float32))
    P = np.zeros((128, 128), np.float32)
    for b in range(2):
        for d in range(8):
            P[b * 64 + d, b * 64 + d + 8] = -1.0
            P[b * 64 + d + 8, b * 64 + d] = 1.0
    put("ptda", P.T.copy())
    P = np.zeros((128, 128), np.float32)
    for d in range(32):
        P[d, d + 32] = -1.0
        P[d + 32, d] = 1.0
    put("ptmla", P.T.copy())
    put("jrow", np.broadcast_to(np.arange(128, dtype=np.float32)[None, :], (128, 128)))
    f_da = (1.0 / (ROPE_THETA ** (np.arange(0, 16, 2, dtype=np.float32) / np.float32(16)))).astype(np.float32)
    f_mla = (1.0 / (ROPE_THETA ** (np.arange(0, 64, 2, dtype=np.float32) / np.float32(64)))).astype(np.float32)
    v = np.zeros((128, 1), np.float32)
    for p in range(128):
        d = p % 64
        if d < 16:
            v[p, 0] = f_da[d % 8]
    put("invf_da", v)
    v = np.zeros((128, 1), np.float32)
    for p in range(64):
        v[p, 0] = f_mla[p % 32]
    put("invf_mla", v)
    return c


def host_params(inp, L):
    pb = np.zeros((L, 128, NP_COLS), np.float32)
    s5rep = np.zeros((L, 3, 128, 2048), np.float32)
    s5B = np.zeros((L, 2, 128, 16, 128), np.float32)
    s5C = np.zeros((L, 2, 128, 16, 128), np.float32)
    p = np.arange(128)
    for l in range(L):
        def put(n, a):
            o, w = PB[n]
            a = np.asarray(a, np.float32)
            if a.ndim == 1:
                a = a[:, None]
            pb[l, :a.shape[0], o:o + w] = a
        def rep(n, vec):
            o, w = PB[n]
            pb[l, :, o:o + w] = np.asarray(vec, np.float32).reshape(1, w)
        put("ng", inp["norm_g"][l].reshape(8, 128).T)
        put("daq", inp["da_q_norm"][l][p % 64])
        put("dak", inp["da_k_norm"][l][p % 64])
        cw = inp["mb_conv_w"][l]
        put("convw", cw.reshape(4, 8, 128).transpose(2, 1, 0).reshape(128, 32))
        put("convb", inp["mb_conv_b"][l].reshape(8, 128).T)
        put("s5d", inp["s5_d"][l].reshape(4, 128).T)
        put("s5bglu", inp["s5_b_glu"][l].reshape(4, 128).T)
        put("mqa", inp["mla_q_a_norm"][l].reshape(2, 128).T)
        put("mkva", inp["mla_kv_a_norm"][l])
        put("mqn_n", inp["mla_q_norm"][l][:128])
        put("mqn_r", inp["mla_q_norm"][l][128:192])
        put("mkn_n", inp["mla_k_norm"][l][:128])
        put("mkn_r", inp["mla_k_norm"][l][128:192])
        lre = inp["s5_lam_re"][l].reshape(16, 128).T
        lim = inp["s5_lam_im"][l].reshape(16, 128).T
        lst = np.repeat(inp["s5_log_step"][l], 64).reshape(16, 128).T
        put("lamre", lre); put("lamim", lim); put("lstep", lst)
        rep("subln", inp["da_subln"][l])
        rep("lq1", inp["da_lambda_q1"][l]); rep("lk1", inp["da_lambda_k1"][l])
        rep("lq2", inp["da_lambda_q2"][l]); rep("lk2", inp["da_lambda_k2"][l])
        rep("dtb", inp["mb_dt_bias"][l]); rep("alog", inp["mb_a_log"][l])
        rep("mbd", np.repeat(inp["mb_d"][l], 64)); rep("mbnorm", inp["mb_norm"][l])
        s5rep[l, 0] = inp["s5_lam_re"][l].reshape(1, 2048)
        s5rep[l, 1] = inp["s5_lam_im"][l].reshape(1, 2048)
        s5rep[l, 2] = np.repeat(inp["s5_log_step"][l], 64).reshape(1, 2048)
        for g in range(32):
            mt, j = g // 2, g % 2
            gi = g % 8
            s5B[l, 0, gi * 16:(gi + 1) * 16, mt, j * 64:(j + 1) * 64] = inp["s5_b_re"][l, g].T
            s5B[l, 1, gi * 16:(gi + 1) * 16, mt, j * 64:(j + 1) * 64] = inp["s5_b_im"][l, g].T
            s5C[l, 0, j * 64:(j + 1) * 64, mt, gi * 16:(gi + 1) * 16] = inp["s5_c_re"][l, g].T
            s5C[l, 1, j * 64:(j + 1) * 64, mt, gi * 16:(gi + 1) * 16] = inp["s5_c_im"][l, g].T
    return pb, s5rep, s5B, s5C


class _Shift:
    def __init__(self, ap):
        self.ap = ap

    def __getitem__(self, idx):
        p, f = idx
        return self.ap[:, f]


class Arena:
    def __init__(self, ap32, ncols):
        self.ap = ap32
        self.n = ncols
        self.off = 0

    def reset(self):
        self.off = 0

    def f32(self, *shape):
        n = int(np.prod(shape))
        assert self.off + n <= self.n, ("arena overflow", self.off, n, self.n)
        v = self.ap[:, self.off:self.off + n]
        self.off += n
        if len(shape) == 2:
            v = v.rearrange("p (a b) -> p a b", a=shape[0])
        return v

    def b16(self, *shape):
        n = int(np.prod(shape))
        n32 = (n + 1) // 2
        assert self.off + n32 <= self.n, ("arena overflow", self.off, n32, self.n)
        v = self.ap[:, self.off:self.off + n32].bitcast(BF16)
        if n32 * 2 != n:
            v = v[:, 0:n]
        self.off += n32
        if len(shape) == 2:
            v = v.rearrange("p (a b) -> p a b", a=shape[0])
        return v


def build_program(S_LEN=2048, NSEQ=2, DEPTH=2, branches=(0, 1, 2, 3), ARENA_COLS=18944, debug=False):
    import contextlib
    NT = S_LEN // 128
    BLK = min(512, S_LEN)
    NB = S_LEN // BLK
    TPB = BLK // 128
    L = DEPTH
    nc = bass.Bass("TRN2", target_bir_lowering=False)
    dx = nc.dram_tensor("x", [NSEQ, S_LEN, D_MODEL], F32, kind="ExternalInput").ap()
    dpos = nc.dram_tensor("pos", [NSEQ, S_LEN], I32, kind="ExternalInput").ap()
    dwin = nc.dram_tensor("w_in", [L, D_MODEL, D_IN], F32, kind="ExternalInput").ap()
    dwbr = nc.dram_tensor("w_br", [L, 4, 512, D_MODEL], F32, kind="ExternalInput").ap()
    dwout = nc.dram_tensor("w_out", [L, D_MODEL, D_MODEL], F32, kind="ExternalInput").ap()
    dwuq = nc.dram_tensor("w_uq", [L, 256, 768], F32, kind="ExternalInput").ap()
    dwukv = nc.dram_tensor("w_ukv", [L, 128, 1024], F32, kind="ExternalInput").ap()
    dwglu = nc.dram_tensor("w_glu", [L, 512, 512], F32, kind="ExternalInput").ap()
    dpb = nc.dram_tensor("pblob", [L, 128, NP_COLS], F32, kind="ExternalInput").ap()
    dcb = nc.dram_tensor("cblob", [128, NC_COLS], F32, kind="ExternalInput").ap()
    ds5rep = nc.dram_tensor("s5rep", [L, 3, 128, 2048], F32, kind="ExternalInput").ap()
    ds5B = nc.dram_tensor("s5B", [L, 2, 128, 16, 128], F32, kind="ExternalInput").ap()
    ds5C = nc.dram_tensor("s5C", [L, 2, 128, 16, 128], F32, kind="ExternalInput").ap()
    dout = nc.dram_tensor("out", [NSEQ, S_LEN, D_MODEL], F32, kind="ExternalOutput").ap()
    if debug:
        dbg_ht = nc.dram_tensor("dbg_ht", [128, 8 * S_LEN], F32, kind="ExternalOutput").ap()
        dbg_yb = nc.dram_tensor("dbg_yb", [128, 4 * S_LEN], F32, kind="ExternalOutput").ap()
        dbg_mg = nc.dram_tensor("dbg_mg", [128, 8 * S_LEN], F32, kind="ExternalOutput").ap()

    S = Sched(nc)
    st = contextlib.ExitStack()

    def sb(name, shape, dt=F32):
        return st.enter_context(nc.sbuf_tensor(name, shape, dt))[:]

    def psum(name, shape, dt=F32):
        return st.enter_context(nc.psum_tensor(name, shape, dt))[:]

    uid = [0]

    def U(prefix):
        uid[0] += 1
        return (prefix, uid[0])

    with st:
        CF = sb("CF", [128, NC_COLS])
        CBF = sb("CBF", [128, 6 * 128], BF16)
        PBL = sb("PBL", [128, NP_COLS])
        DRV = sb("DRV", [128, 256])
        HT = sb("HT", [128, 8, S_LEN], BF16)
        MG = sb("MG", [128, 8, S_LEN], BF16)
        YB = sb("YB", [128, 4, S_LEN], BF16)
        WB = [sb("WB%d" % i, [128, 4096], BF16) for i in range(3)]
        ARN = sb("ARN", [128, ARENA_COLS])
        MSG = [sb("MSG%d" % i, [128, 512], BF16) for i in range(2)]
        MTP = [sb("MTP%d" % i, [128, 512], BF16) for i in range(2)]
        AR = Arena(ARN, ARENA_COLS)
        PS = [psum("PS%d" % i, [128, 512]) for i in range(7)]
        PSB = psum("PSB", [128, 1024], BF16)

        def cf(n):
            o, w = CB[n]
            return CF[:, o:o + w]

        def pbl(n, a=None, b=None):
            o, w = PB[n]
            if a is None:
                return PBL[:, o:o + w]
            return PBL[:, o + a:o + (a + 1 if b is None else b)]

        IDB = CBF[:, 0:128]; ONESB = CBF[:, 128:256]; BLK64B = CBF[:, 256:384]
        MASKUB = CBF[:, 384:512]; PTDAB = CBF[:, 512:640]; PTMLAB = CBF[:, 640:768]
        ONESF = cf("ones"); MASKUF = cf("maskU")
        D_EPS = DRV[:, 0:1]; D_ONE = DRV[:, 1:2]; D_GQ = DRV[:, 2:3]; D_NEGLAM = DRV[:, 3:4]
        D_MQN_N = DRV[:, 4:5]; D_MQN_R = DRV[:, 5:6]; D_T1 = DRV[:, 6:7]; D_T2 = DRV[:, 7:8]
        D_NEGA = DRV[:, 8:16]; D_NEGONES = DRV[:, 16:144]

        wb_rr = [0]

        def wslot():
            i = wb_rr[0] % 3
            wb_rr[0] += 1
            return i

        def load_w(slot, src_ap, view):
            S.dma("pool", lambda e, o=view, i=src_ap: e.dma_start(out=o, in_=i), r=(), w=(("wb", slot),))

        def win_view(l, c0, ncols):
            return dwin[l, :, c0:c0 + ncols].rearrange("(kc p) c -> p kc c", p=128)

        def wb_view(slot, kc, ncols, off=0):
            return WB[slot][:, off:off + kc * ncols].rearrange("p (k c) -> p k c", k=kc)

        def mm(out_ap, pairs, r, w):
            def fn(e, out_ap=out_ap, pairs=pairs):
                ins = None
                n = len(pairs)
                for i, (a, b) in enumerate(pairs):
                    ins = e.matmul(out_ap, lhsT=a, rhs=b, start=(i == 0), stop=(i == n - 1))
                return ins
            cst = 0.06
            for (a_, b_) in pairs:
                ncol = int(np.prod(b_.shape[1:]))
                cst += max(ncol, 64) / 1400.0 * (4.0 if a_.dtype == F32 else 1.0)
            S.pe(fn, r, w, cost=cst)

        def proj_pairs(wv, c0, mw, tok0, ntok):
            return [(wv[:, kc, c0:c0 + mw], HT[:, kc, tok0:tok0 + ntok]) for kc in range(8)]

        def tok_pairs(wv, c0, ncols, tile):
            return [(HT[:, kc, tile * 128:(tile + 1) * 128], wv[:, kc, c0:c0 + ncols]) for kc in range(8)]

        def rsqrt_act(out_ap, in_ap, scale, r, w):
            S.act(lambda e: e.activation(out=out_ap, in_=in_ap, func=AF.Ln, scale=scale, bias=D_EPS[0:out_ap.shape[0], :]), r, w)
            S.act(lambda e: e.activation(out=out_ap, in_=out_ap, func=AF.Exp, scale=-0.5), w, w)

        def sincos(v_ap, tmp_ap, out_sin, out_cos, r, kv, kt, ks, kc_):
            S.dve(lambda e: e.tensor_scalar(out=tmp_ap, in0=v_ap, scalar1=MAGIC, scalar2=None, op0=ALU.add), r + (kv,), (kt,))
            S.dve(lambda e: e.tensor_scalar(out=tmp_ap, in0=tmp_ap, scalar1=-MAGIC, scalar2=None, op0=ALU.add), (kt,), (kt,))
            S.dve(lambda e: e.tensor_tensor(out=tmp_ap, in0=v_ap, in1=tmp_ap, op=ALU.subtract), (kv, kt), (kt,))
            S.act(lambda e: e.activation(out=out_sin, in_=tmp_ap, func=AF.Sin, scale=TWO_PI), (kt,), (ks,))
            S.dve(lambda e: e.tensor_scalar(out=tmp_ap, in0=v_ap, scalar1=0.25, scalar2=MAGIC, op0=ALU.add, op1=ALU.add), (kv, ks), (kt,))
            S.dve(lambda e: e.tensor_scalar(out=tmp_ap, in0=tmp_ap, scalar1=-MAGIC, scalar2=None, op0=ALU.add), (kt,), (kt,))
            S.dve(lambda e: e.scalar_tensor_tensor(out=tmp_ap, in0=v_ap, scalar=0.25, in1=tmp_ap, op0=ALU.add, op1=ALU.subtract), (kv, kt), (kt,))
            S.act(lambda e: e.activation(out=out_cos, in_=tmp_ap, func=AF.Sin, scale=TWO_PI), (kt,), (kc_,))

        S.dma("sp", lambda e: e.dma_start(out=CF, in_=dcb), w=("CF",))
        for i, n in enumerate(["ident", "ones", "blk64", "maskU", "ptda", "ptmla"]):
            S.dve(lambda e, i=i, n=n: e.tensor_copy(out=CBF[:, i * 128:(i + 1) * 128], in_=cf(n)), ("CF",), ("CBF",))
        S.dve(lambda e: e.memset(D_EPS, EPS), (), ("DRVc",))
        S.dve(lambda e: e.memset(D_ONE, 1.0), (), ("DRVc",))
        S.dve(lambda e: e.memset(D_ONE, 1.0), (), ("DRVc",))
        S.dve(lambda e: e.memset(D_NEGONES, -1.0), (), ("DRVc",))
        CONST_R = ("CF", "CBF", "DRVc")

        def layer_setup(l):
            lam_init = 0.8 - 0.6 * float(np.exp(-0.3 * l))
            S.dma("sp", lambda e: e.dma_start(out=PBL, in_=dpb[l]), w=("PBL",))
            kd = "DRV"
            S.dve(lambda e: e.tensor_scalar(out=D_GQ, in0=pbl("daq"), scalar1=0.125, scalar2=None, op0=ALU.mult), ("PBL",), (kd,))
            S.dve(lambda e: e.tensor_scalar(out=D_MQN_N, in0=pbl("mqn_n"), scalar1=float(192 ** -0.5), scalar2=None, op0=ALU.mult), ("PBL",), (kd,))
            S.dve(lambda e: e.tensor_scalar(out=D_MQN_R, in0=pbl("mqn_r"), scalar1=float(192 ** -0.5), scalar2=None, op0=ALU.mult), ("PBL",), (kd,))
            tmp = DRV[:, 144:208]
            S.dve(lambda e: e.tensor_tensor(out=tmp, in0=pbl("lq1"), in1=pbl("lk1"), op=ALU.mult), ("PBL",), (kd,))
            S.dve(lambda e: e.tensor_reduce(out=D_T1, in_=tmp, axis=AX.X, op=ALU.add), (kd,), (kd,))
            S.dve(lambda e: e.tensor_tensor(out=tmp, in0=pbl("lq2"), in1=pbl("lk2"), op=ALU.mult), ("PBL",), (kd,))
            S.dve(lambda e: e.tensor_reduce(out=D_T2, in_=tmp, axis=AX.X, op=ALU.add), (kd,), (kd,))
            S.act(lambda e: e.activation(out=D_T1, in_=D_T1, func=AF.Exp), (kd,), (kd,))
            S.act(lambda e: e.activation(out=D_T2, in_=D_T2, func=AF.Exp), (kd,), (kd,))
            S.dve(lambda e: e.tensor_tensor(out=D_NEGLAM, in0=D_T2, in1=D_T1, op=ALU.subtract), (kd,), (kd,))
            S.dve(lambda e: e.tensor_scalar(out=D_NEGLAM, in0=D_NEGLAM, scalar1=-lam_init, scalar2=None, op0=ALU.add), (kd,), (kd,))
            S.dve(lambda e: e.tensor_scalar(out=pbl("subln"), in0=pbl("subln"), scalar1=float(1.0 - lam_init), scalar2=None, op0=ALU.mult), ("PBL", kd), ("PBL",))
            S.act(lambda e: e.activation(out=D_NEGA, in_=pbl("alog"), func=AF.Exp), ("PBL",), (kd,))
            S.dve(lambda e: e.tensor_scalar(out=D_NEGA, in0=D_NEGA, scalar1=-1.0, scalar2=None, op0=ALU.mult), (kd,), (kd,))
        LR = ("PBL", "DRV") + CONST_R

        def stage0(s, l):
            AR.reset()
            xt = [AR.f32(D_MODEL) for _ in range(2)]
            junk = AR.f32(D_MODEL)
            xn = AR.b16(TPB, D_MODEL)
            ssq = AR.f32(8)
            src = dx if l == 0 else dout
            for n in range(NB):
                for tt in range(TPB):
                    t = n * TPB + tt
                    b = t % 2
                    kx = ("xt", b)
                    S.dma("sp", lambda e, b=b, t=t: e.dma_start(out=xt[b], in_=src[s, t * 128:(t + 1) * 128, :]),
                          r=(("outd", s, t),), w=(kx,))
                    S.act(lambda e, b=b: e.activation(out=junk, in_=xt[b], func=AF.Square), (kx,), ("junk",))
                    S.dve(lambda e: e.tensor_reduce(out=ssq[:, 0:1], in_=junk, axis=AX.X, op=ALU.add), ("junk",), ("ssq",))
                    rsqrt_act(ssq[:, 0:1], ssq[:, 0:1], 1.0 / D_MODEL, ("ssq",) + LR, ("ssq",))
                    S.dve(lambda e, b=b, tt=tt: e.tensor_scalar(out=xn[:, tt, :], in0=xt[b], scalar1=ssq[:, 0:1], scalar2=None, op0=ALU.mult),
                          (kx, "ssq"), (("xn", tt),))
                for kc in range(8):
                    half = kc % 2
                    pt = PSB[:, half * 512:half * 512 + BLK]
                    def fn(e, kc=kc, pt=pt):
                        ins = None
                        for tt in range(TPB):
                            ins = e.transpose(out=pt[:, tt * 128:(tt + 1) * 128], in_=xn[:, tt, kc * 128:(kc + 1) * 128], identity=IDB)
                        return ins
                    S.pe(fn, tuple(("xn", tt) for tt in range(TPB)) + LR, (("psb", 0),))
                    S.dve(lambda e, kc=kc, pt=pt, n=n: e.tensor_scalar(out=HT[:, kc, n * BLK:(n + 1) * BLK], in0=pt, scalar1=pbl("ng", kc), scalar2=None, op0=ALU.mult),
                          (("psb", 0),) + LR, (("HT", n),))

        def merge(s, l, b, first):
            sg = [MSG[i][:, 0:BLK] for i in range(2)]
            tmp = [MTP[i][:, 0:BLK] for i in range(2)]
            sl_br = wslot()
            load_w(sl_br, dwbr[l, b].rearrange("(kc p) c -> p kc c", p=128), wb_view(sl_br, 4, 1024))
            wbr = wb_view(sl_br, 4, 1024)
            it = 0
            for fh in range(2):
                slg = wslot()
                load_w(slg, win_view(l, OFF["gate"] + b * 1024 + fh * 512, 512), wb_view(slg, 8, 512))
                wg = wb_view(slg, 8, 512)
                for f4 in range(4):
                    f = fh * 4 + f4
                    for n in range(NB):
                        pg = PS[(2 * it) % 6]; pb_ = PS[(2 * it + 1) % 6]
                        kpg = ("ps", (2 * it) % 6); kpb = ("ps", (2 * it + 1) % 6)
                        bi = it % 2
                        it += 1
                        mm(pg[:, 0:BLK], proj_pairs(wg, f4 * 128, 128, n * BLK, BLK), (("wb", slg), ("HT", n)), (kpg,))
                        mm(pb_[:, 0:BLK], [(wbr[:, k4, f * 128:(f + 1) * 128], YB[:, k4, n * BLK:(n + 1) * BLK]) for k4 in range(4)],
                           (("wb", sl_br), ("YB", n)), (kpb,))
                        S.act(lambda e, bi=bi, pg=pg: e.activation(out=sg[bi], in_=pg[:, 0:BLK], func=AF.Sigmoid), (kpg,), (("sg", bi),))
                        mgv = MG[:, f, n * BLK:(n + 1) * BLK]
                        if first:
                            S.dve(lambda e, bi=bi, pb_=pb_, mgv=mgv: e.tensor_tensor(out=mgv, in0=sg[bi], in1=pb_[:, 0:BLK], op=ALU.mult),
                                  (("sg", bi), kpb), (("MG", f, n),))
                        else:
                            S.dve(lambda e, bi=bi, pb_=pb_: e.tensor_tensor(out=tmp[bi], in0=sg[bi], in1=pb_[:, 0:BLK], op=ALU.mult),
                                  (("sg", bi), kpb), (("mtmp", bi),))
                            S.pool(lambda e, bi=bi, mgv=mgv: e.tensor_tensor(out=mgv, in0=mgv, in1=tmp[bi], op=ALU.add),
                                   (("mtmp", bi), ("MG", f, n)), (("MG", f, n),))

        def outproj(s, l):
            AR.reset()
            xt = [AR.f32(D_MODEL) for _ in range(2)]
            ot = [AR.f32(D_MODEL) for _ in range(2)]
            src = dx if l == 0 else dout
            sls = []
            for hh in range(2):
                sl = wslot()
                load_w(sl, dwout[l, :, hh * 512:(hh + 1) * 512].rearrange("(kc p) c -> p kc c", p=128), wb_view(sl, 8, 512))
                sls.append(sl)
            for t in range(NT):
                b = t % 2
                S.dma("sp", lambda e, b=b, t=t: e.dma_start(out=xt[b], in_=src[s, t * 128:(t + 1) * 128, :]),
                      r=(("outd", s, t),), w=(("xt", b),))
                for hh in range(2):
                    p = PS[(2 * t + hh) % 4]
                    kp = ("ps", (2 * t + hh) % 4)
                    wv = wb_view(sls[hh], 8, 512)
                    mm(p[:, 0:512], [(MG[:, kc, t * 128:(t + 1) * 128], wv[:, kc, :]) for kc in range(8)],
                       (("wb", sls[hh]),) + tuple(("MG", kc, t // TPB) for kc in range(8)), (kp,))
                    S.dve(lambda e, b=b, hh=hh, p=p: e.tensor_tensor(out=ot[b][:, hh * 512:(hh + 1) * 512], in0=p[:, 0:512], in1=xt[b][:, hh * 512:(hh + 1) * 512], op=ALU.add),
                          (kp, ("xt", b)), (("ot", b, hh),))
                S.dma("sp", lambda e, b=b, t=t: e.dma_start(out=dout[s, t * 128:(t + 1) * 128, :], in_=ot[b]),
                      r=(("ot", b, 0), ("ot", b, 1)), w=(("outd", s, t),))

        def rope_tables(s, invf_col, npart, Ct, St):
            pi_ = AR.f32(BLK).bitcast(I32)
            v = AR.f32(BLK); tmp = AR.f32(BLK)
            for n in range(NB):
                S.dma("sp", lambda e, n=n: e.dma_start(out=pi_[0:npart, :], in_=dpos[s:s + 1, n * BLK:(n + 1) * BLK].partition_broadcast(npart)),
                      w=("rp_pi",))
                S.dve(lambda e: e.tensor_copy(out=v[0:npart, :], in_=pi_[0:npart, :]), ("rp_pi",), ("rp_v",))
                S.dve(lambda e: e.tensor_scalar(out=v[0:npart, :], in0=v[0:npart, :], scalar1=invf_col[0:npart, :], scalar2=float(1.0 / TWO_PI), op0=ALU.mult, op1=ALU.mult),
                      ("rp_v",) + CONST_R, ("rp_v",))
                sincos(v[0:npart, :], tmp[0:npart, :], St[0:npart, n * BLK:(n + 1) * BLK], Ct[0:npart, n * BLK:(n + 1) * BLK],
                       (), "rp_v", "rp_t", ("rp_S", n), ("rp_C", n))

        def rope_apply(dst, xn_ap, pp, kpp, PT, npart, Ct, St, n, t1, t2, rkeys, wkey, par=0):
            mm(pp[0:npart, 0:BLK], [(PT[0:npart, 0:npart], xn_ap)], rkeys + CONST_R, (kpp,))
            S.dve(lambda e: e.tensor_tensor(out=t1[0:npart, :], in0=xn_ap, in1=Ct[0:npart, n * BLK:(n + 1) * BLK], op=ALU.mult),
                  rkeys + (("rp_C", n),), (("rp_t1", par),))
            S.dve(lambda e: e.tensor_tensor(out=t2[0:npart, :], in0=pp[0:npart, 0:BLK], in1=St[0:npart, n * BLK:(n + 1) * BLK], op=ALU.mult),
                  (kpp, ("rp_S", n)), (("rp_t2", par),))
            S.pool(lambda e: e.tensor_tensor(out=dst, in0=t1[0:npart, :], in1=t2[0:npart, :], op=ALU.add), (("rp_t1", par), ("rp_t2", par)), (wkey,))

        def attn_block(qb, parts, vaug, PTt, dest, dkey):
            t0 = qb * TPB
            for j in range(t0 + TPB):
                lo = max(0, j - t0)
                c0 = lo * 128
                bi = j % 2
                st_ = PS[4 + bi]
                kst = ("ps", 4 + bi)
                rk = tuple(p[3](j) for p in parts) + tuple(p[4] for p in parts)
                mm(st_[:, c0:BLK], [(p[0][0:p[2], j * 128:(j + 1) * 128], p[1][0:p[2], c0:BLK]) for p in parts], rk, (kst,))
                S.act(lambda e, bi=bi, st_=st_, c0=c0: e.activation(out=PTt[bi][:, c0:BLK], in_=st_[:, c0:BLK], func=AF.Exp), (kst,), (("ptt", bi),))
                if j >= t0:
                    S.pool(lambda e, bi=bi, c0=c0: e.tensor_tensor(out=PTt[bi][:, c0:c0 + 128], in0=PTt[bi][:, c0:c0 + 128], in1=MASKUB, op=ALU.mult),
                           (("ptt", bi),) + CONST_R, (("ptt", bi),))
                for i in range(lo, TPB):
                    def fn(e, i=i, j=j, bi=bi):
                        return e.matmul(PS[i][:, 0:130], lhsT=PTt[bi][:, i * 128:(i + 1) * 128], rhs=vaug[:, j, :],
                                        start=(j == 0), stop=(j == t0 + i))
                    S.pe(fn, (("ptt", bi), ("vaug", j)), (("ps", i),))
            rec = AR_small["rec"]
            for i in range(TPB):
                S.dve(lambda e, i=i: e.reciprocal(out=rec[:, i:i + 1], in_=PS[i][:, 128:129]), (("ps", i),), (("rec", i),))
                S.dve(lambda e, i=i: e.tensor_scalar(out=dest[:, i, :], in0=PS[i][:, 0:128], scalar1=rec[:, i:i + 1], scalar2=None, op0=ALU.mult),
                      (("ps", i), ("rec", i)), (dkey + (i,),))

        AR_small = {}

        def finish_head(l, h, qb, obf, okeys, gate_c0, wg_view, wg_key, gs):
            pt = PSB[:, 0:BLK]
            def fn(e):
                ins = None
                for i in range(TPB):
                    ins = e.transpose(out=pt[:, i * 128:(i + 1) * 128], in_=obf[:, i, :], identity=IDB)
                return ins
            S.pe(fn, okeys + CONST_R, (("psb", 0),))
            pg = PS[6]
            mm(pg[:, 0:BLK], proj_pairs(wg_view, gate_c0, 128, qb * BLK, BLK), (wg_key, ("HT", qb)), (("ps", 6),))
            S.act(lambda e: e.activation(out=gs, in_=pg[:, 0:BLK], func=AF.Silu), (("ps", 6),), ("gsilu",))
            S.dve(lambda e: e.tensor_tensor(out=YB[:, h, qb * BLK:(qb + 1) * BLK], in0=pt, in1=gs, op=ALU.mult),
                  (("psb", 0), "gsilu"), (("YB", qb),))

        def branch_da(s, l):
            AR.reset()
            Ct = AR.b16(S_LEN); St = AR.b16(S_LEN)
            rope_tables(s, cf("invf_da"), 128, Ct, St)
            qT = AR.b16(S_LEN); kT = AR.b16(S_LEN)
            vaug = AR.b16(NT, 130)
            PTt = [AR.b16(BLK) for _ in range(2)]
            sqs = [AR.b16(BLK) for _ in range(2)]; qns = [AR.b16(BLK) for _ in range(2)]; obf = AR.b16(TPB, 128); gs = AR.f32(BLK)
            rss = [AR.f32(BLK) for _ in range(2)]; t1s = [AR.f32(BLK) for _ in range(2)]; t2s = [AR.f32(BLK) for _ in range(2)]
            d0 = AR.f32(TPB, 128); d1 = AR.f32(TPB, 128); o = AR.f32(TPB, 128); junk = AR.f32(TPB, 128)
            ssq = AR.f32(TPB); AR_small["rec"] = AR.f32(TPB)
            S.dve(lambda e: e.memset(vaug[:, :, 128:130], 1.0), (), tuple(("vaug", j) for j in range(NT)))
            for h in range(4):
                sl = wslot()
                wv = WB[sl].rearrange("p (q k c) -> p q k c", q=4, k=8)
                for qi, nm in enumerate(["da_q", "da_k", "da_v", "da_gate"]):
                    load_w(sl, win_view(l, OFF[nm] + h * 128, 128), wv[:, qi])
                wkey = ("wb", sl)
                def da_qk(which, dstT, gcol, n, par):
                    sq, rs, qn, t1, t2 = sqs[par], rss[par], qns[par], t1s[par], t2s[par]
                    pq = PS[par]; kpq = ("ps", par)
                    pss = PS[2 + par]; kpss = ("ps", 2 + par)
                    ppm = PS[4 + par]; kppm = ("ps", 4 + par)
                    mm(pq[:, 0:BLK], proj_pairs(wv[:, which], 0, 128, n * BLK, BLK), (wkey, ("HT", n)), (kpq,))
                    S.act(lambda e: e.activation(out=sq, in_=pq[:, 0:BLK], func=AF.Square), (kpq,), (("sq", par),))
                    mm(pss[:, 0:BLK], [(BLK64B, sq)], (("sq", par),) + CONST_R, (kpss,))
                    rsqrt_act(rs, pss[:, 0:BLK], 1.0 / 64, (kpss,) + LR, (("rs", par),))
                    S.dve(lambda e: e.scalar_tensor_tensor(out=qn, in0=pq[:, 0:BLK], scalar=gcol, in1=rs, op0=ALU.mult, op1=ALU.mult),
                          (kpq, ("rs", par)) + LR, (("qn", par),))
                    rope_apply(dstT[:, n * BLK:(n + 1) * BLK], qn, ppm, kppm, PTDAB, 128, Ct, St, n, t1, t2, (("qn", par),),
                               (("qT" if which == 0 else "kT"), n), par=par)
                it_ = 0
                for which, dstT, gcol in ((0, qT, D_GQ), (1, kT, pbl("dak"))):
                    for n in range(NB):
                        da_qk(which, dstT, gcol, n, (it_ % 2) if DB_DA else 0)
                        it_ += 1
                for n in range(NB):
                    pv = PS[n % 2]; kpv = ("ps", n % 2)
                    for tt in range(TPB):
                        t = n * TPB + tt
                        mm(pv[:, tt * 128:(tt + 1) * 128], tok_pairs(wv[:, 2], 0, 128, t), (wkey, ("HT", n)), (kpv,))
                    S.act(lambda e, pv=pv, n=n: e.activation(out=vaug[:, n * TPB:(n + 1) * TPB, 0:128], in_=pv[:, 0:BLK].rearrange("p (a b) -> p a b", a=TPB), func=AF.Copy),
                          (kpv,), tuple(("vaug", n * TPB + tt) for tt in range(TPB)))
                for qb in range(NB):
                    for m, dst in ((0, d0), (1, d1)):
                        kv = kT[m * 64:(m + 1) * 64, :]
                        qv = qT[m * 64:(m + 1) * 64, qb * BLK:(qb + 1) * BLK]
                        parts = [(_Shift(kv), _Shift(qv), 64, (lambda j: ("kT", j // TPB)), ("qT", qb))]
                        attn_block(qb, parts, vaug, PTt, dst, ("dd", m))
                    for i in range(TPB):
                        S.dve(lambda e, i=i: e.scalar_tensor_tensor(out=o[:, i, :], in0=d1[:, i, :], scalar=D_NEGLAM, in1=d0[:, i, :], op0=ALU.mult, op1=ALU.add),
                              (("dd", 0, i), ("dd", 1, i)) + LR, (("o", i),))
                    S.act(lambda e: e.activation(out=junk, in_=o, func=AF.Square), tuple(("o", i) for i in range(TPB)), ("ojunk",))
                    S.dve(lambda e: e.tensor_reduce(out=ssq, in_=junk, axis=AX.X, op=ALU.add), ("ojunk",), ("ossq",))
                    rsqrt_act(ssq, ssq, 1.0 / 128, ("ossq",) + LR, ("ossq",))
                    for i in range(TPB):
                        S.dve(lambda e, i=i: e.scalar_tensor_tensor(out=obf[:, i, :], in0=o[:, i, :], scalar=ssq[:, i:i + 1], in1=pbl("subln"), op0=ALU.mult, op1=ALU.mult),
                              (("o", i), "ossq") + LR, (("obf", i),))
                    finish_head(l, h, qb, obf, tuple(("obf", i) for i in range(TPB)), 0, wv[:, 3], wkey, gs)

        def branch_mla(s, l):
            AR.reset()
            Ct = AR.b16(S_LEN); St = AR.b16(S_LEN)
            rope_tables(s, cf("invf_mla"), 64, Ct, St)
            cqn = AR.b16(2, S_LEN); ckvn = AR.b16(S_LEN); krr = AR.b16(S_LEN)
            knope = AR.b16(S_LEN); krope = AR.b16(S_LEN)
            vaug = AR.b16(NT, 130)
            qnope = AR.b16(BLK); qrope = AR.b16(BLK)
            PTt = [AR.b16(BLK) for _ in range(2)]
            sqA = AR.b16(BLK); sqB = AR.b16(BLK); tb = AR.b16(BLK); obf = AR.b16(TPB, 128)
            wuq = AR.b16(2, 768); wukv = AR.b16(1024)
            gs = AR.f32(BLK); rs = AR.f32(BLK); t1 = AR.f32(BLK); t2 = AR.f32(BLK)
            o = AR.f32(TPB, 128); AR_small["rec"] = AR.f32(TPB)
            S.dve(lambda e: e.memset(vaug[:, :, 128:130], 1.0), (), tuple(("vaug", j) for j in range(NT)))
            S.dma("pool", lambda e: e.dma_start(out=wuq, in_=dwuq[l].rearrange("(kc p) c -> p kc c", p=128)), w=("wuq",))
            S.dma("pool", lambda e: e.dma_start(out=wukv, in_=dwukv[l]), w=("wukv",))
            sl = wslot()
            wc = wb_view(sl, 8, 448)
            load_w(sl, win_view(l, OFF["mla_cq"], 448), wc)
            wkey = ("wb", sl)
            for n in range(NB):
                blk = slice(n * BLK, (n + 1) * BLK)
                for c in range(2):
                    mm(PS[c][:, 0:BLK], proj_pairs(wc, c * 128, 128, n * BLK, BLK), (wkey, ("HT", n)), (("ps", c),))
                S.act(lambda e: e.activation(out=sqA, in_=PS[0][:, 0:BLK], func=AF.Square), (("ps", 0),), ("sqA",))
                S.act(lambda e: e.activation(out=sqB, in_=PS[1][:, 0:BLK], func=AF.Square), (("ps", 1),), ("sqB",))
                mm(PS[2][:, 0:BLK], [(ONESB, sqA), (ONESB, sqB)], ("sqA", "sqB") + CONST_R, (("ps", 2),))
                rsqrt_act(rs, PS[2][:, 0:BLK], 1.0 / 256, (("ps", 2),) + LR, ("rs",))
                for c in range(2):
                    S.dve(lambda e, c=c, blk=blk: e.scalar_tensor_tensor(out=cqn[:, c, blk], in0=PS[c][:, 0:BLK], scalar=pbl("mqa", c), in1=rs, op0=ALU.mult, op1=ALU.mult),
                          (("ps", c), "rs") + LR, (("cqn", n),))
                mm(PS[3][:, 0:BLK], proj_pairs(wc, 256, 128, n * BLK, BLK), (wkey, ("HT", n)), (("ps", 3),))
                S.act(lambda e: e.activation(out=sqA, in_=PS[3][:, 0:BLK], func=AF.Square), (("ps", 3),), ("sqA",))
                mm(PS[2][:, 0:BLK], [(ONESB, sqA)], ("sqA",) + CONST_R, (("ps", 2),))
                rsqrt_act(rs, PS[2][:, 0:BLK], 1.0 / 128, (("ps", 2),) + LR, ("rs",))
                S.dve(lambda e, blk=blk: e.scalar_tensor_tensor(out=ckvn[:, blk], in0=PS[3][:, 0:BLK], scalar=pbl("mkva"), in1=rs, op0=ALU.mult, op1=ALU.mult),
                      (("ps", 3), "rs") + LR, (("ckvn", n),))
                mm(PS[6][0:64, 0:BLK], proj_pairs(wc, 384, 64, n * BLK, BLK), (wkey, ("HT", n)), (("ps", 6),))
                S.act(lambda e, blk=blk: e.activation(out=krr[0:64, blk], in_=PS[6][0:64, 0:BLK], func=AF.Copy), (("ps", 6),), (("krr", n),))
            slg = wslot()
            wgv = wb_view(slg, 8, 512)
            load_w(slg, win_view(l, OFF["mla_gate"], 512), wgv)
            for h in range(4):
                for n in range(NB):
                    blk = slice(n * BLK, (n + 1) * BLK)
                    mm(PS[0][:, 0:BLK], [(wukv[:, h * 256:h * 256 + 128], ckvn[:, blk])], ("wukv", ("ckvn", n)), (("ps", 0),))
                    S.act(lambda e: e.activation(out=sqA, in_=PS[0][:, 0:BLK], func=AF.Square), (("ps", 0),), ("sqA",))
                    S.act(lambda e, blk=blk: e.activation(out=sqB[0:64, :], in_=krr[0:64, blk], func=AF.Square), (("krr", n),), ("sqB",))
                    mm(PS[2][:, 0:BLK], [(ONESB, sqA), (ONESB[0:64, :], sqB[0:64, :])], ("sqA", "sqB") + CONST_R, (("ps", 2),))
                    rsqrt_act(rs, PS[2][:, 0:BLK], 1.0 / 192, (("ps", 2),) + LR, ("rs",))
                    S.dve(lambda e, blk=blk: e.scalar_tensor_tensor(out=knope[:, blk], in0=PS[0][:, 0:BLK], scalar=pbl("mkn_n"), in1=rs, op0=ALU.mult, op1=ALU.mult),
                          (("ps", 0), "rs") + LR, (("knope", n),))
                    S.dve(lambda e, blk=blk: e.scalar_tensor_tensor(out=tb[0:64, :], in0=krr[0:64, blk], scalar=pbl("mkn_r")[0:64, :], in1=rs[0:64, :], op0=ALU.mult, op1=ALU.mult),
                          (("krr", n), "rs") + LR, ("tb",))
                    rope_apply(krope[0:64, blk], tb[0:64, :], PS[3], ("ps", 3), PTMLAB, 64, Ct, St, n, t1, t2, ("tb",), ("krope", n))
                    pv = PS[1]
                    for tt in range(TPB):
                        t = n * TPB + tt
                        mm(pv[:, tt * 128:(tt + 1) * 128], [(ckvn[:, t * 128:(t + 1) * 128], wukv[:, h * 256 + 128:h * 256 + 256])],
                           ("wukv", ("ckvn", n)), (("ps", 1),))
                    S.act(lambda e, n=n: e.activation(out=vaug[:, n * TPB:(n + 1) * TPB, 0:128], in_=PS[1][:, 0:BLK].rearrange("p (a b) -> p a b", a=TPB), func=AF.Copy),
                          (("ps", 1),), tuple(("vaug", n * TPB + tt) for tt in range(TPB)))
                for qb in range(NB):
                    blk = slice(qb * BLK, (qb + 1) * BLK)
                    c0 = h * 192
                    mm(PS[0][:, 0:BLK], [(wuq[:, c, c0:c0 + 128], cqn[:, c, blk]) for c in range(2)], ("wuq", ("cqn", qb)), (("ps", 0),))
                    mm(PS[1][0:64, 0:BLK], [(wuq[:, c, c0 + 128:c0 + 192], cqn[:, c, blk]) for c in range(2)], ("wuq", ("cqn", qb)), (("ps", 1),))
                    S.act(lambda e: e.activation(out=sqA, in_=PS[0][:, 0:BLK], func=AF.Square), (("ps", 0),), ("sqA",))
                    S.act(lambda e: e.activation(out=sqB[0:64, :], in_=PS[1][0:64, 0:BLK], func=AF.Square), (("ps", 1),), ("sqB",))
                    mm(PS[2][:, 0:BLK], [(ONESB, sqA), (ONESB[0:64, :], sqB[0:64, :])], ("sqA", "sqB") + CONST_R, (("ps", 2),))
                    rsqrt_act(rs, PS[2][:, 0:BLK], 1.0 / 192, (("ps", 2),) + LR, ("rs",))
                    S.dve(lambda e: e.scalar_tensor_tensor(out=qnope, in0=PS[0][:, 0:BLK], scalar=D_MQN_N, in1=rs, op0=ALU.mult, op1=ALU.mult),
                          (("ps", 0), "rs") + LR, ("qnope",))
                    S.dve(lambda e: e.scalar_tensor_tensor(out=tb[0:64, :], in0=PS[1][0:64, 0:BLK], scalar=D_MQN_R[0:64, :], in1=rs[0:64, :], op0=ALU.mult, op1=ALU.mult),
                          (("ps", 1), "rs") + LR, ("tb",))
                    rope_apply(qrope[0:64, :], tb[0:64, :], PS[3], ("ps", 3), PTMLAB, 64, Ct, St, qb, t1, t2, ("tb",), "qrope")
                    parts = [(knope, qnope, 128, (lambda j: ("knope", j // TPB)), "qnope"),
                             (krope, qrope, 64, (lambda j: ("krope", j // TPB)), "qrope")]
                    attn_block(qb, parts, vaug, PTt, o, ("oo",))
                    for i in range(TPB):
                        S.pool(lambda e, i=i: e.tensor_copy(out=obf[:, i, :], in_=o[:, i, :]), (("oo", i),), (("obf", i),))
                    finish_head(l, h, qb, obf, tuple(("obf", i) for i in range(TPB)), h * 128, wgv, ("wb", slg), gs)

        def branch_ssd(s, l):
            AR.reset()
            Xtok = AR.b16(NT, 512)
            Bfm = AR.b16(2, S_LEN); Cfm = AR.b16(2, S_LEN)
            xfm = AR.b16(BLK); Xdt = AR.b16(512); XdD = AR.b16(512); Btok = AR.b16(256)
            MT = [AR.b16(128) for _ in range(2)]
            Sbf = AR.b16(2, 256); ybt = AR.b16(512); wdt = AR.b16(8, 8)
            raw = [AR.f32(BLK + 3) for _ in range(2)]
            acc = AR.f32(BLK); zs = AR.f32(512); CBm = AR.f32(128)
            aTri = [AR.f32(128) for _ in range(2)]
            Dm = AR.f32(128); E = AR.f32(128)
            ydiag = AR.f32(512); yc = AR.f32(512); junk = AR.f32(512); Sst = AR.f32(2, 256)
            sm = AR.f32(64)
            dt_ = sm[:, 0:8]; a_ = sm[:, 8:16]; cs_ = sm[:, 16:24]; ecs = sm[:, 24:32]; etot = sm[:, 32:40]
            dec = sm[:, 40:48]; ssq = sm[:, 48:50]; tmp8 = sm[:, 56:64]
            slz = wslot(); wz = wb_view(slz, 8, 512)
            load_w(slz, win_view(l, OFF["mb_z"], 512), wz)
            S.dma("pool", lambda e: e.dma_start(out=wdt, in_=win_view(l, OFF["mb_dt"], 8)), w=("wdt",))
            S.dve(lambda e: e.memset(Sst, 0.0), (), ("Sst",))
            S.dve(lambda e: e.memset(Sbf, 0.0), (), ("Sbf",))
            for half in range(2):
                slx = wslot(); wx = wb_view(slx, 8, 512)
                load_w(slx, win_view(l, OFF["mb_xbc"] + half * 512, 512), wx)
                for f4 in range(4):
                    fc = half * 4 + f4
                    for n in range(NB):
                        rb = raw[n % 2]; krb = ("raw", n % 2)
                        p = PS[n % 2]; kp = ("ps", n % 2)
                        mm(p[:, 0:BLK], proj_pairs(wx, f4 * 128, 128, n * BLK, BLK), (("wb", slx), ("HT", n)), (kp,))
                        if n == 0:
                            S.dve(lambda e, rb=rb: e.memset(rb[:, 0:3], 0.0), (), (krb,))
                        else:
                            pr = raw[(n - 1) % 2]
                            S.pool(lambda e, rb=rb, pr=pr: e.tensor_copy(out=rb[:, 0:3], in_=pr[:, BLK:BLK + 3]), (("raw", (n - 1) % 2),), (krb,))
                        S.act(lambda e, rb=rb, p=p: e.activation(out=rb[:, 3:3 + BLK], in_=p[:, 0:BLK], func=AF.Copy), (kp,), (krb,))
                        cw = lambda k, fc=fc: pbl("convw", fc * 4 + k)
                        S.dve(lambda e, rb=rb, fc=fc, cw=cw: e.tensor_scalar(out=acc, in0=rb[:, 3:3 + BLK], scalar1=cw(3), scalar2=pbl("convb", fc), op0=ALU.mult, op1=ALU.add),
                              (krb,) + LR, ("acc",))
                        for k in (2, 1, 0):
                            S.dve(lambda e, rb=rb, k=k, cw=cw: e.scalar_tensor_tensor(out=acc, in0=rb[:, k:k + BLK], scalar=cw(k), in1=acc, op0=ALU.mult, op1=ALU.add),
                                  (krb, "acc") + LR, ("acc",))
                        if fc < 4:
                            dst, kd = xfm, "xfm"
                        elif fc < 6:
                            dst, kd = Bfm[:, fc - 4, n * BLK:(n + 1) * BLK], ("Bfm", n)
                        else:
                            dst, kd = Cfm[:, fc - 6, n * BLK:(n + 1) * BLK], ("Cfm", n)
                        S.act(lambda e, dst=dst: e.activation(out=dst, in_=acc, func=AF.Silu), ("acc",), (kd,))
                        if fc < 4:
                            pt = PSB[:, 0:BLK]
                            def fn(e, pt=pt):
                                ins = None
                                for tt in range(TPB):
                                    ins = e.transpose(out=pt[:, tt * 128:(tt + 1) * 128], in_=xfm[:, tt * 128:(tt + 1) * 128], identity=IDB)
                                return ins
                            S.pe(fn, ("xfm",) + CONST_R, (("psb", 0),))
                            S.dve(lambda e, pt=pt, n=n, fc=fc: e.tensor_copy(out=Xtok[:, n * TPB:(n + 1) * TPB, fc * 128:(fc + 1) * 128],
                                                                            in_=pt.rearrange("p (a b) -> p a b", a=TPB)),
                                  (("psb", 0),), (("Xtok", n),))
            sets = [dict(zs=zs, sm=sm, Xdt=Xdt, XdD=XdD, Btok=Btok, CBm=CBm, ydiag=ydiag, yc=yc, junk=junk, ybt=ybt, Dm=Dm, E=E, aTri=aTri, MT=MT)]
            sets.append(dict(zs=AR.f32(512), sm=AR.f32(64), Xdt=AR.b16(512), XdD=AR.b16(512), Btok=AR.b16(256), CBm=AR.f32(128),
                             ydiag=AR.f32(512), yc=AR.f32(512), junk=AR.f32(512), ybt=AR.b16(512), Dm=AR.f32(128), E=AR.f32(128),
                             aTri=[AR.f32(128) for _ in range(2)], MT=[AR.b16(128) for _ in range(2)]))

            def ssd_chunk(c):
                par = (c % 2) if DB_SSD else 0
                T = sets[par]
                zs, sm, Xdt, XdD, Btok, CBm, ydiag, yc, junk, ybt, Dm, E, aTri, MT = (T[k_] for k_ in
                    ("zs", "sm", "Xdt", "XdD", "Btok", "CBm", "ydiag", "yc", "junk", "ybt", "Dm", "E", "aTri", "MT"))
                dt_ = sm[:, 0:8]; a_ = sm[:, 8:16]; cs_ = sm[:, 16:24]; ecs = sm[:, 24:32]; etot = sm[:, 32:40]
                dec = sm[:, 40:48]; ssq = sm[:, 48:50]; tmp8 = sm[:, 56:64]
                n = c // TPB
                tok = slice(c * 128, (c + 1) * 128)
                mm(PS[0][:, 0:512], tok_pairs(wz, 0, 512, c), (("wb", slz), ("HT", n)), (("ps", 0),))
                S.act(lambda e: e.activation(out=zs, in_=PS[0][:, 0:512], func=AF.Silu), (("ps", 0),), (("zs", par),))
                mm(PS[1][:, 0:8], tok_pairs(wdt, 0, 8, c), ("wdt", ("HT", n)), (("ps", 1),))
                S.dve(lambda e: e.tensor_tensor(out=tmp8, in0=PS[1][:, 0:8], in1=pbl("dtb"), op=ALU.add), (("ps", 1),) + LR, (("tmp8", par),))
                S.act(lambda e: e.activation(out=tmp8, in_=tmp8, func=AF.Exp), (("tmp8", par),), (("tmp8", par),))
                S.act(lambda e: e.activation(out=dt_, in_=tmp8, func=AF.Ln, bias=D_ONE, scale=1.0), (("tmp8", par),) + LR, (("dt", par),))
                S.dve(lambda e: e.tensor_tensor(out=a_, in0=dt_, in1=D_NEGA, op=ALU.mult), (("dt", par),) + LR, (("a", par),))
                mm(PS[1][:, 8:16], [(MASKUF, a_)], (("a", par),) + CONST_R, (("ps", 1),))
                mm(PS[1][:, 16:24], [(ONESF, a_)], (("a", par),) + CONST_R, (("ps", 1),))
                S.act(lambda e: e.activation(out=cs_, in_=PS[1][:, 8:16], func=AF.Copy), (("ps", 1),), (("cs", par),))
                S.act(lambda e: e.activation(out=ecs, in_=PS[1][:, 8:16], func=AF.Exp), (("ps", 1),), (("ecs", par),))
                S.act(lambda e: e.activation(out=etot, in_=PS[1][:, 16:24], func=AF.Exp), (("ps", 1),), (("etot", par),))
                S.dve(lambda e: e.tensor_tensor(out=dec, in0=PS[1][:, 16:24], in1=cs_, op=ALU.subtract), (("ps", 1), ("cs", par)), (("dec", par),))
                S.act(lambda e: e.activation(out=dec, in_=dec, func=AF.Exp), (("dec", par),), (("dec", par),))
                for h in range(8):
                    hs = slice(h * 64, (h + 1) * 64)
                    S.dve(lambda e, h=h, hs=hs, c=c: e.tensor_scalar(out=Xdt[:, hs], in0=Xtok[:, c, hs], scalar1=dt_[:, h:h + 1], scalar2=None, op0=ALU.mult),
                          (("Xtok", n), ("dt", par)), (("Xdt", par),))
                    S.pool(lambda e, h=h, hs=hs: e.tensor_scalar(out=XdD[:, hs], in0=Xdt[:, hs], scalar1=dec[:, h:h + 1], scalar2=None, op0=ALU.mult),
                           (("Xdt", par), ("dec", par)), (("XdD", par),))
                def fnb(e, tok=tok):
                    ins = None
                    for g in range(2):
                        ins = e.transpose(out=PSB[:, 512 + g * 128:512 + (g + 1) * 128], in_=Bfm[:, g, tok], identity=IDB)
                    return ins
                S.pe(fnb, (("Bfm", n),) + CONST_R, (("psb", 0),))
                S.act(lambda e: e.activation(out=Btok, in_=PSB[:, 512:768], func=AF.Copy), (("psb", 0),), (("Btok", par),))
                for gr in range(2):
                    mm(PS[2][:, 0:128], [(Bfm[:, gr, tok], Cfm[:, gr, tok])], (("Bfm", n), ("Cfm", n)), (("ps", 2),))
                    S.dve(lambda e: e.tensor_tensor(out=CBm, in0=PS[2][:, 0:128], in1=MASKUF, op=ALU.mult), (("ps", 2),) + CONST_R, (("CBm", par),))
                    for r_ in range(4):
                        h = gr * 4 + r_
                        hs = slice(h * 64, (h + 1) * 64)
                        at = aTri[h % 2]; kat = ("aTri", h % 2, par)
                        S.dve(lambda e, at=at, h=h: e.tensor_scalar(out=at, in0=MASKUF, scalar1=a_[:, h:h + 1], scalar2=None, op0=ALU.mult),
                              (("a", par),) + CONST_R, (kat,))
                        mm(PS[3][:, 0:128], [(ONESF, at), (at, D_NEGONES)], (kat,) + CONST_R, (("ps", 3),))
                        S.dve(lambda e: e.tensor_scalar(out=Dm, in0=PS[3][:, 0:128], scalar1=0.0, scalar2=None, op0=ALU.min), (("ps", 3),), (("Dm", par),))
                        S.act(lambda e: e.activation(out=E, in_=Dm, func=AF.Exp), (("Dm", par),), (("E", par),))
                        mt = MT[h % 2]; kmt = ("MT", h % 2, par)
                        S.dve(lambda e, mt=mt: e.tensor_tensor(out=mt, in0=E, in1=CBm, op=ALU.mult), (("E", par), ("CBm", par)), (kmt,))
                        mm(PS[4][:, hs], [(mt, Xdt[:, hs])], (kmt, ("Xdt", par)), (("ps", 4),))
                    gs_ = slice(gr * 256, (gr + 1) * 256)
                    mm(PS[5][:, gs_], [(Cfm[:, gr, tok], Sbf[:, gr, :])], (("Cfm", n), "Sbf"), (("ps", 5),))
                    mm(PS[6][:, gs_], [(Btok[:, gr * 128:(gr + 1) * 128], XdD[:, gs_])], (("Btok", par), ("XdD", par)), (("ps", 6),))
                for h in range(8):
                    gr, r_ = h // 4, h % 4
                    S.dve(lambda e, h=h, gr=gr, r_=r_: e.scalar_tensor_tensor(out=Sst[:, gr, r_ * 64:(r_ + 1) * 64], in0=Sst[:, gr, r_ * 64:(r_ + 1) * 64],
                                                                             scalar=etot[:, h:h + 1], in1=PS[6][:, h * 64:(h + 1) * 64], op0=ALU.mult, op1=ALU.add),
                          ("Sst", ("etot", par), ("ps", 6)), ("Sst",))
                S.pool(lambda e: e.tensor_copy(out=Sbf, in_=Sst), ("Sst",), ("Sbf",))
                S.act(lambda e: e.activation(out=ydiag, in_=PS[4][:, 0:512], func=AF.Copy), (("ps", 4),), (("ydiag", par),))
                for h in range(8):
                    hs = slice(h * 64, (h + 1) * 64)
                    S.dve(lambda e, h=h, hs=hs: e.scalar_tensor_tensor(out=yc[:, hs], in0=PS[5][:, hs], scalar=ecs[:, h:h + 1], in1=ydiag[:, hs], op0=ALU.mult, op1=ALU.add),
                          (("ps", 5), ("ecs", par), ("ydiag", par)), (("yc", par),))
                S.pool(lambda e, c=c: e.tensor_tensor(out=ydiag, in0=Xtok[:, c, :], in1=pbl("mbd"), op=ALU.mult), (("Xtok", n), ("yc", par)) + LR, (("ydiag", par),))
                S.dve(lambda e: e.tensor_tensor(out=yc, in0=yc, in1=ydiag, op=ALU.add), (("yc", par), ("ydiag", par)), (("yc", par),))
                S.dve(lambda e: e.tensor_tensor(out=yc, in0=yc, in1=zs, op=ALU.mult), (("yc", par), ("zs", par)), (("yc", par),))
                S.act(lambda e: e.activation(out=junk, in_=yc, func=AF.Square), (("yc", par),), (("junk", par),))
                S.dve(lambda e: e.tensor_reduce(out=ssq, in_=junk.rearrange("p (a b) -> p a b", a=2), axis=AX.X, op=ALU.add), (("junk", par),), (("ssq", par),))
                rsqrt_act(ssq, ssq, 1.0 / 256, (("ssq", par),) + LR, (("ssq", par),))
                for g2 in range(2):
                    gs_ = slice(g2 * 256, (g2 + 1) * 256)
                    S.dve(lambda e, g2=g2, gs_=gs_: e.scalar_tensor_tensor(out=ybt[:, gs_], in0=yc[:, gs_], scalar=ssq[:, g2:g2 + 1], in1=pbl("mbnorm")[:, gs_], op0=ALU.mult, op1=ALU.mult),
                          (("yc", par), ("ssq", par)) + LR, (("ybt", par),))
                def fnt(e):
                    ins = None
                    for k in range(4):
                        ins = e.transpose(out=PSB[:, k * 128:(k + 1) * 128], in_=ybt[:, k * 128:(k + 1) * 128], identity=IDB)
                    return ins
                S.pe(fnt, (("ybt", par),) + CONST_R, (("psb", 0),))
                S.act(lambda e, tok=tok: e.activation(out=YB[:, :, tok], in_=PSB[:, 0:512].rearrange("p (a b) -> p a b", a=4), func=AF.Copy),
                      (("psb", 0),), (("YB", n),))


            for c in range(NT):
                ssd_chunk(c)

        def branch_s5(s, l):
            AR.reset()
            cj = AR.f32(16, 128); sj = AR.f32(16, 128)
            Bre = AR.b16(16, 128); Bim = AR.b16(16, 128); Cre = AR.b16(16, 128); Cim = AR.b16(16, 128)
            smp = AR.f32(192)
            r_ = smp[:, 0:16]; th = smp[:, 16:32]; rc = smp[:, 32:48]; rsn = smp[:, 48:64]; nrs = smp[:, 64:80]
            tA = smp[:, 80:96]; tB = smp[:, 96:112]; tC = smp[:, 112:128]; tD = smp[:, 128:144]; tE = smp[:, 144:160]; tF = smp[:, 160:176]
            stt = AR.f32(16, 2)
            mark = AR.off
            vtab = AR.f32(16, 128); ttab = AR.f32(16, 128)
            S.act(lambda e: e.activation(out=tA, in_=pbl("lstep"), func=AF.Exp), LR, ("s5a",))
            S.dve(lambda e: e.tensor_tensor(out=tB, in0=pbl("lamre"), in1=tA, op=ALU.mult), ("s5a",) + LR, ("s5b",))
            S.act(lambda e: e.activation(out=r_, in_=tB, func=AF.Exp), ("s5b",), ("s5r",))
            S.dve(lambda e: e.tensor_tensor(out=th, in0=pbl("lamim"), in1=tA, op=ALU.mult), ("s5a",) + LR, ("s5th",))
            for mt in range(16):
                S.dve(lambda e, mt=mt: e.tensor_scalar(out=vtab[:, mt, :], in0=cf("jrow"), scalar1=th[:, mt:mt + 1], scalar2=float(1.0 / TWO_PI), op0=ALU.mult, op1=ALU.mult),
                      ("s5th",) + CONST_R, ("s5v",))
            sincos(vtab, ttab, sj, cj, (), "s5v", "s5t", "s5sj", "s5cj")
            S.dve(lambda e: e.tensor_scalar(out=tC, in0=th, scalar1=float(128.0 / TWO_PI), scalar2=None, op0=ALU.mult), ("s5th",), ("s5c",))
            sincos(tC, tD, tE, tF, (), "s5c", "s5d", "s5s128", "s5c128")
            S.dve(lambda e: e.tensor_tensor(out=rc, in0=r_, in1=tF, op=ALU.mult), ("s5r", "s5c128"), ("s5rc",))
            S.dve(lambda e: e.tensor_tensor(out=rsn, in0=r_, in1=tE, op=ALU.mult), ("s5r", "s5s128"), ("s5rs",))
            S.dve(lambda e: e.tensor_scalar(out=nrs, in0=rsn, scalar1=-1.0, scalar2=None, op0=ALU.mult), ("s5rs",), ("s5nrs",))
            MC = ("s5r", "s5rc", "s5rs", "s5nrs", "s5sj", "s5cj")
            S.dma("pool", lambda e: e.dma_start(out=Cre, in_=ds5C[l, 0]), w=("s5Cre",))
            S.dma("pool", lambda e: e.dma_start(out=Cim, in_=ds5C[l, 1]), w=("s5Cim",))
            AR.off = mark + 2 * 16 * 128
            W = 256
            names = ["lr", "li", "ls", "v", "t", "sn", "cs", "abr", "abi", "den", "fr", "fi", "braw_r", "braw_i", "u1", "u2"]
            Q = {nm: AR.f32(W) for nm in names}
            for q in range(2048 // W):
                cs_ = slice(q * W, (q + 1) * W)
                k = "s5q"
                for i, nm in enumerate(["lr", "li", "ls"]):
                    S.dma("sp", lambda e, nm=nm, i=i, cs_=cs_: e.dma_start(out=Q[nm], in_=ds5rep[l, i, :, cs_]), w=(("s5in", nm),))
                mts = slice(q * (W // 128), (q + 1) * (W // 128))
                S.dma("sp", lambda e, mts=mts: e.dma_start(out=Q["braw_r"].rearrange("p (a b) -> p a b", b=128), in_=ds5B[l, 0, :, mts, :]), w=(("s5in", "br"),))
                S.dma("sp", lambda e, mts=mts: e.dma_start(out=Q["braw_i"].rearrange("p (a b) -> p a b", b=128), in_=ds5B[l, 1, :, mts, :]), w=(("s5in", "bi"),))
                S.act(lambda e: e.activation(out=Q["ls"], in_=Q["ls"], func=AF.Exp), (("s5in", "ls"),), (("s5in", "ls"),))
                S.dve(lambda e: e.tensor_tensor(out=Q["u1"], in0=Q["lr"], in1=Q["ls"], op=ALU.mult), (("s5in", "lr"), ("s5in", "ls")), ("q_u1",))
                S.act(lambda e: e.activation(out=Q["u1"], in_=Q["u1"], func=AF.Exp), ("q_u1",), ("q_u1",))
                S.dve(lambda e: e.tensor_tensor(out=Q["v"], in0=Q["li"], in1=Q["ls"], op=ALU.mult), (("s5in", "li"), ("s5in", "ls")), ("q_v",))
                S.dve(lambda e: e.tensor_scalar(out=Q["v"], in0=Q["v"], scalar1=float(1.0 / TWO_PI), scalar2=None, op0=ALU.mult), ("q_v",), ("q_v",))
                sincos(Q["v"], Q["t"], Q["sn"], Q["cs"], (), "q_v", "q_t", "q_sn", "q_cs")
                S.dve(lambda e: e.tensor_tensor(out=Q["abr"], in0=Q["u1"], in1=Q["cs"], op=ALU.mult), ("q_u1", "q_cs"), ("q_abr",))
                S.dve(lambda e: e.tensor_scalar(out=Q["abr"], in0=Q["abr"], scalar1=-1.0, scalar2=None, op0=ALU.add), ("q_abr",), ("q_abr",))
                S.dve(lambda e: e.tensor_tensor(out=Q["abi"], in0=Q["u1"], in1=Q["sn"], op=ALU.mult), ("q_u1", "q_sn"), ("q_abi",))
                S.dve(lambda e: e.tensor_tensor(out=Q["den"], in0=Q["lr"], in1=Q["lr"], op=ALU.mult), (("s5in", "lr"),), ("q_den",))
                S.dve(lambda e: e.tensor_tensor(out=Q["u2"], in0=Q["li"], in1=Q["li"], op=ALU.mult), (("s5in", "li"),), ("q_u2",))
                S.dve(lambda e: e.tensor_tensor(out=Q["den"], in0=Q["den"], in1=Q["u2"], op=ALU.add), ("q_den", "q_u2"), ("q_den",))
                S.dve(lambda e: e.reciprocal(out=Q["den"], in_=Q["den"]), ("q_den",), ("q_den",))
                S.dve(lambda e: e.tensor_tensor(out=Q["fr"], in0=Q["abr"], in1=Q["lr"], op=ALU.mult), ("q_abr", ("s5in", "lr")), ("q_fr",))
                S.dve(lambda e: e.tensor_tensor(out=Q["u2"], in0=Q["abi"], in1=Q["li"], op=ALU.mult), ("q_abi", ("s5in", "li"), "q_den"), ("q_u2",))
                S.dve(lambda e: e.tensor_tensor(out=Q["fr"], in0=Q["fr"], in1=Q["u2"], op=ALU.add), ("q_fr", "q_u2"), ("q_fr",))
                S.dve(lambda e: e.tensor_tensor(out=Q["fr"], in0=Q["fr"], in1=Q["den"], op=ALU.mult), ("q_fr", "q_den"), ("q_fr",))
                S.dve(lambda e: e.tensor_tensor(out=Q["fi"], in0=Q["abi"], in1=Q["lr"], op=ALU.mult), ("q_abi", ("s5in", "lr")), ("q_fi",))
                S.dve(lambda e: e.tensor_tensor(out=Q["u2"], in0=Q["abr"], in1=Q["li"], op=ALU.mult), ("q_abr", ("s5in", "li"), "q_fr"), ("q_u2",))
                S.dve(lambda e: e.tensor_tensor(out=Q["fi"], in0=Q["fi"], in1=Q["u2"], op=ALU.subtract), ("q_fi", "q_u2"), ("q_fi",))
                S.dve(lambda e: e.tensor_tensor(out=Q["fi"], in0=Q["fi"], in1=Q["den"], op=ALU.mult), ("q_fi", "q_den"), ("q_fi",))
                brv = Bre.rearrange("p a b -> p (a b)")[:, cs_]; biv = Bim.rearrange("p a b -> p (a b)")[:, cs_]
                S.dve(lambda e: e.tensor_tensor(out=Q["u1"], in0=Q["fr"], in1=Q["braw_r"], op=ALU.mult), ("q_fr", ("s5in", "br"), "q_abr", "q_abi"), ("q_u1",))
                S.dve(lambda e: e.tensor_tensor(out=Q["u2"], in0=Q["fi"], in1=Q["braw_i"], op=ALU.mult), ("q_fi", ("s5in", "bi")), ("q_u2",))
                S.dve(lambda e, brv=brv: e.tensor_tensor(out=brv, in0=Q["u1"], in1=Q["u2"], op=ALU.subtract), ("q_u1", "q_u2"), ("s5Bre",))
                S.dve(lambda e: e.tensor_tensor(out=Q["u1"], in0=Q["fr"], in1=Q["braw_i"], op=ALU.mult), ("q_fr", ("s5in", "bi"), "s5Bre"), ("q_u1",))
                S.dve(lambda e: e.tensor_tensor(out=Q["u2"], in0=Q["fi"], in1=Q["braw_r"], op=ALU.mult), ("q_fi", ("s5in", "br"), "s5Bre"), ("q_u2",))
                S.dve(lambda e, biv=biv: e.tensor_tensor(out=biv, in0=Q["u1"], in1=Q["u2"], op=ALU.add), ("q_u1", "q_u2"), ("s5Bim",))
            S.barrier()
            AR.off = mark
            ufm = AR.b16(4, BLK); ygb = AR.b16(4, BLK)
            hre_ = [AR.b16(BLK) for _ in range(2)]; nhim_ = [AR.b16(BLK) for _ in range(2)]
            t1_ = [AR.f32(BLK)] * 2; t2_ = [AR.f32(BLK)] * 2; wre_ = [AR.f32(BLK) for _ in range(2)]; wim_ = [AR.f32(BLK) for _ in range(2)]
            gre_ = [AR.f32(BLK) for _ in range(2)]; gim_ = [AR.f32(BLK) for _ in range(2)]
            yp = AR.f32(BLK); x2 = AR.f32(BLK); sgl = AR.f32(BLK); gsv = AR.f32(BLK)
            slu = wslot(); wu = wb_view(slu, 8, 512); load_w(slu, win_view(l, OFF["s5_u"], 512), wu)
            slg = wslot(); wg = wb_view(slg, 8, 512); load_w(slg, win_view(l, OFF["s5_gate"], 512), wg)
            slw = wslot(); wglu = wb_view(slw, 4, 512); load_w(slw, dwglu[l].rearrange("(kc p) c -> p kc c", p=128), wglu)
            GK = 1.5957691216057308
            for n in range(NB):
                for c4 in range(4):
                    p = PS[c4 % 2]; kp = ("ps", c4 % 2)
                    mm(p[:, 0:BLK], proj_pairs(wu, c4 * 128, 128, n * BLK, BLK), (("wb", slu), ("HT", n)), (kp,))
                    S.act(lambda e, c4=c4, p=p: e.activation(out=ufm[:, c4, :], in_=p[:, 0:BLK], func=AF.Copy), (kp,), (("ufm", c4),))
                for mt in range(16):
                    uc = mt // 4
                    pb2 = (mt % 2) if DB_S5 else 0
                    hre, nhim, t1, t2, wre, wim, gre, gim = hre_[pb2], nhim_[pb2], t1_[pb2], t2_[pb2], wre_[pb2], wim_[pb2], gre_[pb2], gim_[pb2]
                    K = lambda nm, pb2=pb2: (nm, 0 if nm in ('t1', 't2') else pb2)
                    cjb = cj[:, mt:mt + 1, :].broadcast_to([128, TPB, 128])
                    sjb = sj[:, mt:mt + 1, :].broadcast_to([128, TPB, 128])
                    v3 = lambda ap: ap.rearrange("p (a b) -> p a b", a=TPB)
                    pr = PS[2 + (mt % 2) * 2]; pi = PS[3 + (mt % 2) * 2]
                    kpr = ("ps", 2 + (mt % 2) * 2); kpi = ("ps", 3 + (mt % 2) * 2)
                    mm(pr[:, 0:BLK], [(Bre[:, mt, :], ufm[:, uc, :])], ("s5Bre", ("ufm", uc)), (kpr,))
                    mm(pi[:, 0:BLK], [(Bim[:, mt, :], ufm[:, uc, :])], ("s5Bim", ("ufm", uc)), (kpi,))
                    S.dve(lambda e, pr=pr, cjb=cjb, t1=t1: e.tensor_tensor(out=v3(t1), in0=v3(pr[:, 0:BLK]), in1=cjb, op=ALU.mult), (kpr,) + MC, (K("t1"),))
                    S.dve(lambda e, pi=pi, sjb=sjb, t2=t2: e.tensor_tensor(out=v3(t2), in0=v3(pi[:, 0:BLK]), in1=sjb, op=ALU.mult), (kpi,) + MC, (K("t2"),))
                    S.pool(lambda e, wre=wre, t1=t1, t2=t2: e.tensor_tensor(out=wre, in0=t1, in1=t2, op=ALU.add), (K("t1"), K("t2")), (K("wre"),))
                    S.dve(lambda e, pi=pi, cjb=cjb, t1=t1: e.tensor_tensor(out=v3(t1), in0=v3(pi[:, 0:BLK]), in1=cjb, op=ALU.mult), (kpi, K("wre")) + MC, (K("t1"),))
                    S.dve(lambda e, pr=pr, sjb=sjb, t2=t2: e.tensor_tensor(out=v3(t2), in0=v3(pr[:, 0:BLK]), in1=sjb, op=ALU.mult), (kpr, K("wre")) + MC, (K("t2"),))
                    S.pool(lambda e, wim=wim, t1=t1, t2=t2: e.tensor_tensor(out=wim, in0=t1, in1=t2, op=ALU.subtract), (K("t1"), K("t2")), (K("wim"),))
                    for k in range(TPB):
                        ck = n * TPB + k
                        c0 = k * 128
                        if ck > 0:
                            if k == 0:
                                lre, lim = stt[:, mt, 0:1], stt[:, mt, 1:2]
                            else:
                                lre, lim = gre[:, c0 - 1:c0], gim[:, c0 - 1:c0]
                            S.dve(lambda e, lre=lre, c0=c0, mt=mt, wre=wre: e.scalar_tensor_tensor(out=wre[:, c0:c0 + 1], in0=lre, scalar=rc[:, mt:mt + 1], in1=wre[:, c0:c0 + 1], op0=ALU.mult, op1=ALU.add),
                                  (K("wre"), K("gre"), ("stt", mt)) + MC, (K("wre"),))
                            S.dve(lambda e, lim=lim, c0=c0, mt=mt, wre=wre: e.scalar_tensor_tensor(out=wre[:, c0:c0 + 1], in0=lim, scalar=nrs[:, mt:mt + 1], in1=wre[:, c0:c0 + 1], op0=ALU.mult, op1=ALU.add),
                                  (K("wre"), K("gim"), ("stt", mt)) + MC, (K("wre"),))
                            S.dve(lambda e, lre=lre, c0=c0, mt=mt, wim=wim: e.scalar_tensor_tensor(out=wim[:, c0:c0 + 1], in0=lre, scalar=rsn[:, mt:mt + 1], in1=wim[:, c0:c0 + 1], op0=ALU.mult, op1=ALU.add),
                                  (K("wim"), K("gre"), ("stt", mt)) + MC, (K("wim"),))
                            S.dve(lambda e, lim=lim, c0=c0, mt=mt, wim=wim: e.scalar_tensor_tensor(out=wim[:, c0:c0 + 1], in0=lim, scalar=rc[:, mt:mt + 1], in1=wim[:, c0:c0 + 1], op0=ALU.mult, op1=ALU.add),
                                  (K("wim"), K("gim"), ("stt", mt)) + MC, (K("wim"),))
                        rb = r_[:, mt:mt + 1].broadcast_to([128, 128])
                        S.dve(lambda e, c0=c0, rb=rb, gre=gre, wre=wre: e.tensor_tensor_scan(out=gre[:, c0:c0 + 128], data0=rb, data1=wre[:, c0:c0 + 128], initial=0.0, op0=ALU.mult, op1=ALU.add),
                              (K("wre"),) + MC, (K("gre"),))
                        S.dve(lambda e, c0=c0, rb=rb, gim=gim, wim=wim: e.tensor_tensor_scan(out=gim[:, c0:c0 + 128], data0=rb, data1=wim[:, c0:c0 + 128], initial=0.0, op0=ALU.mult, op1=ALU.add),
                              (K("wim"),) + MC, (K("gim"),))
                    S.act(lambda e, mt=mt, gre=gre: e.activation(out=stt[:, mt, 0:1], in_=gre[:, BLK - 1:BLK], func=AF.Copy), (K("gre"),), (("stt", mt),))
                    S.act(lambda e, mt=mt, gim=gim: e.activation(out=stt[:, mt, 1:2], in_=gim[:, BLK - 1:BLK], func=AF.Copy), (K("gim"),), (("stt", mt),))
                    S.dve(lambda e, cjb=cjb, t1=t1, gre=gre: e.tensor_tensor(out=v3(t1), in0=v3(gre), in1=cjb, op=ALU.mult), (K("gre"),) + MC, (K("t1"),))
                    S.pool(lambda e, sjb=sjb, t2=t2, gim=gim: e.tensor_tensor(out=v3(t2), in0=v3(gim), in1=sjb, op=ALU.mult), (K("gim"),) + MC, (K("t2"),))
                    S.dve(lambda e, hre=hre, t1=t1, t2=t2: e.tensor_tensor(out=hre, in0=t1, in1=t2, op=ALU.subtract), (K("t1"), K("t2")), (K("hre"),))
                    S.dve(lambda e, sjb=sjb, t1=t1, gre=gre: e.tensor_tensor(out=v3(t1), in0=v3(gre), in1=sjb, op=ALU.mult), (K("gre"), K("hre")) + MC, (K("t1"),))
                    S.pool(lambda e, cjb=cjb, t2=t2, gim=gim: e.tensor_tensor(out=v3(t2), in0=v3(gim), in1=cjb, op=ALU.mult), (K("gim"), K("hre")) + MC, (K("t2"),))
                    S.dve(lambda e, nhim=nhim, t1=t1, t2=t2: e.scalar_tensor_tensor(out=nhim, in0=t1, scalar=-1.0, in1=t2, op0=ALU.mult, op1=ALU.subtract), (K("t1"), K("t2")), (K("nhim"),))
                    py = PS[6]
                    def fny(e, mt=mt, py=py, hre=hre, nhim=nhim):
                        e.matmul(py[:, 0:BLK], lhsT=Cre[:, mt, :], rhs=hre, start=(mt % 4 == 0), stop=False)
                        return e.matmul(py[:, 0:BLK], lhsT=Cim[:, mt, :], rhs=nhim, start=False, stop=(mt % 4 == 3))
                    S.pe(fny, (K("hre"), K("nhim"), "s5Cre", "s5Cim"), (("ps", 6),))
                    if mt % 4 == 3:
                        c4 = mt // 4
                        S.dve(lambda e, c4=c4, py=py: e.scalar_tensor_tensor(out=yp, in0=ufm[:, c4, :], scalar=pbl("s5d", c4), in1=py[:, 0:BLK], op0=ALU.mult, op1=ALU.add),
                              (("ufm", c4), ("ps", 6)) + LR, ("yp",))
                        S.pool(lambda e: e.tensor_tensor(out=x2, in0=yp, in1=yp, op=ALU.mult), ("yp",), ("x2",))
                        S.dve(lambda e: e.tensor_scalar(out=x2, in0=x2, scalar1=float(GK * 0.044715), scalar2=float(GK), op0=ALU.mult, op1=ALU.add), ("x2",), ("x2",))
                        S.dve(lambda e: e.tensor_tensor(out=x2, in0=x2, in1=yp, op=ALU.mult), ("x2", "yp"), ("x2",))
                        S.act(lambda e: e.activation(out=x2, in_=x2, func=AF.Sigmoid), ("x2",), ("x2",))
                        S.dve(lambda e, c4=c4: e.tensor_tensor(out=ygb[:, c4, :], in0=yp, in1=x2, op=ALU.mult), ("yp", "x2"), (("ygb", c4),))
                for c4 in range(4):
                    pg = PS[c4 % 2]; kpg = ("ps", c4 % 2)
                    mm(pg[:, 0:BLK], [(wglu[:, k4, c4 * 128:(c4 + 1) * 128], ygb[:, k4, :]) for k4 in range(4)],
                       (("wb", slw),) + tuple(("ygb", k4) for k4 in range(4)), (kpg,))
                    S.act(lambda e, c4=c4, pg=pg: e.activation(out=sgl, in_=pg[:, 0:BLK], func=AF.Sigmoid, bias=pbl("s5bglu", c4), scale=1.0), (kpg,) + LR, ("sgl",))
                    pgt = PS[2 + c4 % 2]; kpgt = ("ps", 2 + c4 % 2)
                    mm(pgt[:, 0:BLK], proj_pairs(wg, c4 * 128, 128, n * BLK, BLK), (("wb", slg), ("HT", n)), (kpgt,))
                    S.act(lambda e, pgt=pgt: e.activation(out=gsv, in_=pgt[:, 0:BLK], func=AF.Silu), (kpgt,), ("gsv",))
                    S.dve(lambda e, c4=c4: e.tensor_tensor(out=sgl, in0=sgl, in1=ygb[:, c4, :], op=ALU.mult), ("sgl", ("ygb", c4)), ("sgl",))
                    S.dve(lambda e, c4=c4, n=n: e.tensor_tensor(out=YB[:, c4, n * BLK:(n + 1) * BLK], in0=sgl, in1=gsv, op=ALU.mult), ("sgl", "gsv"), (("YB", n),))

        BR = {0: branch_da, 1: branch_ssd, 2: branch_s5, 3: branch_mla}
        for s in range(NSEQ):
            for l in range(L):
                layer_setup(l)
                stage0(s, l)
                S.barrier()
                first = True
                for b in range(4):
                    if b not in branches:
                        continue
                    BR[b](s, l)
                    S.barrier()
                    merge(s, l, b, first)
                    first = False
                if not first:
                    S.barrier()
                if first:
                    S.dve(lambda e: e.memset(MG, 0.0), (), tuple(("MG", f, n) for f in range(8) for n in range(NB)))
                if debug and s == 0 and l == 0:
                    for nm, src_, dst_ in (("ht", HT, dbg_ht), ("yb", YB, dbg_yb), ("mg", MG, dbg_mg)):
                        for q in range(src_.shape[1]):
                            dtmp = ARN[:, 0:S_LEN]
                            S.dve(lambda e, src_=src_, q=q, dtmp=dtmp: e.tensor_copy(out=dtmp, in_=src_[:, q, :]), (), ("dtmp",))
                            S.dma("sp", lambda e, dst_=dst_, q=q, dtmp=dtmp: e.dma_start(out=dst_[:, q * S_LEN:(q + 1) * S_LEN], in_=dtmp), r=("dtmp",), w=())
                    S.barrier()
                outproj(s, l)
                S.barrier()
        S.emit()
    return nc


_PROG_CACHE = {}


def run_cores(inputs, S_LEN, NSEQ, DEPTH, n_cores, branches=(0, 1, 2, 3), debug=False):
    key = (S_LEN, NSEQ, DEPTH, tuple(branches), debug)
    if key not in _PROG_CACHE:
        _PROG_CACHE[key] = build_program(S_LEN, NSEQ, DEPTH, branches, debug=debug)
    nc = _PROG_CACHE[key]
    f32 = lambda a: np.ascontiguousarray(np.asarray(a, dtype=np.float32))
    pb, s5rep, s5B, s5C = host_params({k: np.asarray(v) for k, v in inputs.items()}, DEPTH)
    shared = {
        "w_in": f32(inputs["w_in"])[:DEPTH], "w_br": f32(inputs["w_br"])[:DEPTH], "w_out": f32(inputs["w_out"])[:DEPTH],
        "w_uq": f32(inputs["mla_w_uq"])[:DEPTH], "w_ukv": f32(inputs["mla_w_ukv"])[:DEPTH], "w_glu": f32(inputs["s5_w_glu"])[:DEPTH],
        "pblob": pb, "cblob": host_consts(), "s5rep": s5rep, "s5B": s5B, "s5C": s5C,
    }
    x = f32(inputs["x"])
    pos = np.ascontiguousarray(np.asarray(inputs["positions"], dtype=np.int32))
    in_maps = []
    for c in range(n_cores):
        m = dict(shared)
        m["x"] = np.ascontiguousarray(x[c * NSEQ:(c + 1) * NSEQ])
        m["pos"] = np.ascontiguousarray(pos[c * NSEQ:(c + 1) * NSEQ])
        in_maps.append(m)
    res = run_bass_kernel_spmd(nc, in_maps, core_ids=list(range(n_cores)))
    outs = [np.asarray(r["out"]).reshape(NSEQ, S_LEN, D_MODEL) for r in res.results]
    if debug:
        global DBG
        DBG = {k: np.asarray(res.results[0][k]) for k in ("dbg_ht", "dbg_yb", "dbg_mg")}
    return np.concatenate(outs, axis=0).astype(np.float32)


def kernel(**inputs):
    return run_cores(inputs, 2048, 2, 2, 8)
```

```python
import numpy as np
import concourse.bass as bass
import concourse.mybir as mybir
from concourse.alu_op_type import AluOpType as ALU
from concourse.bass_utils import run_bass_kernel_spmd

F32 = mybir.dt.float32
BF16 = mybir.dt.bfloat16
I32 = mybir.dt.int32
AF = mybir.ActivationFunctionType
AX = mybir.AxisListType


class Op:
    __slots__ = ("eng", "fn", "reads", "writes", "dma", "waits", "inc", "ticket", "idx", "sem", "cost", "seg")

    def __init__(self, eng, fn, reads, writes, dma, cost=None):
        self.eng, self.fn, self.reads, self.writes, self.dma = eng, fn, reads, writes, dma
        self.cost = cost
        self.seg = 0
        self.waits = []
        self.inc = False
        self.ticket = None
        self.sem = None


class Sched:
    ENGS = ("pe", "dve", "act", "pool", "sp")

    def __init__(self, nc, n_dma_sems=12):
        self.nc = nc
        self.ops = []
        self.n_dma_sems = n_dma_sems
        self._bar_ops = set()
        self._seg = 0
        self.reorder = True
        import os
        self.same_sync = True

    DEFCOST = {"pe": 0.4, "dve": 0.55, "act": 0.5, "pool": 0.9, "sp": 0.1}

    def add(self, eng, fn, reads=(), writes=(), dma=False, cost=None):
        if cost is None:
            cost = 0.15 if dma else self.DEFCOST[eng]
        op = Op(eng, fn, tuple(reads), tuple(writes), dma, cost)
        op.idx = len(self.ops)
        op.seg = self._seg
        self.ops.append(op)
        return op

    def pe(self, fn, r=(), w=(), cost=None):
        return self.add("pe", fn, r, w, cost=cost)

    def dve(self, fn, r=(), w=()):
        return self.add("dve", fn, r, w)

    def act(self, fn, r=(), w=()):
        return self.add("act", fn, r, w)

    def pool(self, fn, r=(), w=()):
        return self.add("pool", fn, r, w)

    def dma(self, q, fn, r=(), w=()):
        return self.add(q, fn, r, w, dma=True)

    def barrier(self):
        self._seg += 1
        for e in self.ENGS:
            self.add(e, lambda eng: eng.drain(), (), (("__barX__", e),))
        for e in self.ENGS:
            op = self.add(e, None, tuple(("__barX__", f) for f in self.ENGS), ())
            self._bar_ops.add(op.idx)
        self._seg += 1

    def finalize(self):
        last_writer = {}
        readers = {}
        cnt = {e: 0 for e in self.ENGS}
        dma_sem_total = {}
        dma_rr = {e: 0 for e in self.ENGS}
        waited = {}
        deps_of = []
        all_dma = []
        for op in self.ops:
            deps = set()
            for r in op.reads:
                j = last_writer.get(r)
                if j is not None:
                    deps.add(j)
            for w in op.writes:
                j = last_writer.get(w)
                if j is not None:
                    deps.add(j)
                for j in readers.get(w, ()):
                    deps.add(j)
            if op.idx in self._bar_ops:
                deps.update(all_dma)
            if op.dma:
                all_dma.append(op.idx)
            deps.discard(op.idx)
            deps = set(j for j in deps if self.ops[j].fn is not None)
            deps_of.append(deps)
            for r in op.reads:
                readers.setdefault(r, []).append(op.idx)
            for w in op.writes:
                last_writer[w] = op.idx
                readers[w] = []
        need_inc = [False] * len(self.ops)
        for op in self.ops:
            for j in deps_of[op.idx]:
                pj = self.ops[j]
                if pj.dma or pj.eng != op.eng or op.dma or (self.same_sync and pj.eng != 'pe'):
                    need_inc[j] = True
        for op in self.ops:
            if op.dma:
                need_inc[op.idx] = True
        self.order = self._list_schedule(deps_of) if self.reorder else list(range(len(self.ops)))
        for oi in self.order:
            op = self.ops[oi]
            e = op.eng
            if op.dma:
                k = dma_rr[e] % self.n_dma_sems
                dma_rr[e] += 1
                key = ("d", e, k)
                prev = dma_sem_total.get(key, 0)
                if prev > 0 and waited.get((e, key), 0) < prev:
                    op.waits.append((key, prev))
                    waited[(e, key)] = prev
                dma_sem_total[key] = prev + 16
                op.sem = key
                op.ticket = prev + 16
                op.inc = True
            elif need_inc[op.idx]:
                cnt[e] += 1
                op.sem = ("c", e)
                op.ticket = cnt[e]
                op.inc = True
            for j in sorted(deps_of[op.idx]):
                pj = self.ops[j]
                if (not pj.dma) and pj.eng == e and not op.dma and (e == 'pe' or not self.same_sync):
                    continue
                if (not pj.dma) and pj.eng == e and op.dma:
                    pass
                key, val = pj.sem, pj.ticket
                if waited.get((e, key), 0) >= val:
                    continue
                waited[(e, key)] = val
                op.waits.append((key, val))

    def _list_schedule(self, deps_of):
        import heapq
        ops = self.ops
        n = len(ops)
        order = []
        i = 0
        SYNC = 0.25
        DMA_LAT = 2.5
        while i < n:
            j = i
            seg = ops[i].seg
            while j < n and ops[j].seg == seg:
                j += 1
            idxs = range(i, j)
            if j - i <= 2 or any(ops[k].fn is None for k in idxs) :
                order.extend(idxs)
                i = j
                continue
            indeg = {}
            users = {}
            for k in idxs:
                d = [x for x in deps_of[k] if i <= x < j]
                indeg[k] = len(d)
                for x in d:
                    users.setdefault(x, []).append(k)
            blev = {}
            for k in reversed(idxs):
                m_ = 0.0
                for u in users.get(k, ()):
                    if blev[u] > m_:
                        m_ = blev[u]
                blev[k] = m_ + ops[k].cost + (DMA_LAT if ops[k].dma else 0.0) + SYNC
            finish = {}
            ready_t = {k: 0.0 for k in idxs}
            efree = {e: 0.0 for e in self.ENGS}
            pend = {e: [] for e in self.ENGS}
            avail = {e: [] for e in self.ENGS}
            for k in idxs:
                if indeg[k] == 0:
                    heapq.heappush(pend[ops[k].eng], (0.0, k))
            done = 0
            tot = j - i
            sched = []
            while done < tot:
                best = None
                for e in self.ENGS:
                    T = efree[e]
                    while pend[e] and pend[e][0][0] <= T:
                        k_ = heapq.heappop(pend[e])[1]
                        heapq.heappush(avail[e], (-blev[k_], k_))
                    if avail[e]:
                        cand = (T, avail[e][0][1], e, True)
                    elif pend[e]:
                        cand = (pend[e][0][0], pend[e][0][1], e, False)
                    else:
                        continue
                    if best is None or cand[:2] < best[:2]:
                        best = cand
                start, k, e, from_avail = best
                if from_avail:
                    heapq.heappop(avail[e])
                else:
                    heapq.heappop(pend[e])
                op = ops[k]
                efree[e] = start + op.cost
                fin = start + op.cost + (DMA_LAT if op.dma else 0.0)
                finish[k] = fin
                sched.append((start, k))
                done += 1
                for u in users.get(k, ()):
                    lat = 0.0 if (ops[u].eng == e and not op.dma and e == "pe") else SYNC
                    ready_t[u] = max(ready_t[u], fin + lat)
                    indeg[u] -= 1
                    if indeg[u] == 0:
                        heapq.heappush(pend[ops[u].eng], (ready_t[u], u))
            sched.sort()
            order.extend(k for _, k in sched)
            i = j
        assert len(order) == n and len(set(order)) == n
        return order

    def emit(self, out_waits=()):
        nc = self.nc
        self.finalize()
        keys = set()
        for op in self.ops:
            if op.sem is not None:
                keys.add(op.sem)
        keys = sorted(keys)
        sems = {}
        import contextlib
        with contextlib.ExitStack() as st:
            for i, k in enumerate(keys):
                sems[k] = st.enter_context(nc.semaphore("s_" + "_".join(str(x) for x in k)))
            block = st.enter_context(nc.Block())
            ops = self.ops

            order = self.order

            def run(eng_name, eng):
                last = None
                for oi in order:
                    op = ops[oi]
                    if op.eng != eng_name:
                        continue
                    for (k, v) in op.waits:
                        eng.wait_ge(sems[k], v)
                    if op.fn is None:
                        assert not op.inc
                        continue
                    ins = op.fn(eng)
                    if op.inc:
                        ins.then_inc(sems[op.sem], 16 if op.dma else 1)
                    if op.dma:
                        last = op
                done = {}
                for op in ops:
                    if op.eng == eng_name and op.dma:
                        done[op.sem] = max(done.get(op.sem, 0), op.ticket)
                for k, v in done.items():
                    eng.wait_ge(sems[k], v)

            @block.tensor
            def _(e):
                run("pe", e)

            @block.vector
            def _(e):
                run("dve", e)

            @block.scalar
            def _(e):
                run("act", e)

            @block.gpsimd
            def _(e):
                run("pool", e)

            @block.sync
            def _(e):
                run("sp", e)


D_MODEL = 1024
D_IN = 9672
EPS = 1e-6
ROPE_THETA = 500000.0
OFF = dict(da_q=0, da_k=512, da_v=1024, da_gate=1536, mb_z=2048, mb_xbc=2560, mb_dt=3584, s5_u=3592,
           s5_gate=4104, mla_cq=4616, mla_ckv=4872, mla_krope=5000, mla_gate=5064, gate=5576)
TWO_PI = float(2.0 * np.pi)
import os as _os
DB_S5 = _os.environ.get('DB_S5', '1') == '1'
DB_SSD = _os.environ.get('DB_SSD', '0') == '1'
DB_DA = _os.environ.get('DB_DA', '1') == '1'
MAGIC = 12582912.0

PB = {}
_o = 0
for _n, _w in [("ng", 8), ("daq", 1), ("dak", 1), ("convw", 32), ("convb", 8), ("s5d", 4), ("s5bglu", 4),
               ("mqa", 2), ("mkva", 1), ("mqn_n", 1), ("mqn_r", 1), ("mkn_n", 1), ("mkn_r", 1),
               ("lamre", 16), ("lamim", 16), ("lstep", 16),
               ("subln", 128), ("lq1", 64), ("lk1", 64), ("lq2", 64), ("lk2", 64),
               ("dtb", 8), ("alog", 8), ("mbd", 512), ("mbnorm", 512)]:
    PB[_n] = (_o, _w)
    _o += _w
NP_COLS = _o
CB = {}
_o = 0
for _n, _w in [("ident", 128), ("ones", 128), ("blk64", 128), ("maskU", 128), ("ptda", 128), ("ptmla", 128),
               ("jrow", 128), ("invf_da", 1), ("invf_mla", 1)]:
    CB[_n] = (_o, _w)
    _o += _w
NC_COLS = _o


def host_consts():
    c = np.zeros((128, NC_COLS), np.float32)
    def put(n, a):
        o, w = CB[n]
        c[:a.shape[0], o:o + w] = a
    i = np.arange(128)
    put("ident", np.eye(128, dtype=np.float32))
    put("ones", np.ones((128, 128), np.float32))
    put("blk64", (i[:, None] // 64 == i[None, :] // 64).astype(np.float32))
    put("maskU", (i[:, None] <= i[None, :]).astype(np.float32))
    P = np.zeros((128, 128), np.float32)
    for b in range(2):
        for d in range(8):
            P[b * 64 + d, b * 64 + d + 8] = -1.0
            P[b * 64 + d + 8, b * 64 + d] = 1.0
    put("ptda", P.T.copy())
    P = np.zeros((128, 128), np.float32)
    for d in range(32):
        P[d, d + 32] = -1.0
        P[d + 32, d] = 1.0
    put("ptmla", P.T.copy())
    put("jrow", np.broadcast_to(np.arange(128, dtype=np.float32)[None, :], (128, 128)))
    f_da = (1.0 / (ROPE_THETA ** (np.arange(0, 16, 2, dtype=np.float32) / np.float32(16)))).astype(np.float32)
    f_mla = (1.0 / (ROPE_THETA ** (np.arange(0, 64, 2, dtype=np.float32) / np.float32(64)))).astype(np.float32)
    v = np.zeros((128, 1), np.float32)
    for p in range(128):
        d = p % 64
        if d < 16:
            v[p, 0] = f_da[d % 8]
    put("invf_da", v)
    v = np.zeros((128, 1), np.float32)
    for p in range(64):
        v[p, 0] = f_mla[p % 32]
    put("invf_mla", v)
    return c


def host_params(inp, L):
    pb = np.zeros((L, 128, NP_COLS), np.float32)
    s5rep = np.zeros((L, 3, 128, 2048), np.float32)
    s5B = np.zeros((L, 2, 128, 16, 128), np.float32)
    s5C = np.zeros((L, 2, 128, 16, 128), np.float32)
    p = np.arange(128)
    for l in range(L):
        def put(n, a):
            o, w = PB[n]
            a = np.asarray(a, np.float32)
            if a.ndim == 1:
                a = a[:, None]
            pb[l, :a.shape[0], o:o + w] = a
        def rep(n, vec):
            o, w = PB[n]
            pb[l, :, o:o + w] = np.asarray(vec, np.float32).reshape(1, w)
        put("ng", inp["norm_g"][l].reshape(8, 128).T)
        put("daq", inp["da_q_norm"][l][p % 64])
        put("dak", inp["da_k_norm"][l][p % 64])
        cw = inp["mb_conv_w"][l]
        put("convw", cw.reshape(4, 8, 128).transpose(2, 1, 0).reshape(128, 32))
        put("convb", inp["mb_conv_b"][l].reshape(8, 128).T)
        put("s5d", inp["s5_d"][l].reshape(4, 128).T)
        put("s5bglu", inp["s5_b_glu"][l].reshape(4, 128).T)
        put("mqa", inp["mla_q_a_norm"][l].reshape(2, 128).T)
        put("mkva", inp["mla_kv_a_norm"][l])
        put("mqn_n", inp["mla_q_norm"][l][:128])
        put("mqn_r", inp["mla_q_norm"][l][128:192])
        put("mkn_n", inp["mla_k_norm"][l][:128])
        put("mkn_r", inp["mla_k_norm"][l][128:192])
        lre = inp["s5_lam_re"][l].reshape(16, 128).T
        lim = inp["s5_lam_im"][l].reshape(16, 128).T
        lst = np.repeat(inp["s5_log_step"][l], 64).reshape(16, 128).T
        put("lamre", lre); put("lamim", lim); put("lstep", lst)
        rep("subln", inp["da_subln"][l])
        rep("lq1", inp["da_lambda_q1"][l]); rep("lk1", inp["da_lambda_k1"][l])
        rep("lq2", inp["da_lambda_q2"][l]); rep("lk2", inp["da_lambda_k2"][l])
        rep("dtb", inp["mb_dt_bias"][l]); rep("alog", inp["mb_a_log"][l])
        rep("mbd", np.repeat(inp["mb_d"][l], 64)); rep("mbnorm", inp["mb_norm"][l])
        s5rep[l, 0] = inp["s5_lam_re"][l].reshape(1, 2048)
        s5rep[l, 1] = inp["s5_lam_im"][l].reshape(1, 2048)
        s5rep[l, 2] = np.repeat(inp["s5_log_step"][l], 64).reshape(1, 2048)
        for g in range(32):
            mt, j = g // 2, g % 2
            gi = g % 8
            s5B[l, 0, gi * 16:(gi + 1) * 16, mt, j * 64:(j + 1) * 64] = inp["s5_b_re"][l, g].T
            s5B[l, 1, gi * 16:(gi + 1) * 16, mt, j * 64:(j + 1) * 64] = inp["s5_b_im"][l, g].T
            s5C[l, 0, j * 64:(j + 1) * 64, mt, gi * 16:(gi + 1) * 16] = inp["s5_c_re"][l, g].T
            s5C[l, 1, j * 64:(j + 1) * 64, mt, gi * 16:(gi + 1) * 16] = inp["s5_c_im"][l, g].T
    return pb, s5rep, s5B, s5C


class _Shift:
    def __init__(self, ap):
        self.ap = ap

    def __getitem__(self, idx):
        p, f = idx
        return self.ap[:, f]


class Arena:
    def __init__(self, ap32, ncols):
        self.ap = ap32
        self.n = ncols
        self.off = 0

    def reset(self):
        self.off = 0

    def f32(self, *shape):
        n = int(np.prod(shape))
        assert self.off + n <= self.n, ("arena overflow", self.off, n, self.n)
        v = self.ap[:, self.off:self.off + n]
        self.off += n
        if len(shape) == 2:
            v = v.rearrange("p (a b) -> p a b", a=shape[0])
        return v

    def b16(self, *shape):
        n = int(np.prod(shape))
        n32 = (n + 1) // 2
        assert self.off + n32 <= self.n, ("arena overflow", self.off, n32, self.n)
        v = self.ap[:, self.off:self.off + n32].bitcast(BF16)
        if n32 * 2 != n:
            v = v[:, 0:n]
        self.off += n32
        if len(shape) == 2:
            v = v.rearrange("p (a b) -> p a b", a=shape[0])
        return v


def build_program(S_LEN=2048, NSEQ=2, DEPTH=2, branches=(0, 1, 2, 3), ARENA_COLS=18944, debug=False):
    import contextlib
    NT = S_LEN // 128
    BLK = min(512, S_LEN)
    NB = S_LEN // BLK
    TPB = BLK // 128
    L = DEPTH
    nc = bass.Bass("TRN2", target_bir_lowering=False)
    dx = nc.dram_tensor("x", [NSEQ, S_LEN, D_MODEL], F32, kind="ExternalInput").ap()
    dpos = nc.dram_tensor("pos", [NSEQ, S_LEN], I32, kind="ExternalInput").ap()
    dwin = nc.dram_tensor("w_in", [L, D_MODEL, D_IN], F32, kind="ExternalInput").ap()
    dwbr = nc.dram_tensor("w_br", [L, 4, 512, D_MODEL], F32, kind="ExternalInput").ap()
    dwout = nc.dram_tensor("w_out", [L, D_MODEL, D_MODEL], F32, kind="ExternalInput").ap()
    dwuq = nc.dram_tensor("w_uq", [L, 256, 768], F32, kind="ExternalInput").ap()
    dwukv = nc.dram_tensor("w_ukv", [L, 128, 1024], F32, kind="ExternalInput").ap()
    dwglu = nc.dram_tensor("w_glu", [L, 512, 512], F32, kind="ExternalInput").ap()
    dpb = nc.dram_tensor("pblob", [L, 128, NP_COLS], F32, kind="ExternalInput").ap()
    dcb = nc.dram_tensor("cblob", [128, NC_COLS], F32, kind="ExternalInput").ap()
    ds5rep = nc.dram_tensor("s5rep", [L, 3, 128, 2048], F32, kind="ExternalInput").ap()
    ds5B = nc.dram_tensor("s5B", [L, 2, 128, 16, 128], F32, kind="ExternalInput").ap()
    ds5C = nc.dram_tensor("s5C", [L, 2, 128, 16, 128], F32, kind="ExternalInput").ap()
    dout = nc.dram_tensor("out", [NSEQ, S_LEN, D_MODEL], F32, kind="ExternalOutput").ap()
    if debug:
        dbg_ht = nc.dram_tensor("dbg_ht", [128, 8 * S_LEN], F32, kind="ExternalOutput").ap()
        dbg_yb = nc.dram_tensor("dbg_yb", [128, 4 * S_LEN], F32, kind="ExternalOutput").ap()
        dbg_mg = nc.dram_tensor("dbg_mg", [128, 8 * S_LEN], F32, kind="ExternalOutput").ap()

    S = Sched(nc)
    st = contextlib.ExitStack()

    def sb(name, shape, dt=F32):
        return st.enter_context(nc.sbuf_tensor(name, shape, dt))[:]

    def psum(name, shape, dt=F32):
        return st.enter_context(nc.psum_tensor(name, shape, dt))[:]

    uid = [0]

    def U(prefix):
        uid[0] += 1
        return (prefix, uid[0])

    with st:
        CF = sb("CF", [128, NC_COLS])
        CBF = sb("CBF", [128, 6 * 128], BF16)
        PBL = sb("PBL", [128, NP_COLS])
        DRV = sb("DRV", [128, 256])
        HT = sb("HT", [128, 8, S_LEN], BF16)
        MG = sb("MG", [128, 8, S_LEN], BF16)
        YB = sb("YB", [128, 4, S_LEN], BF16)
        WB = [sb("WB%d" % i, [128, 4096], BF16) for i in range(3)]
        ARN = sb("ARN", [128, ARENA_COLS])
        MSG = [sb("MSG%d" % i, [128, 512], BF16) for i in range(2)]
        MTP = [sb("MTP%d" % i, [128, 512], BF16) for i in range(2)]
        AR = Arena(ARN, ARENA_COLS)
        PS = [psum("PS%d" % i, [128, 512]) for i in range(7)]
        PSB = psum("PSB", [128, 1024], BF16)

        def cf(n):
            o, w = CB[n]
            return CF[:, o:o + w]

        def pbl(n, a=None, b=None):
            o, w = PB[n]
            if a is None:
                return PBL[:, o:o + w]
            return PBL[:, o + a:o + (a + 1 if b is None else b)]

        IDB = CBF[:, 0:128]; ONESB = CBF[:, 128:256]; BLK64B = CBF[:, 256:384]
        MASKUB = CBF[:, 384:512]; PTDAB = CBF[:, 512:640]; PTMLAB = CBF[:, 640:768]
        ONESF = cf("ones"); MASKUF = cf("maskU")
        D_EPS = DRV[:, 0:1]; D_ONE = DRV[:, 1:2]; D_GQ = DRV[:, 2:3]; D_NEGLAM = DRV[:, 3:4]
        D_MQN_N = DRV[:, 4:5]; D_MQN_R = DRV[:, 5:6]; D_T1 = DRV[:, 6:7]; D_T2 = DRV[:, 7:8]
        D_NEGA = DRV[:, 8:16]; D_NEGONES = DRV[:, 16:144]

        wb_rr = [0]

        def wslot():
            i = wb_rr[0] % 3
            wb_rr[0] += 1
            return i

        def load_w(slot, src_ap, view):
            S.dma("pool", lambda e, o=view, i=src_ap: e.dma_start(out=o, in_=i), r=(), w=(("wb", slot),))

        def win_view(l, c0, ncols):
            return dwin[l, :, c0:c0 + ncols].rearrange("(kc p) c -> p kc c", p=128)

        def wb_view(slot, kc, ncols, off=0):
            return WB[slot][:, off:off + kc * ncols].rearrange("p (k c) -> p k c", k=kc)

        def mm(out_ap, pairs, r, w):
            def fn(e, out_ap=out_ap, pairs=pairs):
                ins = None
                n = len(pairs)
                for i, (a, b) in enumerate(pairs):
                    ins = e.matmul(out_ap, lhsT=a, rhs=b, start=(i == 0), stop=(i == n - 1))
                return ins
            cst = 0.06
            for (a_, b_) in pairs:
                ncol = int(np.prod(b_.shape[1:]))
                cst += max(ncol, 64) / 1400.0 * (4.0 if a_.dtype == F32 else 1.0)
            S.pe(fn, r, w, cost=cst)

        def proj_pairs(wv, c0, mw, tok0, ntok):
            return [(wv[:, kc, c0:c0 + mw], HT[:, kc, tok0:tok0 + ntok]) for kc in range(8)]

        def tok_pairs(wv, c0, ncols, tile):
            return [(HT[:, kc, tile * 128:(tile + 1) * 128], wv[:, kc, c0:c0 + ncols]) for kc in range(8)]

        def rsqrt_act(out_ap, in_ap, scale, r, w):
            S.act(lambda e: e.activation(out=out_ap, in_=in_ap, func=AF.Ln, scale=scale, bias=D_EPS[0:out_ap.shape[0], :]), r, w)
            S.act(lambda e: e.activation(out=out_ap, in_=out_ap, func=AF.Exp, scale=-0.5), w, w)

        def sincos(v_ap, tmp_ap, out_sin, out_cos, r, kv, kt, ks, kc_):
            S.dve(lambda e: e.tensor_scalar(out=tmp_ap, in0=v_ap, scalar1=MAGIC, scalar2=None, op0=ALU.add), r + (kv,), (kt,))
            S.dve(lambda e: e.tensor_scalar(out=tmp_ap, in0=tmp_ap, scalar1=-MAGIC, scalar2=None, op0=ALU.add), (kt,), (kt,))
            S.dve(lambda e: e.tensor_tensor(out=tmp_ap, in0=v_ap, in1=tmp_ap, op=ALU.subtract), (kv, kt), (kt,))
            S.act(lambda e: e.activation(out=out_sin, in_=tmp_ap, func=AF.Sin, scale=TWO_PI), (kt,), (ks,))
            S.dve(lambda e: e.tensor_scalar(out=tmp_ap, in0=v_ap, scalar1=0.25, scalar2=MAGIC, op0=ALU.add, op1=ALU.add), (kv, ks), (kt,))
            S.dve(lambda e: e.tensor_scalar(out=tmp_ap, in0=tmp_ap, scalar1=-MAGIC, scalar2=None, op0=ALU.add), (kt,), (kt,))
            S.dve(lambda e: e.scalar_tensor_tensor(out=tmp_ap, in0=v_ap, scalar=0.25, in1=tmp_ap, op0=ALU.add, op1=ALU.subtract), (kv, kt), (kt,))
            S.act(lambda e: e.activation(out=out_cos, in_=tmp_ap, func=AF.Sin, scale=TWO_PI), (kt,), (kc_,))

        S.dma("sp", lambda e: e.dma_start(out=CF, in_=dcb), w=("CF",))
        for i, n in enumerate(["ident", "ones", "blk64", "maskU", "ptda", "ptmla"]):
            S.dve(lambda e, i=i, n=n: e.tensor_copy(out=CBF[:, i * 128:(i + 1) * 128], in_=cf(n)), ("CF",), ("CBF",))
        S.dve(lambda e: e.memset(D_EPS, EPS), (), ("DRVc",))
        S.dve(lambda e: e.memset(D_ONE, 1.0), (), ("DRVc",))
        S.dve(lambda e: e.memset(D_ONE, 1.0), (), ("DRVc",))
        S.dve(lambda e: e.memset(D_NEGONES, -1.0), (), ("DRVc",))
        CONST_R = ("CF", "CBF", "DRVc")

        def layer_setup(l):
            lam_init = 0.8 - 0.6 * float(np.exp(-0.3 * l))
            S.dma("sp", lambda e: e.dma_start(out=PBL, in_=dpb[l]), w=("PBL",))
            kd = "DRV"
            S.dve(lambda e: e.tensor_scalar(out=D_GQ, in0=pbl("daq"), scalar1=0.125, scalar2=None, op0=ALU.mult), ("PBL",), (kd,))
            S.dve(lambda e: e.tensor_scalar(out=D_MQN_N, in0=pbl("mqn_n"), scalar1=float(192 ** -0.5), scalar2=None, op0=ALU.mult), ("PBL",), (kd,))
            S.dve(lambda e: e.tensor_scalar(out=D_MQN_R, in0=pbl("mqn_r"), scalar1=float(192 ** -0.5), scalar2=None, op0=ALU.mult), ("PBL",), (kd,))
            tmp = DRV[:, 144:208]
            S.dve(lambda e: e.tensor_tensor(out=tmp, in0=pbl("lq1"), in1=pbl("lk1"), op=ALU.mult), ("PBL",), (kd,))
            S.dve(lambda e: e.tensor_reduce(out=D_T1, in_=tmp, axis=AX.X, op=ALU.add), (kd,), (kd,))
            S.dve(lambda e: e.tensor_tensor(out=tmp, in0=pbl("lq2"), in1=pbl("lk2"), op=ALU.mult), ("PBL",), (kd,))
            S.dve(lambda e: e.tensor_reduce(out=D_T2, in_=tmp, axis=AX.X, op=ALU.add), (kd,), (kd,))
            S.act(lambda e: e.activation(out=D_T1, in_=D_T1, func=AF.Exp), (kd,), (kd,))
            S.act(lambda e: e.activation(out=D_T2, in_=D_T2, func=AF.Exp), (kd,), (kd,))
            S.dve(lambda e: e.tensor_tensor(out=D_NEGLAM, in0=D_T2, in1=D_T1, op=ALU.subtract), (kd,), (kd,))
            S.dve(lambda e: e.tensor_scalar(out=D_NEGLAM, in0=D_NEGLAM, scalar1=-lam_init, scalar2=None, op0=ALU.add), (kd,), (kd,))
            S.dve(lambda e: e.tensor_scalar(out=pbl("subln"), in0=pbl("subln"), scalar1=float(1.0 - lam_init), scalar2=None, op0=ALU.mult), ("PBL", kd), ("PBL",))
            S.act(lambda e: e.activation(out=D_NEGA, in_=pbl("alog"), func=AF.Exp), ("PBL",), (kd,))
            S.dve(lambda e: e.tensor_scalar(out=D_NEGA, in0=D_NEGA, scalar1=-1.0, scalar2=None, op0=ALU.mult), (kd,), (kd,))
        LR = ("PBL", "DRV") + CONST_R

        def stage0(s, l):
            AR.reset()
            xt = [AR.f32(D_MODEL) for _ in range(2)]
            junk = AR.f32(D_MODEL)
            xn = AR.b16(TPB, D_MODEL)
            ssq = AR.f32(8)
            src = dx if l == 0 else dout
            for n in range(NB):
                for tt in range(TPB):
                    t = n * TPB + tt
                    b = t % 2
                    kx = ("xt", b)
                    S.dma("sp", lambda e, b=b, t=t: e.dma_start(out=xt[b], in_=src[s, t * 128:(t + 1) * 128, :]),
                          r=(("outd", s, t),), w=(kx,))
                    S.act(lambda e, b=b: e.activation(out=junk, in_=xt[b], func=AF.Square), (kx,), ("junk",))
                    S.dve(lambda e: e.tensor_reduce(out=ssq[:, 0:1], in_=junk, axis=AX.X, op=ALU.add), ("junk",), ("ssq",))
                    rsqrt_act(ssq[:, 0:1], ssq[:, 0:1], 1.0 / D_MODEL, ("ssq",) + LR, ("ssq",))
                    S.dve(lambda e, b=b, tt=tt: e.tensor_scalar(out=xn[:, tt, :], in0=xt[b], scalar1=ssq[:, 0:1], scalar2=None, op0=ALU.mult),
                          (kx, "ssq"), (("xn", tt),))
                for kc in range(8):
                    half = kc % 2
                    pt = PSB[:, half * 512:half * 512 + BLK]
                    def fn(e, kc=kc, pt=pt):
                        ins = None
                        for tt in range(TPB):
                            ins = e.transpose(out=pt[:, tt * 128:(tt + 1) * 128], in_=xn[:, tt, kc * 128:(kc + 1) * 128], identity=IDB)
                        return ins
                    S.pe(fn, tuple(("xn", tt) for tt in range(TPB)) + LR, (("psb", 0),))
                    S.dve(lambda e, kc=kc, pt=pt, n=n: e.tensor_scalar(out=HT[:, kc, n * BLK:(n + 1) * BLK], in0=pt, scalar1=pbl("ng", kc), scalar2=None, op0=ALU.mult),
                          (("psb", 0),) + LR, (("HT", n),))

        def merge(s, l, b, first):
            sg = [MSG[i][:, 0:BLK] for i in range(2)]
            tmp = [MTP[i][:, 0:BLK] for i in range(2)]
            sl_br = wslot()
            load_w(sl_br, dwbr[l, b].rearrange("(kc p) c -> p kc c", p=128), wb_view(sl_br, 4, 1024))
            wbr = wb_view(sl_br, 4, 1024)
            it = 0
            for fh in range(2):
                slg = wslot()
                load_w(slg, win_view(l, OFF["gate"] + b * 1024 + fh * 512, 512), wb_view(slg, 8, 512))
                wg = wb_view(slg, 8, 512)
                for f4 in range(4):
                    f = fh * 4 + f4
                    for n in range(NB):
                        pg = PS[(2 * it) % 6]; pb_ = PS[(2 * it + 1) % 6]
                        kpg = ("ps", (2 * it) % 6); kpb = ("ps", (2 * it + 1) % 6)
                        bi = it % 2
                        it += 1
                        mm(pg[:, 0:BLK], proj_pairs(wg, f4 * 128, 128, n * BLK, BLK), (("wb", slg), ("HT", n)), (kpg,))
                        mm(pb_[:, 0:BLK], [(wbr[:, k4, f * 128:(f + 1) * 128], YB[:, k4, n * BLK:(n + 1) * BLK]) for k4 in range(4)],
                           (("wb", sl_br), ("YB", n)), (kpb,))
                        S.act(lambda e, bi=bi, pg=pg: e.activation(out=sg[bi], in_=pg[:, 0:BLK], func=AF.Sigmoid), (kpg,), (("sg", bi),))
                        mgv = MG[:, f, n * BLK:(n + 1) * BLK]
                        if first:
                            S.dve(lambda e, bi=bi, pb_=pb_, mgv=mgv: e.tensor_tensor(out=mgv, in0=sg[bi], in1=pb_[:, 0:BLK], op=ALU.mult),
                                  (("sg", bi), kpb), (("MG", f, n),))
                        else:
                            S.dve(lambda e, bi=bi, pb_=pb_: e.tensor_tensor(out=tmp[bi], in0=sg[bi], in1=pb_[:, 0:BLK], op=ALU.mult),
                                  (("sg", bi), kpb), (("mtmp", bi),))
                            S.pool(lambda e, bi=bi, mgv=mgv: e.tensor_tensor(out=mgv, in0=mgv, in1=tmp[bi], op=ALU.add),
                                   (("mtmp", bi), ("MG", f, n)), (("MG", f, n),))

        def outproj(s, l):
            AR.reset()
            xt = [AR.f32(D_MODEL) for _ in range(2)]
            ot = [AR.f32(D_MODEL) for _ in range(2)]
            src = dx if l == 0 else dout
            sls = []
            for hh in range(2):
                sl = wslot()
                load_w(sl, dwout[l, :, hh * 512:(hh + 1) * 512].rearrange("(kc p) c -> p kc c", p=128), wb_view(sl, 8, 512))
                sls.append(sl)
            for t in range(NT):
                b = t % 2
                S.dma("sp", lambda e, b=b, t=t: e.dma_start(out=xt[b], in_=src[s, t * 128:(t + 1) * 128, :]),
                      r=(("outd", s, t),), w=(("xt", b),))
                for hh in range(2):
                    p = PS[(2 * t + hh) % 4]
                    kp = ("ps", (2 * t + hh) % 4)
                    wv = wb_view(sls[hh], 8, 512)
                    mm(p[:, 0:512], [(MG[:, kc, t * 128:(t + 1) * 128], wv[:, kc, :]) for kc in range(8)],
                       (("wb", sls[hh]),) + tuple(("MG", kc, t // TPB) for kc in range(8)), (kp,))
                    S.dve(lambda e, b=b, hh=hh, p=p: e.tensor_tensor(out=ot[b][:, hh * 512:(hh + 1) * 512], in0=p[:, 0:512], in1=xt[b][:, hh * 512:(hh + 1) * 512], op=ALU.add),
                          (kp, ("xt", b)), (("ot", b, hh),))
                S.dma("sp", lambda e, b=b, t=t: e.dma_start(out=dout[s, t * 128:(t + 1) * 128, :], in_=ot[b]),
                      r=(("ot", b, 0), ("ot", b, 1)), w=(("outd", s, t),))

        def rope_tables(s, invf_col, npart, Ct, St):
            pi_ = AR.f32(BLK).bitcast(I32)
            v = AR.f32(BLK); tmp = AR.f32(BLK)
            for n in range(NB):
                S.dma("sp", lambda e, n=n: e.dma_start(out=pi_[0:npart, :], in_=dpos[s:s + 1, n * BLK:(n + 1) * BLK].partition_broadcast(npart)),
                      w=("rp_pi",))
                S.dve(lambda e: e.tensor_copy(out=v[0:npart, :], in_=pi_[0:npart, :]), ("rp_pi",), ("rp_v",))
                S.dve(lambda e: e.tensor_scalar(out=v[0:npart, :], in0=v[0:npart, :], scalar1=invf_col[0:npart, :], scalar2=float(1.0 / TWO_PI), op0=ALU.mult, op1=ALU.mult),
                      ("rp_v",) + CONST_R, ("rp_v",))
                sincos(v[0:npart, :], tmp[0:npart, :], St[0:npart, n * BLK:(n + 1) * BLK], Ct[0:npart, n * BLK:(n + 1) * BLK],
                       (), "rp_v", "rp_t", ("rp_S", n), ("rp_C", n))

        def rope_apply(dst, xn_ap, pp, kpp, PT, npart, Ct, St, n, t1, t2, rkeys, wkey, par=0):
            mm(pp[0:npart, 0:BLK], [(PT[0:npart, 0:npart], xn_ap)], rkeys + CONST_R, (kpp,))
            S.dve(lambda e: e.tensor_tensor(out=t1[0:npart, :], in0=xn_ap, in1=Ct[0:npart, n * BLK:(n + 1) * BLK], op=ALU.mult),
                  rkeys + (("rp_C", n),), (("rp_t1", par),))
            S.dve(lambda e: e.tensor_tensor(out=t2[0:npart, :], in0=pp[0:npart, 0:BLK], in1=St[0:npart, n * BLK:(n + 1) * BLK], op=ALU.mult),
                  (kpp, ("rp_S", n)), (("rp_t2", par),))
            S.pool(lambda e: e.tensor_tensor(out=dst, in0=t1[0:npart, :], in1=t2[0:npart, :], op=ALU.add), (("rp_t1", par), ("rp_t2", par)), (wkey,))

        def attn_block(qb, parts, vaug, PTt, dest, dkey):
            t0 = qb * TPB
            for j in range(t0 + TPB):
                lo = max(0, j - t0)
                c0 = lo * 128
                bi = j % 2
                st_ = PS[4 + bi]
                kst = ("ps", 4 + bi)
                rk = tuple(p[3](j) for p in parts) + tuple(p[4] for p in parts)
                mm(st_[:, c0:BLK], [(p[0][0:p[2], j * 128:(j + 1) * 128], p[1][0:p[2], c0:BLK]) for p in parts], rk, (kst,))
                S.act(lambda e, bi=bi, st_=st_, c0=c0: e.activation(out=PTt[bi][:, c0:BLK], in_=st_[:, c0:BLK], func=AF.Exp), (kst,), (("ptt", bi),))
                if j >= t0:
                    S.pool(lambda e, bi=bi, c0=c0: e.tensor_tensor(out=PTt[bi][:, c0:c0 + 128], in0=PTt[bi][:, c0:c0 + 128], in1=MASKUB, op=ALU.mult),
                           (("ptt", bi),) + CONST_R, (("ptt", bi),))
                for i in range(lo, TPB):
                    def fn(e, i=i, j=j, bi=bi):
                        return e.matmul(PS[i][:, 0:130], lhsT=PTt[bi][:, i * 128:(i + 1) * 128], rhs=vaug[:, j, :],
                                        start=(j == 0), stop=(j == t0 + i))
                    S.pe(fn, (("ptt", bi), ("vaug", j)), (("ps", i),))
            rec = AR_small["rec"]
            for i in range(TPB):
                S.dve(lambda e, i=i: e.reciprocal(out=rec[:, i:i + 1], in_=PS[i][:, 128:129]), (("ps", i),), (("rec", i),))
                S.dve(lambda e, i=i: e.tensor_scalar(out=dest[:, i, :], in0=PS[i][:, 0:128], scalar1=rec[:, i:i + 1], scalar2=None, op0=ALU.mult),
                      (("ps", i), ("rec", i)), (dkey + (i,),))

        AR_small = {}

        def finish_head(l, h, qb, obf, okeys, gate_c0, wg_view, wg_key, gs):
            pt = PSB[:, 0:BLK]
            def fn(e):
                ins = None
                for i in range(TPB):
                    ins = e.transpose(out=pt[:, i * 128:(i + 1) * 128], in_=obf[:, i, :], identity=IDB)
                return ins
            S.pe(fn, okeys + CONST_R, (("psb", 0),))
            pg = PS[6]
            mm(pg[:, 0:BLK], proj_pairs(wg_view, gate_c0, 128, qb * BLK, BLK), (wg_key, ("HT", qb)), (("ps", 6),))
            S.act(lambda e: e.activation(out=gs, in_=pg[:, 0:BLK], func=AF.Silu), (("ps", 6),), ("gsilu",))
            S.dve(lambda e: e.tensor_tensor(out=YB[:, h, qb * BLK:(qb + 1) * BLK], in0=pt, in1=gs, op=ALU.mult),
                  (("psb", 0), "gsilu"), (("YB", qb),))

        def branch_da(s, l):
            AR.reset()
            Ct = AR.b16(S_LEN); St = AR.b16(S_LEN)
            rope_tables(s, cf("invf_da"), 128, Ct, St)
            qT = AR.b16(S_LEN); kT = AR.b16(S_LEN)
            vaug = AR.b16(NT, 130)
            PTt = [AR.b16(BLK) for _ in range(2)]
            sqs = [AR.b16(BLK) for _ in range(2)]; qns = [AR.b16(BLK) for _ in range(2)]; obf = AR.b16(TPB, 128); gs = AR.f32(BLK)
            rss = [AR.f32(BLK) for _ in range(2)]; t1s = [AR.f32(BLK) for _ in range(2)]; t2s = [AR.f32(BLK) for _ in range(2)]
            d0 = AR.f32(TPB, 128); d1 = AR.f32(TPB, 128); o = AR.f32(TPB, 128); junk = AR.f32(TPB, 128)
            ssq = AR.f32(TPB); AR_small["rec"] = AR.f32(TPB)
            S.dve(lambda e: e.memset(vaug[:, :, 128:130], 1.0), (), tuple(("vaug", j) for j in range(NT)))
            for h in range(4):
                sl = wslot()
                wv = WB[sl].rearrange("p (q k c) -> p q k c", q=4, k=8)
                for qi, nm in enumerate(["da_q", "da_k", "da_v", "da_gate"]):
                    load_w(sl, win_view(l, OFF[nm] + h * 128, 128), wv[:, qi])
                wkey = ("wb", sl)
                def da_qk(which, dstT, gcol, n, par):
                    sq, rs, qn, t1, t2 = sqs[par], rss[par], qns[par], t1s[par], t2s[par]
                    pq = PS[par]; kpq = ("ps", par)
                    pss = PS[2 + par]; kpss = ("ps", 2 + par)
                    ppm = PS[4 + par]; kppm = ("ps", 4 + par)
                    mm(pq[:, 0:BLK], proj_pairs(wv[:, which], 0, 128, n * BLK, BLK), (wkey, ("HT", n)), (kpq,))
                    S.act(lambda e: e.activation(out=sq, in_=pq[:, 0:BLK], func=AF.Square), (kpq,), (("sq", par),))
                    mm(pss[:, 0:BLK], [(BLK64B, sq)], (("sq", par),) + CONST_R, (kpss,))
                    rsqrt_act(rs, pss[:, 0:BLK], 1.0 / 64, (kpss,) + LR, (("rs", par),))
                    S.dve(lambda e: e.scalar_tensor_tensor(out=qn, in0=pq[:, 0:BLK], scalar=gcol, in1=rs, op0=ALU.mult, op1=ALU.mult),
                          (kpq, ("rs", par)) + LR, (("qn", par),))
                    rope_apply(dstT[:, n * BLK:(n + 1) * BLK], qn, ppm, kppm, PTDAB, 128, Ct, St, n, t1, t2, (("qn", par),),
                               (("qT" if which == 0 else "kT"), n), par=par)
                it_ = 0
                for which, dstT, gcol in ((0, qT, D_GQ), (1, kT, pbl("dak"))):
                    for n in range(NB):
                        da_qk(which, dstT, gcol, n, (it_ % 2) if DB_DA else 0)
                        it_ += 1
                for n in range(NB):
                    pv = PS[n % 2]; kpv = ("ps", n % 2)
                    for tt in range(TPB):
                        t = n * TPB + tt
                        mm(pv[:, tt * 128:(tt + 1) * 128], tok_pairs(wv[:, 2], 0, 128, t), (wkey, ("HT", n)), (kpv,))
                    S.act(lambda e, pv=pv, n=n: e.activation(out=vaug[:, n * TPB:(n + 1) * TPB, 0:128], in_=pv[:, 0:BLK].rearrange("p (a b) -> p a b", a=TPB), func=AF.Copy),
                          (kpv,), tuple(("vaug", n * TPB + tt) for tt in range(TPB)))
                for qb in range(NB):
                    for m, dst in ((0, d0), (1, d1)):
                        kv = kT[m * 64:(m + 1) * 64, :]
                        qv = qT[m * 64:(m + 1) * 64, qb * BLK:(qb + 1) * BLK]
                        parts = [(_Shift(kv), _Shift(qv), 64, (lambda j: ("kT", j // TPB)), ("qT", qb))]
                        attn_block(qb, parts, vaug, PTt, dst, ("dd", m))
                    for i in range(TPB):
                        S.dve(lambda e, i=i: e.scalar_tensor_tensor(out=o[:, i, :], in0=d1[:, i, :], scalar=D_NEGLAM, in1=d0[:, i, :], op0=ALU.mult, op1=ALU.add),
                              (("dd", 0, i), ("dd", 1, i)) + LR, (("o", i),))
                    S.act(lambda e: e.activation(out=junk, in_=o, func=AF.Square), tuple(("o", i) for i in range(TPB)), ("ojunk",))
                    S.dve(lambda e: e.tensor_reduce(out=ssq, in_=junk, axis=AX.X, op=ALU.add), ("ojunk",), ("ossq",))
                    rsqrt_act(ssq, ssq, 1.0 / 128, ("ossq",) + LR, ("ossq",))
                    for i in range(TPB):
                        S.dve(lambda e, i=i: e.scalar_tensor_tensor(out=obf[:, i, :], in0=o[:, i, :], scalar=ssq[:, i:i + 1], in1=pbl("subln"), op0=ALU.mult, op1=ALU.mult),
                              (("o", i), "ossq") + LR, (("obf", i),))
                    finish_head(l, h, qb, obf, tuple(("obf", i) for i in range(TPB)), 0, wv[:, 3], wkey, gs)

        def branch_mla(s, l):
            AR.reset()
            Ct = AR.b16(S_LEN); St = AR.b16(S_LEN)
            rope_tables(s, cf("invf_mla"), 64, Ct, St)
            cqn = AR.b16(2, S_LEN); ckvn = AR.b16(S_LEN); krr = AR.b16(S_LEN)
            knope = AR.b16(S_LEN); krope = AR.b16(S_LEN)
            vaug = AR.b16(NT, 130)
            qnope = AR.b16(BLK); qrope = AR.b16(BLK)
            PTt = [AR.b16(BLK) for _ in range(2)]
            sqA = AR.b16(BLK); sqB = AR.b16(BLK); tb = AR.b16(BLK); obf = AR.b16(TPB, 128)
            wuq = AR.b16(2, 768); wukv = AR.b16(1024)
            gs = AR.f32(BLK); rs = AR.f32(BLK); t1 = AR.f32(BLK); t2 = AR.f32(BLK)
            o = AR.f32(TPB, 128); AR_small["rec"] = AR.f32(TPB)
            S.dve(lambda e: e.memset(vaug[:, :, 128:130], 1.0), (), tuple(("vaug", j) for j in range(NT)))
            S.dma("pool", lambda e: e.dma_start(out=wuq, in_=dwuq[l].rearrange("(kc p) c -> p kc c", p=128)), w=("wuq",))
            S.dma("pool", lambda e: e.dma_start(out=wukv, in_=dwukv[l]), w=("wukv",))
            sl = wslot()
            wc = wb_view(sl, 8, 448)
            load_w(sl, win_view(l, OFF["mla_cq"], 448), wc)
            wkey = ("wb", sl)
            for n in range(NB):
                blk = slice(n * BLK, (n + 1) * BLK)
                for c in range(2):
                    mm(PS[c][:, 0:BLK], proj_pairs(wc, c * 128, 128, n * BLK, BLK), (wkey, ("HT", n)), (("ps", c),))
                S.act(lambda e: e.activation(out=sqA, in_=PS[0][:, 0:BLK], func=AF.Square), (("ps", 0),), ("sqA",))
                S.act(lambda e: e.activation(out=sqB, in_=PS[1][:, 0:BLK], func=AF.Square), (("ps", 1),), ("sqB",))
                mm(PS[2][:, 0:BLK], [(ONESB, sqA), (ONESB, sqB)], ("sqA", "sqB") + CONST_R, (("ps", 2),))
                rsqrt_act(rs, PS[2][:, 0:BLK], 1.0 / 256, (("ps", 2),) + LR, ("rs",))
                for c in range(2):
                    S.dve(lambda e, c=c, blk=blk: e.scalar_tensor_tensor(out=cqn[:, c, blk], in0=PS[c][:, 0:BLK], scalar=pbl("mqa", c), in1=rs, op0=ALU.mult, op1=ALU.mult),
                          (("ps", c), "rs") + LR, (("cqn", n),))
                mm(PS[3][:, 0:BLK], proj_pairs(wc, 256, 128, n * BLK, BLK), (wkey, ("HT", n)), (("ps", 3),))
                S.act(lambda e: e.activation(out=sqA, in_=PS[3][:, 0:BLK], func=AF.Square), (("ps", 3),), ("sqA",))
                mm(PS[2][:, 0:BLK], [(ONESB, sqA)], ("sqA",) + CONST_R, (("ps", 2),))
                rsqrt_act(rs, PS[2][:, 0:BLK], 1.0 / 128, (("ps", 2),) + LR, ("rs",))
                S.dve(lambda e, blk=blk: e.scalar_tensor_tensor(out=ckvn[:, blk], in0=PS[3][:, 0:BLK], scalar=pbl("mkva"), in1=rs, op0=ALU.mult, op1=ALU.mult),
                      (("ps", 3), "rs") + LR, (("ckvn", n),))
                mm(PS[6][0:64, 0:BLK], proj_pairs(wc, 384, 64, n * BLK, BLK), (wkey, ("HT", n)), (("ps", 6),))
                S.act(lambda e, blk=blk: e.activation(out=krr[0:64, blk], in_=PS[6][0:64, 0:BLK], func=AF.Copy), (("ps", 6),), (("krr", n),))
            slg = wslot()
            wgv = wb_view(slg, 8, 512)
            load_w(slg, win_view(l, OFF["mla_gate"], 512), wgv)
            for h in range(4):
                for n in range(NB):
                    blk = slice(n * BLK, (n + 1) * BLK)
                    mm(PS[0][:, 0:BLK], [(wukv[:, h * 256:h * 256 + 128], ckvn[:, blk])], ("wukv", ("ckvn", n)), (("ps", 0),))
                    S.act(lambda e: e.activation(out=sqA, in_=PS[0][:, 0:BLK], func=AF.Square), (("ps", 0),), ("sqA",))
                    S.act(lambda e, blk=blk: e.activation(out=sqB[0:64, :], in_=krr[0:64, blk], func=AF.Square), (("krr", n),), ("sqB",))
                    mm(PS[2][:, 0:BLK], [(ONESB, sqA), (ONESB[0:64, :], sqB[0:64, :])], ("sqA", "sqB") + CONST_R, (("ps", 2),))
                    rsqrt_act(rs, PS[2][:, 0:BLK], 1.0 / 192, (("ps", 2),) + LR, ("rs",))
                    S.dve(lambda e, blk=blk: e.scalar_tensor_tensor(out=knope[:, blk], in0=PS[0][:, 0:BLK], scalar=pbl("mkn_n"), in1=rs, op0=ALU.mult, op1=ALU.mult),
                          (("ps", 0), "rs") + LR, (("knope", n),))
                    S.dve(lambda e, blk=blk: e.scalar_tensor_tensor(out=tb[0:64, :], in0=krr[0:64, blk], scalar=pbl("mkn_r")[0:64, :], in1=rs[0:64, :], op0=ALU.mult, op1=ALU.mult),
                          (("krr", n), "rs") + LR, ("tb",))
                    rope_apply(krope[0:64, blk], tb[0:64, :], PS[3], ("ps", 3), PTMLAB, 64, Ct, St, n, t1, t2, ("tb",), ("krope", n))
                    pv = PS[1]
                    for tt in range(TPB):
                        t = n * TPB + tt
                        mm(pv[:, tt * 128:(tt + 1) * 128], [(ckvn[:, t * 128:(t + 1) * 128], wukv[:, h * 256 + 128:h * 256 + 256])],
                           ("wukv", ("ckvn", n)), (("ps", 1),))
                    S.act(lambda e, n=n: e.activation(out=vaug[:, n * TPB:(n + 1) * TPB, 0:128], in_=PS[1][:, 0:BLK].rearrange("p (a b) -> p a b", a=TPB), func=AF.Copy),
                          (("ps", 1),), tuple(("vaug", n * TPB + tt) for tt in range(TPB)))
                for qb in range(NB):
                    blk = slice(qb * BLK, (qb + 1) * BLK)
                    c0 = h * 192
                    mm(PS[0][:, 0:BLK], [(wuq[:, c, c0:c0 + 128], cqn[:, c, blk]) for c in range(2)], ("wuq", ("cqn", qb)), (("ps", 0),))
                    mm(PS[1][0:64, 0:BLK], [(wuq[:, c, c0 + 128:c0 + 192], cqn[:, c, blk]) for c in range(2)], ("wuq", ("cqn", qb)), (("ps", 1),))
                    S.act(lambda e: e.activation(out=sqA, in_=PS[0][:, 0:BLK], func=AF.Square), (("ps", 0),), ("sqA",))
                    S.act(lambda e: e.activation(out=sqB[0:64, :], in_=PS[1][0:64, 0:BLK], func=AF.Square), (("ps", 1),), ("sqB",))
                    mm(PS[2][:, 0:BLK], [(ONESB, sqA), (ONESB[0:64, :], sqB[0:64, :])], ("sqA", "sqB") + CONST_R, (("ps", 2),))
                    rsqrt_act(rs, PS[2][:, 0:BLK], 1.0 / 192, (("ps", 2),) + LR, ("rs",))
                    S.dve(lambda e: e.scalar_tensor_tensor(out=qnope, in0=PS[0][:, 0:BLK], scalar=D_MQN_N, in1=rs, op0=ALU.mult, op1=ALU.mult),
                          (("ps", 0), "rs") + LR, ("qnope",))
                    S.dve(lambda e: e.scalar_tensor_tensor(out=tb[0:64, :], in0=PS[1][0:64, 0:BLK], scalar=D_MQN_R[0:64, :], in1=rs[0:64, :], op0=ALU.mult, op1=ALU.mult),
                          (("ps", 1), "rs") + LR, ("tb",))
                    rope_apply(qrope[0:64, :], tb[0:64, :], PS[3], ("ps", 3), PTMLAB, 64, Ct, St, qb, t1, t2, ("tb",), "qrope")
                    parts = [(knope, qnope, 128, (lambda j: ("knope", j // TPB)), "qnope"),
                             (krope, qrope, 64, (lambda j: ("krope", j // TPB)), "qrope")]
                    attn_block(qb, parts, vaug, PTt, o, ("oo",))
                    for i in range(TPB):
                        S.pool(lambda e, i=i: e.tensor_copy(out=obf[:, i, :], in_=o[:, i, :]), (("oo", i),), (("obf", i),))
                    finish_head(l, h, qb, obf, tuple(("obf", i) for i in range(TPB)), h * 128, wgv, ("wb", slg), gs)

        def branch_ssd(s, l):
            AR.reset()
            Xtok = AR.b16(NT, 512)
            Bfm = AR.b16(2, S_LEN); Cfm = AR.b16(2, S_LEN)
            xfm = AR.b16(BLK); Xdt = AR.b16(512); XdD = AR.b16(512); Btok = AR.b16(256)
            MT = [AR.b16(128) for _ in range(2)]
            Sbf = AR.b16(2, 256); ybt = AR.b16(512); wdt = AR.b16(8, 8)
            raw = [AR.f32(BLK + 3) for _ in range(2)]
            acc = AR.f32(BLK); zs = AR.f32(512); CBm = AR.f32(128)
            aTri = [AR.f32(128) for _ in range(2)]
            Dm = AR.f32(128); E = AR.f32(128)
            ydiag = AR.f32(512); yc = AR.f32(512); junk = AR.f32(512); Sst = AR.f32(2, 256)
            sm = AR.f32(64)
            dt_ = sm[:, 0:8]; a_ = sm[:, 8:16]; cs_ = sm[:, 16:24]; ecs = sm[:, 24:32]; etot = sm[:, 32:40]
            dec = sm[:, 40:48]; ssq = sm[:, 48:50]; tmp8 = sm[:, 56:64]
            slz = wslot(); wz = wb_view(slz, 8, 512)
            load_w(slz, win_view(l, OFF["mb_z"], 512), wz)
            S.dma("pool", lambda e: e.dma_start(out=wdt, in_=win_view(l, OFF["mb_dt"], 8)), w=("wdt",))
            S.dve(lambda e: e.memset(Sst, 0.0), (), ("Sst",))
            S.dve(lambda e: e.memset(Sbf, 0.0), (), ("Sbf",))
            for half in range(2):
                slx = wslot(); wx = wb_view(slx, 8, 512)
                load_w(slx, win_view(l, OFF["mb_xbc"] + half * 512, 512), wx)
                for f4 in range(4):
                    fc = half * 4 + f4
                    for n in range(NB):
                        rb = raw[n % 2]; krb = ("raw", n % 2)
                        p = PS[n % 2]; kp = ("ps", n % 2)
                        mm(p[:, 0:BLK], proj_pairs(wx, f4 * 128, 128, n * BLK, BLK), (("wb", slx), ("HT", n)), (kp,))
                        if n == 0:
                            S.dve(lambda e, rb=rb: e.memset(rb[:, 0:3], 0.0), (), (krb,))
                        else:
                            pr = raw[(n - 1) % 2]
                            S.pool(lambda e, rb=rb, pr=pr: e.tensor_copy(out=rb[:, 0:3], in_=pr[:, BLK:BLK + 3]), (("raw", (n - 1) % 2),), (krb,))
                        S.act(lambda e, rb=rb, p=p: e.activation(out=rb[:, 3:3 + BLK], in_=p[:, 0:BLK], func=AF.Copy), (kp,), (krb,))
                        cw = lambda k, fc=fc: pbl("convw", fc * 4 + k)
                        S.dve(lambda e, rb=rb, fc=fc, cw=cw: e.tensor_scalar(out=acc, in0=rb[:, 3:3 + BLK], scalar1=cw(3), scalar2=pbl("convb", fc), op0=ALU.mult, op1=ALU.add),
                              (krb,) + LR, ("acc",))
                        for k in (2, 1, 0):
                            S.dve(lambda e, rb=rb, k=k, cw=cw: e.scalar_tensor_tensor(out=acc, in0=rb[:, k:k + BLK], scalar=cw(k), in1=acc, op0=ALU.mult, op1=ALU.add),
                                  (krb, "acc") + LR, ("acc",))
                        if fc < 4:
                            dst, kd = xfm, "xfm"
                        elif fc < 6:
                            dst, kd = Bfm[:, fc - 4, n * BLK:(n + 1) * BLK], ("Bfm", n)
                        else:
                            dst, kd = Cfm[:, fc - 6, n * BLK:(n + 1) * BLK], ("Cfm", n)
                        S.act(lambda e, dst=dst: e.activation(out=dst, in_=acc, func=AF.Silu), ("acc",), (kd,))
                        if fc < 4:
                            pt = PSB[:, 0:BLK]
                            def fn(e, pt=pt):
                                ins = None
                                for tt in range(TPB):
                                    ins = e.transpose(out=pt[:, tt * 128:(tt + 1) * 128], in_=xfm[:, tt * 128:(tt + 1) * 128], identity=IDB)
                                return ins
                            S.pe(fn, ("xfm",) + CONST_R, (("psb", 0),))
                            S.dve(lambda e, pt=pt, n=n, fc=fc: e.tensor_copy(out=Xtok[:, n * TPB:(n + 1) * TPB, fc * 128:(fc + 1) * 128],
                                                                            in_=pt.rearrange("p (a b) -> p a b", a=TPB)),
                                  (("psb", 0),), (("Xtok", n),))
            sets = [dict(zs=zs, sm=sm, Xdt=Xdt, XdD=XdD, Btok=Btok, CBm=CBm, ydiag=ydiag, yc=yc, junk=junk, ybt=ybt, Dm=Dm, E=E, aTri=aTri, MT=MT)]
            sets.append(dict(zs=AR.f32(512), sm=AR.f32(64), Xdt=AR.b16(512), XdD=AR.b16(512), Btok=AR.b16(256), CBm=AR.f32(128),
                             ydiag=AR.f32(512), yc=AR.f32(512), junk=AR.f32(512), ybt=AR.b16(512), Dm=AR.f32(128), E=AR.f32(128),
                             aTri=[AR.f32(128) for _ in range(2)], MT=[AR.b16(128) for _ in range(2)]))

            def ssd_chunk(c):
                par = (c % 2) if DB_SSD else 0
                T = sets[par]
                zs, sm, Xdt, XdD, Btok, CBm, ydiag, yc, junk, ybt, Dm, E, aTri, MT = (T[k_] for k_ in
                    ("zs", "sm", "Xdt", "XdD", "Btok", "CBm", "ydiag", "yc", "junk", "ybt", "Dm", "E", "aTri", "MT"))
                dt_ = sm[:, 0:8]; a_ = sm[:, 8:16]; cs_ = sm[:, 16:24]; ecs = sm[:, 24:32]; etot = sm[:, 32:40]
                dec = sm[:, 40:48]; ssq = sm[:, 48:50]; tmp8 = sm[:, 56:64]
                n = c // TPB
                tok = slice(c * 128, (c + 1) * 128)
                mm(PS[0][:, 0:512], tok_pairs(wz, 0, 512, c), (("wb", slz), ("HT", n)), (("ps", 0),))
                S.act(lambda e: e.activation(out=zs, in_=PS[0][:, 0:512], func=AF.Silu), (("ps", 0),), (("zs", par),))
                mm(PS[1][:, 0:8], tok_pairs(wdt, 0, 8, c), ("wdt", ("HT", n)), (("ps", 1),))
                S.dve(lambda e: e.tensor_tensor(out=tmp8, in0=PS[1][:, 0:8], in1=pbl("dtb"), op=ALU.add), (("ps", 1),) + LR, (("tmp8", par),))
                S.act(lambda e: e.activation(out=tmp8, in_=tmp8, func=AF.Exp), (("tmp8", par),), (("tmp8", par),))
                S.act(lambda e: e.activation(out=dt_, in_=tmp8, func=AF.Ln, bias=D_ONE, scale=1.0), (("tmp8", par),) + LR, (("dt", par),))
                S.dve(lambda e: e.tensor_tensor(out=a_, in0=dt_, in1=D_NEGA, op=ALU.mult), (("dt", par),) + LR, (("a", par),))
                mm(PS[1][:, 8:16], [(MASKUF, a_)], (("a", par),) + CONST_R, (("ps", 1),))
                mm(PS[1][:, 16:24], [(ONESF, a_)], (("a", par),) + CONST_R, (("ps", 1),))
                S.act(lambda e: e.activation(out=cs_, in_=PS[1][:, 8:16], func=AF.Copy), (("ps", 1),), (("cs", par),))
                S.act(lambda e: e.activation(out=ecs, in_=PS[1][:, 8:16], func=AF.Exp), (("ps", 1),), (("ecs", par),))
                S.act(lambda e: e.activation(out=etot, in_=PS[1][:, 16:24], func=AF.Exp), (("ps", 1),), (("etot", par),))
                S.dve(lambda e: e.tensor_tensor(out=dec, in0=PS[1][:, 16:24], in1=cs_, op=ALU.subtract), (("ps", 1), ("cs", par)), (("dec", par),))
                S.act(lambda e: e.activation(out=dec, in_=dec, func=AF.Exp), (("dec", par),), (("dec", par),))
                for h in range(8):
                    hs = slice(h * 64, (h + 1) * 64)
                    S.dve(lambda e, h=h, hs=hs, c=c: e.tensor_scalar(out=Xdt[:, hs], in0=Xtok[:, c, hs], scalar1=dt_[:, h:h + 1], scalar2=None, op0=ALU.mult),
                          (("Xtok", n), ("dt", par)), (("Xdt", par),))
                    S.pool(lambda e, h=h, hs=hs: e.tensor_scalar(out=XdD[:, hs], in0=Xdt[:, hs], scalar1=dec[:, h:h + 1], scalar2=None, op0=ALU.mult),
                           (("Xdt", par), ("dec", par)), (("XdD", par),))
                def fnb(e, tok=tok):
                    ins = None
                    for g in range(2):
                        ins = e.transpose(out=PSB[:, 512 + g * 128:512 + (g + 1) * 128], in_=Bfm[:, g, tok], identity=IDB)
                    return ins
                S.pe(fnb, (("Bfm", n),) + CONST_R, (("psb", 0),))
                S.act(lambda e: e.activation(out=Btok, in_=PSB[:, 512:768], func=AF.Copy), (("psb", 0),), (("Btok", par),))
                for gr in range(2):
                    mm(PS[2][:, 0:128], [(Bfm[:, gr, tok], Cfm[:, gr, tok])], (("Bfm", n), ("Cfm", n)), (("ps", 2),))
                    S.dve(lambda e: e.tensor_tensor(out=CBm, in0=PS[2][:, 0:128], in1=MASKUF, op=ALU.mult), (("ps", 2),) + CONST_R, (("CBm", par),))
                    for r_ in range(4):
                        h = gr * 4 + r_
                        hs = slice(h * 64, (h + 1) * 64)
                        at = aTri[h % 2]; kat = ("aTri", h % 2, par)
                        S.dve(lambda e, at=at, h=h: e.tensor_scalar(out=at, in0=MASKUF, scalar1=a_[:, h:h + 1], scalar2=None, op0=ALU.mult),
                              (("a", par),) + CONST_R, (kat,))
                        mm(PS[3][:, 0:128], [(ONESF, at), (at, D_NEGONES)], (kat,) + CONST_R, (("ps", 3),))
                        S.dve(lambda e: e.tensor_scalar(out=Dm, in0=PS[3][:, 0:128], scalar1=0.0, scalar2=None, op0=ALU.min), (("ps", 3),), (("Dm", par),))
                        S.act(lambda e: e.activation(out=E, in_=Dm, func=AF.Exp), (("Dm", par),), (("E", par),))
                        mt = MT[h % 2]; kmt = ("MT", h % 2, par)
                        S.dve(lambda e, mt=mt: e.tensor_tensor(out=mt, in0=E, in1=CBm, op=ALU.mult), (("E", par), ("CBm", par)), (kmt,))
                        mm(PS[4][:, hs], [(mt, Xdt[:, hs])], (kmt, ("Xdt", par)), (("ps", 4),))
                    gs_ = slice(gr * 256, (gr + 1) * 256)
                    mm(PS[5][:, gs_], [(Cfm[:, gr, tok], Sbf[:, gr, :])], (("Cfm", n), "Sbf"), (("ps", 5),))
                    mm(PS[6][:, gs_], [(Btok[:, gr * 128:(gr + 1) * 128], XdD[:, gs_])], (("Btok", par), ("XdD", par)), (("ps", 6),))
                for h in range(8):
                    gr, r_ = h // 4, h % 4
                    S.dve(lambda e, h=h, gr=gr, r_=r_: e.scalar_tensor_tensor(out=Sst[:, gr, r_ * 64:(r_ + 1) * 64], in0=Sst[:, gr, r_ * 64:(r_ + 1) * 64],
                                                                             scalar=etot[:, h:h + 1], in1=PS[6][:, h * 64:(h + 1) * 64], op0=ALU.mult, op1=ALU.add),
                          ("Sst", ("etot", par), ("ps", 6)), ("Sst",))
                S.pool(lambda e: e.tensor_copy(out=Sbf, in_=Sst), ("Sst",), ("Sbf",))
                S.act(lambda e: e.activation(out=ydiag, in_=PS[4][:, 0:512], func=AF.Copy), (("ps", 4),), (("ydiag", par),))
                for h in range(8):
                    hs = slice(h * 64, (h + 1) * 64)
                    S.dve(lambda e, h=h, hs=hs: e.scalar_tensor_tensor(out=yc[:, hs], in0=PS[5][:, hs], scalar=ecs[:, h:h + 1], in1=ydiag[:, hs], op0=ALU.mult, op1=ALU.add),
                          (("ps", 5), ("ecs", par), ("ydiag", par)), (("yc", par),))
                S.pool(lambda e, c=c: e.tensor_tensor(out=ydiag, in0=Xtok[:, c, :], in1=pbl("mbd"), op=ALU.mult), (("Xtok", n), ("yc", par)) + LR, (("ydiag", par),))
                S.dve(lambda e: e.tensor_tensor(out=yc, in0=yc, in1=ydiag, op=ALU.add), (("yc", par), ("ydiag", par)), (("yc", par),))
                S.dve(lambda e: e.tensor_tensor(out=yc, in0=yc, in1=zs, op=ALU.mult), (("yc", par), ("zs", par)), (("yc", par),))
                S.act(lambda e: e.activation(out=junk, in_=yc, func=AF.Square), (("yc", par),), (("junk", par),))
                S.dve(lambda e: e.tensor_reduce(out=ssq, in_=junk.rearrange("p (a b) -> p a b", a=2), axis=AX.X, op=ALU.add), (("junk", par),), (("ssq", par),))
                rsqrt_act(ssq, ssq, 1.0 / 256, (("ssq", par),) + LR, (("ssq", par),))
                for g2 in range(2):
                    gs_ = slice(g2 * 256, (g2 + 1) * 256)
                    S.dve(lambda e, g2=g2, gs_=gs_: e.scalar_tensor_tensor(out=ybt[:, gs_], in0=yc[:, gs_], scalar=ssq[:, g2:g2 + 1], in1=pbl("mbnorm")[:, gs_], op0=ALU.mult, op1=ALU.mult),
                          (("yc", par), ("ssq", par)) + LR, (("ybt", par),))
                def fnt(e):
                    ins = None
                    for k in range(4):
                        ins = e.transpose(out=PSB[:, k * 128:(k + 1) * 128], in_=ybt[:, k * 128:(k + 1) * 128], identity=IDB)
                    return ins
                S.pe(fnt, (("ybt", par),) + CONST_R, (("psb", 0),))
                S.act(lambda e, tok=tok: e.activation(out=YB[:, :, tok], in_=PSB[:, 0:512].rearrange("p (a b) -> p a b", a=4), func=AF.Copy),
                      (("psb", 0),), (("YB", n),))


            for c in range(NT):
                ssd_chunk(c)

        def branch_s5(s, l):
            AR.reset()
            cj = AR.f32(16, 128); sj = AR.f32(16, 128)
            Bre = AR.b16(16, 128); Bim = AR.b16(16, 128); Cre = AR.b16(16, 128); Cim = AR.b16(16, 128)
            smp = AR.f32(192)
            r_ = smp[:, 0:16]; th = smp[:, 16:32]; rc = smp[:, 32:48]; rsn = smp[:, 48:64]; nrs = smp[:, 64:80]
            tA = smp[:, 80:96]; tB = smp[:, 96:112]; tC = smp[:, 112:128]; tD = smp[:, 128:144]; tE = smp[:, 144:160]; tF = smp[:, 160:176]
            stt = AR.f32(16, 2)
            mark = AR.off
            vtab = AR.f32(16, 128); ttab = AR.f32(16, 128)
            S.act(lambda e: e.activation(out=tA, in_=pbl("lstep"), func=AF.Exp), LR, ("s5a",))
            S.dve(lambda e: e.tensor_tensor(out=tB, in0=pbl("lamre"), in1=tA, op=ALU.mult), ("s5a",) + LR, ("s5b",))
            S.act(lambda e: e.activation(out=r_, in_=tB, func=AF.Exp), ("s5b",), ("s5r",))
            S.dve(lambda e: e.tensor_tensor(out=th, in0=pbl("lamim"), in1=tA, op=ALU.mult), ("s5a",) + LR, ("s5th",))
            for mt in range(16):
                S.dve(lambda e, mt=mt: e.tensor_scalar(out=vtab[:, mt, :], in0=cf("jrow"), scalar1=th[:, mt:mt + 1], scalar2=float(1.0 / TWO_PI), op0=ALU.mult, op1=ALU.mult),
                      ("s5th",) + CONST_R, ("s5v",))
            sincos(vtab, ttab, sj, cj, (), "s5v", "s5t", "s5sj", "s5cj")
            S.dve(lambda e: e.tensor_scalar(out=tC, in0=th, scalar1=float(128.0 / TWO_PI), scalar2=None, op0=ALU.mult), ("s5th",), ("s5c",))
            sincos(tC, tD, tE, tF, (), "s5c", "s5d", "s5s128", "s5c128")
            S.dve(lambda e: e.tensor_tensor(out=rc, in0=r_, in1=tF, op=ALU.mult), ("s5r", "s5c128"), ("s5rc",))
            S.dve(lambda e: e.tensor_tensor(out=rsn, in0=r_, in1=tE, op=ALU.mult), ("s5r", "s5s128"), ("s5rs",))
            S.dve(lambda e: e.tensor_scalar(out=nrs, in0=rsn, scalar1=-1.0, scalar2=None, op0=ALU.mult), ("s5rs",), ("s5nrs",))
            MC = ("s5r", "s5rc", "s5rs", "s5nrs", "s5sj", "s5cj")
            S.dma("pool", lambda e: e.dma_start(out=Cre, in_=ds5C[l, 0]), w=("s5Cre",))
            S.dma("pool", lambda e: e.dma_start(out=Cim, in_=ds5C[l, 1]), w=("s5Cim",))
            AR.off = mark + 2 * 16 * 128
            W = 256
            names = ["lr", "li", "ls", "v", "t", "sn", "cs", "abr", "abi", "den", "fr", "fi", "braw_r", "braw_i", "u1", "u2"]
            Q = {nm: AR.f32(W) for nm in names}
            for q in range(2048 // W):
                cs_ = slice(q * W, (q + 1) * W)
                k = "s5q"
                for i, nm in enumerate(["lr", "li", "ls"]):
                    S.dma("sp", lambda e, nm=nm, i=i, cs_=cs_: e.dma_start(out=Q[nm], in_=ds5rep[l, i, :, cs_]), w=(("s5in", nm),))
                mts = slice(q * (W // 128), (q + 1) * (W // 128))
                S.dma("sp", lambda e, mts=mts: e.dma_start(out=Q["braw_r"].rearrange("p (a b) -> p a b", b=128), in_=ds5B[l, 0, :, mts, :]), w=(("s5in", "br"),))
                S.dma("sp", lambda e, mts=mts: e.dma_start(out=Q["braw_i"].rearrange("p (a b) -> p a b", b=128), in_=ds5B[l, 1, :, mts, :]), w=(("s5in", "bi"),))
                S.act(lambda e: e.activation(out=Q["ls"], in_=Q["ls"], func=AF.Exp), (("s5in", "ls"),), (("s5in", "ls"),))
                S.dve(lambda e: e.tensor_tensor(out=Q["u1"], in0=Q["lr"], in1=Q["ls"], op=ALU.mult), (("s5in", "lr"), ("s5in", "ls")), ("q_u1",))
                S.act(lambda e: e.activation(out=Q["u1"], in_=Q["u1"], func=AF.Exp), ("q_u1",), ("q_u1",))
                S.dve(lambda e: e.tensor_tensor(out=Q["v"], in0=Q["li"], in1=Q["ls"], op=ALU.mult), (("s5in", "li"), ("s5in", "ls")), ("q_v",))
                S.dve(lambda e: e.tensor_scalar(out=Q["v"], in0=Q["v"], scalar1=float(1.0 / TWO_PI), scalar2=None, op0=ALU.mult), ("q_v",), ("q_v",))
                sincos(Q["v"], Q["t"], Q["sn"], Q["cs"], (), "q_v", "q_t", "q_sn", "q_cs")
                S.dve(lambda e: e.tensor_tensor(out=Q["abr"], in0=Q["u1"], in1=Q["cs"], op=ALU.mult), ("q_u1", "q_cs"), ("q_abr",))
                S.dve(lambda e: e.tensor_scalar(out=Q["abr"], in0=Q["abr"], scalar1=-1.0, scalar2=None, op0=ALU.add), ("q_abr",), ("q_abr",))
                S.dve(lambda e: e.tensor_tensor(out=Q["abi"], in0=Q["u1"], in1=Q["sn"], op=ALU.mult), ("q_u1", "q_sn"), ("q_abi",))
                S.dve(lambda e: e.tensor_tensor(out=Q["den"], in0=Q["lr"], in1=Q["lr"], op=ALU.mult), (("s5in", "lr"),), ("q_den",))
                S.dve(lambda e: e.tensor_tensor(out=Q["u2"], in0=Q["li"], in1=Q["li"], op=ALU.mult), (("s5in", "li"),), ("q_u2",))
                S.dve(lambda e: e.tensor_tensor(out=Q["den"], in0=Q["den"], in1=Q["u2"], op=ALU.add), ("q_den", "q_u2"), ("q_den",))
                S.dve(lambda e: e.reciprocal(out=Q["den"], in_=Q["den"]), ("q_den",), ("q_den",))
                S.dve(lambda e: e.tensor_tensor(out=Q["fr"], in0=Q["abr"], in1=Q["lr"], op=ALU.mult), ("q_abr", ("s5in", "lr")), ("q_fr",))
                S.dve(lambda e: e.tensor_tensor(out=Q["u2"], in0=Q["abi"], in1=Q["li"], op=ALU.mult), ("q_abi", ("s5in", "li"), "q_den"), ("q_u2",))
                S.dve(lambda e: e.tensor_tensor(out=Q["fr"], in0=Q["fr"], in1=Q["u2"], op=ALU.add), ("q_fr", "q_u2"), ("q_fr",))
                S.dve(lambda e: e.tensor_tensor(out=Q["fr"], in0=Q["fr"], in1=Q["den"], op=ALU.mult), ("q_fr", "q_den"), ("q_fr",))
                S.dve(lambda e: e.tensor_tensor(out=Q["fi"], in0=Q["abi"], in1=Q["lr"], op=ALU.mult), ("q_abi", ("s5in", "lr")), ("q_fi",))
                S.dve(lambda e: e.tensor_tensor(out=Q["u2"], in0=Q["abr"], in1=Q["li"], op=ALU.mult), ("q_abr", ("s5in", "li"), "q_fr"), ("q_u2",))
                S.dve(lambda e: e.tensor_tensor(out=Q["fi"], in0=Q["fi"], in1=Q["u2"], op=ALU.subtract), ("q_fi", "q_u2"), ("q_fi",))
                S.dve(lambda e: e.tensor_tensor(out=Q["fi"], in0=Q["fi"], in1=Q["den"], op=ALU.mult), ("q_fi", "q_den"), ("q_fi",))
                brv = Bre.rearrange("p a b -> p (a b)")[:, cs_]; biv = Bim.rearrange("p a b -> p (a b)")[:, cs_]
                S.dve(lambda e: e.tensor_tensor(out=Q["u1"], in0=Q["fr"], in1=Q["braw_r"], op=ALU.mult), ("q_fr", ("s5in", "br"), "q_abr", "q_abi"), ("q_u1",))
                S.dve(lambda e: e.tensor_tensor(out=Q["u2"], in0=Q["fi"], in1=Q["braw_i"], op=ALU.mult), ("q_fi", ("s5in", "bi")), ("q_u2",))
                S.dve(lambda e, brv=brv: e.tensor_tensor(out=brv, in0=Q["u1"], in1=Q["u2"], op=ALU.subtract), ("q_u1", "q_u2"), ("s5Bre",))
                S.dve(lambda e: e.tensor_tensor(out=Q["u1"], in0=Q["fr"], in1=Q["braw_i"], op=ALU.mult), ("q_fr", ("s5in", "bi"), "s5Bre"), ("q_u1",))
                S.dve(lambda e: e.tensor_tensor(out=Q["u2"], in0=Q["fi"], in1=Q["braw_r"], op=ALU.mult), ("q_fi", ("s5in", "br"), "s5Bre"), ("q_u2",))
                S.dve(lambda e, biv=biv: e.tensor_tensor(out=biv, in0=Q["u1"], in1=Q["u2"], op=ALU.add), ("q_u1", "q_u2"), ("s5Bim",))
            S.barrier()
            AR.off = mark
            ufm = AR.b16(4, BLK); ygb = AR.b16(4, BLK)
            hre_ = [AR.b16(BLK) for _ in range(2)]; nhim_ = [AR.b16(BLK) for _ in range(2)]
            t1_ = [AR.f32(BLK)] * 2; t2_ = [AR.f32(BLK)] * 2; wre_ = [AR.f32(BLK) for _ in range(2)]; wim_ = [AR.f32(BLK) for _ in range(2)]
            gre_ = [AR.f32(BLK) for _ in range(2)]; gim_ = [AR.f32(BLK) for _ in range(2)]
            yp = AR.f32(BLK); x2 = AR.f32(BLK); sgl = AR.f32(BLK); gsv = AR.f32(BLK)
            slu = wslot(); wu = wb_view(slu, 8, 512); load_w(slu, win_view(l, OFF["s5_u"], 512), wu)
            slg = wslot(); wg = wb_view(slg, 8, 512); load_w(slg, win_view(l, OFF["s5_gate"], 512), wg)
            slw = wslot(); wglu = wb_view(slw, 4, 512); load_w(slw, dwglu[l].rearrange("(kc p) c -> p kc c", p=128), wglu)
            GK = 1.5957691216057308
            for n in range(NB):
                for c4 in range(4):
                    p = PS[c4 % 2]; kp = ("ps", c4 % 2)
                    mm(p[:, 0:BLK], proj_pairs(wu, c4 * 128, 128, n * BLK, BLK), (("wb", slu), ("HT", n)), (kp,))
                    S.act(lambda e, c4=c4, p=p: e.activation(out=ufm[:, c4, :], in_=p[:, 0:BLK], func=AF.Copy), (kp,), (("ufm", c4),))
                for mt in range(16):
                    uc = mt // 4
                    pb2 = (mt % 2) if DB_S5 else 0
                    hre, nhim, t1, t2, wre, wim, gre, gim = hre_[pb2], nhim_[pb2], t1_[pb2], t2_[pb2], wre_[pb2], wim_[pb2], gre_[pb2], gim_[pb2]
                    K = lambda nm, pb2=pb2: (nm, 0 if nm in ('t1', 't2') else pb2)
                    cjb = cj[:, mt:mt + 1, :].broadcast_to([128, TPB, 128])
                    sjb = sj[:, mt:mt + 1, :].broadcast_to([128, TPB, 128])
                    v3 = lambda ap: ap.rearrange("p (a b) -> p a b", a=TPB)
                    pr = PS[2 + (mt % 2) * 2]; pi = PS[3 + (mt % 2) * 2]
                    kpr = ("ps", 2 + (mt % 2) * 2); kpi = ("ps", 3 + (mt % 2) * 2)
                    mm(pr[:, 0:BLK], [(Bre[:, mt, :], ufm[:, uc, :])], ("s5Bre", ("ufm", uc)), (kpr,))
                    mm(pi[:, 0:BLK], [(Bim[:, mt, :], ufm[:, uc, :])], ("s5Bim", ("ufm", uc)), (kpi,))
                    S.dve(lambda e, pr=pr, cjb=cjb, t1=t1: e.tensor_tensor(out=v3(t1), in0=v3(pr[:, 0:BLK]), in1=cjb, op=ALU.mult), (kpr,) + MC, (K("t1"),))
                    S.dve(lambda e, pi=pi, sjb=sjb, t2=t2: e.tensor_tensor(out=v3(t2), in0=v3(pi[:, 0:BLK]), in1=sjb, op=ALU.mult), (kpi,) + MC, (K("t2"),))
                    S.pool(lambda e, wre=wre, t1=t1, t2=t2: e.tensor_tensor(out=wre, in0=t1, in1=t2, op=ALU.add), (K("t1"), K("t2")), (K("wre"),))
                    S.dve(lambda e, pi=pi, cjb=cjb, t1=t1: e.tensor_tensor(out=v3(t1), in0=v3(pi[:, 0:BLK]), in1=cjb, op=ALU.mult), (kpi, K("wre")) + MC, (K("t1"),))
                    S.dve(lambda e, pr=pr, sjb=sjb, t2=t2: e.tensor_tensor(out=v3(t2), in0=v3(pr[:, 0:BLK]), in1=sjb, op=ALU.mult), (kpr, K("wre")) + MC, (K("t2"),))
                    S.pool(lambda e, wim=wim, t1=t1, t2=t2: e.tensor_tensor(out=wim, in0=t1, in1=t2, op=ALU.subtract), (K("t1"), K("t2")), (K("wim"),))
                    for k in range(TPB):
                        ck = n * TPB + k
                        c0 = k * 128
                        if ck > 0:
                            if k == 0:
                                lre, lim = stt[:, mt, 0:1], stt[:, mt, 1:2]
                            else:
                                lre, lim = gre[:, c0 - 1:c0], gim[:, c0 - 1:c0]
                            S.dve(lambda e, lre=lre, c0=c0, mt=mt, wre=wre: e.scalar_tensor_tensor(out=wre[:, c0:c0 + 1], in0=lre, scalar=rc[:, mt:mt + 1], in1=wre[:, c0:c0 + 1], op0=ALU.mult, op1=ALU.add),
                                  (K("wre"), K("gre"), ("stt", mt)) + MC, (K("wre"),))
                            S.dve(lambda e, lim=lim, c0=c0, mt=mt, wre=wre: e.scalar_tensor_tensor(out=wre[:, c0:c0 + 1], in0=lim, scalar=nrs[:, mt:mt + 1], in1=wre[:, c0:c0 + 1], op0=ALU.mult, op1=ALU.add),
                                  (K("wre"), K("gim"), ("stt", mt)) + MC, (K("wre"),))
                            S.dve(lambda e, lre=lre, c0=c0, mt=mt, wim=wim: e.scalar_tensor_tensor(out=wim[:, c0:c0 + 1], in0=lre, scalar=rsn[:, mt:mt + 1], in1=wim[:, c0:c0 + 1], op0=ALU.mult, op1=ALU.add),
                                  (K("wim"), K("gre"), ("stt", mt)) + MC, (K("wim"),))
                            S.dve(lambda e, lim=lim, c0=c0, mt=mt, wim=wim: e.scalar_tensor_tensor(out=wim[:, c0:c0 + 1], in0=lim, scalar=rc[:, mt:mt + 1], in1=wim[:, c0:c0 + 1], op0=ALU.mult, op1=ALU.add),
                                  (K("wim"), K("gim"), ("stt", mt)) + MC, (K("wim"),))
                        rb = r_[:, mt:mt + 1].broadcast_to([128, 128])
                        S.dve(lambda e, c0=c0, rb=rb, gre=gre, wre=wre: e.tensor_tensor_scan(out=gre[:, c0:c0 + 128], data0=rb, data1=wre[:, c0:c0 + 128], initial=0.0, op0=ALU.mult, op1=ALU.add),
                              (K("wre"),) + MC, (K("gre"),))
                        S.dve(lambda e, c0=c0, rb=rb, gim=gim, wim=wim: e.tensor_tensor_scan(out=gim[:, c0:c0 + 128], data0=rb, data1=wim[:, c0:c0 + 128], initial=0.0, op0=ALU.mult, op1=ALU.add),
                              (K("wim"),) + MC, (K("gim"),))
                    S.act(lambda e, mt=mt, gre=gre: e.activation(out=stt[:, mt, 0:1], in_=gre[:, BLK - 1:BLK], func=AF.Copy), (K("gre"),), (("stt", mt),))
                    S.act(lambda e, mt=mt, gim=gim: e.activation(out=stt[:, mt, 1:2], in_=gim[:, BLK - 1:BLK], func=AF.Copy), (K("gim"),), (("stt", mt),))
                    S.dve(lambda e, cjb=cjb, t1=t1, gre=gre: e.tensor_tensor(out=v3(t1), in0=v3(gre), in1=cjb, op=ALU.mult), (K("gre"),) + MC, (K("t1"),))
                    S.pool(lambda e, sjb=sjb, t2=t2, gim=gim: e.tensor_tensor(out=v3(t2), in0=v3(gim), in1=sjb, op=ALU.mult), (K("gim"),) + MC, (K("t2"),))
                    S.dve(lambda e, hre=hre, t1=t1, t2=t2: e.tensor_tensor(out=hre, in0=t1, in1=t2, op=ALU.subtract), (K("t1"), K("t2")), (K("hre"),))
                    S.dve(lambda e, sjb=sjb, t1=t1, gre=gre: e.tensor_tensor(out=v3(t1), in0=v3(gre), in1=sjb, op=ALU.mult), (K("gre"), K("hre")) + MC, (K("t1"),))
                    S.pool(lambda e, cjb=cjb, t2=t2, gim=gim: e.tensor_tensor(out=v3(t2), in0=v3(gim), in1=cjb, op=ALU.mult), (K("gim"), K("hre")) + MC, (K("t2"),))
                    S.dve(lambda e, nhim=nhim, t1=t1, t2=t2: e.scalar_tensor_tensor(out=nhim, in0=t1, scalar=-1.0, in1=t2, op0=ALU.mult, op1=ALU.subtract), (K("t1"), K("t2")), (K("nhim"),))
                    py = PS[6]
                    def fny(e, mt=mt, py=py, hre=hre, nhim=nhim):
                        e.matmul(py[:, 0:BLK], lhsT=Cre[:, mt, :], rhs=hre, start=(mt % 4 == 0), stop=False)
                        return e.matmul(py[:, 0:BLK], lhsT=Cim[:, mt, :], rhs=nhim, start=False, stop=(mt % 4 == 3))
                    S.pe(fny, (K("hre"), K("nhim"), "s5Cre", "s5Cim"), (("ps", 6),))
                    if mt % 4 == 3:
                        c4 = mt // 4
                        S.dve(lambda e, c4=c4, py=py: e.scalar_tensor_tensor(out=yp, in0=ufm[:, c4, :], scalar=pbl("s5d", c4), in1=py[:, 0:BLK], op0=ALU.mult, op1=ALU.add),
                              (("ufm", c4), ("ps", 6)) + LR, ("yp",))
                        S.pool(lambda e: e.tensor_tensor(out=x2, in0=yp, in1=yp, op=ALU.mult), ("yp",), ("x2",))
                        S.dve(lambda e: e.tensor_scalar(out=x2, in0=x2, scalar1=float(GK * 0.044715), scalar2=float(GK), op0=ALU.mult, op1=ALU.add), ("x2",), ("x2",))
                        S.dve(lambda e: e.tensor_tensor(out=x2, in0=x2, in1=yp, op=ALU.mult), ("x2", "yp"), ("x2",))
                        S.act(lambda e: e.activation(out=x2, in_=x2, func=AF.Sigmoid), ("x2",), ("x2",))
                        S.dve(lambda e, c4=c4: e.tensor_tensor(out=ygb[:, c4, :], in0=yp, in1=x2, op=ALU.mult), ("yp", "x2"), (("ygb", c4),))
                for c4 in range(4):
                    pg = PS[c4 % 2]; kpg = ("ps", c4 % 2)
                    mm(pg[:, 0:BLK], [(wglu[:, k4, c4 * 128:(c4 + 1) * 128], ygb[:, k4, :]) for k4 in range(4)],
                       (("wb", slw),) + tuple(("ygb", k4) for k4 in range(4)), (kpg,))
                    S.act(lambda e, c4=c4, pg=pg: e.activation(out=sgl, in_=pg[:, 0:BLK], func=AF.Sigmoid, bias=pbl("s5bglu", c4), scale=1.0), (kpg,) + LR, ("sgl",))
                    pgt = PS[2 + c4 % 2]; kpgt = ("ps", 2 + c4 % 2)
                    mm(pgt[:, 0:BLK], proj_pairs(wg, c4 * 128, 128, n * BLK, BLK), (("wb", slg), ("HT", n)), (kpgt,))
                    S.act(lambda e, pgt=pgt: e.activation(out=gsv, in_=pgt[:, 0:BLK], func=AF.Silu), (kpgt,), ("gsv",))
                    S.dve(lambda e, c4=c4: e.tensor_tensor(out=sgl, in0=sgl, in1=ygb[:, c4, :], op=ALU.mult), ("sgl", ("ygb", c4)), ("sgl",))
                    S.dve(lambda e, c4=c4, n=n: e.tensor_tensor(out=YB[:, c4, n * BLK:(n + 1) * BLK], in0=sgl, in1=gsv, op=ALU.mult), ("sgl", "gsv"), (("YB", n),))

        BR = {0: branch_da, 1: branch_ssd, 2: branch_s5, 3: branch_mla}
        for s in range(NSEQ):
            for l in range(L):
                layer_setup(l)
                stage0(s, l)
                S.barrier()
                first = True
                for b in range(4):
                    if b not in branches:
                        continue
                    BR[b](s, l)
                    S.barrier()
                    merge(s, l, b, first)
                    first = False
                if not first:
                    S.barrier()
                if first:
                    S.dve(lambda e: e.memset(MG, 0.0), (), tuple(("MG", f, n) for f in range(8) for n in range(NB)))
                if debug and s == 0 and l == 0:
                    for nm, src_, dst_ in (("ht", HT, dbg_ht), ("yb", YB, dbg_yb), ("mg", MG, dbg_mg)):
                        for q in range(src_.shape[1]):
                            dtmp = ARN[:, 0:S_LEN]
                            S.dve(lambda e, src_=src_, q=q, dtmp=dtmp: e.tensor_copy(out=dtmp, in_=src_[:, q, :]), (), ("dtmp",))
                            S.dma("sp", lambda e, dst_=dst_, q=q, dtmp=dtmp: e.dma_start(out=dst_[:, q * S_LEN:(q + 1) * S_LEN], in_=dtmp), r=("dtmp",), w=())
                    S.barrier()
                outproj(s, l)
                S.barrier()
        S.emit()
    return nc


_PROG_CACHE = {}


def run_cores(inputs, S_LEN, NSEQ, DEPTH, n_cores, branches=(0, 1, 2, 3), debug=False):
    key = (S_LEN, NSEQ, DEPTH, tuple(branches), debug)
    if key not in _PROG_CACHE:
        _PROG_CACHE[key] = build_program(S_LEN, NSEQ, DEPTH, branches, debug=debug)
    nc = _PROG_CACHE[key]
    f32 = lambda a: np.ascontiguousarray(np.asarray(a, dtype=np.float32))
    pb, s5rep, s5B, s5C = host_params({k: np.asarray(v) for k, v in inputs.items()}, DEPTH)
    shared = {
        "w_in": f32(inputs["w_in"])[:DEPTH], "w_br": f32(inputs["w_br"])[:DEPTH], "w_out": f32(inputs["w_out"])[:DEPTH],
        "w_uq": f32(inputs["mla_w_uq"])[:DEPTH], "w_ukv": f32(inputs["mla_w_ukv"])[:DEPTH], "w_glu": f32(inputs["s5_w_glu"])[:DEPTH],
        "pblob": pb, "cblob": host_consts(), "s5rep": s5rep, "s5B": s5B, "s5C": s5C,
    }
    x = f32(inputs["x"])
    pos = np.ascontiguousarray(np.asarray(inputs["positions"], dtype=np.int32))
    in_maps = []
    for c in range(n_cores):
        m = dict(shared)
        m["x"] = np.ascontiguousarray(x[c * NSEQ:(c + 1) * NSEQ])
        m["pos"] = np.ascontiguousarray(pos[c * NSEQ:(c + 1) * NSEQ])
        in_maps.append(m)
    res = run_bass_kernel_spmd(nc, in_maps, core_ids=list(range(n_cores)))
    outs = [np.asarray(r["out"]).reshape(NSEQ, S_LEN, D_MODEL) for r in res.results]
    if debug:
        global DBG
        DBG = {k: np.asarray(res.results[0][k]) for k in ("dbg_ht", "dbg_yb", "dbg_mg")}
    return np.concatenate(outs, axis=0).astype(np.float32)


def kernel(**inputs):
    return run_cores(inputs, 2048, 2, 2, 8)
```
